# Optimizing a Trainium2 kernel written in Bass

```python
import math
import jax, jax.numpy as jnp
from jax import lax
import numpy as np

D_MODEL = 1024
BATCH = 4
SEQ = 8192
DEPTH = 2

CTX_LEN = 256
GRID_W = 64
D_MIX = D_MODEL
D_CONV = D_MIX // 2
D_RWKV = D_MIX - D_CONV
RWKV_HEAD = 64
RWKV_HEADS = D_RWKV // RWKV_HEAD
W_LORA = 64
A_LORA = 64
G_LORA = 128
N_DIR = 2
D_FF = ((8 * D_MODEL // 3 + 127) // 128) * 128
PROJ = 3 * D_CONV + 3 * D_RWKV + W_LORA + A_LORA + G_LORA
RMS_EPS = 1e-6
GN_EPS = 64e-5
DECAY_SCALE = math.exp(-0.5)

kernel_name = "hybrid_conv_rwkv7_convffn_dit"


def rms_norm(x, g):
    xf = x.astype(jnp.float32)
    y = xf * lax.rsqrt(jnp.mean(xf * xf, axis=-1, keepdims=True) + RMS_EPS)
    return (y * g.astype(jnp.float32)).astype(x.dtype)


def modulate(h, shift, scale):
    return h * (1 + scale) + shift


def split_proj(p):
    sizes = (D_CONV, D_CONV, D_CONV, D_RWKV, D_RWKV, D_RWKV, W_LORA, A_LORA, G_LORA)
    idx = [int(i) for i in np.cumsum(sizes)[:-1]]
    return jnp.split(p, idx, axis=-1)


def shortconv1d(u, w):
    up = jnp.pad(u, ((0, 0), (1, 1), (0, 0)))
    return up[:, :-2] * w[0] + up[:, 1:-1] * w[1] + up[:, 2:] * w[2]


def dwconv_grid(u, w, b, rows, cols):
    bsz, t, ch = u.shape
    img = u.reshape(bsz, rows, cols, ch)
    out = lax.conv_general_dilated(img, w[:, :, None, :].astype(u.dtype), (1, 1), "SAME",
                                   dimension_numbers=("NHWC", "HWIO", "NHWC"),
                                   feature_group_count=ch)
    return out.reshape(bsz, t, ch) + b


def to_heads(t):
    return t.astype(jnp.float32).reshape(t.shape[0], t.shape[1], RWKV_HEADS, RWKV_HEAD)


def wkv_scan(state0, r, decay, kk, b, v, k, reverse):
    def step(S, inp):
        r_t, w_t, kk_t, b_t, v_t, k_t = inp
        sa = jnp.einsum("bhvk,bhk->bhv", S, kk_t)
        S = (S * w_t[:, :, None, :] - sa[..., None] * b_t[:, :, None, :]
             + v_t[..., None] * k_t[:, :, None, :])
        return S, jnp.einsum("bhvk,bhk->bhv", S, r_t)
    xs = tuple(jnp.swapaxes(t, 0, 1) for t in (r, decay, kk, b, v, k))
    S, ys = lax.scan(step, state0, xs, reverse=reverse)
    return S, jnp.swapaxes(ys, 0, 1)


def rwkv_scans(r, k, v, w_lo, a_lo, lp, init):
    kk = to_heads(k * lp["k_k"])
    kk = kk * lax.rsqrt(jnp.maximum(jnp.sum(kk * kk, axis=-1, keepdims=True), 1e-24))
    rh, vh = to_heads(r), to_heads(v)
    y_sum = None
    finals = []
    for d in range(N_DIR):
        decay = jnp.exp(-DECAY_SCALE * jax.nn.sigmoid(lp["w0"][d] + jnp.tanh(w_lo) @ lp["w_up"][d]))
        a = jax.nn.sigmoid(lp["a0"][d] + a_lo @ lp["a_up"][d])
        k_d = k * (1 + (a - 1) * lp["k_a"])
        state, y = wkv_scan(init[d], rh, to_heads(decay), kk, kk * to_heads(a), vh, to_heads(k_d),
                            reverse=(d == 1))
        y_sum = y if y_sum is None else y_sum + y
        finals.append(state)
    return y_sum, finals


def rwkv_output(y, r, k, v, g_lo, lp):
    bsz, t = y.shape[:2]
    mu = jnp.mean(y, axis=-1, keepdims=True)
    var = jnp.mean(jnp.square(y - mu), axis=-1, keepdims=True)
    yn = ((y - mu) * lax.rsqrt(var + GN_EPS)).reshape(bsz, t, D_RWKV) * lp["ln_g"] + lp["ln_b"]
    bonus = jnp.sum(to_heads(r) * to_heads(k) * lp["r_k"].astype(jnp.float32), axis=-1, keepdims=True) * to_heads(v)
    gate = jax.nn.sigmoid(g_lo) @ lp["g_up"]
    return ((yn + bonus.reshape(bsz, t, D_RWKV)) * gate).astype(r.dtype)


def mixer_block(h, lp, init, need_output):
    cb, cc, cx, r, k, v, w_lo, a_lo, g_lo = split_proj(h @ lp["w_in"])
    y, finals = rwkv_scans(r, k, v, w_lo, a_lo, lp, init)
    if not need_output:
        return None, finals
    y_conv = cb * shortconv1d(cc * cx, lp["conv_a_w"])
    y_rwkv = rwkv_output(y, r, k, v, g_lo, lp)
    return jnp.concatenate([y_conv, y_rwkv], axis=-1) @ lp["w_out"], finals


def conv_ffn(h, lp, rows, cols):
    gate, val = jnp.split(h @ lp["ffn_w_up"], 2, axis=-1)
    gate = dwconv_grid(gate, lp["ffn_conv_w"], lp["ffn_conv_b"], rows, cols)
    return (jax.nn.silu(gate) * val) @ lp["ffn_w_down"]


def setup_inputs(seed: int = 0) -> dict:
    key = jax.random.key(seed)
    ks = jax.random.split(key, 32)
    L, D = DEPTH, D_MODEL

    def nrm(k, shape, s):
        return jax.random.normal(k, shape, jnp.float32) * s

    return {
        "x": nrm(ks[0], (BATCH, SEQ, D), 1.0),
        "c": nrm(ks[1], (BATCH, D), 1.0),
        "ctx": nrm(ks[2], (BATCH, CTX_LEN, D), 1.0),
        "c_ctx": nrm(ks[3], (D,), 1.0),
        "ada_w": nrm(ks[4], (L, D, 6 * D), 0.5 * D ** -0.5),
        "ada_b": nrm(ks[5], (L, 6 * D), 0.01),
        "norm1_g": 1.0 + nrm(ks[6], (L, D), 0.1),
        "norm2_g": 1.0 + nrm(ks[7], (L, D), 0.1),
        "w_in": nrm(ks[8], (L, D, PROJ), D ** -0.5),
        "conv_a_w": nrm(ks[9], (L, 3, D_CONV), 3 ** -0.5),
        "rw_w0": nrm(ks[10], (L, N_DIR, D_RWKV), 0.5),
        "rw_w_up": nrm(ks[11], (L, N_DIR, W_LORA, D_RWKV), 0.5 * W_LORA ** -0.5),
        "rw_a0": nrm(ks[12], (L, N_DIR, D_RWKV), 0.5),
        "rw_a_up": nrm(ks[13], (L, N_DIR, A_LORA, D_RWKV), 0.5 * A_LORA ** -0.5),
        "rw_k_k": 0.85 + nrm(ks[14], (L, D_RWKV), 0.05),
        "rw_k_a": 1.0 + nrm(ks[15], (L, D_RWKV), 0.05),
        "rw_r_k": nrm(ks[16], (L, RWKV_HEADS, RWKV_HEAD), 0.1),
        "rw_g_up": nrm(ks[17], (L, G_LORA, D_RWKV), G_LORA ** -0.5),
        "rw_ln_g": 1.0 + nrm(ks[18], (L, D_RWKV), 0.1),
        "rw_ln_b": nrm(ks[19], (L, D_RWKV), 0.01),
        "w_out": nrm(ks[20], (L, D_MIX, D), D_MIX ** -0.5),
        "ffn_w_up": nrm(ks[21], (L, D, 2 * D_FF), D ** -0.5),
        "ffn_conv_w": nrm(ks[22], (L, 3, 3, D_FF), 1.0 / 3.0),
        "ffn_conv_b": nrm(ks[23], (L, D_FF), 0.01),
        "ffn_w_down": nrm(ks[24], (L, D_FF, D), D_FF ** -0.5),
        "final_g": 1.0 + nrm(ks[25], (D,), 0.1),
    }


def reference(x, c, ctx, c_ctx, ada_w, ada_b, norm1_g, norm2_g, w_in, conv_a_w,
              rw_w0, rw_w_up, rw_a0, rw_a_up, rw_k_k, rw_k_a, rw_r_k, rw_g_up, rw_ln_g, rw_ln_b,
              w_out, ffn_w_up, ffn_conv_w, ffn_conv_b, ffn_w_down, final_g):
    bsz = x.shape[0]
    rows = x.shape[1] // GRID_W
    zero_state = jnp.zeros((bsz, RWKV_HEADS, RWKV_HEAD, RWKV_HEAD), jnp.float32)
    for l in range(DEPTH):
        last = l == DEPTH - 1
        lp = {
            "w_in": w_in[l], "conv_a_w": conv_a_w[l], "w0": rw_w0[l], "w_up": rw_w_up[l],
            "a0": rw_a0[l], "a_up": rw_a_up[l], "k_k": rw_k_k[l], "k_a": rw_k_a[l],
            "r_k": rw_r_k[l], "g_up": rw_g_up[l], "ln_g": rw_ln_g[l], "ln_b": rw_ln_b[l],
            "w_out": w_out[l], "ffn_w_up": ffn_w_up[l], "ffn_conv_w": ffn_conv_w[l],
            "ffn_conv_b": ffn_conv_b[l], "ffn_w_down": ffn_w_down[l],
        }
        mod = jax.nn.silu(c) @ ada_w[l] + ada_b[l]
        sh1, sc1, g1, sh2, sc2, g2 = [m[:, None, :] for m in jnp.split(mod, 6, axis=-1)]
        csh1, csc1, cg1, csh2, csc2, cg2 = jnp.split(jax.nn.silu(c_ctx) @ ada_w[l] + ada_b[l], 6, axis=-1)

        hc = modulate(rms_norm(ctx, norm1_g[l]), csh1, csc1)
        out_c, ctx_states = mixer_block(hc, lp, (zero_state, zero_state), not last)

        hx = modulate(rms_norm(x, norm1_g[l]), sh1, sc1)
        out_x, _ = mixer_block(hx, lp, ctx_states, True)
        x = x + g1 * out_x
        hx = modulate(rms_norm(x, norm2_g[l]), sh2, sc2)
        x = x + g2 * conv_ffn(hx, lp, rows, GRID_W)

        if not last:
            ctx = ctx + cg1 * out_c
            hc = modulate(rms_norm(ctx, norm2_g[l]), csh2, csc2)
            ctx = ctx + cg2 * conv_ffn(hc, lp, 1, ctx.shape[1])
    return rms_norm(x, final_g)
```

```python
import math
import numpy as np
from contextlib import ExitStack
import concourse.bass as bass
import concourse.mybir as mybir
from concourse.bass_utils import run_bass_kernel_spmd

F32 = mybir.dt.float32
BF16 = mybir.dt.bfloat16
F32R = mybir.dt.float32r
AF = mybir.ActivationFunctionType
ALU = mybir.AluOpType
AX = mybir.AxisListType

D = 1024
KC = 8
PROJ = 3328
NPC = 26
DFF = 2816
FC = 22
GW = 64
DS = math.exp(-0.5)
RMS_EPS = 1e-6
GN_EPS = 64e-5
NVEC = 48 + 10 * FC
SCAN_DT = F32

C_ID = 0
C_MG = 128
C_ML = C_MG + 1024
C_TF = C_ML + 256
C_BO = C_TF + 512
C_ON = C_BO + 128
C_IB = C_ON + 128
NCST = C_IB + 64


def make_consts():
    c = np.zeros((128, NCST), np.float32)
    s = np.arange(128)[:, None]
    t = np.arange(128)[None, :]
    c[:, C_ID:C_ID + 128] = np.eye(128)
    for d in range(2):
        lt = (s < t) if d == 0 else (s > t)
        le = (s <= t) if d == 0 else (s >= t)
        g = c[:, C_MG + 512 * d:C_MG + 512 * (d + 1)]
        g[:, 0:128] = -1.0 * lt
        g[:, 128:256] = le
        g[:, 256:384] = lt
        g[:, 384:512] = le
        c[:, C_ML + 128 * d:C_ML + 128 * (d + 1)] = -1.0 * lt.T
        f = c[:, C_TF + 256 * d:C_TF + 256 * (d + 1)]
        f[:, 0:128] = -DS * le
        f[:, 128:256] = -DS * lt
    c[:, C_BO:C_BO + 128] = (s // 64 == t // 64)
    c[:, C_ON:C_ON + 128] = 1.0
    c[:, C_IB:C_IB + 64] = (s % 64 == np.arange(64)[None, :])
    return c


class Buf:
    __slots__ = ("name", "w", "rd", "ps")

    def __init__(self, name="", ps=False):
        self.name = name
        self.w = None
        self.rd = []
        self.ps = ps


ENGMAP = {"pe": "tensor", "dve": "vector", "act": "scalar", "pool": "gpsimd", "sp": "sync"}


class Eng:
    def __init__(self, name, sem, h):
        self.name = name
        self.sem = sem
        self.h = h
        self.cnt = 0
        self.waited = {}


class Prog:
    def __init__(self, nc, sems, dma_sems):
        self.nc = nc
        self.E = {n: Eng(n, sems[n], getattr(nc, ENGMAP[n])) for n in ENGMAP}
        self.dma_sems = dma_sems
        self.dma_cnt = [0] * len(dma_sems)
        self.dma_rr = 0
        self.ninst = 0
        self.stop = False

    def _deps(self, reads, writes):
        d = []
        for b in reads:
            if b.w is not None:
                d.append(b.w)
        for b in writes:
            if b.w is not None:
                d.append(b.w)
            d.extend(b.rd)
        return d

    def _waits(self, e, deps, skip_self):
        need = {}
        for (sem, val, owner) in deps:
            if skip_self and owner is e:
                continue
            k = id(sem)
            if e.waited.get(k, 0) >= val:
                continue
            if k not in need or need[k][1] < val:
                need[k] = (sem, val)
        for k, (sem, val) in need.items():
            e.waited[k] = val
            e.h.wait_ge(sem, val)

    def op(self, en, fn, reads=(), writes=()):
        if self.stop:
            return
        e = self.E[en]
        deps = self._deps(reads, writes)
        for b in reads:
            if b.ps:
                deps.extend(t for t in b.rd if t[2] is not e)
        self._waits(e, deps, skip_self=(en == "pe"))
        e.cnt += 1
        tok = (e.sem, e.cnt, e)
        fn(e.h).then_inc(e.sem, 1)
        for b in writes:
            b.w = tok
            b.rd = []
        for b in reads:
            b.rd.append(tok)
        self.ninst += 1

    def dma(self, out, in_, reads=(), writes=(), q="sp", slow=False):
        if self.stop:
            return
        e = self.E[q]
        deps = self._deps(reads, writes)
        i = self.dma_rr
        self.dma_rr = (self.dma_rr + 1) % len(self.dma_sems)
        sem = self.dma_sems[i]
        if self.dma_cnt[i] > 0:
            deps.append((sem, self.dma_cnt[i], None))
        self._waits(e, deps, skip_self=False)
        self.dma_cnt[i] += 16
        tok = (sem, self.dma_cnt[i], None)
        if slow:
            e.h.dma_start(out=out, in_=in_, allow_slow_non_contiguous=True).then_inc(sem, 16)
        else:
            e.h.dma_start(out=out, in_=in_).then_inc(sem, 16)
        for b in writes:
            b.w = tok
            b.rd = []
        for b in reads:
            b.rd.append(tok)
        self.ninst += 1
        return tok

    def barrier(self):
        if self.stop:
            return
        toks = [(e.sem, e.cnt, e) for e in self.E.values() if e.cnt > 0]
        toks += [(s, c, None) for s, c in zip(self.dma_sems, self.dma_cnt) if c > 0]
        for e in self.E.values():
            self._waits(e, toks, skip_self=True)


class _Stop(Exception):
    pass


def build_program(T, CT, L=2, final_norm=True, dbg=None):
    assert T % 512 == 0 and CT % 128 == 0
    nc = bass.Bass("TRN2", target_bir_lowering=False)
    dt_ = nc.dram_tensor

    def din(name, shape, dt=F32):
        return dt_(name, shape, dt, kind="ExternalInput").ap()

    def dint(name, shape, dt=F32):
        return dt_(name, shape, dt, kind="Internal").ap()

    x_in = din("x", [T, D])
    ctx_in = din("ctx", [CT, D])
    cs_in = din("cs", [128, KC, 2])
    cst_in = din("cst", [128, NCST])
    ada_w = din("ada_w", [L, D, 6 * D])
    rows_in = din("rows", [L, 2, 6 * D + 2 * D])
    fin_g = din("fin_g", [D])
    w_in = din("w_in", [L, D, PROJ])
    w_out = din("w_out", [L, D, D])
    w_up = din("w_up", [L, D, 2 * DFF])
    w_dn = din("w_dn", [L, DFF, D])
    vec_in = din("vec", [L, 128, NVEC])
    rowv_in = din("rowv", [L, 1, 1024])
    lup_in = din("lup", [L, 128, 2, 512])
    gup_in = din("gup", [L, 128, 512])
    out = dt_("out", [T, D], F32, kind="ExternalOutput").ap()

    xs = [dint("xs0", [T, D]), dint("xs1", [T, D])]
    cxs = [dint("cxs0", [CT, D]), dint("cxs1", [CT, D])]
    modr = dint("modr", [2, 6, D])
    TS = max(T, CT)
    sp_base = dint("sp_base", [16, 128, TS])
    sp_wa = dint("sp_wa", [128, TS])
    sp_g = dint("sp_g", [8, 128, TS])
    sp_ycv = dint("sp_ycv", [4, 128, TS + 2], BF16)
    sp_y1 = dint("sp_y1", [TS, 512])

    st = ExitStack()
    with st:
        sems = {n: st.enter_context(nc.semaphore(n)) for n in ENGMAP}
        dsem = [st.enter_context(nc.semaphore(f"dq{i}")) for i in range(24)]
        p = Prog(nc, sems, dsem)

        uid = [0]

        def sb(name, shape, dt=F32, stack=st):
            uid[0] += 1
            return stack.enter_context(nc.sbuf_tensor(f"s{uid[0]}_{name}", shape, dt))

        def ps(name, shape, dt=F32):
            uid[0] += 1
            return st.enter_context(nc.psum_tensor(f"p{uid[0]}_{name}", shape, dt))

        cst = sb("cst", [128, NCST]); Bcst = Buf()
        vec = sb("vec", [128, NVEC]); Bvec = Buf()
        omka = sb("omka", [128, 4]); Bomka = Buf()
        bc = {n: sb("bc_" + n, [128, D]) for n in ("gs", "sh", "gt")}
        Bbc = {n: Buf() for n in bc}
        PA = [ps("PA0", [128, 512]), ps("PA1", [128, 512])]; BPA = [Buf(ps=True), Buf(ps=True)]
        _bpg = Buf(ps=True)
        PG = ps("PG", [128, 512]); BPG = [_bpg, _bpg]
        PN = [ps("PN0", [128, 512]), ps("PN1", [128, 512])]
        _bpn = [Buf(ps=True), Buf(ps=True)]
        BPN = [[_bpn[0]] * 3, [_bpn[1]] * 3]
        _bpm = Buf(ps=True)
        PM = ps("PM", [128, 512]); BPM = [_bpm] * 5
        PS_ = ps("PS", [128, 512]); BPS = Buf(ps=True)
        PY = ps("PY", [128, 512]); BPY = Buf(ps=True)
        pa_rr = [0]

        def pa():
            i = pa_rr[0]
            pa_rr[0] ^= 1
            return PA[i], BPA[i]

        ident = cst[:, C_ID:C_ID + 128]
        p.dma(cst[:], cst_in, writes=[Bcst])
        identR = sb("identR", [128, 128], SCAN_DT); BidR = Buf()
        p.op("dve", lambda e: e.tensor_copy(out=identR[:], in_=ident), reads=[Bcst], writes=[BidR])

        def mm(o, l, r, start, stop, reads, writes):
            p.op("pe", lambda e: e.matmul(o, l, r, start=start, stop=stop), reads=reads, writes=writes)

        def tr(o, i_, reads, writes):
            p.op("pe", lambda e: e.transpose(o, i_, ident), reads=list(reads) + [Bcst], writes=writes)

        def act(o, i_, func, reads, writes, bias=None, scale=None, accum=None):
            kw = {}
            if bias is not None:
                kw["bias"] = bias
            if scale is not None:
                kw["scale"] = scale
            if accum is not None:
                kw["accum_out"] = accum
            p.op("act", lambda e: e.activation(out=o, in_=i_, func=func, **kw), reads=reads, writes=writes)

        def tt(en, o, a, b, op, reads, writes):
            p.op(en, lambda e: e.tensor_tensor(out=o, in0=a, in1=b, op=op), reads=reads, writes=writes)

        def ts(en, o, a, s1, s2, op0, op1, reads, writes):
            if s2 is None:
                p.op(en, lambda e: e.tensor_scalar(out=o, in0=a, scalar1=s1, scalar2=None, op0=op0),
                     reads=reads, writes=writes)
            else:
                p.op(en, lambda e: e.tensor_scalar(out=o, in0=a, scalar1=s1, scalar2=s2, op0=op0, op1=op1),
                     reads=reads, writes=writes)

        def stt(en, o, a, s, b, op0, op1, reads, writes):
            p.op(en, lambda e: e.scalar_tensor_tensor(out=o, in0=a, scalar=s, in1=b, op0=op0, op1=op1),
                 reads=reads, writes=writes)

        def cp(en, o, i_, reads, writes):
            if en == "act":
                act(o, i_, AF.Copy, reads, writes)
            else:
                p.op(en, lambda e: e.tensor_copy(out=o, in_=i_), reads=reads, writes=writes)

        def ck(k):
            if dbg == k and not p.stop:
                p.barrier()
                p.stop = True

        try:
          for l in range(L):
              last = (l == L - 1)
              x_src = x_in if l == 0 else xs[1]
              c_src = ctx_in if l == 0 else cxs[1]
              x_mid, c_mid = xs[0], cxs[0]
              x_dst, c_dst = xs[1], cxs[1]
              if last:
                  x_dst = out

              p.barrier()
              with ExitStack() as s0:
                  rows = sb("rows", [2, 8 * D], stack=s0); Brows = Buf()
                  mod = sb("mod", [2, 6 * D], stack=s0); Bmod = Buf()
                  drv = sb("drv", [2, 6, D], stack=s0); Bdrv = Buf()
                  cs = sb("cs", [128, KC, 2], stack=s0); Bcs = Buf()
                  scs = sb("scs", [128, KC, 2], stack=s0); Bscs = Buf()
                  stg = [sb(f"adastg{i}", [128, KC, 512], stack=s0) for i in range(2)]; Bstg = [Buf(), Buf()]
                  p.dma(rows[:], rows_in[l], writes=[Brows])
                  p.dma(cs[:], cs_in, writes=[Bcs])
                  p.dma(vec[:], vec_in[l], writes=[Bvec])
                  ts("dve", omka[:], vec[:, 4:8], -1.0, 1.0, ALU.mult, ALU.add, [Bvec], [Bomka])
                  act(scs[:], cs[:], AF.Silu, [Bcs], [Bscs])
                  for cb in range(12):
                      sg, Bsg = stg[cb % 2], Bstg[cb % 2]
                      p.dma(sg[:], ada_w[l][:, cb * 512:(cb + 1) * 512].rearrange("(k p) n -> p k n", p=128),
                            writes=[Bsg])
                      P_, BP_ = pa()
                      for k in range(KC):
                          mm(P_[0:2, :], scs[:, k, :], sg[:, k, :], k == 0, k == KC - 1, [Bscs, Bsg], [BP_])
                      tt("dve", mod[:, cb * 512:(cb + 1) * 512], P_[0:2, :], rows[:, cb * 512:(cb + 1) * 512],
                         ALU.add, [BP_, Brows], [Bmod])
                  stt("dve", drv[:, 0, :], mod[:, D:2 * D], 1.0, rows[:, 6 * D:7 * D], ALU.add, ALU.mult,
                      [Bmod, Brows], [Bdrv])
                  cp("dve", drv[:, 1, :], mod[:, 0:D], [Bmod], [Bdrv])
                  cp("dve", drv[:, 2, :], mod[:, 2 * D:3 * D], [Bmod], [Bdrv])
                  stt("dve", drv[:, 3, :], mod[:, 4 * D:5 * D], 1.0, rows[:, 7 * D:8 * D], ALU.add, ALU.mult,
                      [Bmod, Brows], [Bdrv])
                  cp("dve", drv[:, 4, :], mod[:, 3 * D:4 * D], [Bmod], [Bdrv])
                  cp("dve", drv[:, 5, :], mod[:, 5 * D:6 * D], [Bmod], [Bdrv])
                  Bmodr = Buf()
                  p.dma(modr, drv[:], reads=[Bdrv], writes=[Bmodr])
                  p.barrier()
              ck(0)

              def load_bc(setid, which):
                  for n, r in (("gs", 0), ("sh", 1), ("gt", 2)):
                      p.dma(bc[n][:], modr[setid, 3 * which + r].partition_broadcast(128),
                            reads=[Bmodr], writes=[Bbc[n]])

              with ExitStack() as s1:
                  def sb1(name, shape, dt=F32):
                      return sb(name, shape, dt, stack=s1)
                  rowv = sb1("rowv", [1, 1024]); Browv = Buf()
                  lup = sb1("lup", [128, 2, 512]); Blup = Buf()
                  gup = sb1("gup", [128, 512], BF16); Bgup = Buf()
                  Sst = [sb1(f"S{d}", [64, 8, 64], SCAN_DT) for d in range(2)]; BS = [Buf(), Buf()]
                  Sc = [sb1(f"Sc{d}", [64, 8, 64], SCAN_DT) for d in range(2)]; BSc = [Buf(), Buf()]
                  p.dma(rowv[:], rowv_in[l], writes=[Browv])
                  p.dma(lup[:], lup_in[l], writes=[Blup])
                  win = sb1("win", [128, KC, PROJ], BF16); Bwin = Buf()
                  wout = sb1("wout", [128, KC, D], BF16); Bwout = Buf()
                  with ExitStack() as sw:
                      wstg = [sb(f"wstg{i}", [128, PROJ], stack=sw) for i in range(2)]; Bwstg = [Buf(), Buf()]
                      for k in range(KC):
                          sg, Bsg = wstg[k % 2], Bwstg[k % 2]
                          p.dma(sg[:], w_in[l][k * 128:(k + 1) * 128, :], writes=[Bsg])
                          cp("dve" if k % 2 == 0 else "pool", win[:, k, :], sg[:], [Bsg], [Bwin])
                      for k in range(KC):
                          sg, Bsg = wstg[k % 2], Bwstg[k % 2]
                          p.dma(sg[:, 0:D], w_out[l][k * 128:(k + 1) * 128, :], writes=[Bsg])
                          cp("dve" if k % 2 == 0 else "pool", wout[:, k, :], sg[:, 0:D], [Bsg], [Bwout])
                      p.dma(wstg[0][:, 0:512], gup_in[l], writes=[Bwstg[0]])
                      cp("dve", gup[:], wstg[0][:, 0:512], [Bwstg[0]], [Bgup])
                      p.barrier()
                  ck(1)

                  SBT = 256
                  xt = [sb1(f"xt{i}", [128, D]) for i in range(2)]; Bxt = [Buf(), Buf()]
                  hb = sb1("hb", [128, D]); Bhb = Buf()
                  ssq = sb1("ssq", [128, 2]); Bssq = Buf()
                  hT = sb1("hT", [128, KC, SBT], BF16); BhT = Buf()
                  base = sb1("base", [128, 16, SBT]); Bbase = [Buf() for _ in range(16)]
                  WA = sb1("WA", [128, SBT]); BWA = Buf()
                  sgl = sb1("sgl", [128, SBT], BF16); Bsgl = Buf()
                  G12 = sb1("G12", [128, 8, SBT]); BG12 = [Buf() for _ in range(8)]
                  ycv = sb1("ycv", [128, 4, SBT], BF16); Bycv = [Buf() for _ in range(4)]
                  ubuf_t = sb1("ubuf", [128, 4 * (SBT + 2)]); Bub = [Buf() for _ in range(4)]
                  ubuf = ubuf_t[:].rearrange("p (j n) -> p j n", j=4)
                  cbb_t = sb1("cbb", [128, 4 * (SBT + 1)]); Bcbb = [Buf() for _ in range(4)]
                  cbb = cbb_t[:].rearrange("p (j n) -> p j n", j=4)
                  tmpA = [sb1(f"tmpA{i}", [128, SBT]) for i in range(4)]; BtA = [Buf() for _ in range(4)]
                  KR = sb1("KR", [128, 4, 256], SCAN_DT); BKR = [Buf() for _ in range(4)]
                  BtF = sb1("BtF", [128, 4, 128], SCAN_DT); BBt = [Buf() for _ in range(4)]
                  KtF = sb1("KtF", [128, 4, 128], SCAN_DT); BKt = [Buf() for _ in range(4)]
                  BpF = sb1("BpF", [128, 4, 128]); BBp = [Buf() for _ in range(4)]
                  KpF = sb1("KpF", [128, 4, 128]); BKp = [Buf() for _ in range(4)]
                  KaF32 = sb1("KaF32", [128, 4, 128]); BKa32 = [Buf() for _ in range(4)]
                  DG = sb1("DG", [128, 4, 64], SCAN_DT); BDG = [Buf() for _ in range(4)]
                  EG = sb1("EG", [128, 3, 128]); BEG = Buf()
                  aFt = sb1("aFt", [128, 128]); BaF = Buf()
                  tB = [sb1(f"tB{i}", [128, 128]) for i in range(3)]; BtB = [Buf() for _ in range(3)]
                  sigT = sb1("sigT", [128, 512]); BsigT = Buf()
                  TM = {n: sb1("TM_" + n, [128, 512], SCAN_DT) for n in ("Ka", "Bp", "Kp", "V")}
                  BTM = {n: Buf() for n in TM}
                  LT = [[sb1(f"LT{u}{i}", [128, 256], SCAN_DT) for i in range(2)] for u in range(2)]
                  BLT = [[[Buf(), Buf()] for i in range(2)] for u in range(2)]
                  Nn = [[sb1(f"Nn{u}{i}", [128, 128], SCAN_DT) for i in range(2)] for u in range(2)]
                  BNn = [[Buf() for i in range(2)] for u in range(2)]
                  Gm = [sb1(f"Gm{u}", [128, 384], SCAN_DT) for u in range(2)]; BGm = [Buf(), Buf()]
                  TtF = [sb1(f"TtF{u}", [128, 128], SCAN_DT) for u in range(2)]; BTt = [Buf(), Buf()]
                  AkV = [sb1(f"AkV{u}", [128, 64], SCAN_DT) for u in range(2)]; BAkV = [Buf(), Buf()]
                  XnS = [sb1(f"XnS{u}", [128, 128], SCAN_DT) for u in range(2)]; BXn = [Buf(), Buf()]
                  PTs = [sb1(f"PTs{u}", [64, 64], SCAN_DT) for u in range(2)]; BPTs = [Buf(), Buf()]
                  RhT = [sb1(f"RhT{u}", [64, 128], SCAN_DT) for u in range(2)]; BRh = [Buf(), Buf()]
                  ysum = ubuf_t[:, 0:512]; Bys = Buf()
                  yn = ubuf_t[:, 512:1024]; Byn = Buf()
                  y1t = cbb_t[:, 0:512]; By1 = Buf()
                  gst = sb1("gst", [128, 8, 4]); Bgst = Buf()
                  catT = hT[:, :, 0:128]; Bcat = Buf()
                  xo, Bxo = hb, Bhb

                  def vcol(i, j):
                      return vec[:, 4 * i + j:4 * i + j + 1]

                  def norm_tile(src_ap, xt_i, col0):
                      X, BX = xt[xt_i], Bxt[xt_i]
                      p.dma(X[:], src_ap, writes=[BX])
                      act(hb[:], X[:], AF.Square, [BX], [Bhb, Bssq], accum=ssq[:, 0:1])
                      ts("dve", ssq[:, 1:2], ssq[:, 0:1], 1.0 / D, RMS_EPS, ALU.mult, ALU.add, [Bssq], [Bssq])
                      act(ssq[:, 1:2], ssq[:, 1:2], AF.Ln, [Bssq], [Bssq])
                      act(ssq[:, 1:2], ssq[:, 1:2], AF.Exp, [Bssq], [Bssq], scale=-0.5)
                      stt("dve", hb[:], X[:], ssq[:, 1:2], bc["gs"][:], ALU.mult, ALU.mult,
                          [BX, Bssq, Bbc["gs"]], [Bhb])
                      tt("pool", hb[:], hb[:], bc["sh"][:], ALU.add, [Bhb, Bbc["sh"]], [Bhb])
                      for half in range(2):
                          P_, BP_ = pa()
                          for q in range(4):
                              k = half * 4 + q
                              tr(P_[:, q * 128:(q + 1) * 128], hb[:, k * 128:(k + 1) * 128], [Bhb], [BP_])
                          cp("act", hT[:, half * 4:half * 4 + 4, col0:col0 + 128],
                             P_[:].rearrange("p (q n) -> p q n", q=4), [BP_], [BhT])
                      return X, BX

                  def proj_chunk(c, n):
                      P_, BP_ = pa()
                      for k in range(KC):
                          mm(P_[:, 0:n], win[:, k, c * 128:(c + 1) * 128], hT[:, k, 0:n], k == 0, k == KC - 1,
                             [Bwin, BhT], [BP_])
                      return P_, BP_

                  def stage1(seg_src, t0, n, first, seg_T):
                      for i in range(n // 128):
                          norm_tile(seg_src[t0 + i * 128:t0 + (i + 1) * 128, :], i % 2, i * 128)
                      for j in range(4):
                          if first:
                              p.op("pool", lambda e, j=j: e.memset(ubuf[:, j, 0:2], 0.0), writes=[Bub[j]])
                              p.op("pool", lambda e, j=j: e.memset(cbb[:, j, 0:1], 0.0), writes=[Bcbb[j]])
                          Pb, BPb = proj_chunk(j, n)
                          cp("act", cbb[:, j, 1:n + 1], Pb[:, 0:n], [BPb], [Bcbb[j]])
                          Pc, BPc = proj_chunk(4 + j, n)
                          cp("act", tmpA[0][:, 0:n], Pc[:, 0:n], [BPc], [BtA[0]])
                          Px, BPx = proj_chunk(8 + j, n)
                          tt("dve", ubuf[:, j, 2:n + 2], tmpA[0][:, 0:n], Px[:, 0:n], ALU.mult, [BtA[0], BPx], [Bub[j]])
                          ts("dve", tmpA[1][:, 0:n], ubuf[:, j, 0:n], vcol(5, j), None, ALU.mult, None,
                             [Bub[j], Bvec], [BtA[1]])
                          stt("dve", tmpA[1][:, 0:n], ubuf[:, j, 1:n + 1], vcol(6, j), tmpA[1][:, 0:n], ALU.mult, ALU.add,
                              [Bub[j], Bvec, BtA[1]], [BtA[1]])
                          stt("dve", tmpA[1][:, 0:n], ubuf[:, j, 2:n + 2], vcol(7, j), tmpA[1][:, 0:n], ALU.mult, ALU.add,
                              [Bub[j], Bvec, BtA[1]], [BtA[1]])
                          tt("pool", ycv[:, j, 0:n], tmpA[1][:, 0:n], cbb[:, j, 0:n], ALU.mult, [BtA[1], Bcbb[j]], [Bycv[j]])
                          p.dma(sp_ycv[j][:, t0:t0 + n], ycv[:, j, 0:n], reads=[Bycv[j]])
                          if t0 + n == seg_T:
                              ts("dve", tmpA[1][:, 0:1], ubuf[:, j, n:n + 1], vcol(5, j), None, ALU.mult, None,
                                 [Bub[j], Bvec], [BtA[1]])
                              stt("dve", tmpA[1][:, 0:1], ubuf[:, j, n + 1:n + 2], vcol(6, j), tmpA[1][:, 0:1],
                                  ALU.mult, ALU.add, [Bub[j], Bvec, BtA[1]], [BtA[1]])
                              tt("pool", ycv[:, j, 0:1], tmpA[1][:, 0:1], cbb[:, j, n:n + 1], ALU.mult,
                                 [BtA[1], Bcbb[j]], [Bycv[j]])
                              p.dma(sp_ycv[j][:, t0 + n:t0 + n + 1], ycv[:, j, 0:1], reads=[Bycv[j]], slow=True)
                          else:
                              cp("pool", ubuf[:, j, 0:2], ubuf[:, j, n:n + 2], [Bub[j]], [Bub[j]])
                              cp("pool", cbb[:, j, 0:1], cbb[:, j, n:n + 1], [Bcbb[j]], [Bcbb[j]])
                      Pw, BPw = proj_chunk(24, n)
                      act(WA[0:64, 0:n], Pw[0:64, 0:n], AF.Tanh, [BPw], [BWA])
                      cp("act", WA[64:128, 0:n], Pw[64:128, 0:n], [BPw], [BWA])
                      p.dma(sp_wa[:, t0:t0 + n], WA[:, 0:n], reads=[BWA])
                      Pg, BPg = proj_chunk(25, n)
                      act(sgl[:, 0:n], Pg[:, 0:n], AF.Sigmoid, [BPg], [Bsgl])
                      for j in range(4):
                          rj, kpj, kj, vj = base[:, j, 0:n], base[:, 4 + j, 0:n], base[:, 8 + j, 0:n], base[:, 12 + j, 0:n]
                          Pr, BPr = proj_chunk(12 + j, n)
                          cp("act", rj, Pr[:, 0:n], [BPr], [Bbase[j]])
                          Pk, BPk = proj_chunk(16 + j, n)
                          cp("act", kj, Pk[:, 0:n], [BPk], [Bbase[8 + j]])
                          Pv, BPv = proj_chunk(20 + j, n)
                          cp("dve", vj, Pv[:, 0:n], [BPv], [Bbase[12 + j]])
                          act(tmpA[2][:, 0:n], kj, AF.Square, [Bbase[8 + j], Bvec], [BtA[2]], scale=vcol(0, j))
                          Pq, BPq = pa()
                          mm(Pq[:, 0:n], cst[:, C_BO:C_BO + 128], tmpA[2][:, 0:n], True, True, [Bcst, BtA[2]], [BPq])
                          ts("dve", tmpA[2][:, 0:n], Pq[:, 0:n], 1e-24, None, ALU.max, None, [BPq], [BtA[2]])
                          act(tmpA[2][:, 0:n], tmpA[2][:, 0:n], AF.Ln, [BtA[2]], [BtA[2]])
                          act(tmpA[2][:, 0:n], tmpA[2][:, 0:n], AF.Exp, [BtA[2]], [BtA[2]], scale=-0.5)
                          stt("dve", kpj, kj, vcol(0, j), tmpA[2][:, 0:n], ALU.mult, ALU.mult,
                              [Bbase[8 + j], Bvec, BtA[2]], [Bbase[4 + j]])
                          stt("dve", tmpA[3][:, 0:n], rj, vcol(2, j), kj, ALU.mult, ALU.mult,
                              [Bbase[j], Bvec, Bbase[8 + j]], [BtA[3]])
                          Pq2, BPq2 = pa()
                          mm(Pq2[:, 0:n], cst[:, C_BO:C_BO + 128], tmpA[3][:, 0:n], True, True, [Bcst, BtA[3]], [BPq2])
                          tt("dve", tmpA[3][:, 0:n], Pq2[:, 0:n], vj, ALU.mult, [BPq2, Bbase[12 + j]], [BtA[3]])
                          Pq3, BPq3 = pa()
                          mm(Pq3[:, 0:n], gup[:, j * 128:(j + 1) * 128], sgl[:, 0:n], True, True, [Bgup, Bsgl], [BPq3])
                          ts("dve", G12[:, j, 0:n], Pq3[:, 0:n], vcol(3, j), None, ALU.mult, None, [BPq3, Bvec], [BG12[j]])
                          stt("dve", G12[:, 4 + j, 0:n], tmpA[3][:, 0:n], vcol(4, j), Pq3[:, 0:n], ALU.add, ALU.mult,
                              [BtA[3], Bvec, BPq3], [BG12[4 + j]])
                          for q, Bq in ((j, Bbase[j]), (4 + j, Bbase[4 + j]), (8 + j, Bbase[8 + j]), (12 + j, Bbase[12 + j])):
                              p.dma(sp_base[q][:, t0:t0 + n], base[:, q, 0:n], reads=[Bq])
                          p.dma(sp_g[j][:, t0:t0 + n], G12[:, j, 0:n], reads=[BG12[j]])
                          p.dma(sp_g[4 + j][:, t0:t0 + n], G12[:, 4 + j, 0:n], reads=[BG12[4 + j]])

                  def load_stage(t0, n):
                      for q in range(16):
                          p.dma(base[:, q, 0:n], sp_base[q][:, t0:t0 + n], writes=[Bbase[q]])
                      p.dma(WA[:, 0:n], sp_wa[:, t0:t0 + n], writes=[BWA])
                      for q in range(8):
                          p.dma(G12[:, q, 0:n], sp_g[q][:, t0:t0 + n], writes=[BG12[q]])
                      for j in range(4):
                          p.dma(ycv[:, j, 0:n], sp_ycv[j][:, t0 + 1:t0 + n + 1], writes=[Bycv[j]])

                  def scan_tile(c0, d, Sb, BSb):
                      mG = cst[:, C_MG + 512 * d:C_MG + 512 * (d + 1)]
                      mL = cst[:, C_ML + 128 * d:C_ML + 128 * (d + 1)]
                      tF = cst[:, C_TF + 256 * d:C_TF + 256 * (d + 1)]
                      gcol = 127 if d == 0 else 0
                      cs_ = slice(c0, c0 + 128)
                      P_, BP_ = pa()
                      mm(P_[:], WA[0:64, cs_], lup[0:64, d, :], True, False, [BWA, Blup], [BP_])
                      mm(P_[:], cst[0:1, C_ON:C_ON + 128], rowv[0:1, d * 512:(d + 1) * 512], False, True,
                         [Bcst, Browv], [BP_])
                      act(sigT[:], P_[:], AF.Sigmoid, [BP_], [BsigT])
                      for j in range(4):
                          Pc, BPc = pa()
                          mm(Pc[:, 0:256], sigT[:, j * 128:(j + 1) * 128], tF, True, True, [BsigT, Bcst], [BPc])
                          act(EG[:, 0, :], Pc[:, 0:128], AF.Exp, [BPc], [BEG])
                          act(EG[:, 1, :], Pc[:, 0:128], AF.Exp, [BPc], [BEG], scale=-1.0)
                          act(EG[:, 2, :], Pc[:, 128:256], AF.Exp, [BPc], [BEG])
                          Pa, BPa_ = pa()
                          mm(Pa[:, 0:128], lup[64:128, d, j * 128:(j + 1) * 128], WA[64:128, cs_], True, True,
                             [Blup, BWA], [BPa_])
                          act(aFt[:], Pa[:, 0:128], AF.Sigmoid, [BPa_, Bvec], [BaF], bias=vcol(10 + d, j))
                          rj, kpj, kj = base[:, j, cs_], base[:, 4 + j, cs_], base[:, 8 + j, cs_]
                          tt("pool", KR[:, j, 128:256], rj, EG[:, 0, :], ALU.mult, [Bbase[j], BEG], [BKR[j]])
                          tt("dve", KaF32[:, j, :], kpj, EG[:, 2, :], ALU.mult, [Bbase[4 + j], BEG], [BKa32[j]])
                          cp("pool", KR[:, j, 0:128], KaF32[:, j, :], [BKa32[j]], [BKR[j]])
                          tt("pool", tB[0][:], kpj, aFt[:], ALU.mult, [Bbase[4 + j], BaF], [BtB[0]])
                          tt("dve", tB[1][:], tB[0][:], EG[:, 1, :], ALU.mult, [BtB[0], BEG], [BtB[1]])
                          cp("pool", BtF[:, j, :], tB[1][:], [BtB[1]], [BBt[j]])
                          ts("dve", BpF[:, j, :], tB[1][:], EG[:, 0, gcol:gcol + 1], None, ALU.mult, None,
                             [BtB[1], BEG], [BBp[j]])
                          ts("dve", tB[0][:], aFt[:], vcol(1, j), omka[:, j:j + 1], ALU.mult, ALU.add,
                             [BaF, Bvec, Bomka], [BtB[0]])
                          tt("pool", tB[0][:], tB[0][:], kj, ALU.mult, [BtB[0], Bbase[8 + j]], [BtB[0]])
                          tt("dve", tB[2][:], tB[0][:], EG[:, 1, :], ALU.mult, [BtB[0], BEG], [BtB[2]])
                          cp("pool", KtF[:, j, :], tB[2][:], [BtB[2]], [BKt[j]])
                          ts("dve", KpF[:, j, :], tB[2][:], EG[:, 0, gcol:gcol + 1], None, ALU.mult, None,
                             [BtB[2], BEG], [BKp[j]])
                          ts("dve", DG[:, j, :], cst[:, C_IB:C_IB + 64], EG[:, 0, gcol:gcol + 1], None, ALU.mult, None,
                             [Bcst, BEG], [BDG[j]])
                      ck(30)
                      for name, srcf, Bsrc in (("Ka", lambda j: KaF32[:, j, :], BKa32), ("Bp", lambda j: BpF[:, j, :], BBp),
                                               ("Kp", lambda j: KpF[:, j, :], BKp),
                                               ("V", lambda j: base[:, 12 + j, cs_], Bbase[12:16])):
                          P_, BP_ = pa()
                          for j in range(4):
                              tr(P_[:, j * 128:(j + 1) * 128], srcf(j), [Bsrc[j]], [BP_])
                          cp("act", TM[name][:], P_[:], [BP_], [BTM[name]])

                      ck(31)
                      def unit_stages(h, u):
                          j, p0 = h // 2, 64 * (h % 2)
                          hs = slice(h * 64, (h + 1) * 64)
                          Bt_h, Kt_h, KR_h = BtF[p0:p0 + 64, j, :], KtF[p0:p0 + 64, j, :], KR[p0:p0 + 64, j, :]
                          Ka_h = KR[p0:p0 + 64, j, 0:128]
                          pn, Bpn = PN[u], BPN[u]
                          stages = []

                          def s_gram():
                              mm(PG[:, 0:256], Bt_h, KR_h, True, True, [BBt[j], BKR[j]], [BPG[0]])
                              mm(PG[:, 256:512], Kt_h, KR_h, True, True, [BKt[j], BKR[j]], [BPG[1]])
                              mm(pn[:, 384:512], Ka_h, Bt_h, True, True, [BKR[j], BBt[j]], [Bpn[2]])
                              tt("dve", LT[u][0][:, 0:128], PG[:, 0:128], mG[:, 0:128], ALU.mult, [BPG[0], Bcst],
                                 [BLT[u][0][0]])
                              tt("dve", Gm[u][:, 0:128], PG[:, 128:256], mG[:, 128:256], ALU.mult, [BPG[0], Bcst], [BGm[u]])
                              tt("dve", Gm[u][:, 128:384], PG[:, 256:512], mG[:, 256:512], ALU.mult, [BPG[1], Bcst], [BGm[u]])
                              tt("dve", Nn[u][0][:], pn[:, 384:512], mL, ALU.mult, [Bpn[2], Bcst], [BNn[u][0]])
                              tt("pool", LT[u][1][:, 128:256], LT[u][0][:, 0:128], ident, ALU.add,
                                 [BLT[u][0][0], Bcst], [BLT[u][1][1]])
                          stages.append(s_gram)

                          def s_l0():
                              mm(pn[:, 0:128], Nn[u][0][:], LT[u][0][:, 0:128], True, True, [BNn[u][0], BLT[u][0][0]], [Bpn[0]])
                              mm(pn[:, 256:384], LT[u][0][:, 0:128], Nn[u][0][:], True, True, [BLT[u][0][0], BNn[u][0]], [Bpn[1]])
                              cp("act", LT[u][1][:, 0:128], pn[:, 0:128], [Bpn[0]], [BLT[u][1][0]])
                              cp("act", Nn[u][1][:], pn[:, 256:384], [Bpn[1]], [BNn[u][1]])
                          stages.append(s_l0)

                          def mk_level(k):
                              a, b = k % 2, (k + 1) % 2
                              def s():
                                  mm(pn[:, 0:128], Nn[u][a][:], LT[u][a][:, 0:128], True, True,
                                     [BNn[u][a], BLT[u][a][0]], [Bpn[0]])
                                  mm(pn[:, 128:256], Nn[u][a][:], LT[u][a][:, 128:256], True, True,
                                     [BNn[u][a], BLT[u][a][1]], [Bpn[0]])
                                  mm(pn[:, 256:384], LT[u][a][:, 0:128], Nn[u][a][:], True, True,
                                     [BLT[u][a][0], BNn[u][a]], [Bpn[1]])
                                  ck(52)
                                  cp("act", LT[u][b][:, 0:128], pn[:, 0:128], [Bpn[0]], [BLT[u][b][0]])
                                  ck(53)
                                  tt("dve", LT[u][b][:, 128:256], pn[:, 128:256], LT[u][a][:, 128:256], ALU.add,
                                     [Bpn[0], BLT[u][a][1]], [BLT[u][b][1]])
                                  ck(54)
                                  cp("act", Nn[u][b][:], pn[:, 256:384], [Bpn[1]], [BNn[u][b]])
                              return s
                          for k in range(1, 6):
                              stages.append(mk_level(k))

                          def s_l6():
                              mm(pn[:, 128:256], Nn[u][0][:], LT[u][0][:, 128:256], True, True,
                                 [BNn[u][0], BLT[u][0][1]], [Bpn[0]])
                              tt("dve", TtF[u][:], pn[:, 128:256], LT[u][0][:, 128:256], ALU.add,
                                 [Bpn[0], BLT[u][0][1]], [BTt[u]])
                              mm(PM[:, 0:64], Gm[u][:, 128:256], TM["V"][:, hs], True, True, [BGm[u], BTM["V"]], [BPM[0]])
                              cp("act", AkV[u][:], PM[:, 0:64], [BPM[0]], [BAkV[u]])
                          stages.append(s_l6)

                          def s_x():
                              mm(PM[:, 64:128], TtF[u][:], TM["Ka"][:, hs], True, True, [BTt[u], BTM["Ka"]], [BPM[1]])
                              mm(PM[:, 128:192], TtF[u][:], AkV[u][:], True, True, [BTt[u], BAkV[u]], [BPM[1]])
                              act(XnS[u][:], PM[:, 64:192], AF.Copy, [BPM[1]], [BXn[u]], scale=-1.0)
                          stages.append(s_x)

                          def s_fin():
                              mm(PM[0:64, 256:384], identR[p0:p0 + 64, p0:p0 + 64], KR[p0:p0 + 64, j, 128:256], True, False,
                                 [BidR, BKR[j]], [BPM[3]])
                              mm(PM[0:64, 256:384], XnS[u][:, 0:64], Gm[u][:, 0:128], False, True, [BXn[u], BGm[u]], [BPM[3]])
                              cp("dve", RhT[u][:], PM[0:64, 256:384], [BPM[3]], [BRh[u]])
                              mm(PM[0:64, 192:256], XnS[u][:, 0:64], TM["Bp"][:, hs], True, False, [BXn[u], BTM["Bp"]], [BPM[2]])
                              mm(PM[0:64, 192:256], identR[p0:p0 + 64, p0:p0 + 64], DG[p0:p0 + 64, j, :], False, True,
                                 [BidR, BDG[j]], [BPM[2]])
                              cp("act", PTs[u][:], PM[0:64, 192:256], [BPM[2]], [BPTs[u]])
                              mm(PY[:, hs], Gm[u][:, 256:384], TM["V"][:, hs], True, False, [BGm[u], BTM["V"]], [BPY])
                              mm(PY[:, hs], Gm[u][:, 0:128], XnS[u][:, 64:128], False, False, [BGm[u], BXn[u]], [BPY])
                              mm(PY[:, hs], RhT[u][:], Sb[:, h, :], False, True, [BRh[u], BSb], [BPY])
                              mm(PS_[0:64, hs], TM["Kp"][:, hs], TM["V"][:, hs], True, False, [BTM["Kp"], BTM["V"]], [BPS])
                              mm(PS_[0:64, hs], TM["Bp"][:, hs], XnS[u][:, 64:128], False, False, [BTM["Bp"], BXn[u]], [BPS])
                              mm(PS_[0:64, hs], PTs[u][:], Sb[:, h, :], False, True, [BPTs[u], BSb], [BPS])
                          stages.append(s_fin)
                          return stages

                      for hp in range(4):
                          sa, sb_ = unit_stages(2 * hp, 0), unit_stages(2 * hp + 1, 1)
                          for idx_, a_ in enumerate(sa):
                              a_()
                              ck(40 + idx_)
                          for b_ in sb_:
                              b_()
                      cp("act", Sb[:].rearrange("p h n -> p (h n)"), PS_[0:64, :], [BPS], [BSb])

                  def post_tile(src, dst, t0, c0, X, BX):
                      cs_ = slice(c0, c0 + 128)
                      tt("dve", ysum, PY[:], y1t, ALU.add, [BPY, By1], [Bys])
                      y3 = ysum.rearrange("p (h n) -> p h n", h=8)
                      p.op("dve", lambda e: e.tensor_reduce(out=gst[:, :, 0], in_=y3, axis=AX.X, op=ALU.add),
                           reads=[Bys], writes=[Bgst])
                      tt("pool", yn, ysum, ysum, ALU.mult, [Bys], [Byn])
                      p.op("dve", lambda e: e.tensor_reduce(out=gst[:, :, 1], in_=yn.rearrange("p (h n) -> p h n", h=8),
                                                            axis=AX.X, op=ALU.add), reads=[Byn], writes=[Bgst])
                      ts("dve", gst[:, :, 0], gst[:, :, 0], 1.0 / 64, None, ALU.mult, None, [Bgst], [Bgst])
                      tt("dve", gst[:, :, 2], gst[:, :, 0], gst[:, :, 0], ALU.mult, [Bgst], [Bgst])
                      stt("dve", gst[:, :, 3], gst[:, :, 1], 1.0 / 64, gst[:, :, 2], ALU.mult, ALU.subtract, [Bgst], [Bgst])
                      ts("dve", gst[:, :, 3], gst[:, :, 3], GN_EPS, None, ALU.add, None, [Bgst], [Bgst])
                      act(gst[:, :, 3], gst[:, :, 3], AF.Ln, [Bgst], [Bgst])
                      act(gst[:, :, 3], gst[:, :, 3], AF.Exp, [Bgst], [Bgst], scale=-0.5)
                      for h in range(8):
                          ts("dve", yn[:, h * 64:(h + 1) * 64], ysum[:, h * 64:(h + 1) * 64],
                             gst[:, h, 0:1], gst[:, h, 3:4], ALU.subtract, ALU.mult, [Bys, Bgst], [Byn])
                      P_, BP_ = pa()
                      for j in range(4):
                          tr(P_[:, j * 128:(j + 1) * 128], yn[:, j * 128:(j + 1) * 128], [Byn], [BP_])
                      for j in range(4):
                          tt("dve", tB[0][:], P_[:, j * 128:(j + 1) * 128], G12[:, j, cs_], ALU.mult, [BP_, BG12[j]], [BtB[0]])
                          tt("pool", catT[:, 4 + j, :], tB[0][:], G12[:, 4 + j, cs_], ALU.add, [BtB[0], BG12[4 + j]], [Bcat])
                          cp("pool", catT[:, j, :], ycv[:, j, cs_], [Bycv[j]], [Bcat])
                      for half in range(2):
                          Po, BPo = pa()
                          for k in range(KC):
                              mm(Po[:], catT[:, k, :], wout[:, k, half * 512:(half + 1) * 512], k == 0, k == KC - 1,
                                 [Bcat, Bwout], [BPo])
                          tt("dve", xo[:, half * 512:(half + 1) * 512], Po[:], bc["gt"][:, half * 512:(half + 1) * 512],
                             ALU.mult, [BPo, Bbc["gt"]], [Bxo])
                      tt("pool", xo[:], xo[:], X[:], ALU.add, [Bxo, BX], [Bxo])
                      p.dma(dst[t0:t0 + 128, :], xo[:], reads=[Bxo])

                  def mixer_segment(src, dst, seg_T, setid, Sin, need_out, Sfin):
                      load_bc(setid, 0)
                      nsb = (seg_T + SBT - 1) // SBT
                      for d in range(2):
                          if Sin is None:
                              ts("dve", Sst[d][:].rearrange("p h n -> p (h n)"), cst[0:64, C_MG:C_MG + 512], 0.0, None,
                                 ALU.mult, None, [Bcst], [BS[d]])
                          else:
                              cp("pool", Sst[d][:], Sin[d][0][:], [Sin[d][1]], [BS[d]])
                      for s_ in range(nsb):
                          t0 = s_ * SBT
                          n = min(SBT, seg_T - t0)
                          stage1(src, t0, n, s_ == 0, seg_T)
                          ck(2)
                          for i in range(n // 128):
                              scan_tile(i * 128, 0, Sst[0], BS[0])
                              ck(3)
                              if need_out:
                                  cp("dve", sigT[:], PY[:], [BPY], [BsigT])
                                  p.dma(sp_y1[t0 + i * 128:t0 + (i + 1) * 128, :], sigT[:], reads=[BsigT])
                      p.barrier()
                      ck(4)
                      for s_ in reversed(range(nsb)):
                          t0 = s_ * SBT
                          n = min(SBT, seg_T - t0)
                          load_stage(t0, n)
                          for i in reversed(range(n // 128)):
                              scan_tile(i * 128, 1, Sst[1], BS[1])
                              if need_out:
                                  tk = t0 + i * 128
                                  p.dma(y1t, sp_y1[tk:tk + 128, :], writes=[By1])
                                  X, BX = xt[i % 2], Bxt[i % 2]
                                  p.dma(X[:], src[tk:tk + 128, :], writes=[BX])
                                  post_tile(src, dst, tk, i * 128, X, BX)
                      if Sfin is not None:
                          for d in range(2):
                              cp("pool", Sfin[d][0][:], Sst[d][:], [BS[d]], [Sfin[d][1]])
                      p.barrier()

                  mixer_segment(c_src, c_mid, CT, 1, None, not last, [(Sc[0], BSc[0]), (Sc[1], BSc[1])])
                  ck(5)
                  mixer_segment(x_src, x_mid, T, 0, [(Sc[0], BSc[0]), (Sc[1], BSc[1])], True, None)
                  p.barrier()
                  ck(6)

              with ExitStack() as s3:
                  def sb3(name, shape, dt=F32):
                      return sb(name, shape, dt, stack=s3)
                  wup = sb3("wup", [128, KC, 2 * DFF], BF16); Bwup = Buf()
                  wdn = sb3("wdn", [128, FC, D], BF16); Bwdn = Buf()
                  with ExitStack() as sw3:
                      wst3 = [sb(f"wst3{i}", [128, 2048], stack=sw3) for i in range(2)]; Bw3 = [Buf(), Buf()]
                      ii = 0
                      for k in range(KC):
                          for c_ in range(0, 2 * DFF, 2048):
                              w_ = min(2048, 2 * DFF - c_)
                              sg, Bsg = wst3[ii % 2], Bw3[ii % 2]
                              p.dma(sg[:, 0:w_], w_up[l][k * 128:(k + 1) * 128, c_:c_ + w_], writes=[Bsg])
                              cp("dve" if ii % 2 == 0 else "pool", wup[:, k, c_:c_ + w_], sg[:, 0:w_], [Bsg], [Bwup])
                              ii += 1
                      for c_ in range(FC):
                          sg, Bsg = wst3[ii % 2], Bw3[ii % 2]
                          p.dma(sg[:, 0:D], w_dn[l][c_ * 128:(c_ + 1) * 128, :], writes=[Bsg])
                          cp("dve" if ii % 2 == 0 else "pool", wdn[:, c_, :], sg[:, 0:D], [Bsg], [Bwdn])
                          ii += 1
                      p.barrier()
                  xt3 = [sb3(f"x3{i}", [128, D]) for i in range(2)]; Bx3 = [Buf() for _ in range(2)]
                  xr, Bxr = xt3[1], Bx3[1]
                  hb3 = sb3("hb3", [128, D]); Bhb3 = Buf()
                  ssq3 = sb3("ssq3", [128, 2]); Bssq3 = Buf()
                  hT3 = sb3("hT3", [128, KC, 640], BF16); BhT3 = Buf()
                  acc = sb3("acc", [128, 512]); Bacc = Buf()
                  sil, Bsil = acc, Bacc
                  actT = sb3("actT", [128, FC, 512], BF16); Bact = [Buf() for _ in range(FC)]
                  xo3, Bxo3 = hb3, Bhb3
                  fgb = sb3("fgb", [128, D]); Bfgb = Buf()
                  if last and final_norm:
                      p.dma(fgb[:], fin_g.partition_broadcast(128), writes=[Bfgb])

                  def ffn_segment(src, dst, seg_T, setid, gw, is_out):
                      load_bc(setid, 1)
                      BT = 512 if seg_T >= 512 else seg_T
                      nrow_tot = seg_T // gw
                      nrow = BT // gw
                      for b0 in range(0, seg_T, BT):
                          halo = gw if gw < seg_T else 0
                          lo, hi = max(0, b0 - halo), min(seg_T, b0 + BT + halo)
                          nw = hi - lo
                          off = b0 - lo
                          tpos, xi = lo, 0
                          while tpos < hi:
                              w_ = min(128, hi - tpos)
                              X, BX = xt3[xi % 2], Bx3[xi % 2]
                              if w_ < 128:
                                  p.op("pool", lambda e, X=X: e.memset(X[:], 0.0), writes=[BX])
                              p.dma(X[0:w_, :], src[tpos:tpos + w_, :], writes=[BX])
                              act(hb3[:], X[:], AF.Square, [BX], [Bhb3, Bssq3], accum=ssq3[:, 0:1])
                              ts("dve", ssq3[:, 1:2], ssq3[:, 0:1], 1.0 / D, RMS_EPS, ALU.mult, ALU.add, [Bssq3], [Bssq3])
                              act(ssq3[:, 1:2], ssq3[:, 1:2], AF.Ln, [Bssq3], [Bssq3])
                              act(ssq3[:, 1:2], ssq3[:, 1:2], AF.Exp, [Bssq3], [Bssq3], scale=-0.5)
                              stt("dve", hb3[:], X[:], ssq3[:, 1:2], bc["gs"][:], ALU.mult, ALU.mult,
                                  [BX, Bssq3, Bbc["gs"]], [Bhb3])
                              tt("pool", hb3[:], hb3[:], bc["sh"][:], ALU.add, [Bhb3, Bbc["sh"]], [Bhb3])
                              col0 = tpos - lo
                              for half in range(2):
                                  P_, BP_ = pa()
                                  for q in range(4):
                                      k = half * 4 + q
                                      tr(P_[:, q * 128:(q + 1) * 128], hb3[:, k * 128:(k + 1) * 128], [Bhb3], [BP_])
                                  cp("act", hT3[:, half * 4:half * 4 + 4, col0:col0 + w_],
                                     P_[:].rearrange("p (q n) -> p q n", q=4)[:, :, 0:w_], [BP_], [BhT3])
                              tpos += w_
                              xi += 1
                          wr0 = off // gw
                          for c_ in range(FC):
                              segs = [(0, min(nw, 512), PG, BPG[0])]
                              if nw > 512:
                                  segs.append((512, nw, PN[0], BPN[0][0]))
                              for (a_, b_, P_, BP_) in segs:
                                  for k in range(KC):
                                      mm(P_[:, 0:b_ - a_], wup[:, k, c_ * 128:(c_ + 1) * 128], hT3[:, k, a_:b_],
                                         k == 0, k == KC - 1, [Bwup, BhT3], [BP_])
                              wc = lambda ty, tx: vec[:, 48 + (ty * 3 + tx) * FC + c_:48 + (ty * 3 + tx) * FC + c_ + 1]
                              bcol = vec[:, 48 + 9 * FC + c_:48 + 9 * FC + c_ + 1]

                              def tap(ty, tx, first):
                                  dy, dx = ty - 1, tx - 1
                                  r0g = b0 // gw
                                  rows_ok = [r for r in range(nrow) if 0 <= r0g + r + dy < nrow_tot]
                                  if not rows_ok:
                                      return
                                  ra, rb = rows_ok[0], rows_ok[-1] + 1
                                  ca, cb2 = (1, gw) if dx == -1 else ((0, gw - 1) if dx == 1 else (0, gw))
                                  rsplit = 512 // gw
                                  r = ra
                                  while r < rb:
                                      wr = wr0 + r + dy
                                      if wr < rsplit:
                                          re_ = min(rb, rsplit - wr0 - dy)
                                          P_, B_, wbase = PG, BPG[0], 0
                                      else:
                                          re_ = rb
                                          P_, B_, wbase = PN[0], BPN[0][0], rsplit
                                      nr = re_ - r
                                      iv = P_[:, (wr - wbase) * gw:(wr - wbase + nr) * gw].rearrange("p (r c) -> p r c", c=gw)[:, :, ca + dx:cb2 + dx]
                                      ov = acc[:, r * gw:(r + nr) * gw].rearrange("p (r c) -> p r c", c=gw)[:, :, ca:cb2]
                                      if first:
                                          act(ov, iv, AF.Identity, [B_, Bvec], [Bacc], bias=bcol, scale=wc(ty, tx))
                                      else:
                                          stt("dve", ov, iv, wc(ty, tx), ov, ALU.mult, ALU.add, [B_, Bvec, Bacc], [Bacc])
                                      r = re_
                              tap(1, 1, True)
                              for ty in range(3):
                                  for tx in range(3):
                                      if (ty, tx) != (1, 1) and not (gw >= seg_T and ty != 1):
                                          tap(ty, tx, False)
                              act(sil[:, 0:BT], acc[:, 0:BT], AF.Silu, [], [Bacc])
                              Pv, BPv = pa()
                              for k in range(KC):
                                  mm(Pv[:, 0:BT], wup[:, k, DFF + c_ * 128:DFF + (c_ + 1) * 128], hT3[:, k, off:off + BT],
                                     k == 0, k == KC - 1, [Bwup, BhT3], [BPv])
                              tt("dve", actT[:, c_, 0:BT], sil[:, 0:BT], Pv[:, 0:BT], ALU.mult, [Bsil, BPv], [Bact[c_]])
                          for i in range(BT // 128):
                              tk = b0 + i * 128
                              p.dma(xr[:], src[tk:tk + 128, :], writes=[Bxr])
                              for half in range(2):
                                  Po, BPo = pa()
                                  for c_ in range(FC):
                                      mm(Po[:], actT[:, c_, i * 128:(i + 1) * 128], wdn[:, c_, half * 512:(half + 1) * 512],
                                         c_ == 0, c_ == FC - 1, [Bact[c_], Bwdn], [BPo])
                                  tt("dve", xo3[:, half * 512:(half + 1) * 512], Po[:], bc["gt"][:, half * 512:(half + 1) * 512],
                                     ALU.mult, [BPo, Bbc["gt"]], [Bxo3])
                              tt("pool", xo3[:], xo3[:], xr[:], ALU.add, [Bxo3, Bxr], [Bxo3])
                              if is_out and final_norm:
                                  act(xt3[0][:], xo3[:], AF.Square, [Bxo3], [Bx3[0], Bssq3], accum=ssq3[:, 0:1])
                                  ts("dve", ssq3[:, 1:2], ssq3[:, 0:1], 1.0 / D, RMS_EPS, ALU.mult, ALU.add, [Bssq3], [Bssq3])
                                  act(ssq3[:, 1:2], ssq3[:, 1:2], AF.Ln, [Bssq3], [Bssq3])
                                  act(ssq3[:, 1:2], ssq3[:, 1:2], AF.Exp, [Bssq3], [Bssq3], scale=-0.5)
                                  stt("dve", xo3[:], xo3[:], ssq3[:, 1:2], fgb[:], ALU.mult, ALU.mult, [Bxo3, Bssq3, Bfgb], [Bxo3])
                              p.dma(dst[tk:tk + 128, :], xo3[:], reads=[Bxo3])

                  if not last:
                      ffn_segment(c_mid, c_dst, CT, 1, CT, False)
                  ffn_segment(x_mid, x_dst, T, 0, GW, last)
                  p.barrier()
        except _Stop:
            pass
        p.barrier()
        nc._prog_ninst = p.ninst
    return nc


def prep_inputs(x, c, ctx, c_ctx, ada_w, ada_b, norm1_g, norm2_g, w_in, conv_a_w, rw_w0, rw_w_up, rw_a0, rw_a_up,
                rw_k_k, rw_k_a, rw_r_k, rw_g_up, rw_ln_g, rw_ln_b, w_out, ffn_w_up, ffn_conv_w, ffn_conv_b,
                ffn_w_down, final_g, b):
    L = ada_w.shape[0]
    f = lambda a: np.ascontiguousarray(a, dtype=np.float32)
    cs = np.stack([c[b].reshape(KC, 128).T, c_ctx.reshape(KC, 128).T], axis=-1)
    rows = np.concatenate([ada_b, norm1_g, norm2_g], axis=1)[:, None, :].repeat(2, axis=1)

    def ch(v, n):
        return v.reshape(n, 128).T
    vec = np.zeros((L, 128, NVEC), np.float32)
    for l in range(L):
        cols = [ch(rw_k_k[l], 4), ch(rw_k_a[l], 4), ch(rw_r_k[l].reshape(-1), 4), ch(rw_ln_g[l], 4), ch(rw_ln_b[l], 4),
                ch(conv_a_w[l, 0], 4), ch(conv_a_w[l, 1], 4), ch(conv_a_w[l, 2], 4),
                ch(rw_w0[l, 0], 4), ch(rw_w0[l, 1], 4), ch(rw_a0[l, 0], 4), ch(rw_a0[l, 1], 4)]
        for ty in range(3):
            for tx in range(3):
                cols.append(ch(ffn_conv_w[l, ty, tx], FC))
        cols.append(ch(ffn_conv_b[l], FC))
        vec[l] = np.concatenate(cols, axis=1)
    rowv = rw_w0.reshape(L, 1, 1024)
    lup = np.concatenate([rw_w_up.transpose(0, 2, 1, 3), rw_a_up.transpose(0, 2, 1, 3)], axis=1)
    return {
        "x": f(x[b]), "ctx": f(ctx[b]), "cs": f(cs), "cst": make_consts(), "ada_w": f(ada_w), "rows": f(rows),
        "fin_g": f(final_g), "w_in": f(w_in), "w_out": f(w_out), "w_up": f(ffn_w_up), "w_dn": f(ffn_w_down),
        "vec": f(vec), "rowv": f(rowv), "lup": f(lup), "gup": f(rw_g_up),
    }


def kernel(**inputs):
    inputs = {k: np.asarray(v) for k, v in inputs.items()}
    B, T, _ = inputs["x"].shape
    CT = inputs["ctx"].shape[1]
    nc = build_program(T, CT, inputs["ada_w"].shape[0])
    in_maps = [prep_inputs(b=b, **inputs) for b in range(B)]
    res = run_bass_kernel_spmd(nc, in_maps, core_ids=list(range(B)))
    return np.stack([r["out"] for r in res.results], axis=0).astype(np.float32)
```

```python
import math
import numpy as np
from contextlib import ExitStack
import concourse.bass as bass
import concourse.mybir as mybir
from concourse.bass_utils import run_bass_kernel_spmd

F32 = mybir.dt.float32
BF16 = mybir.dt.bfloat16
F32R = mybir.dt.float32r
AF = mybir.ActivationFunctionType
ALU = mybir.AluOpType
AX = mybir.AxisListType

D = 1024
KC = 8
PROJ = 3328
NPC = 26
DFF = 2816
FC = 22
GW = 64
DS = math.exp(-0.5)
RMS_EPS = 1e-6
GN_EPS = 64e-5
NVEC = 48 + 10 * FC
SCAN_DT = F32

C_ID = 0
C_MG = 128
C_ML = C_MG + 1024
C_TF = C_ML + 256
C_BO = C_TF + 512
C_ON = C_BO + 128
C_IB = C_ON + 128
NCST = C_IB + 64


def make_consts():
    c = np.zeros((128, NCST), np.float32)
    s = np.arange(128)[:, None]
    t = np.arange(128)[None, :]
    c[:, C_ID:C_ID + 128] = np.eye(128)
    for d in range(2):
        lt = (s < t) if d == 0 else (s > t)
        le = (s <= t) if d == 0 else (s >= t)
        g = c[:, C_MG + 512 * d:C_MG + 512 * (d + 1)]
        g[:, 0:128] = -1.0 * lt
        g[:, 128:256] = le
        g[:, 256:384] = lt
        g[:, 384:512] = le
        c[:, C_ML + 128 * d:C_ML + 128 * (d + 1)] = -1.0 * lt.T
        f = c[:, C_TF + 256 * d:C_TF + 256 * (d + 1)]
        f[:, 0:128] = -DS * le
        f[:, 128:256] = -DS * lt
    c[:, C_BO:C_BO + 128] = (s // 64 == t // 64)
    c[:, C_ON:C_ON + 128] = 1.0
    c[:, C_IB:C_IB + 64] = (s % 64 == np.arange(64)[None, :])
    return c


class Buf:
    __slots__ = ("name", "w", "rd", "ps")

    def __init__(self, name="", ps=False):
        self.name = name
        self.w = None
        self.rd = []
        self.ps = ps


ENGMAP = {"pe": "tensor", "dve": "vector", "act": "scalar", "pool": "gpsimd", "sp": "sync"}


class Eng:
    def __init__(self, name, sem, h):
        self.name = name
        self.sem = sem
        self.h = h
        self.cnt = 0
        self.waited = {}


class Prog:
    def __init__(self, nc, sems, dma_sems):
        self.nc = nc
        self.E = {n: Eng(n, sems[n], getattr(nc, ENGMAP[n])) for n in ENGMAP}
        self.dma_sems = dma_sems
        self.dma_cnt = [0] * len(dma_sems)
        self.dma_rr = 0
        self.ninst = 0
        self.stop = False

    def _deps(self, reads, writes):
        d = []
        for b in reads:
            if b.w is not None:
                d.append(b.w)
        for b in writes:
            if b.w is not None:
                d.append(b.w)
            d.extend(b.rd)
        return d

    def _waits(self, e, deps, skip_self):
        need = {}
        for (sem, val, owner) in deps:
            if skip_self and owner is e:
                continue
            k = id(sem)
            if e.waited.get(k, 0) >= val:
                continue
            if k not in need or need[k][1] < val:
                need[k] = (sem, val)
        for k, (sem, val) in need.items():
            e.waited[k] = val
            e.h.wait_ge(sem, val)

    def op(self, en, fn, reads=(), writes=()):
        if self.stop:
            return
        e = self.E[en]
        deps = self._deps(reads, writes)
        for b in reads:
            if b.ps:
                deps.extend(t for t in b.rd if t[2] is not e)
        self._waits(e, deps, skip_self=(en == "pe"))
        e.cnt += 1
        tok = (e.sem, e.cnt, e)
        fn(e.h).then_inc(e.sem, 1)
        for b in writes:
            b.w = tok
            b.rd = []
        for b in reads:
            b.rd.append(tok)
        self.ninst += 1

    def dma(self, out, in_, reads=(), writes=(), q="sp", slow=False):
        if self.stop:
            return
        e = self.E[q]
        deps = self._deps(reads, writes)
        i = self.dma_rr
        self.dma_rr = (self.dma_rr + 1) % len(self.dma_sems)
        sem = self.dma_sems[i]
        if self.dma_cnt[i] > 0:
            deps.append((sem, self.dma_cnt[i], None))
        self._waits(e, deps, skip_self=False)
        self.dma_cnt[i] += 16
        tok = (sem, self.dma_cnt[i], None)
        if slow:
            e.h.dma_start(out=out, in_=in_, allow_slow_non_contiguous=True).then_inc(sem, 16)
        else:
            e.h.dma_start(out=out, in_=in_).then_inc(sem, 16)
        for b in writes:
            b.w = tok
            b.rd = []
        for b in reads:
            b.rd.append(tok)
        self.ninst += 1
        return tok

    def barrier(self):
        if self.stop:
            return
        toks = [(e.sem, e.cnt, e) for e in self.E.values() if e.cnt > 0]
        toks += [(s, c, None) for s, c in zip(self.dma_sems, self.dma_cnt) if c > 0]
        for e in self.E.values():
            self._waits(e, toks, skip_self=True)


class _Stop(Exception):
    pass


def build_program(T, CT, L=2, final_norm=True, dbg=None):
    assert T % 512 == 0 and CT % 128 == 0
    nc = bass.Bass("TRN2", target_bir_lowering=False)
    dt_ = nc.dram_tensor

    def din(name, shape, dt=F32):
        return dt_(name, shape, dt, kind="ExternalInput").ap()

    def dint(name, shape, dt=F32):
        return dt_(name, shape, dt, kind="Internal").ap()

    x_in = din("x", [T, D])
    ctx_in = din("ctx", [CT, D])
    cs_in = din("cs", [128, KC, 2])
    cst_in = din("cst", [128, NCST])
    ada_w = din("ada_w", [L, D, 6 * D])
    rows_in = din("rows", [L, 2, 6 * D + 2 * D])
    fin_g = din("fin_g", [D])
    w_in = din("w_in", [L, D, PROJ])
    w_out = din("w_out", [L, D, D])
    w_up = din("w_up", [L, D, 2 * DFF])
    w_dn = din("w_dn", [L, DFF, D])
    vec_in = din("vec", [L, 128, NVEC])
    rowv_in = din("rowv", [L, 1, 1024])
    lup_in = din("lup", [L, 128, 2, 512])
    gup_in = din("gup", [L, 128, 512])
    out = dt_("out", [T, D], F32, kind="ExternalOutput").ap()

    xs = [dint("xs0", [T, D]), dint("xs1", [T, D])]
    cxs = [dint("cxs0", [CT, D]), dint("cxs1", [CT, D])]
    modr = dint("modr", [2, 6, D])
    TS = max(T, CT)
    sp_base = dint("sp_base", [16, 128, TS])
    sp_wa = dint("sp_wa", [128, TS])
    sp_g = dint("sp_g", [8, 128, TS])
    sp_ycv = dint("sp_ycv", [4, 128, TS + 2], BF16)
    sp_y1 = dint("sp_y1", [TS, 512])

    st = ExitStack()
    with st:
        sems = {n: st.enter_context(nc.semaphore(n)) for n in ENGMAP}
        dsem = [st.enter_context(nc.semaphore(f"dq{i}")) for i in range(24)]
        p = Prog(nc, sems, dsem)

        uid = [0]

        def sb(name, shape, dt=F32, stack=st):
            uid[0] += 1
            return stack.enter_context(nc.sbuf_tensor(f"s{uid[0]}_{name}", shape, dt))

        def ps(name, shape, dt=F32):
            uid[0] += 1
            return st.enter_context(nc.psum_tensor(f"p{uid[0]}_{name}", shape, dt))

        cst = sb("cst", [128, NCST]); Bcst = Buf()
        vec = sb("vec", [128, NVEC]); Bvec = Buf()
        omka = sb("omka", [128, 4]); Bomka = Buf()
        bc = {n: sb("bc_" + n, [128, D]) for n in ("gs", "sh", "gt")}
        Bbc = {n: Buf() for n in bc}
        PA = [ps("PA0", [128, 512]), ps("PA1", [128, 512])]; BPA = [Buf(ps=True), Buf(ps=True)]
        _bpg = Buf(ps=True)
        PG = ps("PG", [128, 512]); BPG = [_bpg, _bpg]
        PN = [ps("PN0", [128, 512]), ps("PN1", [128, 512])]
        _bpn = [Buf(ps=True), Buf(ps=True)]
        BPN = [[_bpn[0]] * 3, [_bpn[1]] * 3]
        _bpm = Buf(ps=True)
        PM = ps("PM", [128, 512]); BPM = [_bpm] * 5
        PS_ = ps("PS", [128, 512]); BPS = Buf(ps=True)
        PY = ps("PY", [128, 512]); BPY = Buf(ps=True)
        pa_rr = [0]

        def pa():
            i = pa_rr[0]
            pa_rr[0] ^= 1
            return PA[i], BPA[i]

        ident = cst[:, C_ID:C_ID + 128]
        p.dma(cst[:], cst_in, writes=[Bcst])
        identR = sb("identR", [128, 128], SCAN_DT); BidR = Buf()
        p.op("dve", lambda e: e.tensor_copy(out=identR[:], in_=ident), reads=[Bcst], writes=[BidR])

        def mm(o, l, r, start, stop, reads, writes):
            p.op("pe", lambda e: e.matmul(o, l, r, start=start, stop=stop), reads=reads, writes=writes)

        def tr(o, i_, reads, writes):
            p.op("pe", lambda e: e.transpose(o, i_, ident), reads=list(reads) + [Bcst], writes=writes)

        def act(o, i_, func, reads, writes, bias=None, scale=None, accum=None):
            kw = {}
            if bias is not None:
                kw["bias"] = bias
            if scale is not None:
                kw["scale"] = scale
            if accum is not None:
                kw["accum_out"] = accum
            p.op("act", lambda e: e.activation(out=o, in_=i_, func=func, **kw), reads=reads, writes=writes)

        def tt(en, o, a, b, op, reads, writes):
            p.op(en, lambda e: e.tensor_tensor(out=o, in0=a, in1=b, op=op), reads=reads, writes=writes)

        def ts(en, o, a, s1, s2, op0, op1, reads, writes):
            if s2 is None:
                p.op(en, lambda e: e.tensor_scalar(out=o, in0=a, scalar1=s1, scalar2=None, op0=op0),
                     reads=reads, writes=writes)
            else:
                p.op(en, lambda e: e.tensor_scalar(out=o, in0=a, scalar1=s1, scalar2=s2, op0=op0, op1=op1),
                     reads=reads, writes=writes)

        def stt(en, o, a, s, b, op0, op1, reads, writes):
            p.op(en, lambda e: e.scalar_tensor_tensor(out=o, in0=a, scalar=s, in1=b, op0=op0, op1=op1),
                 reads=reads, writes=writes)

        def cp(en, o, i_, reads, writes):
            if en == "act":
                act(o, i_, AF.Copy, reads, writes)
            else:
                p.op(en, lambda e: e.tensor_copy(out=o, in_=i_), reads=reads, writes=writes)

        def ck(k):
            if dbg == k and not p.stop:
                p.barrier()
                p.stop = True

        try:
          for l in range(L):
              last = (l == L - 1)
              x_src = x_in if l == 0 else xs[1]
              c_src = ctx_in if l == 0 else cxs[1]
              x_mid, c_mid = xs[0], cxs[0]
              x_dst, c_dst = xs[1], cxs[1]
              if last:
                  x_dst = out

              p.barrier()
              with ExitStack() as s0:
                  rows = sb("rows", [2, 8 * D], stack=s0); Brows = Buf()
                  mod = sb("mod", [2, 6 * D], stack=s0); Bmod = Buf()
                  drv = sb("drv", [2, 6, D], stack=s0); Bdrv = Buf()
                  cs = sb("cs", [128, KC, 2], stack=s0); Bcs = Buf()
                  scs = sb("scs", [128, KC, 2], stack=s0); Bscs = Buf()
                  stg = [sb(f"adastg{i}", [128, KC, 512], stack=s0) for i in range(2)]; Bstg = [Buf(), Buf()]
                  p.dma(rows[:], rows_in[l], writes=[Brows])
                  p.dma(cs[:], cs_in, writes=[Bcs])
                  p.dma(vec[:], vec_in[l], writes=[Bvec])
                  ts("dve", omka[:], vec[:, 4:8], -1.0, 1.0, ALU.mult, ALU.add, [Bvec], [Bomka])
                  act(scs[:], cs[:], AF.Silu, [Bcs], [Bscs])
                  for cb in range(12):
                      sg, Bsg = stg[cb % 2], Bstg[cb % 2]
                      p.dma(sg[:], ada_w[l][:, cb * 512:(cb + 1) * 512].rearrange("(k p) n -> p k n", p=128),
                            writes=[Bsg])
                      P_, BP_ = pa()
                      for k in range(KC):
                          mm(P_[0:2, :], scs[:, k, :], sg[:, k, :], k == 0, k == KC - 1, [Bscs, Bsg], [BP_])
                      tt("dve", mod[:, cb * 512:(cb + 1) * 512], P_[0:2, :], rows[:, cb * 512:(cb + 1) * 512],
                         ALU.add, [BP_, Brows], [Bmod])
                  stt("dve", drv[:, 0, :], mod[:, D:2 * D], 1.0, rows[:, 6 * D:7 * D], ALU.add, ALU.mult,
                      [Bmod, Brows], [Bdrv])
                  cp("dve", drv[:, 1, :], mod[:, 0:D], [Bmod], [Bdrv])
                  cp("dve", drv[:, 2, :], mod[:, 2 * D:3 * D], [Bmod], [Bdrv])
                  stt("dve", drv[:, 3, :], mod[:, 4 * D:5 * D], 1.0, rows[:, 7 * D:8 * D], ALU.add, ALU.mult,
                      [Bmod, Brows], [Bdrv])
                  cp("dve", drv[:, 4, :], mod[:, 3 * D:4 * D], [Bmod], [Bdrv])
                  cp("dve", drv[:, 5, :], mod[:, 5 * D:6 * D], [Bmod], [Bdrv])
                  Bmodr = Buf()
                  p.dma(modr, drv[:], reads=[Bdrv], writes=[Bmodr])
                  p.barrier()
              ck(0)

              def load_bc(setid, which):
                  for n, r in (("gs", 0), ("sh", 1), ("gt", 2)):
                      p.dma(bc[n][:], modr[setid, 3 * which + r].partition_broadcast(128),
                            reads=[Bmodr], writes=[Bbc[n]])

              with ExitStack() as s1:
                  def sb1(name, shape, dt=F32):
                      return sb(name, shape, dt, stack=s1)
                  rowv = sb1("rowv", [1, 1024]); Browv = Buf()
                  lup = sb1("lup", [128, 2, 512]); Blup = Buf()
                  gup = sb1("gup", [128, 512], BF16); Bgup = Buf()
                  Sst = [sb1(f"S{d}", [64, 8, 64], SCAN_DT) for d in range(2)]; BS = [Buf(), Buf()]
                  Sc = [sb1(f"Sc{d}", [64, 8, 64], SCAN_DT) for d in range(2)]; BSc = [Buf(), Buf()]
                  p.dma(rowv[:], rowv_in[l], writes=[Browv])
                  p.dma(lup[:], lup_in[l], writes=[Blup])
                  win = sb1("win", [128, KC, PROJ], BF16); Bwin = Buf()
                  wout = sb1("wout", [128, KC, D], BF16); Bwout = Buf()
                  with ExitStack() as sw:
                      wstg = [sb(f"wstg{i}", [128, PROJ], stack=sw) for i in range(2)]; Bwstg = [Buf(), Buf()]
                      for k in range(KC):
                          sg, Bsg = wstg[k % 2], Bwstg[k % 2]
                          p.dma(sg[:], w_in[l][k * 128:(k + 1) * 128, :], writes=[Bsg])
                          cp("dve" if k % 2 == 0 else "pool", win[:, k, :], sg[:], [Bsg], [Bwin])
                      for k in range(KC):
                          sg, Bsg = wstg[k % 2], Bwstg[k % 2]
                          p.dma(sg[:, 0:D], w_out[l][k * 128:(k + 1) * 128, :], writes=[Bsg])
                          cp("dve" if k % 2 == 0 else "pool", wout[:, k, :], sg[:, 0:D], [Bsg], [Bwout])
                      p.dma(wstg[0][:, 0:512], gup_in[l], writes=[Bwstg[0]])
                      cp("dve", gup[:], wstg[0][:, 0:512], [Bwstg[0]], [Bgup])
                      p.barrier()
                  ck(1)

                  SBT = 256
                  xt = [sb1(f"xt{i}", [128, D]) for i in range(2)]; Bxt = [Buf(), Buf()]
                  hb = sb1("hb", [128, D]); Bhb = Buf()
                  ssq = sb1("ssq", [128, 2]); Bssq = Buf()
                  hT = sb1("hT", [128, KC, SBT], BF16); BhT = Buf()
                  base = sb1("base", [128, 16, SBT]); Bbase = [Buf() for _ in range(16)]
                  WA = sb1("WA", [128, SBT]); BWA = Buf()
                  sgl = sb1("sgl", [128, SBT], BF16); Bsgl = Buf()
                  G12 = sb1("G12", [128, 8, SBT]); BG12 = [Buf() for _ in range(8)]
                  ycv = sb1("ycv", [128, 4, SBT], BF16); Bycv = [Buf() for _ in range(4)]
                  ubuf_t = sb1("ubuf", [128, 4 * (SBT + 2)]); Bub = [Buf() for _ in range(4)]
                  ubuf = ubuf_t[:].rearrange("p (j n) -> p j n", j=4)
                  cbb_t = sb1("cbb", [128, 4 * (SBT + 1)]); Bcbb = [Buf() for _ in range(4)]
                  cbb = cbb_t[:].rearrange("p (j n) -> p j n", j=4)
                  tmpA = [sb1(f"tmpA{i}", [128, SBT]) for i in range(4)]; BtA = [Buf() for _ in range(4)]
                  KR = sb1("KR", [128, 4, 256], SCAN_DT); BKR = [Buf() for _ in range(4)]
                  BtF = sb1("BtF", [128, 4, 128], SCAN_DT); BBt = [Buf() for _ in range(4)]
                  KtF = sb1("KtF", [128, 4, 128], SCAN_DT); BKt = [Buf() for _ in range(4)]
                  BpF = sb1("BpF", [128, 4, 128]); BBp = [Buf() for _ in range(4)]
                  KpF = sb1("KpF", [128, 4, 128]); BKp = [Buf() for _ in range(4)]
                  KaF32 = sb1("KaF32", [128, 4, 128]); BKa32 = [Buf() for _ in range(4)]
                  DG = sb1("DG", [128, 4, 64], SCAN_DT); BDG = [Buf() for _ in range(4)]
                  EG = sb1("EG", [128, 3, 128]); BEG = Buf()
                  aFt = sb1("aFt", [128, 128]); BaF = Buf()
                  tB = [sb1(f"tB{i}", [128, 128]) for i in range(3)]; BtB = [Buf() for _ in range(3)]
                  sigT = sb1("sigT", [128, 512]); BsigT = Buf()
                  TM = {n: sb1("TM_" + n, [128, 512], SCAN_DT) for n in ("Ka", "Bp", "Kp", "V")}
                  BTM = {n: Buf() for n in TM}
                  LT = [[sb1(f"LT{u}{i}", [128, 256], SCAN_DT) for i in range(2)] for u in range(2)]
                  BLT = [[[Buf(), Buf()] for i in range(2)] for u in range(2)]
                  Nn = [[sb1(f"Nn{u}{i}", [128, 128], SCAN_DT) for i in range(2)] for u in range(2)]
                  BNn = [[Buf() for i in range(2)] for u in range(2)]
                  Gm = [sb1(f"Gm{u}", [128, 384], SCAN_DT) for u in range(2)]; BGm = [Buf(), Buf()]
                  TtF = [sb1(f"TtF{u}", [128, 128], SCAN_DT) for u in range(2)]; BTt = [Buf(), Buf()]
                  AkV = [sb1(f"AkV{u}", [128, 64], SCAN_DT) for u in range(2)]; BAkV = [Buf(), Buf()]
                  XnS = [sb1(f"XnS{u}", [128, 128], SCAN_DT) for u in range(2)]; BXn = [Buf(), Buf()]
                  PTs = [sb1(f"PTs{u}", [64, 64], SCAN_DT) for u in range(2)]; BPTs = [Buf(), Buf()]
                  RhT = [sb1(f"RhT{u}", [64, 128], SCAN_DT) for u in range(2)]; BRh = [Buf(), Buf()]
                  ysum = ubuf_t[:, 0:512]; Bys = Buf()
                  yn = ubuf_t[:, 512:1024]; Byn = Buf()
                  y1t = cbb_t[:, 0:512]; By1 = Buf()
                  gst = sb1("gst", [128, 8, 4]); Bgst = Buf()
                  catT = hT[:, :, 0:128]; Bcat = Buf()
                  xo, Bxo = hb, Bhb

                  def vcol(i, j):
                      return vec[:, 4 * i + j:4 * i + j + 1]

                  def norm_tile(src_ap, xt_i, col0):
                      X, BX = xt[xt_i], Bxt[xt_i]
                      p.dma(X[:], src_ap, writes=[BX])
                      act(hb[:], X[:], AF.Square, [BX], [Bhb, Bssq], accum=ssq[:, 0:1])
                      ts("dve", ssq[:, 1:2], ssq[:, 0:1], 1.0 / D, RMS_EPS, ALU.mult, ALU.add, [Bssq], [Bssq])
                      act(ssq[:, 1:2], ssq[:, 1:2], AF.Ln, [Bssq], [Bssq])
                      act(ssq[:, 1:2], ssq[:, 1:2], AF.Exp, [Bssq], [Bssq], scale=-0.5)
                      stt("dve", hb[:], X[:], ssq[:, 1:2], bc["gs"][:], ALU.mult, ALU.mult,
                          [BX, Bssq, Bbc["gs"]], [Bhb])
                      tt("pool", hb[:], hb[:], bc["sh"][:], ALU.add, [Bhb, Bbc["sh"]], [Bhb])
                      for half in range(2):
                          P_, BP_ = pa()
                          for q in range(4):
                              k = half * 4 + q
                              tr(P_[:, q * 128:(q + 1) * 128], hb[:, k * 128:(k + 1) * 128], [Bhb], [BP_])
                          cp("act", hT[:, half * 4:half * 4 + 4, col0:col0 + 128],
                             P_[:].rearrange("p (q n) -> p q n", q=4), [BP_], [BhT])
                      return X, BX

                  def proj_chunk(c, n):
                      P_, BP_ = pa()
                      for k in range(KC):
                          mm(P_[:, 0:n], win[:, k, c * 128:(c + 1) * 128], hT[:, k, 0:n], k == 0, k == KC - 1,
                             [Bwin, BhT], [BP_])
                      return P_, BP_

                  def stage1(seg_src, t0, n, first, seg_T):
                      for i in range(n // 128):
                          norm_tile(seg_src[t0 + i * 128:t0 + (i + 1) * 128, :], i % 2, i * 128)
                      for j in range(4):
                          if first:
                              p.op("pool", lambda e, j=j: e.memset(ubuf[:, j, 0:2], 0.0), writes=[Bub[j]])
                              p.op("pool", lambda e, j=j: e.memset(cbb[:, j, 0:1], 0.0), writes=[Bcbb[j]])
                          Pb, BPb = proj_chunk(j, n)
                          cp("act", cbb[:, j, 1:n + 1], Pb[:, 0:n], [BPb], [Bcbb[j]])
                          Pc, BPc = proj_chunk(4 + j, n)
                          cp("act", tmpA[0][:, 0:n], Pc[:, 0:n], [BPc], [BtA[0]])
                          Px, BPx = proj_chunk(8 + j, n)
                          tt("dve", ubuf[:, j, 2:n + 2], tmpA[0][:, 0:n], Px[:, 0:n], ALU.mult, [BtA[0], BPx], [Bub[j]])
                          ts("dve", tmpA[1][:, 0:n], ubuf[:, j, 0:n], vcol(5, j), None, ALU.mult, None,
                             [Bub[j], Bvec], [BtA[1]])
                          stt("dve", tmpA[1][:, 0:n], ubuf[:, j, 1:n + 1], vcol(6, j), tmpA[1][:, 0:n], ALU.mult, ALU.add,
                              [Bub[j], Bvec, BtA[1]], [BtA[1]])
                          stt("dve", tmpA[1][:, 0:n], ubuf[:, j, 2:n + 2], vcol(7, j), tmpA[1][:, 0:n], ALU.mult, ALU.add,
                              [Bub[j], Bvec, BtA[1]], [BtA[1]])
                          tt("pool", ycv[:, j, 0:n], tmpA[1][:, 0:n], cbb[:, j, 0:n], ALU.mult, [BtA[1], Bcbb[j]], [Bycv[j]])
                          p.dma(sp_ycv[j][:, t0:t0 + n], ycv[:, j, 0:n], reads=[Bycv[j]])
                          if t0 + n == seg_T:
                              ts("dve", tmpA[1][:, 0:1], ubuf[:, j, n:n + 1], vcol(5, j), None, ALU.mult, None,
                                 [Bub[j], Bvec], [BtA[1]])
                              stt("dve", tmpA[1][:, 0:1], ubuf[:, j, n + 1:n + 2], vcol(6, j), tmpA[1][:, 0:1],
                                  ALU.mult, ALU.add, [Bub[j], Bvec, BtA[1]], [BtA[1]])
                              tt("pool", ycv[:, j, 0:1], tmpA[1][:, 0:1], cbb[:, j, n:n + 1], ALU.mult,
                                 [BtA[1], Bcbb[j]], [Bycv[j]])
                              p.dma(sp_ycv[j][:, t0 + n:t0 + n + 1], ycv[:, j, 0:1], reads=[Bycv[j]], slow=True)
                          else:
                              cp("pool", ubuf[:, j, 0:2], ubuf[:, j, n:n + 2], [Bub[j]], [Bub[j]])
                              cp("pool", cbb[:, j, 0:1], cbb[:, j, n:n + 1], [Bcbb[j]], [Bcbb[j]])
                      Pw, BPw = proj_chunk(24, n)
                      act(WA[0:64, 0:n], Pw[0:64, 0:n], AF.Tanh, [BPw], [BWA])
                      cp("act", WA[64:128, 0:n], Pw[64:128, 0:n], [BPw], [BWA])
                      p.dma(sp_wa[:, t0:t0 + n], WA[:, 0:n], reads=[BWA])
                      Pg, BPg = proj_chunk(25, n)
                      act(sgl[:, 0:n], Pg[:, 0:n], AF.Sigmoid, [BPg], [Bsgl])
                      for j in range(4):
                          rj, kpj, kj, vj = base[:, j, 0:n], base[:, 4 + j, 0:n], base[:, 8 + j, 0:n], base[:, 12 + j, 0:n]
                          Pr, BPr = proj_chunk(12 + j, n)
                          cp("act", rj, Pr[:, 0:n], [BPr], [Bbase[j]])
                          Pk, BPk = proj_chunk(16 + j, n)
                          cp("act", kj, Pk[:, 0:n], [BPk], [Bbase[8 + j]])
                          Pv, BPv = proj_chunk(20 + j, n)
                          cp("dve", vj, Pv[:, 0:n], [BPv], [Bbase[12 + j]])
                          act(tmpA[2][:, 0:n], kj, AF.Square, [Bbase[8 + j], Bvec], [BtA[2]], scale=vcol(0, j))
                          Pq, BPq = pa()
                          mm(Pq[:, 0:n], cst[:, C_BO:C_BO + 128], tmpA[2][:, 0:n], True, True, [Bcst, BtA[2]], [BPq])
                          ts("dve", tmpA[2][:, 0:n], Pq[:, 0:n], 1e-24, None, ALU.max, None, [BPq], [BtA[2]])
                          act(tmpA[2][:, 0:n], tmpA[2][:, 0:n], AF.Ln, [BtA[2]], [BtA[2]])
                          act(tmpA[2][:, 0:n], tmpA[2][:, 0:n], AF.Exp, [BtA[2]], [BtA[2]], scale=-0.5)
                          stt("dve", kpj, kj, vcol(0, j), tmpA[2][:, 0:n], ALU.mult, ALU.mult,
                              [Bbase[8 + j], Bvec, BtA[2]], [Bbase[4 + j]])
                          stt("dve", tmpA[3][:, 0:n], rj, vcol(2, j), kj, ALU.mult, ALU.mult,
                              [Bbase[j], Bvec, Bbase[8 + j]], [BtA[3]])
                          Pq2, BPq2 = pa()
                          mm(Pq2[:, 0:n], cst[:, C_BO:C_BO + 128], tmpA[3][:, 0:n], True, True, [Bcst, BtA[3]], [BPq2])
                          tt("dve", tmpA[3][:, 0:n], Pq2[:, 0:n], vj, ALU.mult, [BPq2, Bbase[12 + j]], [BtA[3]])
                          Pq3, BPq3 = pa()
                          mm(Pq3[:, 0:n], gup[:, j * 128:(j + 1) * 128], sgl[:, 0:n], True, True, [Bgup, Bsgl], [BPq3])
                          ts("dve", G12[:, j, 0:n], Pq3[:, 0:n], vcol(3, j), None, ALU.mult, None, [BPq3, Bvec], [BG12[j]])
                          stt("dve", G12[:, 4 + j, 0:n], tmpA[3][:, 0:n], vcol(4, j), Pq3[:, 0:n], ALU.add, ALU.mult,
                              [BtA[3], Bvec, BPq3], [BG12[4 + j]])
                          for q, Bq in ((j, Bbase[j]), (4 + j, Bbase[4 + j]), (8 + j, Bbase[8 + j]), (12 + j, Bbase[12 + j])):
                              p.dma(sp_base[q][:, t0:t0 + n], base[:, q, 0:n], reads=[Bq])
                          p.dma(sp_g[j][:, t0:t0 + n], G12[:, j, 0:n], reads=[BG12[j]])
                          p.dma(sp_g[4 + j][:, t0:t0 + n], G12[:, 4 + j, 0:n], reads=[BG12[4 + j]])

                  def load_stage(t0, n):
                      for q in range(16):
                          p.dma(base[:, q, 0:n], sp_base[q][:, t0:t0 + n], writes=[Bbase[q]])
                      p.dma(WA[:, 0:n], sp_wa[:, t0:t0 + n], writes=[BWA])
                      for q in range(8):
                          p.dma(G12[:, q, 0:n], sp_g[q][:, t0:t0 + n], writes=[BG12[q]])
                      for j in range(4):
                          p.dma(ycv[:, j, 0:n], sp_ycv[j][:, t0 + 1:t0 + n + 1], writes=[Bycv[j]])

                  def scan_tile(c0, d, Sb, BSb):
                      mG = cst[:, C_MG + 512 * d:C_MG + 512 * (d + 1)]
                      mL = cst[:, C_ML + 128 * d:C_ML + 128 * (d + 1)]
                      tF = cst[:, C_TF + 256 * d:C_TF + 256 * (d + 1)]
                      gcol = 127 if d == 0 else 0
                      cs_ = slice(c0, c0 + 128)
                      P_, BP_ = pa()
                      mm(P_[:], WA[0:64, cs_], lup[0:64, d, :], True, False, [BWA, Blup], [BP_])
                      mm(P_[:], cst[0:1, C_ON:C_ON + 128], rowv[0:1, d * 512:(d + 1) * 512], False, True,
                         [Bcst, Browv], [BP_])
                      act(sigT[:], P_[:], AF.Sigmoid, [BP_], [BsigT])
                      for j in range(4):
                          Pc, BPc = pa()
                          mm(Pc[:, 0:256], sigT[:, j * 128:(j + 1) * 128], tF, True, True, [BsigT, Bcst], [BPc])
                          act(EG[:, 0, :], Pc[:, 0:128], AF.Exp, [BPc], [BEG])
                          act(EG[:, 1, :], Pc[:, 0:128], AF.Exp, [BPc], [BEG], scale=-1.0)
                          act(EG[:, 2, :], Pc[:, 128:256], AF.Exp, [BPc], [BEG])
                          Pa, BPa_ = pa()
                          mm(Pa[:, 0:128], lup[64:128, d, j * 128:(j + 1) * 128], WA[64:128, cs_], True, True,
                             [Blup, BWA], [BPa_])
                          act(aFt[:], Pa[:, 0:128], AF.Sigmoid, [BPa_, Bvec], [BaF], bias=vcol(10 + d, j))
                          rj, kpj, kj = base[:, j, cs_], base[:, 4 + j, cs_], base[:, 8 + j, cs_]
                          tt("pool", KR[:, j, 128:256], rj, EG[:, 0, :], ALU.mult, [Bbase[j], BEG], [BKR[j]])
                          tt("dve", KaF32[:, j, :], kpj, EG[:, 2, :], ALU.mult, [Bbase[4 + j], BEG], [BKa32[j]])
                          cp("pool", KR[:, j, 0:128], KaF32[:, j, :], [BKa32[j]], [BKR[j]])
                          tt("pool", tB[0][:], kpj, aFt[:], ALU.mult, [Bbase[4 + j], BaF], [BtB[0]])
                          tt("dve", tB[1][:], tB[0][:], EG[:, 1, :], ALU.mult, [BtB[0], BEG], [BtB[1]])
                          cp("pool", BtF[:, j, :], tB[1][:], [BtB[1]], [BBt[j]])
                          ts("dve", BpF[:, j, :], tB[1][:], EG[:, 0, gcol:gcol + 1], None, ALU.mult, None,
                             [BtB[1], BEG], [BBp[j]])
                          ts("dve", tB[0][:], aFt[:], vcol(1, j), omka[:, j:j + 1], ALU.mult, ALU.add,
                             [BaF, Bvec, Bomka], [BtB[0]])
                          tt("pool", tB[0][:], tB[0][:], kj, ALU.mult, [BtB[0], Bbase[8 + j]], [BtB[0]])
                          tt("dve", tB[2][:], tB[0][:], EG[:, 1, :], ALU.mult, [BtB[0], BEG], [BtB[2]])
                          cp("pool", KtF[:, j, :], tB[2][:], [BtB[2]], [BKt[j]])
                          ts("dve", KpF[:, j, :], tB[2][:], EG[:, 0, gcol:gcol + 1], None, ALU.mult, None,
                             [BtB[2], BEG], [BKp[j]])
                          ts("dve", DG[:, j, :], cst[:, C_IB:C_IB + 64], EG[:, 0, gcol:gcol + 1], None, ALU.mult, None,
                             [Bcst, BEG], [BDG[j]])
                      ck(30)
                      for name, srcf, Bsrc in (("Ka", lambda j: KaF32[:, j, :], BKa32), ("Bp", lambda j: BpF[:, j, :], BBp),
                                               ("Kp", lambda j: KpF[:, j, :], BKp),
                                               ("V", lambda j: base[:, 12 + j, cs_], Bbase[12:16])):
                          P_, BP_ = pa()
                          for j in range(4):
                              tr(P_[:, j * 128:(j + 1) * 128], srcf(j), [Bsrc[j]], [BP_])
                          cp("act", TM[name][:], P_[:], [BP_], [BTM[name]])

                      ck(31)
                      def unit_stages(h, u):
                          j, p0 = h // 2, 64 * (h % 2)
                          hs = slice(h * 64, (h + 1) * 64)
                          Bt_h, Kt_h, KR_h = BtF[p0:p0 + 64, j, :], KtF[p0:p0 + 64, j, :], KR[p0:p0 + 64, j, :]
                          Ka_h = KR[p0:p0 + 64, j, 0:128]
                          pn, Bpn = PN[u], BPN[u]
                          stages = []

                          def s_gram():
                              mm(PG[:, 0:256], Bt_h, KR_h, True, True, [BBt[j], BKR[j]], [BPG[0]])
                              mm(PG[:, 256:512], Kt_h, KR_h, True, True, [BKt[j], BKR[j]], [BPG[1]])
                              mm(pn[:, 384:512], Ka_h, Bt_h, True, True, [BKR[j], BBt[j]], [Bpn[2]])
                              tt("dve", LT[u][0][:, 0:128], PG[:, 0:128], mG[:, 0:128], ALU.mult, [BPG[0], Bcst],
                                 [BLT[u][0][0]])
                              tt("dve", Gm[u][:, 0:128], PG[:, 128:256], mG[:, 128:256], ALU.mult, [BPG[0], Bcst], [BGm[u]])
                              tt("dve", Gm[u][:, 128:384], PG[:, 256:512], mG[:, 256:512], ALU.mult, [BPG[1], Bcst], [BGm[u]])
                              tt("dve", Nn[u][0][:], pn[:, 384:512], mL, ALU.mult, [Bpn[2], Bcst], [BNn[u][0]])
                              tt("pool", LT[u][1][:, 128:256], LT[u][0][:, 0:128], ident, ALU.add,
                                 [BLT[u][0][0], Bcst], [BLT[u][1][1]])
                          stages.append(s_gram)

                          def s_l0():
                              mm(pn[:, 0:128], Nn[u][0][:], LT[u][0][:, 0:128], True, True, [BNn[u][0], BLT[u][0][0]], [Bpn[0]])
                              mm(pn[:, 256:384], LT[u][0][:, 0:128], Nn[u][0][:], True, True, [BLT[u][0][0], BNn[u][0]], [Bpn[1]])
                              cp("act", LT[u][1][:, 0:128], pn[:, 0:128], [Bpn[0]], [BLT[u][1][0]])
                              cp("act", Nn[u][1][:], pn[:, 256:384], [Bpn[1]], [BNn[u][1]])
                          stages.append(s_l0)

                          def mk_level(k):
                              a, b = k % 2, (k + 1) % 2
                              def s():
                                  mm(pn[:, 0:128], Nn[u][a][:], LT[u][a][:, 0:128], True, True,
                                     [BNn[u][a], BLT[u][a][0]], [Bpn[0]])
                                  mm(pn[:, 128:256], Nn[u][a][:], LT[u][a][:, 128:256], True, True,
                                     [BNn[u][a], BLT[u][a][1]], [Bpn[0]])
                                  mm(pn[:, 256:384], LT[u][a][:, 0:128], Nn[u][a][:], True, True,
                                     [BLT[u][a][0], BNn[u][a]], [Bpn[1]])
                                  cp("act", LT[u][b][:, 0:128], pn[:, 0:128], [Bpn[0]], [BLT[u][b][0]])
                                  tt("dve", LT[u][b][:, 128:256], pn[:, 128:256], LT[u][a][:, 128:256], ALU.add,
                                     [Bpn[0], BLT[u][a][1]], [BLT[u][b][1]])
                                  cp("act", Nn[u][b][:], pn[:, 256:384], [Bpn[1]], [BNn[u][b]])
                              return s
                          for k in range(1, 6):
                              stages.append(mk_level(k))

                          def s_l6():
                              mm(pn[:, 128:256], Nn[u][0][:], LT[u][0][:, 128:256], True, True,
                                 [BNn[u][0], BLT[u][0][1]], [Bpn[0]])
                              tt("dve", TtF[u][:], pn[:, 128:256], LT[u][0][:, 128:256], ALU.add,
                                 [Bpn[0], BLT[u][0][1]], [BTt[u]])
                              mm(PM[:, 0:64], Gm[u][:, 128:256], TM["V"][:, hs], True, True, [BGm[u], BTM["V"]], [BPM[0]])
                              cp("act", AkV[u][:], PM[:, 0:64], [BPM[0]], [BAkV[u]])
                          stages.append(s_l6)

                          def s_x():
                              mm(PM[:, 64:128], TtF[u][:], TM["Ka"][:, hs], True, True, [BTt[u], BTM["Ka"]], [BPM[1]])
                              mm(PM[:, 128:192], TtF[u][:], AkV[u][:], True, True, [BTt[u], BAkV[u]], [BPM[1]])
                              act(XnS[u][:], PM[:, 64:192], AF.Copy, [BPM[1]], [BXn[u]], scale=-1.0)
                          stages.append(s_x)

                          def s_fin():
                              mm(PM[0:64, 256:384], identR[p0:p0 + 64, p0:p0 + 64], KR[p0:p0 + 64, j, 128:256], True, False,
                                 [BidR, BKR[j]], [BPM[3]])
                              mm(PM[0:64, 256:384], XnS[u][:, 0:64], Gm[u][:, 0:128], False, True, [BXn[u], BGm[u]], [BPM[3]])
                              cp("dve", RhT[u][:], PM[0:64, 256:384], [BPM[3]], [BRh[u]])
                              mm(PM[0:64, 192:256], XnS[u][:, 0:64], TM["Bp"][:, hs], True, False, [BXn[u], BTM["Bp"]], [BPM[2]])
                              mm(PM[0:64, 192:256], identR[p0:p0 + 64, p0:p0 + 64], DG[p0:p0 + 64, j, :], False, True,
                                 [BidR, BDG[j]], [BPM[2]])
                              cp("act", PTs[u][:], PM[0:64, 192:256], [BPM[2]], [BPTs[u]])
                              mm(PY[:, hs], Gm[u][:, 256:384], TM["V"][:, hs], True, False, [BGm[u], BTM["V"]], [BPY])
                              mm(PY[:, hs], Gm[u][:, 0:128], XnS[u][:, 64:128], False, False, [BGm[u], BXn[u]], [BPY])
                              mm(PY[:, hs], RhT[u][:], Sb[:, h, :], False, True, [BRh[u], BSb], [BPY])
                              mm(PS_[0:64, hs], TM["Kp"][:, hs], TM["V"][:, hs], True, False, [BTM["Kp"], BTM["V"]], [BPS])
                              mm(PS_[0:64, hs], TM["Bp"][:, hs], XnS[u][:, 64:128], False, False, [BTM["Bp"], BXn[u]], [BPS])
                              mm(PS_[0:64, hs], PTs[u][:], Sb[:, h, :], False, True, [BPTs[u], BSb], [BPS])
                          stages.append(s_fin)
                          return stages

                      for hp in range(4):
                          sa, sb_ = unit_stages(2 * hp, 0), unit_stages(2 * hp + 1, 1)
                          for idx_, (a_, b_) in enumerate(zip(sa, sb_)):
                              a_()
                              b_()
                      cp("act", Sb[:].rearrange("p h n -> p (h n)"), PS_[0:64, :], [BPS], [BSb])

                  def post_tile(src, dst, t0, c0, X, BX):
                      cs_ = slice(c0, c0 + 128)
                      tt("dve", ysum, PY[:], y1t, ALU.add, [BPY, By1], [Bys])
                      y3 = ysum.rearrange("p (h n) -> p h n", h=8)
                      p.op("dve", lambda e: e.tensor_reduce(out=gst[:, :, 0], in_=y3, axis=AX.X, op=ALU.add),
                           reads=[Bys], writes=[Bgst])
                      tt("pool", yn, ysum, ysum, ALU.mult, [Bys], [Byn])
                      p.op("dve", lambda e: e.tensor_reduce(out=gst[:, :, 1], in_=yn.rearrange("p (h n) -> p h n", h=8),
                                                            axis=AX.X, op=ALU.add), reads=[Byn], writes=[Bgst])
                      ts("dve", gst[:, :, 0], gst[:, :, 0], 1.0 / 64, None, ALU.mult, None, [Bgst], [Bgst])
                      tt("dve", gst[:, :, 2], gst[:, :, 0], gst[:, :, 0], ALU.mult, [Bgst], [Bgst])
                      stt("dve", gst[:, :, 3], gst[:, :, 1], 1.0 / 64, gst[:, :, 2], ALU.mult, ALU.subtract, [Bgst], [Bgst])
                      ts("dve", gst[:, :, 3], gst[:, :, 3], GN_EPS, None, ALU.add, None, [Bgst], [Bgst])
                      act(gst[:, :, 3], gst[:, :, 3], AF.Ln, [Bgst], [Bgst])
                      act(gst[:, :, 3], gst[:, :, 3], AF.Exp, [Bgst], [Bgst], scale=-0.5)
                      for h in range(8):
                          ts("dve", yn[:, h * 64:(h + 1) * 64], ysum[:, h * 64:(h + 1) * 64],
                             gst[:, h, 0:1], gst[:, h, 3:4], ALU.subtract, ALU.mult, [Bys, Bgst], [Byn])
                      P_, BP_ = pa()
                      for j in range(4):
                          tr(P_[:, j * 128:(j + 1) * 128], yn[:, j * 128:(j + 1) * 128], [Byn], [BP_])
                      for j in range(4):
                          tt("dve", tB[0][:], P_[:, j * 128:(j + 1) * 128], G12[:, j, cs_], ALU.mult, [BP_, BG12[j]], [BtB[0]])
                          tt("pool", catT[:, 4 + j, :], tB[0][:], G12[:, 4 + j, cs_], ALU.add, [BtB[0], BG12[4 + j]], [Bcat])
                          cp("pool", catT[:, j, :], ycv[:, j, cs_], [Bycv[j]], [Bcat])
                      for half in range(2):
                          Po, BPo = pa()
                          for k in range(KC):
                              mm(Po[:], catT[:, k, :], wout[:, k, half * 512:(half + 1) * 512], k == 0, k == KC - 1,
                                 [Bcat, Bwout], [BPo])
                          tt("dve", xo[:, half * 512:(half + 1) * 512], Po[:], bc["gt"][:, half * 512:(half + 1) * 512],
                             ALU.mult, [BPo, Bbc["gt"]], [Bxo])
                      tt("pool", xo[:], xo[:], X[:], ALU.add, [Bxo, BX], [Bxo])
                      p.dma(dst[t0:t0 + 128, :], xo[:], reads=[Bxo])

                  def mixer_segment(src, dst, seg_T, setid, Sin, need_out, Sfin):
                      load_bc(setid, 0)
                      nsb = (seg_T + SBT - 1) // SBT
                      for d in range(2):
                          if Sin is None:
                              ts("dve", Sst[d][:].rearrange("p h n -> p (h n)"), cst[0:64, C_MG:C_MG + 512], 0.0, None,
                                 ALU.mult, None, [Bcst], [BS[d]])
                          else:
                              cp("pool", Sst[d][:], Sin[d][0][:], [Sin[d][1]], [BS[d]])
                      for s_ in range(nsb):
                          t0 = s_ * SBT
                          n = min(SBT, seg_T - t0)
                          stage1(src, t0, n, s_ == 0, seg_T)
                          ck(2)
                          for i in range(n // 128):
                              scan_tile(i * 128, 0, Sst[0], BS[0])
                              ck(3)
                              if need_out:
                                  cp("dve", sigT[:], PY[:], [BPY], [BsigT])
                                  p.dma(sp_y1[t0 + i * 128:t0 + (i + 1) * 128, :], sigT[:], reads=[BsigT])
                      p.barrier()
                      ck(4)
                      for s_ in reversed(range(nsb)):
                          t0 = s_ * SBT
                          n = min(SBT, seg_T - t0)
                          load_stage(t0, n)
                          for i in reversed(range(n // 128)):
                              scan_tile(i * 128, 1, Sst[1], BS[1])
                              if need_out:
                                  tk = t0 + i * 128
                                  p.dma(y1t, sp_y1[tk:tk + 128, :], writes=[By1])
                                  X, BX = xt[i % 2], Bxt[i % 2]
                                  p.dma(X[:], src[tk:tk + 128, :], writes=[BX])
                                  post_tile(src, dst, tk, i * 128, X, BX)
                      if Sfin is not None:
                          for d in range(2):
                              cp("pool", Sfin[d][0][:], Sst[d][:], [BS[d]], [Sfin[d][1]])
                      p.barrier()

                  mixer_segment(c_src, c_mid, CT, 1, None, not last, [(Sc[0], BSc[0]), (Sc[1], BSc[1])])
                  ck(5)
                  mixer_segment(x_src, x_mid, T, 0, [(Sc[0], BSc[0]), (Sc[1], BSc[1])], True, None)
                  p.barrier()
                  ck(6)

              with ExitStack() as s3:
                  def sb3(name, shape, dt=F32):
                      return sb(name, shape, dt, stack=s3)
                  wup = sb3("wup", [128, KC, 2 * DFF], BF16); Bwup = Buf()
                  wdn = sb3("wdn", [128, FC, D], BF16); Bwdn = Buf()
                  with ExitStack() as sw3:
                      wst3 = [sb(f"wst3{i}", [128, 2048], stack=sw3) for i in range(2)]; Bw3 = [Buf(), Buf()]
                      ii = 0
                      for k in range(KC):
                          for c_ in range(0, 2 * DFF, 2048):
                              w_ = min(2048, 2 * DFF - c_)
                              sg, Bsg = wst3[ii % 2], Bw3[ii % 2]
                              p.dma(sg[:, 0:w_], w_up[l][k * 128:(k + 1) * 128, c_:c_ + w_], writes=[Bsg])
                              cp("dve" if ii % 2 == 0 else "pool", wup[:, k, c_:c_ + w_], sg[:, 0:w_], [Bsg], [Bwup])
                              ii += 1
                      for c_ in range(FC):
                          sg, Bsg = wst3[ii % 2], Bw3[ii % 2]
                          p.dma(sg[:, 0:D], w_dn[l][c_ * 128:(c_ + 1) * 128, :], writes=[Bsg])
                          cp("dve" if ii % 2 == 0 else "pool", wdn[:, c_, :], sg[:, 0:D], [Bsg], [Bwdn])
                          ii += 1
                      p.barrier()
                  xt3 = [sb3(f"x3{i}", [128, D]) for i in range(2)]; Bx3 = [Buf() for _ in range(2)]
                  xr, Bxr = xt3[1], Bx3[1]
                  hb3 = sb3("hb3", [128, D]); Bhb3 = Buf()
                  ssq3 = sb3("ssq3", [128, 2]); Bssq3 = Buf()
                  hT3 = sb3("hT3", [128, KC, 640], BF16); BhT3 = Buf()
                  accs = [sb3(f"acc{i}", [128, 512]) for i in range(2)]; Baccs = [Buf(), Buf()]
                  actT = sb3("actT", [128, FC, 512], BF16); Bact = [Buf() for _ in range(FC)]
                  xo3, Bxo3 = hb3, Bhb3
                  fgb = sb3("fgb", [128, D]); Bfgb = Buf()
                  if last and final_norm:
                      p.dma(fgb[:], fin_g.partition_broadcast(128), writes=[Bfgb])

                  def ffn_segment(src, dst, seg_T, setid, gw, is_out):
                      load_bc(setid, 1)
                      BT = 512 if seg_T >= 512 else seg_T
                      nrow_tot = seg_T // gw
                      nrow = BT // gw
                      for b0 in range(0, seg_T, BT):
                          halo = gw if gw < seg_T else 0
                          lo, hi = max(0, b0 - halo), min(seg_T, b0 + BT + halo)
                          nw = hi - lo
                          off = b0 - lo
                          tpos, xi = lo, 0
                          while tpos < hi:
                              w_ = min(128, hi - tpos)
                              X, BX = xt3[xi % 2], Bx3[xi % 2]
                              if w_ < 128:
                                  p.op("pool", lambda e, X=X: e.memset(X[:], 0.0), writes=[BX])
                              p.dma(X[0:w_, :], src[tpos:tpos + w_, :], writes=[BX])
                              act(hb3[:], X[:], AF.Square, [BX], [Bhb3, Bssq3], accum=ssq3[:, 0:1])
                              ts("dve", ssq3[:, 1:2], ssq3[:, 0:1], 1.0 / D, RMS_EPS, ALU.mult, ALU.add, [Bssq3], [Bssq3])
                              act(ssq3[:, 1:2], ssq3[:, 1:2], AF.Ln, [Bssq3], [Bssq3])
                              act(ssq3[:, 1:2], ssq3[:, 1:2], AF.Exp, [Bssq3], [Bssq3], scale=-0.5)
                              stt("dve", hb3[:], X[:], ssq3[:, 1:2], bc["gs"][:], ALU.mult, ALU.mult,
                                  [BX, Bssq3, Bbc["gs"]], [Bhb3])
                              tt("pool", hb3[:], hb3[:], bc["sh"][:], ALU.add, [Bhb3, Bbc["sh"]], [Bhb3])
                              col0 = tpos - lo
                              for half in range(2):
                                  P_, BP_ = pa()
                                  for q in range(4):
                                      k = half * 4 + q
                                      tr(P_[:, q * 128:(q + 1) * 128], hb3[:, k * 128:(k + 1) * 128], [Bhb3], [BP_])
                                  cp("act", hT3[:, half * 4:half * 4 + 4, col0:col0 + w_],
                                     P_[:].rearrange("p (q n) -> p q n", q=4)[:, :, 0:w_], [BP_], [BhT3])
                              tpos += w_
                              xi += 1
                          wr0 = off // gw
                          for c_ in range(FC):
                              acc, Bacc = accs[c_ % 2], Baccs[c_ % 2]
                              sil, Bsil = acc, Bacc
                              if c_ % 2 == 0:
                                  GA, BGA, GB, BGB = PG, BPG[0], PN[0], BPN[0][0]
                              else:
                                  GA, BGA, GB, BGB = PY, BPY, PS_, BPS
                              segs = [(0, min(nw, 512), GA, BGA)]
                              if nw > 512:
                                  segs.append((512, nw, GB, BGB))
                              for (a_, b_, P_, BP_) in segs:
                                  for k in range(KC):
                                      mm(P_[:, 0:b_ - a_], wup[:, k, c_ * 128:(c_ + 1) * 128], hT3[:, k, a_:b_],
                                         k == 0, k == KC - 1, [Bwup, BhT3], [BP_])
                              wc = lambda ty, tx: vec[:, 48 + (ty * 3 + tx) * FC + c_:48 + (ty * 3 + tx) * FC + c_ + 1]
                              bcol = vec[:, 48 + 9 * FC + c_:48 + 9 * FC + c_ + 1]

                              def tap(ty, tx, first):
                                  dy, dx = ty - 1, tx - 1
                                  r0g = b0 // gw
                                  rows_ok = [r for r in range(nrow) if 0 <= r0g + r + dy < nrow_tot]
                                  if not rows_ok:
                                      return
                                  ra, rb = rows_ok[0], rows_ok[-1] + 1
                                  ca, cb2 = (1, gw) if dx == -1 else ((0, gw - 1) if dx == 1 else (0, gw))
                                  rsplit = 512 // gw
                                  r = ra
                                  while r < rb:
                                      wr = wr0 + r + dy
                                      if wr < rsplit:
                                          re_ = min(rb, rsplit - wr0 - dy)
                                          P_, B_, wbase = GA, BGA, 0
                                      else:
                                          re_ = rb
                                          P_, B_, wbase = GB, BGB, rsplit
                                      nr = re_ - r
                                      iv = P_[:, (wr - wbase) * gw:(wr - wbase + nr) * gw].rearrange("p (r c) -> p r c", c=gw)[:, :, ca + dx:cb2 + dx]
                                      ov = acc[:, r * gw:(r + nr) * gw].rearrange("p (r c) -> p r c", c=gw)[:, :, ca:cb2]
                                      if first:
                                          act(ov, iv, AF.Identity, [B_, Bvec], [Bacc], bias=bcol, scale=wc(ty, tx))
                                      else:
                                          stt("dve", ov, iv, wc(ty, tx), ov, ALU.mult, ALU.add, [B_, Bvec, Bacc], [Bacc])
                                      r = re_
                              tap(1, 1, True)
                              for ty in range(3):
                                  for tx in range(3):
                                      if (ty, tx) != (1, 1) and not (gw >= seg_T and ty != 1):
                                          tap(ty, tx, False)
                              act(sil[:, 0:BT], acc[:, 0:BT], AF.Silu, [], [Bacc])
                              Pv, BPv = pa()
                              for k in range(KC):
                                  mm(Pv[:, 0:BT], wup[:, k, DFF + c_ * 128:DFF + (c_ + 1) * 128], hT3[:, k, off:off + BT],
                                     k == 0, k == KC - 1, [Bwup, BhT3], [BPv])
                              tt("dve", actT[:, c_, 0:BT], sil[:, 0:BT], Pv[:, 0:BT], ALU.mult, [Bsil, BPv], [Bact[c_]])
                          for i in range(BT // 128):
                              tk = b0 + i * 128
                              p.dma(xr[:], src[tk:tk + 128, :], writes=[Bxr])
                              for half in range(2):
                                  Po, BPo = pa()
                                  for c_ in range(FC):
                                      mm(Po[:], actT[:, c_, i * 128:(i + 1) * 128], wdn[:, c_, half * 512:(half + 1) * 512],
                                         c_ == 0, c_ == FC - 1, [Bact[c_], Bwdn], [BPo])
                                  tt("dve", xo3[:, half * 512:(half + 1) * 512], Po[:], bc["gt"][:, half * 512:(half + 1) * 512],
                                     ALU.mult, [BPo, Bbc["gt"]], [Bxo3])
                              tt("pool", xo3[:], xo3[:], xr[:], ALU.add, [Bxo3, Bxr], [Bxo3])
                              if is_out and final_norm:
                                  act(xt3[0][:], xo3[:], AF.Square, [Bxo3], [Bx3[0], Bssq3], accum=ssq3[:, 0:1])
                                  ts("dve", ssq3[:, 1:2], ssq3[:, 0:1], 1.0 / D, RMS_EPS, ALU.mult, ALU.add, [Bssq3], [Bssq3])
                                  act(ssq3[:, 1:2], ssq3[:, 1:2], AF.Ln, [Bssq3], [Bssq3])
                                  act(ssq3[:, 1:2], ssq3[:, 1:2], AF.Exp, [Bssq3], [Bssq3], scale=-0.5)
                                  stt("dve", xo3[:], xo3[:], ssq3[:, 1:2], fgb[:], ALU.mult, ALU.mult, [Bxo3, Bssq3, Bfgb], [Bxo3])
                              p.dma(dst[tk:tk + 128, :], xo3[:], reads=[Bxo3])

                  if not last:
                      ffn_segment(c_mid, c_dst, CT, 1, CT, False)
                  ffn_segment(x_mid, x_dst, T, 0, GW, last)
                  p.barrier()
        except _Stop:
            pass
        p.barrier()
        nc._prog_ninst = p.ninst
    return nc


def prep_inputs(x, c, ctx, c_ctx, ada_w, ada_b, norm1_g, norm2_g, w_in, conv_a_w, rw_w0, rw_w_up, rw_a0, rw_a_up,
                rw_k_k, rw_k_a, rw_r_k, rw_g_up, rw_ln_g, rw_ln_b, w_out, ffn_w_up, ffn_conv_w, ffn_conv_b,
                ffn_w_down, final_g, b):
    L = ada_w.shape[0]
    f = lambda a: np.ascontiguousarray(a, dtype=np.float32)
    cs = np.stack([c[b].reshape(KC, 128).T, c_ctx.reshape(KC, 128).T], axis=-1)
    rows = np.concatenate([ada_b, norm1_g, norm2_g], axis=1)[:, None, :].repeat(2, axis=1)

    def ch(v, n):
        return v.reshape(n, 128).T
    vec = np.zeros((L, 128, NVEC), np.float32)
    for l in range(L):
        cols = [ch(rw_k_k[l], 4), ch(rw_k_a[l], 4), ch(rw_r_k[l].reshape(-1), 4), ch(rw_ln_g[l], 4), ch(rw_ln_b[l], 4),
                ch(conv_a_w[l, 0], 4), ch(conv_a_w[l, 1], 4), ch(conv_a_w[l, 2], 4),
                ch(rw_w0[l, 0], 4), ch(rw_w0[l, 1], 4), ch(rw_a0[l, 0], 4), ch(rw_a0[l, 1], 4)]
        for ty in range(3):
            for tx in range(3):
                cols.append(ch(ffn_conv_w[l, ty, tx], FC))
        cols.append(ch(ffn_conv_b[l], FC))
        vec[l] = np.concatenate(cols, axis=1)
    rowv = rw_w0.reshape(L, 1, 1024)
    lup = np.concatenate([rw_w_up.transpose(0, 2, 1, 3), rw_a_up.transpose(0, 2, 1, 3)], axis=1)
    return {
        "x": f(x[b]), "ctx": f(ctx[b]), "cs": f(cs), "cst": make_consts(), "ada_w": f(ada_w), "rows": f(rows),
        "fin_g": f(final_g), "w_in": f(w_in), "w_out": f(w_out), "w_up": f(ffn_w_up), "w_dn": f(ffn_w_down),
        "vec": f(vec), "rowv": f(rowv), "lup": f(lup), "gup": f(rw_g_up),
    }


def kernel(**inputs):
    inputs = {k: np.asarray(v) for k, v in inputs.items()}
    B, T, _ = inputs["x"].shape
    CT = inputs["ctx"].shape[1]
    nc = build_program(T, CT, inputs["ada_w"].shape[0])
    in_maps = [prep_inputs(b=b, **inputs) for b in range(B)]
    res = run_bass_kernel_spmd(nc, in_maps, core_ids=list(range(B)))
    return np.stack([r["out"] for r in res.results], axis=0).astype(np.float32)
```

```python
import math
import numpy as np
from contextlib import ExitStack
import concourse.bass as bass
import concourse.mybir as mybir
from concourse.bass_utils import run_bass_kernel_spmd

F32 = mybir.dt.float32
BF16 = mybir.dt.bfloat16
F32R = mybir.dt.float32r
AF = mybir.ActivationFunctionType
ALU = mybir.AluOpType
AX = mybir.AxisListType

D = 1024
KC = 8
PROJ = 3328
NPC = 26
DFF = 2816
FC = 22
GW = 64
DS = math.exp(-0.5)
RMS_EPS = 1e-6
GN_EPS = 64e-5
NVEC = 48 + 10 * FC
SCAN_DT = F32

C_ID = 0
C_MG = 128
C_ML = C_MG + 1024
C_TF = C_ML + 256
C_BO = C_TF + 512
C_ON = C_BO + 128
C_IB = C_ON + 128
NCST = C_IB + 64


def make_consts():
    c = np.zeros((128, NCST), np.float32)
    s = np.arange(128)[:, None]
    t = np.arange(128)[None, :]
    c[:, C_ID:C_ID + 128] = np.eye(128)
    for d in range(2):
        lt = (s < t) if d == 0 else (s > t)
        le = (s <= t) if d == 0 else (s >= t)
        g = c[:, C_MG + 512 * d:C_MG + 512 * (d + 1)]
        g[:, 0:128] = -1.0 * lt
        g[:, 128:256] = le
        g[:, 256:384] = lt
        g[:, 384:512] = le
        c[:, C_ML + 128 * d:C_ML + 128 * (d + 1)] = -1.0 * lt.T
        f = c[:, C_TF + 256 * d:C_TF + 256 * (d + 1)]
        f[:, 0:128] = -DS * le
        f[:, 128:256] = -DS * lt
    c[:, C_BO:C_BO + 128] = (s // 64 == t // 64)
    c[:, C_ON:C_ON + 128] = 1.0
    c[:, C_IB:C_IB + 64] = (s % 64 == np.arange(64)[None, :])
    return c


class Buf:
    __slots__ = ("name", "w", "rd", "ps")

    def __init__(self, name="", ps=False):
        self.name = name
        self.w = None
        self.rd = []
        self.ps = ps


ENGMAP = {"pe": "tensor", "dve": "vector", "act": "scalar", "pool": "gpsimd", "sp": "sync"}


class Eng:
    def __init__(self, name, sem, h):
        self.name = name
        self.sem = sem
        self.h = h
        self.cnt = 0
        self.waited = {}


class Prog:
    def __init__(self, nc, sems, dma_sems):
        self.nc = nc
        self.E = {n: Eng(n, sems[n], getattr(nc, ENGMAP[n])) for n in ENGMAP}
        self.dma_sems = dma_sems
        self.dma_cnt = [0] * len(dma_sems)
        self.dma_rr = 0
        self.ninst = 0
        self.stop = False

    def _deps(self, reads, writes):
        d = []
        for b in reads:
            if b.w is not None:
                d.append(b.w)
        for b in writes:
            if b.w is not None:
                d.append(b.w)
            d.extend(b.rd)
        return d

    def _waits(self, e, deps, skip_self):
        need = {}
        for (sem, val, owner) in deps:
            if skip_self and owner is e:
                continue
            k = id(sem)
            if e.waited.get(k, 0) >= val:
                continue
            if k not in need or need[k][1] < val:
                need[k] = (sem, val)
        for k, (sem, val) in need.items():
            e.waited[k] = val
            e.h.wait_ge(sem, val)

    def op(self, en, fn, reads=(), writes=()):
        if self.stop:
            return
        e = self.E[en]
        deps = self._deps(reads, writes)
        for b in reads:
            if b.ps:
                deps.extend(t for t in b.rd if t[2] is not e)
        self._waits(e, deps, skip_self=(en == "pe"))
        e.cnt += 1
        tok = (e.sem, e.cnt, e)
        fn(e.h).then_inc(e.sem, 1)
        for b in writes:
            b.w = tok
            b.rd = []
        for b in reads:
            b.rd.append(tok)
        self.ninst += 1

    def dma(self, out, in_, reads=(), writes=(), q="sp", slow=False):
        if self.stop:
            return
        e = self.E[q]
        deps = self._deps(reads, writes)
        i = self.dma_rr
        self.dma_rr = (self.dma_rr + 1) % len(self.dma_sems)
        sem = self.dma_sems[i]
        if self.dma_cnt[i] > 0:
            deps.append((sem, self.dma_cnt[i], None))
        self._waits(e, deps, skip_self=False)
        self.dma_cnt[i] += 16
        tok = (sem, self.dma_cnt[i], None)
        if slow:
            e.h.dma_start(out=out, in_=in_, allow_slow_non_contiguous=True).then_inc(sem, 16)
        else:
            e.h.dma_start(out=out, in_=in_).then_inc(sem, 16)
        for b in writes:
            b.w = tok
            b.rd = []
        for b in reads:
            b.rd.append(tok)
        self.ninst += 1
        return tok

    def barrier(self):
        if self.stop:
            return
        toks = [(e.sem, e.cnt, e) for e in self.E.values() if e.cnt > 0]
        toks += [(s, c, None) for s, c in zip(self.dma_sems, self.dma_cnt) if c > 0]
        for e in self.E.values():
            self._waits(e, toks, skip_self=True)


class _Stop(Exception):
    pass


def build_program(T, CT, L=2, final_norm=True, dbg=None):
    assert T % 512 == 0 and CT % 128 == 0
    nc = bass.Bass("TRN2", target_bir_lowering=False)
    dt_ = nc.dram_tensor

    def din(name, shape, dt=F32):
        return dt_(name, shape, dt, kind="ExternalInput").ap()

    def dint(name, shape, dt=F32):
        return dt_(name, shape, dt, kind="Internal").ap()

    x_in = din("x", [T, D])
    ctx_in = din("ctx", [CT, D])
    cs_in = din("cs", [128, KC, 2])
    cst_in = din("cst", [128, NCST])
    ada_w = din("ada_w", [L, D, 6 * D])
    rows_in = din("rows", [L, 2, 6 * D + 2 * D])
    fin_g = din("fin_g", [D])
    w_in = din("w_in", [L, D, PROJ])
    w_out = din("w_out", [L, D, D])
    w_up = din("w_up", [L, D, 2 * DFF])
    w_dn = din("w_dn", [L, DFF, D])
    vec_in = din("vec", [L, 128, NVEC])
    rowv_in = din("rowv", [L, 1, 1024])
    lup_in = din("lup", [L, 128, 2, 512])
    gup_in = din("gup", [L, 128, 512])
    out = dt_("out", [T, D], F32, kind="ExternalOutput").ap()

    xs = [dint("xs0", [T, D]), dint("xs1", [T, D])]
    cxs = [dint("cxs0", [CT, D]), dint("cxs1", [CT, D])]
    modr = dint("modr", [2, 6, D])
    TS = max(T, CT)
    sp_base = dint("sp_base", [16, 128, TS])
    sp_wa = dint("sp_wa", [128, TS])
    sp_g = dint("sp_g", [8, 128, TS])
    sp_ycv = dint("sp_ycv", [4, 128, TS + 2], BF16)
    sp_y1 = dint("sp_y1", [TS, 512])

    st = ExitStack()
    with st:
        sems = {n: st.enter_context(nc.semaphore(n)) for n in ENGMAP}
        dsem = [st.enter_context(nc.semaphore(f"dq{i}")) for i in range(24)]
        p = Prog(nc, sems, dsem)

        uid = [0]

        def sb(name, shape, dt=F32, stack=st):
            uid[0] += 1
            return stack.enter_context(nc.sbuf_tensor(f"s{uid[0]}_{name}", shape, dt))

        def ps(name, shape, dt=F32):
            uid[0] += 1
            return st.enter_context(nc.psum_tensor(f"p{uid[0]}_{name}", shape, dt))

        cst = sb("cst", [128, NCST]); Bcst = Buf()
        vec = sb("vec", [128, NVEC]); Bvec = Buf()
        omka = sb("omka", [128, 4]); Bomka = Buf()
        bc = {n: sb("bc_" + n, [128, D]) for n in ("gs", "sh", "gt")}
        Bbc = {n: Buf() for n in bc}
        PA = [ps("PA0", [128, 512]), ps("PA1", [128, 512])]; BPA = [Buf(ps=True), Buf(ps=True)]
        _bpg = Buf(ps=True)
        PG = ps("PG", [128, 512]); BPG = [_bpg, _bpg]
        PN = [ps("PN0", [128, 512]), ps("PN1", [128, 512])]
        _bpn = [Buf(ps=True), Buf(ps=True)]
        BPN = [[_bpn[0]] * 3, [_bpn[1]] * 3]
        _bpm = Buf(ps=True)
        PM = ps("PM", [128, 512]); BPM = [_bpm] * 5
        PS_ = ps("PS", [128, 512]); BPS = Buf(ps=True)
        PY = ps("PY", [128, 512]); BPY = Buf(ps=True)
        pa_rr = [0]

        def pa():
            i = pa_rr[0]
            pa_rr[0] ^= 1
            return PA[i], BPA[i]

        ident = cst[:, C_ID:C_ID + 128]
        p.dma(cst[:], cst_in, writes=[Bcst])
        identR = sb("identR", [128, 128], SCAN_DT); BidR = Buf()
        identS = identR
        p.op("dve", lambda e: e.tensor_copy(out=identR[:], in_=ident), reads=[Bcst], writes=[BidR])

        def mm(o, l, r, start, stop, reads, writes):
            p.op("pe", lambda e: e.matmul(o, l, r, start=start, stop=stop), reads=reads, writes=writes)

        def tr(o, i_, reads, writes):
            p.op("pe", lambda e: e.transpose(o, i_, ident), reads=list(reads) + [Bcst], writes=writes)

        def act(o, i_, func, reads, writes, bias=None, scale=None, accum=None):
            kw = {}
            if bias is not None:
                kw["bias"] = bias
            if scale is not None:
                kw["scale"] = scale
            if accum is not None:
                kw["accum_out"] = accum
            p.op("act", lambda e: e.activation(out=o, in_=i_, func=func, **kw), reads=reads, writes=writes)

        def tt(en, o, a, b, op, reads, writes):
            p.op(en, lambda e: e.tensor_tensor(out=o, in0=a, in1=b, op=op), reads=reads, writes=writes)

        def ts(en, o, a, s1, s2, op0, op1, reads, writes):
            if s2 is None:
                p.op(en, lambda e: e.tensor_scalar(out=o, in0=a, scalar1=s1, scalar2=None, op0=op0),
                     reads=reads, writes=writes)
            else:
                p.op(en, lambda e: e.tensor_scalar(out=o, in0=a, scalar1=s1, scalar2=s2, op0=op0, op1=op1),
                     reads=reads, writes=writes)

        def stt(en, o, a, s, b, op0, op1, reads, writes):
            p.op(en, lambda e: e.scalar_tensor_tensor(out=o, in0=a, scalar=s, in1=b, op0=op0, op1=op1),
                 reads=reads, writes=writes)

        def cp(en, o, i_, reads, writes):
            if en == "act":
                act(o, i_, AF.Copy, reads, writes)
            else:
                p.op(en, lambda e: e.tensor_copy(out=o, in_=i_), reads=reads, writes=writes)

        def ck(k):
            if dbg == k and not p.stop:
                p.barrier()
                p.stop = True

        try:
          for l in range(L):
              last = (l == L - 1)
              x_src = x_in if l == 0 else xs[1]
              c_src = ctx_in if l == 0 else cxs[1]
              x_mid, c_mid = xs[0], cxs[0]
              x_dst, c_dst = xs[1], cxs[1]
              if last:
                  x_dst = out

              p.barrier()
              with ExitStack() as s0:
                  rows = sb("rows", [2, 8 * D], stack=s0); Brows = Buf()
                  mod = sb("mod", [2, 6 * D], stack=s0); Bmod = Buf()
                  drv = sb("drv", [2, 6, D], stack=s0); Bdrv = Buf()
                  cs = sb("cs", [128, KC, 2], stack=s0); Bcs = Buf()
                  scs = sb("scs", [128, KC, 2], stack=s0); Bscs = Buf()
                  stg = [sb(f"adastg{i}", [128, KC, 512], stack=s0) for i in range(2)]; Bstg = [Buf(), Buf()]
                  p.dma(rows[:], rows_in[l], writes=[Brows])
                  p.dma(cs[:], cs_in, writes=[Bcs])
                  p.dma(vec[:], vec_in[l], writes=[Bvec])
                  ts("dve", omka[:], vec[:, 4:8], -1.0, 1.0, ALU.mult, ALU.add, [Bvec], [Bomka])
                  act(scs[:], cs[:], AF.Silu, [Bcs], [Bscs])
                  for cb in range(12):
                      sg, Bsg = stg[cb % 2], Bstg[cb % 2]
                      p.dma(sg[:], ada_w[l][:, cb * 512:(cb + 1) * 512].rearrange("(k p) n -> p k n", p=128),
                            writes=[Bsg])
                      P_, BP_ = pa()
                      for k in range(KC):
                          mm(P_[0:2, :], scs[:, k, :], sg[:, k, :], k == 0, k == KC - 1, [Bscs, Bsg], [BP_])
                      tt("dve", mod[:, cb * 512:(cb + 1) * 512], P_[0:2, :], rows[:, cb * 512:(cb + 1) * 512],
                         ALU.add, [BP_, Brows], [Bmod])
                  stt("dve", drv[:, 0, :], mod[:, D:2 * D], 1.0, rows[:, 6 * D:7 * D], ALU.add, ALU.mult,
                      [Bmod, Brows], [Bdrv])
                  cp("dve", drv[:, 1, :], mod[:, 0:D], [Bmod], [Bdrv])
                  cp("dve", drv[:, 2, :], mod[:, 2 * D:3 * D], [Bmod], [Bdrv])
                  stt("dve", drv[:, 3, :], mod[:, 4 * D:5 * D], 1.0, rows[:, 7 * D:8 * D], ALU.add, ALU.mult,
                      [Bmod, Brows], [Bdrv])
                  cp("dve", drv[:, 4, :], mod[:, 3 * D:4 * D], [Bmod], [Bdrv])
                  cp("dve", drv[:, 5, :], mod[:, 5 * D:6 * D], [Bmod], [Bdrv])
                  Bmodr = Buf()
                  p.dma(modr, drv[:], reads=[Bdrv], writes=[Bmodr])
                  p.barrier()
              ck(0)

              def load_bc(setid, which):
                  for n, r in (("gs", 0), ("sh", 1), ("gt", 2)):
                      p.dma(bc[n][:], modr[setid, 3 * which + r].partition_broadcast(128),
                            reads=[Bmodr], writes=[Bbc[n]])

              with ExitStack() as s1:
                  def sb1(name, shape, dt=F32):
                      return sb(name, shape, dt, stack=s1)
                  rowv = sb1("rowv", [1, 1024]); Browv = Buf()
                  lup = sb1("lup", [128, 2, 512]); Blup = Buf()
                  gup = sb1("gup", [128, 512], BF16); Bgup = Buf()
                  Sst = [sb1(f"S{d}", [64, 8, 64], SCAN_DT) for d in range(2)]; BS = [Buf(), Buf()]
                  Sc = [sb1(f"Sc{d}", [64, 8, 64], SCAN_DT) for d in range(2)]; BSc = [Buf(), Buf()]
                  p.dma(rowv[:], rowv_in[l], writes=[Browv])
                  p.dma(lup[:], lup_in[l], writes=[Blup])
                  win = sb1("win", [128, KC, PROJ], BF16); Bwin = Buf()
                  wout = sb1("wout", [128, KC, D], BF16); Bwout = Buf()
                  with ExitStack() as sw:
                      wstg = [sb(f"wstg{i}", [128, PROJ], stack=sw) for i in range(2)]; Bwstg = [Buf(), Buf()]
                      for k in range(KC):
                          sg, Bsg = wstg[k % 2], Bwstg[k % 2]
                          p.dma(sg[:], w_in[l][k * 128:(k + 1) * 128, :], writes=[Bsg])
                          cp("dve" if k % 2 == 0 else "pool", win[:, k, :], sg[:], [Bsg], [Bwin])
                      for k in range(KC):
                          sg, Bsg = wstg[k % 2], Bwstg[k % 2]
                          p.dma(sg[:, 0:D], w_out[l][k * 128:(k + 1) * 128, :], writes=[Bsg])
                          cp("dve" if k % 2 == 0 else "pool", wout[:, k, :], sg[:, 0:D], [Bsg], [Bwout])
                      p.dma(wstg[0][:, 0:512], gup_in[l], writes=[Bwstg[0]])
                      cp("dve", gup[:], wstg[0][:, 0:512], [Bwstg[0]], [Bgup])
                      p.barrier()
                  ck(1)

                  SBT = 256
                  xt = [sb1(f"xt{i}", [128, D]) for i in range(2)]; Bxt = [Buf(), Buf()]
                  hb = sb1("hb", [128, D]); Bhb = Buf()
                  ssq = sb1("ssq", [128, 2]); Bssq = Buf()
                  hT = sb1("hT", [128, KC, SBT], BF16); BhT = Buf()
                  base = sb1("base", [128, 16, SBT]); Bbase = [Buf() for _ in range(16)]
                  WA = sb1("WA", [128, SBT]); BWA = Buf()
                  sgl = sb1("sgl", [128, SBT], BF16); Bsgl = Buf()
                  G12 = sb1("G12", [128, 8, SBT]); BG12 = [Buf() for _ in range(8)]
                  ycv = sb1("ycv", [128, 4, SBT], BF16); Bycv = [Buf() for _ in range(4)]
                  ubuf_t = sb1("ubuf", [128, 4 * (SBT + 2)]); Bub = [Buf() for _ in range(4)]
                  ubuf = ubuf_t[:].rearrange("p (j n) -> p j n", j=4)
                  cbb_t = sb1("cbb", [128, 4 * (SBT + 1)]); Bcbb = [Buf() for _ in range(4)]
                  cbb = cbb_t[:].rearrange("p (j n) -> p j n", j=4)
                  tmpA = [sb1(f"tmpA{i}", [128, SBT]) for i in range(4)]; BtA = [Buf() for _ in range(4)]
                  KR = sb1("KR", [128, 4, 256], SCAN_DT); BKR = [Buf() for _ in range(4)]
                  BtF = sb1("BtF", [128, 4, 128], SCAN_DT); BBt = [Buf() for _ in range(4)]
                  KtF = sb1("KtF", [128, 4, 128], SCAN_DT); BKt = [Buf() for _ in range(4)]
                  BpF = sb1("BpF", [128, 4, 128]); BBp = [Buf() for _ in range(4)]
                  KpF = sb1("KpF", [128, 4, 128]); BKp = [Buf() for _ in range(4)]
                  KaF32 = sb1("KaF32", [128, 4, 128]); BKa32 = [Buf() for _ in range(4)]
                  DG = sb1("DG", [128, 4, 64], SCAN_DT); BDG = [Buf() for _ in range(4)]
                  EG = sb1("EG", [128, 3, 128]); BEG = Buf()
                  aFt = sb1("aFt", [128, 128]); BaF = Buf()
                  tB = [sb1(f"tB{i}", [128, 128]) for i in range(3)]; BtB = [Buf() for _ in range(3)]
                  sigT = sb1("sigT", [128, 512]); BsigT = Buf()
                  TM = {n: sb1("TM_" + n, [128, 512], SCAN_DT) for n in ("Ka", "Bp", "Kp", "V")}
                  BTM = {n: Buf() for n in TM}
                  LT = [[sb1(f"LT{u}{i}", [128, 256], SCAN_DT) for i in range(2)] for u in range(2)]
                  BLT = [[[Buf(), Buf()] for i in range(2)] for u in range(2)]
                  Nn = [[sb1(f"Nn{u}{i}", [128, 128], SCAN_DT) for i in range(2)] for u in range(2)]
                  BNn = [[Buf() for i in range(2)] for u in range(2)]
                  Gm = [sb1(f"Gm{u}", [128, 384], SCAN_DT) for u in range(2)]; BGm = [Buf(), Buf()]
                  TtF = [sb1(f"TtF{u}", [128, 128], SCAN_DT) for u in range(2)]; BTt = [Buf(), Buf()]
                  AkV = [sb1(f"AkV{u}", [128, 64], SCAN_DT) for u in range(2)]; BAkV = [Buf(), Buf()]
                  XnS = [sb1(f"XnS{u}", [128, 128], SCAN_DT) for u in range(2)]; BXn = [Buf(), Buf()]
                  PTs = [sb1(f"PTs{u}", [64, 64], SCAN_DT) for u in range(2)]; BPTs = [Buf(), Buf()]
                  RhT = [sb1(f"RhT{u}", [64, 128], SCAN_DT) for u in range(2)]; BRh = [Buf(), Buf()]
                  ysum = ubuf_t[:, 0:512]; Bys = Buf()
                  yn = ubuf_t[:, 512:1024]; Byn = Buf()
                  y1t = cbb_t[:, 0:512]; By1 = Buf()
                  gst = sb1("gst", [128, 8, 4]); Bgst = Buf()
                  catT = hT[:, :, 0:128]; Bcat = Buf()
                  xo, Bxo = hb, Bhb

                  def vcol(i, j):
                      return vec[:, 4 * i + j:4 * i + j + 1]

                  def norm_tile(src_ap, xt_i, col0):
                      X, BX = xt[xt_i], Bxt[xt_i]
                      p.dma(X[:], src_ap, writes=[BX])
                      act(hb[:], X[:], AF.Square, [BX], [Bhb, Bssq], accum=ssq[:, 0:1])
                      ts("dve", ssq[:, 1:2], ssq[:, 0:1], 1.0 / D, RMS_EPS, ALU.mult, ALU.add, [Bssq], [Bssq])
                      act(ssq[:, 1:2], ssq[:, 1:2], AF.Ln, [Bssq], [Bssq])
                      act(ssq[:, 1:2], ssq[:, 1:2], AF.Exp, [Bssq], [Bssq], scale=-0.5)
                      stt("dve", hb[:], X[:], ssq[:, 1:2], bc["gs"][:], ALU.mult, ALU.mult,
                          [BX, Bssq, Bbc["gs"]], [Bhb])
                      tt("pool", hb[:], hb[:], bc["sh"][:], ALU.add, [Bhb, Bbc["sh"]], [Bhb])
                      for half in range(2):
                          P_, BP_ = pa()
                          for q in range(4):
                              k = half * 4 + q
                              tr(P_[:, q * 128:(q + 1) * 128], hb[:, k * 128:(k + 1) * 128], [Bhb], [BP_])
                          cp("act", hT[:, half * 4:half * 4 + 4, col0:col0 + 128],
                             P_[:].rearrange("p (q n) -> p q n", q=4), [BP_], [BhT])
                      return X, BX

                  def proj_chunk(c, n):
                      P_, BP_ = pa()
                      for k in range(KC):
                          mm(P_[:, 0:n], win[:, k, c * 128:(c + 1) * 128], hT[:, k, 0:n], k == 0, k == KC - 1,
                             [Bwin, BhT], [BP_])
                      return P_, BP_

                  def stage1(seg_src, t0, n, first, seg_T):
                      for i in range(n // 128):
                          norm_tile(seg_src[t0 + i * 128:t0 + (i + 1) * 128, :], i % 2, i * 128)
                      for j in range(4):
                          if first:
                              p.op("pool", lambda e, j=j: e.memset(ubuf[:, j, 0:2], 0.0), writes=[Bub[j]])
                              p.op("pool", lambda e, j=j: e.memset(cbb[:, j, 0:1], 0.0), writes=[Bcbb[j]])
                          Pb, BPb = proj_chunk(j, n)
                          cp("act", cbb[:, j, 1:n + 1], Pb[:, 0:n], [BPb], [Bcbb[j]])
                          Pc, BPc = proj_chunk(4 + j, n)
                          cp("act", tmpA[0][:, 0:n], Pc[:, 0:n], [BPc], [BtA[0]])
                          Px, BPx = proj_chunk(8 + j, n)
                          tt("dve", ubuf[:, j, 2:n + 2], tmpA[0][:, 0:n], Px[:, 0:n], ALU.mult, [BtA[0], BPx], [Bub[j]])
                          ts("dve", tmpA[1][:, 0:n], ubuf[:, j, 0:n], vcol(5, j), None, ALU.mult, None,
                             [Bub[j], Bvec], [BtA[1]])
                          stt("dve", tmpA[1][:, 0:n], ubuf[:, j, 1:n + 1], vcol(6, j), tmpA[1][:, 0:n], ALU.mult, ALU.add,
                              [Bub[j], Bvec, BtA[1]], [BtA[1]])
                          stt("dve", tmpA[1][:, 0:n], ubuf[:, j, 2:n + 2], vcol(7, j), tmpA[1][:, 0:n], ALU.mult, ALU.add,
                              [Bub[j], Bvec, BtA[1]], [BtA[1]])
                          tt("pool", ycv[:, j, 0:n], tmpA[1][:, 0:n], cbb[:, j, 0:n], ALU.mult, [BtA[1], Bcbb[j]], [Bycv[j]])
                          p.dma(sp_ycv[j][:, t0:t0 + n], ycv[:, j, 0:n], reads=[Bycv[j]])
                          if t0 + n == seg_T:
                              ts("dve", tmpA[1][:, 0:1], ubuf[:, j, n:n + 1], vcol(5, j), None, ALU.mult, None,
                                 [Bub[j], Bvec], [BtA[1]])
                              stt("dve", tmpA[1][:, 0:1], ubuf[:, j, n + 1:n + 2], vcol(6, j), tmpA[1][:, 0:1],
                                  ALU.mult, ALU.add, [Bub[j], Bvec, BtA[1]], [BtA[1]])
                              tt("pool", ycv[:, j, 0:1], tmpA[1][:, 0:1], cbb[:, j, n:n + 1], ALU.mult,
                                 [BtA[1], Bcbb[j]], [Bycv[j]])
                              p.dma(sp_ycv[j][:, t0 + n:t0 + n + 1], ycv[:, j, 0:1], reads=[Bycv[j]], slow=True)
                          else:
                              cp("pool", ubuf[:, j, 0:2], ubuf[:, j, n:n + 2], [Bub[j]], [Bub[j]])
                              cp("pool", cbb[:, j, 0:1], cbb[:, j, n:n + 1], [Bcbb[j]], [Bcbb[j]])
                      Pw, BPw = proj_chunk(24, n)
                      act(WA[0:64, 0:n], Pw[0:64, 0:n], AF.Tanh, [BPw], [BWA])
                      cp("act", WA[64:128, 0:n], Pw[64:128, 0:n], [BPw], [BWA])
                      p.dma(sp_wa[:, t0:t0 + n], WA[:, 0:n], reads=[BWA])
                      Pg, BPg = proj_chunk(25, n)
                      act(sgl[:, 0:n], Pg[:, 0:n], AF.Sigmoid, [BPg], [Bsgl])
                      for j in range(4):
                          rj, kpj, kj, vj = base[:, j, 0:n], base[:, 4 + j, 0:n], base[:, 8 + j, 0:n], base[:, 12 + j, 0:n]
                          Pr, BPr = proj_chunk(12 + j, n)
                          cp("act", rj, Pr[:, 0:n], [BPr], [Bbase[j]])
                          Pk, BPk = proj_chunk(16 + j, n)
                          cp("act", kj, Pk[:, 0:n], [BPk], [Bbase[8 + j]])
                          Pv, BPv = proj_chunk(20 + j, n)
                          cp("dve", vj, Pv[:, 0:n], [BPv], [Bbase[12 + j]])
                          act(tmpA[2][:, 0:n], kj, AF.Square, [Bbase[8 + j], Bvec], [BtA[2]], scale=vcol(0, j))
                          Pq, BPq = pa()
                          mm(Pq[:, 0:n], cst[:, C_BO:C_BO + 128], tmpA[2][:, 0:n], True, True, [Bcst, BtA[2]], [BPq])
                          ts("dve", tmpA[2][:, 0:n], Pq[:, 0:n], 1e-24, None, ALU.max, None, [BPq], [BtA[2]])
                          act(tmpA[2][:, 0:n], tmpA[2][:, 0:n], AF.Ln, [BtA[2]], [BtA[2]])
                          act(tmpA[2][:, 0:n], tmpA[2][:, 0:n], AF.Exp, [BtA[2]], [BtA[2]], scale=-0.5)
                          stt("dve", kpj, kj, vcol(0, j), tmpA[2][:, 0:n], ALU.mult, ALU.mult,
                              [Bbase[8 + j], Bvec, BtA[2]], [Bbase[4 + j]])
                          stt("dve", tmpA[3][:, 0:n], rj, vcol(2, j), kj, ALU.mult, ALU.mult,
                              [Bbase[j], Bvec, Bbase[8 + j]], [BtA[3]])
                          Pq2, BPq2 = pa()
                          mm(Pq2[:, 0:n], cst[:, C_BO:C_BO + 128], tmpA[3][:, 0:n], True, True, [Bcst, BtA[3]], [BPq2])
                          tt("dve", tmpA[3][:, 0:n], Pq2[:, 0:n], vj, ALU.mult, [BPq2, Bbase[12 + j]], [BtA[3]])
                          Pq3, BPq3 = pa()
                          mm(Pq3[:, 0:n], gup[:, j * 128:(j + 1) * 128], sgl[:, 0:n], True, True, [Bgup, Bsgl], [BPq3])
                          ts("dve", G12[:, j, 0:n], Pq3[:, 0:n], vcol(3, j), None, ALU.mult, None, [BPq3, Bvec], [BG12[j]])
                          stt("dve", G12[:, 4 + j, 0:n], tmpA[3][:, 0:n], vcol(4, j), Pq3[:, 0:n], ALU.add, ALU.mult,
                              [BtA[3], Bvec, BPq3], [BG12[4 + j]])
                          for q, Bq in ((j, Bbase[j]), (4 + j, Bbase[4 + j]), (8 + j, Bbase[8 + j]), (12 + j, Bbase[12 + j])):
                              p.dma(sp_base[q][:, t0:t0 + n], base[:, q, 0:n], reads=[Bq])
                          p.dma(sp_g[j][:, t0:t0 + n], G12[:, j, 0:n], reads=[BG12[j]])
                          p.dma(sp_g[4 + j][:, t0:t0 + n], G12[:, 4 + j, 0:n], reads=[BG12[4 + j]])

                  def load_stage(t0, n):
                      for q in range(16):
                          p.dma(base[:, q, 0:n], sp_base[q][:, t0:t0 + n], writes=[Bbase[q]])
                      p.dma(WA[:, 0:n], sp_wa[:, t0:t0 + n], writes=[BWA])
                      for q in range(8):
                          p.dma(G12[:, q, 0:n], sp_g[q][:, t0:t0 + n], writes=[BG12[q]])
                      for j in range(4):
                          p.dma(ycv[:, j, 0:n], sp_ycv[j][:, t0 + 1:t0 + n + 1], writes=[Bycv[j]])

                  def scan_tile(c0, d, Sb, BSb):
                      mG = cst[:, C_MG + 512 * d:C_MG + 512 * (d + 1)]
                      mL = cst[:, C_ML + 128 * d:C_ML + 128 * (d + 1)]
                      tF = cst[:, C_TF + 256 * d:C_TF + 256 * (d + 1)]
                      gcol = 127 if d == 0 else 0
                      cs_ = slice(c0, c0 + 128)
                      P_, BP_ = pa()
                      mm(P_[:], WA[0:64, cs_], lup[0:64, d, :], True, False, [BWA, Blup], [BP_])
                      mm(P_[:], cst[0:1, C_ON:C_ON + 128], rowv[0:1, d * 512:(d + 1) * 512], False, True,
                         [Bcst, Browv], [BP_])
                      act(sigT[:], P_[:], AF.Sigmoid, [BP_], [BsigT])
                      for j in range(4):
                          Pc, BPc = pa()
                          mm(Pc[:, 0:256], sigT[:, j * 128:(j + 1) * 128], tF, True, True, [BsigT, Bcst], [BPc])
                          act(EG[:, 0, :], Pc[:, 0:128], AF.Exp, [BPc], [BEG])
                          act(EG[:, 1, :], Pc[:, 0:128], AF.Exp, [BPc], [BEG], scale=-1.0)
                          act(EG[:, 2, :], Pc[:, 128:256], AF.Exp, [BPc], [BEG])
                          Pa, BPa_ = pa()
                          mm(Pa[:, 0:128], lup[64:128, d, j * 128:(j + 1) * 128], WA[64:128, cs_], True, True,
                             [Blup, BWA], [BPa_])
                          act(aFt[:], Pa[:, 0:128], AF.Sigmoid, [BPa_, Bvec], [BaF], bias=vcol(10 + d, j))
                          rj, kpj, kj = base[:, j, cs_], base[:, 4 + j, cs_], base[:, 8 + j, cs_]
                          tt("pool", KR[:, j, 128:256], rj, EG[:, 0, :], ALU.mult, [Bbase[j], BEG], [BKR[j]])
                          tt("dve", KaF32[:, j, :], kpj, EG[:, 2, :], ALU.mult, [Bbase[4 + j], BEG], [BKa32[j]])
                          cp("pool", KR[:, j, 0:128], KaF32[:, j, :], [BKa32[j]], [BKR[j]])
                          tt("pool", tB[0][:], kpj, aFt[:], ALU.mult, [Bbase[4 + j], BaF], [BtB[0]])
                          tt("dve", tB[1][:], tB[0][:], EG[:, 1, :], ALU.mult, [BtB[0], BEG], [BtB[1]])
                          cp("pool", BtF[:, j, :], tB[1][:], [BtB[1]], [BBt[j]])
                          ts("dve", BpF[:, j, :], tB[1][:], EG[:, 0, gcol:gcol + 1], None, ALU.mult, None,
                             [BtB[1], BEG], [BBp[j]])
                          ts("dve", tB[0][:], aFt[:], vcol(1, j), omka[:, j:j + 1], ALU.mult, ALU.add,
                             [BaF, Bvec, Bomka], [BtB[0]])
                          tt("pool", tB[0][:], tB[0][:], kj, ALU.mult, [BtB[0], Bbase[8 + j]], [BtB[0]])
                          tt("dve", tB[2][:], tB[0][:], EG[:, 1, :], ALU.mult, [BtB[0], BEG], [BtB[2]])
                          cp("pool", KtF[:, j, :], tB[2][:], [BtB[2]], [BKt[j]])
                          ts("dve", KpF[:, j, :], tB[2][:], EG[:, 0, gcol:gcol + 1], None, ALU.mult, None,
                             [BtB[2], BEG], [BKp[j]])
                          ts("dve", DG[:, j, :], cst[:, C_IB:C_IB + 64], EG[:, 0, gcol:gcol + 1], None, ALU.mult, None,
                             [Bcst, BEG], [BDG[j]])
                      ck(30)
                      for name, srcf, Bsrc in (("Ka", lambda j: KaF32[:, j, :], BKa32), ("Bp", lambda j: BpF[:, j, :], BBp),
                                               ("Kp", lambda j: KpF[:, j, :], BKp),
                                               ("V", lambda j: base[:, 12 + j, cs_], Bbase[12:16])):
                          P_, BP_ = pa()
                          for j in range(4):
                              tr(P_[:, j * 128:(j + 1) * 128], srcf(j), [Bsrc[j]], [BP_])
                          cp("act", TM[name][:], P_[:], [BP_], [BTM[name]])

                      ck(31)
                      def unit_stages(h, u):
                          j, p0 = h // 2, 64 * (h % 2)
                          hs = slice(h * 64, (h + 1) * 64)
                          Bt_h, Kt_h, KR_h = BtF[p0:p0 + 64, j, :], KtF[p0:p0 + 64, j, :], KR[p0:p0 + 64, j, :]
                          Ka_h = KR[p0:p0 + 64, j, 0:128]
                          pn, Bpn = PN[u], BPN[u]
                          stages = []

                          def s_gram():
                              mm(PG[:, 0:256], Bt_h, KR_h, True, True, [BBt[j], BKR[j]], [BPG[0]])
                              mm(PG[:, 256:512], Kt_h, KR_h, True, True, [BKt[j], BKR[j]], [BPG[1]])
                              mm(pn[:, 384:512], Ka_h, Bt_h, True, True, [BKR[j], BBt[j]], [Bpn[2]])
                              tt("dve", LT[u][0][:, 0:128], PG[:, 0:128], mG[:, 0:128], ALU.mult, [BPG[0], Bcst],
                                 [BLT[u][0][0]])
                              tt("dve", Gm[u][:, 0:128], PG[:, 128:256], mG[:, 128:256], ALU.mult, [BPG[0], Bcst], [BGm[u]])
                              tt("dve", Gm[u][:, 128:384], PG[:, 256:512], mG[:, 256:512], ALU.mult, [BPG[1], Bcst], [BGm[u]])
                              tt("dve", Nn[u][0][:], pn[:, 384:512], mL, ALU.mult, [Bpn[2], Bcst], [BNn[u][0]])
                              cp("pool", LT[u][0][:, 128:256], identS[:], [BidR], [BLT[u][0][1]])
                          stages.append(s_gram)

                          ev = "act" if u == 0 else "dve"

                          def mk_level(k):
                              a, b = k % 2, (k + 1) % 2
                              def s():
                                  mm(pn[:, 0:128], Nn[u][a][:], LT[u][a][:, 0:128], True, True,
                                     [BNn[u][a], BLT[u][a][0]], [Bpn[0]])
                                  mm(pn[:, 128:256], Nn[u][a][:], LT[u][a][:, 128:256], True, False,
                                     [BNn[u][a], BLT[u][a][1]], [Bpn[0]])
                                  mm(pn[:, 128:256], identS[:], LT[u][a][:, 128:256], False, True,
                                     [BidR, BLT[u][a][1]], [Bpn[0]])
                                  mm(pn[:, 256:384], LT[u][a][:, 0:128], Nn[u][a][:], True, True,
                                     [BLT[u][a][0], BNn[u][a]], [Bpn[1]])
                                  cp(ev, LT[u][b][:], pn[:, 0:256], [Bpn[0]], [BLT[u][b][0], BLT[u][b][1]])
                                  cp(ev, Nn[u][b][:], pn[:, 256:384], [Bpn[1]], [BNn[u][b]])
                              return s
                          for k in range(0, 6):
                              stages.append(mk_level(k))

                          def s_l6():
                              mm(pn[:, 128:256], Nn[u][0][:], LT[u][0][:, 128:256], True, False,
                                 [BNn[u][0], BLT[u][0][1]], [Bpn[0]])
                              mm(pn[:, 128:256], identS[:], LT[u][0][:, 128:256], False, True,
                                 [BidR, BLT[u][0][1]], [Bpn[0]])
                              cp(ev, TtF[u][:], pn[:, 128:256], [Bpn[0]], [BTt[u]])
                              mm(PM[:, 0:64], Gm[u][:, 128:256], TM["V"][:, hs], True, True, [BGm[u], BTM["V"]], [BPM[0]])
                              cp(ev, AkV[u][:], PM[:, 0:64], [BPM[0]], [BAkV[u]])
                          stages.append(s_l6)

                          def s_x():
                              mm(PM[:, 64:128], TtF[u][:], TM["Ka"][:, hs], True, True, [BTt[u], BTM["Ka"]], [BPM[1]])
                              mm(PM[:, 128:192], TtF[u][:], AkV[u][:], True, True, [BTt[u], BAkV[u]], [BPM[1]])
                              if ev == "act":
                                  act(XnS[u][:], PM[:, 64:192], AF.Copy, [BPM[1]], [BXn[u]], scale=-1.0)
                              else:
                                  ts("dve", XnS[u][:], PM[:, 64:192], -1.0, None, ALU.mult, None, [BPM[1]], [BXn[u]])
                          stages.append(s_x)

                          def s_fin():
                              mm(PM[0:64, 256:384], identR[p0:p0 + 64, p0:p0 + 64], KR[p0:p0 + 64, j, 128:256], True, False,
                                 [BidR, BKR[j]], [BPM[3]])
                              mm(PM[0:64, 256:384], XnS[u][:, 0:64], Gm[u][:, 0:128], False, True, [BXn[u], BGm[u]], [BPM[3]])
                              cp(ev, RhT[u][:], PM[0:64, 256:384], [BPM[3]], [BRh[u]])
                              mm(PM[0:64, 192:256], XnS[u][:, 0:64], TM["Bp"][:, hs], True, False, [BXn[u], BTM["Bp"]], [BPM[2]])
                              mm(PM[0:64, 192:256], identR[p0:p0 + 64, p0:p0 + 64], DG[p0:p0 + 64, j, :], False, True,
                                 [BidR, BDG[j]], [BPM[2]])
                              cp(ev, PTs[u][:], PM[0:64, 192:256], [BPM[2]], [BPTs[u]])
                              mm(PY[:, hs], Gm[u][:, 256:384], TM["V"][:, hs], True, False, [BGm[u], BTM["V"]], [BPY])
                              mm(PY[:, hs], Gm[u][:, 0:128], XnS[u][:, 64:128], False, False, [BGm[u], BXn[u]], [BPY])
                              mm(PY[:, hs], RhT[u][:], Sb[:, h, :], False, True, [BRh[u], BSb], [BPY])
                              mm(PS_[0:64, hs], TM["Kp"][:, hs], TM["V"][:, hs], True, False, [BTM["Kp"], BTM["V"]], [BPS])
                              mm(PS_[0:64, hs], TM["Bp"][:, hs], XnS[u][:, 64:128], False, False, [BTM["Bp"], BXn[u]], [BPS])
                              mm(PS_[0:64, hs], PTs[u][:], Sb[:, h, :], False, True, [BPTs[u], BSb], [BPS])
                          stages.append(s_fin)
                          return stages

                      for hp in range(4):
                          sa, sb_ = unit_stages(2 * hp, 0), unit_stages(2 * hp + 1, 1)
                          for idx_, (a_, b_) in enumerate(zip(sa, sb_)):
                              a_()
                              b_()
                      cp("act", Sb[:].rearrange("p h n -> p (h n)"), PS_[0:64, :], [BPS], [BSb])

                  def post_tile(src, dst, t0, c0, X, BX):
                      cs_ = slice(c0, c0 + 128)
                      tt("dve", ysum, PY[:], y1t, ALU.add, [BPY, By1], [Bys])
                      y3 = ysum.rearrange("p (h n) -> p h n", h=8)
                      p.op("dve", lambda e: e.tensor_reduce(out=gst[:, :, 0], in_=y3, axis=AX.X, op=ALU.add),
                           reads=[Bys], writes=[Bgst])
                      tt("pool", yn, ysum, ysum, ALU.mult, [Bys], [Byn])
                      p.op("dve", lambda e: e.tensor_reduce(out=gst[:, :, 1], in_=yn.rearrange("p (h n) -> p h n", h=8),
                                                            axis=AX.X, op=ALU.add), reads=[Byn], writes=[Bgst])
                      ts("dve", gst[:, :, 0], gst[:, :, 0], 1.0 / 64, None, ALU.mult, None, [Bgst], [Bgst])
                      tt("dve", gst[:, :, 2], gst[:, :, 0], gst[:, :, 0], ALU.mult, [Bgst], [Bgst])
                      stt("dve", gst[:, :, 3], gst[:, :, 1], 1.0 / 64, gst[:, :, 2], ALU.mult, ALU.subtract, [Bgst], [Bgst])
                      ts("dve", gst[:, :, 3], gst[:, :, 3], GN_EPS, None, ALU.add, None, [Bgst], [Bgst])
                      act(gst[:, :, 3], gst[:, :, 3], AF.Ln, [Bgst], [Bgst])
                      act(gst[:, :, 3], gst[:, :, 3], AF.Exp, [Bgst], [Bgst], scale=-0.5)
                      for h in range(8):
                          ts("dve", yn[:, h * 64:(h + 1) * 64], ysum[:, h * 64:(h + 1) * 64],
                             gst[:, h, 0:1], gst[:, h, 3:4], ALU.subtract, ALU.mult, [Bys, Bgst], [Byn])
                      P_, BP_ = pa()
                      for j in range(4):
                          tr(P_[:, j * 128:(j + 1) * 128], yn[:, j * 128:(j + 1) * 128], [Byn], [BP_])
                      for j in range(4):
                          tt("dve", tB[0][:], P_[:, j * 128:(j + 1) * 128], G12[:, j, cs_], ALU.mult, [BP_, BG12[j]], [BtB[0]])
                          tt("pool", catT[:, 4 + j, :], tB[0][:], G12[:, 4 + j, cs_], ALU.add, [BtB[0], BG12[4 + j]], [Bcat])
                          cp("pool", catT[:, j, :], ycv[:, j, cs_], [Bycv[j]], [Bcat])
                      for half in range(2):
                          Po, BPo = pa()
                          for k in range(KC):
                              mm(Po[:], catT[:, k, :], wout[:, k, half * 512:(half + 1) * 512], k == 0, k == KC - 1,
                                 [Bcat, Bwout], [BPo])
                          tt("dve", xo[:, half * 512:(half + 1) * 512], Po[:], bc["gt"][:, half * 512:(half + 1) * 512],
                             ALU.mult, [BPo, Bbc["gt"]], [Bxo])
                      tt("pool", xo[:], xo[:], X[:], ALU.add, [Bxo, BX], [Bxo])
                      p.dma(dst[t0:t0 + 128, :], xo[:], reads=[Bxo])

                  def mixer_segment(src, dst, seg_T, setid, Sin, need_out, Sfin):
                      load_bc(setid, 0)
                      nsb = (seg_T + SBT - 1) // SBT
                      for d in range(2):
                          if Sin is None:
                              ts("dve", Sst[d][:].rearrange("p h n -> p (h n)"), cst[0:64, C_MG:C_MG + 512], 0.0, None,
                                 ALU.mult, None, [Bcst], [BS[d]])
                          else:
                              cp("pool", Sst[d][:], Sin[d][0][:], [Sin[d][1]], [BS[d]])
                      for s_ in range(nsb):
                          t0 = s_ * SBT
                          n = min(SBT, seg_T - t0)
                          stage1(src, t0, n, s_ == 0, seg_T)
                          ck(2)
                          for i in range(n // 128):
                              scan_tile(i * 128, 0, Sst[0], BS[0])
                              ck(3)
                              if need_out:
                                  cp("dve", sigT[:], PY[:], [BPY], [BsigT])
                                  p.dma(sp_y1[t0 + i * 128:t0 + (i + 1) * 128, :], sigT[:], reads=[BsigT])
                      p.barrier()
                      ck(4)
                      for s_ in reversed(range(nsb)):
                          t0 = s_ * SBT
                          n = min(SBT, seg_T - t0)
                          load_stage(t0, n)
                          for i in reversed(range(n // 128)):
                              scan_tile(i * 128, 1, Sst[1], BS[1])
                              if need_out:
                                  tk = t0 + i * 128
                                  p.dma(y1t, sp_y1[tk:tk + 128, :], writes=[By1])
                                  X, BX = xt[i % 2], Bxt[i % 2]
                                  p.dma(X[:], src[tk:tk + 128, :], writes=[BX])
                                  post_tile(src, dst, tk, i * 128, X, BX)
                      if Sfin is not None:
                          for d in range(2):
                              cp("pool", Sfin[d][0][:], Sst[d][:], [BS[d]], [Sfin[d][1]])
                      p.barrier()

                  mixer_segment(c_src, c_mid, CT, 1, None, not last, [(Sc[0], BSc[0]), (Sc[1], BSc[1])])
                  ck(5)
                  mixer_segment(x_src, x_mid, T, 0, [(Sc[0], BSc[0]), (Sc[1], BSc[1])], True, None)
                  p.barrier()
                  ck(6)

              with ExitStack() as s3:
                  def sb3(name, shape, dt=F32):
                      return sb(name, shape, dt, stack=s3)
                  wup = sb3("wup", [128, KC, 2 * DFF], BF16); Bwup = Buf()
                  wdn = sb3("wdn", [128, FC, D], BF16); Bwdn = Buf()
                  with ExitStack() as sw3:
                      wst3 = [sb(f"wst3{i}", [128, 2048], stack=sw3) for i in range(2)]; Bw3 = [Buf(), Buf()]
                      ii = 0
                      for k in range(KC):
                          for c_ in range(0, 2 * DFF, 2048):
                              w_ = min(2048, 2 * DFF - c_)
                              sg, Bsg = wst3[ii % 2], Bw3[ii % 2]
                              p.dma(sg[:, 0:w_], w_up[l][k * 128:(k + 1) * 128, c_:c_ + w_], writes=[Bsg])
                              cp("dve" if ii % 2 == 0 else "pool", wup[:, k, c_:c_ + w_], sg[:, 0:w_], [Bsg], [Bwup])
                              ii += 1
                      for c_ in range(FC):
                          sg, Bsg = wst3[ii % 2], Bw3[ii % 2]
                          p.dma(sg[:, 0:D], w_dn[l][c_ * 128:(c_ + 1) * 128, :], writes=[Bsg])
                          cp("dve" if ii % 2 == 0 else "pool", wdn[:, c_, :], sg[:, 0:D], [Bsg], [Bwdn])
                          ii += 1
                      p.barrier()
                  xt3 = [sb3(f"x3{i}", [128, D]) for i in range(2)]; Bx3 = [Buf() for _ in range(2)]
                  xr, Bxr = xt3[1], Bx3[1]
                  hb3 = sb3("hb3", [128, D]); Bhb3 = Buf()
                  ssq3 = sb3("ssq3", [128, 2]); Bssq3 = Buf()
                  hT3 = sb3("hT3", [128, KC, 640], BF16); BhT3 = Buf()
                  accs = [sb3(f"acc{i}", [128, 512]) for i in range(2)]; Baccs = [Buf(), Buf()]
                  actT = sb3("actT", [128, FC, 512], BF16); Bact = [Buf() for _ in range(FC)]
                  xo3, Bxo3 = hb3, Bhb3
                  fgb = sb3("fgb", [128, D]); Bfgb = Buf()
                  if last and final_norm:
                      p.dma(fgb[:], fin_g.partition_broadcast(128), writes=[Bfgb])

                  def ffn_segment(src, dst, seg_T, setid, gw, is_out):
                      load_bc(setid, 1)
                      BT = 512 if seg_T >= 512 else seg_T
                      nrow_tot = seg_T // gw
                      nrow = BT // gw
                      for b0 in range(0, seg_T, BT):
                          halo = gw if gw < seg_T else 0
                          lo, hi = max(0, b0 - halo), min(seg_T, b0 + BT + halo)
                          nw = hi - lo
                          off = b0 - lo
                          tpos, xi = lo, 0
                          while tpos < hi:
                              w_ = min(128, hi - tpos)
                              X, BX = xt3[xi % 2], Bx3[xi % 2]
                              if w_ < 128:
                                  p.op("pool", lambda e, X=X: e.memset(X[:], 0.0), writes=[BX])
                              p.dma(X[0:w_, :], src[tpos:tpos + w_, :], writes=[BX])
                              act(hb3[:], X[:], AF.Square, [BX], [Bhb3, Bssq3], accum=ssq3[:, 0:1])
                              ts("dve", ssq3[:, 1:2], ssq3[:, 0:1], 1.0 / D, RMS_EPS, ALU.mult, ALU.add, [Bssq3], [Bssq3])
                              act(ssq3[:, 1:2], ssq3[:, 1:2], AF.Ln, [Bssq3], [Bssq3])
                              act(ssq3[:, 1:2], ssq3[:, 1:2], AF.Exp, [Bssq3], [Bssq3], scale=-0.5)
                              stt("dve", hb3[:], X[:], ssq3[:, 1:2], bc["gs"][:], ALU.mult, ALU.mult,
                                  [BX, Bssq3, Bbc["gs"]], [Bhb3])
                              tt("pool", hb3[:], hb3[:], bc["sh"][:], ALU.add, [Bhb3, Bbc["sh"]], [Bhb3])
                              col0 = tpos - lo
                              for half in range(2):
                                  P_, BP_ = pa()
                                  for q in range(4):
                                      k = half * 4 + q
                                      tr(P_[:, q * 128:(q + 1) * 128], hb3[:, k * 128:(k + 1) * 128], [Bhb3], [BP_])
                                  cp("act", hT3[:, half * 4:half * 4 + 4, col0:col0 + w_],
                                     P_[:].rearrange("p (q n) -> p q n", q=4)[:, :, 0:w_], [BP_], [BhT3])
                              tpos += w_
                              xi += 1
                          wr0 = off // gw
                          for c_ in range(FC):
                              acc, Bacc = accs[c_ % 2], Baccs[c_ % 2]
                              sil, Bsil = acc, Bacc
                              if c_ % 2 == 0:
                                  GA, BGA, GB, BGB = PG, BPG[0], PN[0], BPN[0][0]
                              else:
                                  GA, BGA, GB, BGB = PY, BPY, PS_, BPS
                              segs = [(0, min(nw, 512), GA, BGA)]
                              if nw > 512:
                                  segs.append((512, nw, GB, BGB))
                              for (a_, b_, P_, BP_) in segs:
                                  for k in range(KC):
                                      mm(P_[:, 0:b_ - a_], wup[:, k, c_ * 128:(c_ + 1) * 128], hT3[:, k, a_:b_],
                                         k == 0, k == KC - 1, [Bwup, BhT3], [BP_])
                              wc = lambda ty, tx: vec[:, 48 + (ty * 3 + tx) * FC + c_:48 + (ty * 3 + tx) * FC + c_ + 1]
                              bcol = vec[:, 48 + 9 * FC + c_:48 + 9 * FC + c_ + 1]

                              def tap(ty, tx, first):
                                  dy, dx = ty - 1, tx - 1
                                  r0g = b0 // gw
                                  rows_ok = [r for r in range(nrow) if 0 <= r0g + r + dy < nrow_tot]
                                  if not rows_ok:
                                      return
                                  ra, rb = rows_ok[0], rows_ok[-1] + 1
                                  ca, cb2 = (1, gw) if dx == -1 else ((0, gw - 1) if dx == 1 else (0, gw))
                                  rsplit = 512 // gw
                                  r = ra
                                  while r < rb:
                                      wr = wr0 + r + dy
                                      if wr < rsplit:
                                          re_ = min(rb, rsplit - wr0 - dy)
                                          P_, B_, wbase = GA, BGA, 0
                                      else:
                                          re_ = rb
                                          P_, B_, wbase = GB, BGB, rsplit
                                      nr = re_ - r
                                      iv = P_[:, (wr - wbase) * gw:(wr - wbase + nr) * gw].rearrange("p (r c) -> p r c", c=gw)[:, :, ca + dx:cb2 + dx]
                                      ov = acc[:, r * gw:(r + nr) * gw].rearrange("p (r c) -> p r c", c=gw)[:, :, ca:cb2]
                                      if first:
                                          act(ov, iv, AF.Identity, [B_, Bvec], [Bacc], bias=bcol, scale=wc(ty, tx))
                                      else:
                                          stt("dve", ov, iv, wc(ty, tx), ov, ALU.mult, ALU.add, [B_, Bvec, Bacc], [Bacc])
                                      r = re_
                              tap(1, 1, True)
                              for ty in range(3):
                                  for tx in range(3):
                                      if (ty, tx) != (1, 1) and not (gw >= seg_T and ty != 1):
                                          tap(ty, tx, False)
                              act(sil[:, 0:BT], acc[:, 0:BT], AF.Silu, [], [Bacc])
                              Pv, BPv = pa()
                              for k in range(KC):
                                  mm(Pv[:, 0:BT], wup[:, k, DFF + c_ * 128:DFF + (c_ + 1) * 128], hT3[:, k, off:off + BT],
                                     k == 0, k == KC - 1, [Bwup, BhT3], [BPv])
                              tt("dve", actT[:, c_, 0:BT], sil[:, 0:BT], Pv[:, 0:BT], ALU.mult, [Bsil, BPv], [Bact[c_]])
                          for i in range(BT // 128):
                              tk = b0 + i * 128
                              p.dma(xr[:], src[tk:tk + 128, :], writes=[Bxr])
                              for half in range(2):
                                  Po, BPo = pa()
                                  for c_ in range(FC):
                                      mm(Po[:], actT[:, c_, i * 128:(i + 1) * 128], wdn[:, c_, half * 512:(half + 1) * 512],
                                         c_ == 0, c_ == FC - 1, [Bact[c_], Bwdn], [BPo])
                                  tt("dve", xo3[:, half * 512:(half + 1) * 512], Po[:], bc["gt"][:, half * 512:(half + 1) * 512],
                                     ALU.mult, [BPo, Bbc["gt"]], [Bxo3])
                              tt("pool", xo3[:], xo3[:], xr[:], ALU.add, [Bxo3, Bxr], [Bxo3])
                              if is_out and final_norm:
                                  act(xt3[0][:], xo3[:], AF.Square, [Bxo3], [Bx3[0], Bssq3], accum=ssq3[:, 0:1])
                                  ts("dve", ssq3[:, 1:2], ssq3[:, 0:1], 1.0 / D, RMS_EPS, ALU.mult, ALU.add, [Bssq3], [Bssq3])
                                  act(ssq3[:, 1:2], ssq3[:, 1:2], AF.Ln, [Bssq3], [Bssq3])
                                  act(ssq3[:, 1:2], ssq3[:, 1:2], AF.Exp, [Bssq3], [Bssq3], scale=-0.5)
                                  stt("dve", xo3[:], xo3[:], ssq3[:, 1:2], fgb[:], ALU.mult, ALU.mult, [Bxo3, Bssq3, Bfgb], [Bxo3])
                              p.dma(dst[tk:tk + 128, :], xo3[:], reads=[Bxo3])

                  if not last:
                      ffn_segment(c_mid, c_dst, CT, 1, CT, False)
                  ffn_segment(x_mid, x_dst, T, 0, GW, last)
                  p.barrier()
        except _Stop:
            pass
        p.barrier()
        nc._prog_ninst = p.ninst
    return nc


def prep_inputs(x, c, ctx, c_ctx, ada_w, ada_b, norm1_g, norm2_g, w_in, conv_a_w, rw_w0, rw_w_up, rw_a0, rw_a_up,
                rw_k_k, rw_k_a, rw_r_k, rw_g_up, rw_ln_g, rw_ln_b, w_out, ffn_w_up, ffn_conv_w, ffn_conv_b,
                ffn_w_down, final_g, b):
    L = ada_w.shape[0]
    f = lambda a: np.ascontiguousarray(a, dtype=np.float32)
    cs = np.stack([c[b].reshape(KC, 128).T, c_ctx.reshape(KC, 128).T], axis=-1)
    rows = np.concatenate([ada_b, norm1_g, norm2_g], axis=1)[:, None, :].repeat(2, axis=1)

    def ch(v, n):
        return v.reshape(n, 128).T
    vec = np.zeros((L, 128, NVEC), np.float32)
    for l in range(L):
        cols = [ch(rw_k_k[l], 4), ch(rw_k_a[l], 4), ch(rw_r_k[l].reshape(-1), 4), ch(rw_ln_g[l], 4), ch(rw_ln_b[l], 4),
                ch(conv_a_w[l, 0], 4), ch(conv_a_w[l, 1], 4), ch(conv_a_w[l, 2], 4),
                ch(rw_w0[l, 0], 4), ch(rw_w0[l, 1], 4), ch(rw_a0[l, 0], 4), ch(rw_a0[l, 1], 4)]
        for ty in range(3):
            for tx in range(3):
                cols.append(ch(ffn_conv_w[l, ty, tx], FC))
        cols.append(ch(ffn_conv_b[l], FC))
        vec[l] = np.concatenate(cols, axis=1)
    rowv = rw_w0.reshape(L, 1, 1024)
    lup = np.concatenate([rw_w_up.transpose(0, 2, 1, 3), rw_a_up.transpose(0, 2, 1, 3)], axis=1)
    return {
        "x": f(x[b]), "ctx": f(ctx[b]), "cs": f(cs), "cst": make_consts(), "ada_w": f(ada_w), "rows": f(rows),
        "fin_g": f(final_g), "w_in": f(w_in), "w_out": f(w_out), "w_up": f(ffn_w_up), "w_dn": f(ffn_w_down),
        "vec": f(vec), "rowv": f(rowv), "lup": f(lup), "gup": f(rw_g_up),
    }


def kernel(**inputs):
    inputs = {k: np.asarray(v) for k, v in inputs.items()}
    B, T, _ = inputs["x"].shape
    CT = inputs["ctx"].shape[1]
    nc = build_program(T, CT, inputs["ada_w"].shape[0])
    in_maps = [prep_inputs(b=b, **inputs) for b in range(B)]
    res = run_bass_kernel_spmd(nc, in_maps, core_ids=list(range(B)))
    return np.stack([r["out"] for r in res.results], axis=0).astype(np.float32)
```

```python
import math
import numpy as np
from contextlib import ExitStack
import concourse.bass as bass
import concourse.mybir as mybir
from concourse.bass_utils import run_bass_kernel_spmd

F32 = mybir.dt.float32
BF16 = mybir.dt.bfloat16
F32R = mybir.dt.float32r
AF = mybir.ActivationFunctionType
ALU = mybir.AluOpType
AX = mybir.AxisListType

D = 1024
KC = 8
PROJ = 3328
NPC = 26
DFF = 2816
FC = 22
GW = 64
DS = math.exp(-0.5)
RMS_EPS = 1e-6
GN_EPS = 64e-5
NVEC = 48 + 10 * FC
SCAN_DT = BF16

C_ID = 0
C_MG = 128
C_ML = C_MG + 1024
C_TF = C_ML + 256
C_BO = C_TF + 512
C_ON = C_BO + 128
C_IB = C_ON + 128
NCST = C_IB + 64


def make_consts():
    c = np.zeros((128, NCST), np.float32)
    s = np.arange(128)[:, None]
    t = np.arange(128)[None, :]
    c[:, C_ID:C_ID + 128] = np.eye(128)
    for d in range(2):
        lt = (s < t) if d == 0 else (s > t)
        le = (s <= t) if d == 0 else (s >= t)
        g = c[:, C_MG + 512 * d:C_MG + 512 * (d + 1)]
        g[:, 0:128] = -1.0 * lt
        g[:, 128:256] = le
        g[:, 256:384] = lt
        g[:, 384:512] = le
        c[:, C_ML + 128 * d:C_ML + 128 * (d + 1)] = -1.0 * lt.T
        f = c[:, C_TF + 256 * d:C_TF + 256 * (d + 1)]
        f[:, 0:128] = -DS * le
        f[:, 128:256] = -DS * lt
    c[:, C_BO:C_BO + 128] = (s // 64 == t // 64)
    c[:, C_ON:C_ON + 128] = 1.0
    c[:, C_IB:C_IB + 64] = (s % 64 == np.arange(64)[None, :])
    return c


class Buf:
    __slots__ = ("name", "w", "rd", "ps")

    def __init__(self, name="", ps=False):
        self.name = name
        self.w = None
        self.rd = []
        self.ps = ps


ENGMAP = {"pe": "tensor", "dve": "vector", "act": "scalar", "pool": "gpsimd", "sp": "sync"}


class Eng:
    def __init__(self, name, sem, h):
        self.name = name
        self.sem = sem
        self.h = h
        self.cnt = 0
        self.waited = {}


class Prog:
    def __init__(self, nc, sems, dma_sems):
        self.nc = nc
        self.E = {n: Eng(n, sems[n], getattr(nc, ENGMAP[n])) for n in ENGMAP}
        self.dma_sems = dma_sems
        self.dma_cnt = [0] * len(dma_sems)
        self.dma_rr = 0
        self.ninst = 0
        self.stop = False

    def _deps(self, reads, writes):
        d = []
        for b in reads:
            if b.w is not None:
                d.append(b.w)
        for b in writes:
            if b.w is not None:
                d.append(b.w)
            d.extend(b.rd)
        return d

    def _waits(self, e, deps, skip_self):
        need = {}
        for (sem, val, owner) in deps:
            if skip_self and owner is e:
                continue
            k = id(sem)
            if e.waited.get(k, 0) >= val:
                continue
            if k not in need or need[k][1] < val:
                need[k] = (sem, val)
        for k, (sem, val) in need.items():
            e.waited[k] = val
            e.h.wait_ge(sem, val)

    def op(self, en, fn, reads=(), writes=()):
        if self.stop:
            return
        e = self.E[en]
        deps = self._deps(reads, writes)
        for b in reads:
            if b.ps:
                deps.extend(t for t in b.rd if t[2] is not e)
        self._waits(e, deps, skip_self=(en == "pe"))
        e.cnt += 1
        tok = (e.sem, e.cnt, e)
        fn(e.h).then_inc(e.sem, 1)
        for b in writes:
            b.w = tok
            b.rd = []
        for b in reads:
            b.rd.append(tok)
        self.ninst += 1

    def dma(self, out, in_, reads=(), writes=(), q="sp", slow=False):
        if self.stop:
            return
        e = self.E[q]
        deps = self._deps(reads, writes)
        i = self.dma_rr
        self.dma_rr = (self.dma_rr + 1) % len(self.dma_sems)
        sem = self.dma_sems[i]
        if self.dma_cnt[i] > 0:
            deps.append((sem, self.dma_cnt[i], None))
        self._waits(e, deps, skip_self=False)
        self.dma_cnt[i] += 16
        tok = (sem, self.dma_cnt[i], None)
        if slow:
            e.h.dma_start(out=out, in_=in_, allow_slow_non_contiguous=True).then_inc(sem, 16)
        else:
            e.h.dma_start(out=out, in_=in_).then_inc(sem, 16)
        for b in writes:
            b.w = tok
            b.rd = []
        for b in reads:
            b.rd.append(tok)
        self.ninst += 1
        return tok

    def barrier(self):
        if self.stop:
            return
        toks = [(e.sem, e.cnt, e) for e in self.E.values() if e.cnt > 0]
        toks += [(s, c, None) for s, c in zip(self.dma_sems, self.dma_cnt) if c > 0]
        for e in self.E.values():
            self._waits(e, toks, skip_self=True)


class _Stop(Exception):
    pass


def build_program(T, CT, L=2, final_norm=True, dbg=None):
    assert T % 512 == 0 and CT % 128 == 0
    nc = bass.Bass("TRN2", target_bir_lowering=False)
    dt_ = nc.dram_tensor

    def din(name, shape, dt=F32):
        return dt_(name, shape, dt, kind="ExternalInput").ap()

    def dint(name, shape, dt=F32):
        return dt_(name, shape, dt, kind="Internal").ap()

    x_in = din("x", [T, D])
    ctx_in = din("ctx", [CT, D])
    cs_in = din("cs", [128, KC, 2])
    cst_in = din("cst", [128, NCST])
    ada_w = din("ada_w", [L, D, 6 * D])
    rows_in = din("rows", [L, 2, 6 * D + 2 * D])
    fin_g = din("fin_g", [D])
    w_in = din("w_in", [L, D, PROJ])
    w_out = din("w_out", [L, D, D])
    w_up = din("w_up", [L, D, 2 * DFF])
    w_dn = din("w_dn", [L, DFF, D])
    vec_in = din("vec", [L, 128, NVEC])
    rowv_in = din("rowv", [L, 1, 1024])
    lup_in = din("lup", [L, 128, 2, 512])
    gup_in = din("gup", [L, 128, 512])
    out = dt_("out", [T, D], F32, kind="ExternalOutput").ap()

    xs = [dint("xs0", [T, D]), dint("xs1", [T, D])]
    cxs = [dint("cxs0", [CT, D]), dint("cxs1", [CT, D])]
    modr = dint("modr", [2, 6, D])
    TS = max(T, CT)
    sp_base = dint("sp_base", [16, 128, TS])
    sp_wa = dint("sp_wa", [128, TS])
    sp_g = dint("sp_g", [8, 128, TS])
    sp_ycv = dint("sp_ycv", [4, 128, TS + 2], BF16)
    sp_y1 = dint("sp_y1", [TS, 512])

    st = ExitStack()
    with st:
        sems = {n: st.enter_context(nc.semaphore(n)) for n in ENGMAP}
        dsem = [st.enter_context(nc.semaphore(f"dq{i}")) for i in range(24)]
        p = Prog(nc, sems, dsem)

        uid = [0]

        def sb(name, shape, dt=F32, stack=st):
            uid[0] += 1
            return stack.enter_context(nc.sbuf_tensor(f"s{uid[0]}_{name}", shape, dt))

        def ps(name, shape, dt=F32):
            uid[0] += 1
            return st.enter_context(nc.psum_tensor(f"p{uid[0]}_{name}", shape, dt))

        cst = sb("cst", [128, NCST]); Bcst = Buf()
        vec = sb("vec", [128, NVEC]); Bvec = Buf()
        omka = sb("omka", [128, 4]); Bomka = Buf()
        bc = {n: sb("bc_" + n, [128, D]) for n in ("gs", "sh", "gt")}
        Bbc = {n: Buf() for n in bc}
        PA = [ps("PA0", [128, 512]), ps("PA1", [128, 512])]; BPA = [Buf(ps=True), Buf(ps=True)]
        _bpg = Buf(ps=True)
        PG = ps("PG", [128, 512]); BPG = [_bpg, _bpg]
        PN = [ps("PN0", [128, 512]), ps("PN1", [128, 512])]
        _bpn = [Buf(ps=True), Buf(ps=True)]
        BPN = [[_bpn[0]] * 3, [_bpn[1]] * 3]
        _bpm = Buf(ps=True)
        PM = ps("PM", [128, 512]); BPM = [_bpm] * 5
        PS_ = ps("PS", [128, 512]); BPS = Buf(ps=True)
        PY = ps("PY", [128, 512]); BPY = Buf(ps=True)
        pa_rr = [0]

        def pa():
            i = pa_rr[0]
            pa_rr[0] ^= 1
            return PA[i], BPA[i]

        ident = cst[:, C_ID:C_ID + 128]
        p.dma(cst[:], cst_in, writes=[Bcst])
        identR = sb("identR", [128, 128], SCAN_DT); BidR = Buf()
        identS = identR
        p.op("dve", lambda e: e.tensor_copy(out=identR[:], in_=ident), reads=[Bcst], writes=[BidR])

        def mm(o, l, r, start, stop, reads, writes):
            p.op("pe", lambda e: e.matmul(o, l, r, start=start, stop=stop), reads=reads, writes=writes)

        def tr(o, i_, reads, writes):
            p.op("pe", lambda e: e.transpose(o, i_, ident), reads=list(reads) + [Bcst], writes=writes)

        def act(o, i_, func, reads, writes, bias=None, scale=None, accum=None):
            kw = {}
            if bias is not None:
                kw["bias"] = bias
            if scale is not None:
                kw["scale"] = scale
            if accum is not None:
                kw["accum_out"] = accum
            p.op("act", lambda e: e.activation(out=o, in_=i_, func=func, **kw), reads=reads, writes=writes)

        def tt(en, o, a, b, op, reads, writes):
            p.op(en, lambda e: e.tensor_tensor(out=o, in0=a, in1=b, op=op), reads=reads, writes=writes)

        def ts(en, o, a, s1, s2, op0, op1, reads, writes):
            if s2 is None:
                p.op(en, lambda e: e.tensor_scalar(out=o, in0=a, scalar1=s1, scalar2=None, op0=op0),
                     reads=reads, writes=writes)
            else:
                p.op(en, lambda e: e.tensor_scalar(out=o, in0=a, scalar1=s1, scalar2=s2, op0=op0, op1=op1),
                     reads=reads, writes=writes)

        def stt(en, o, a, s, b, op0, op1, reads, writes):
            p.op(en, lambda e: e.scalar_tensor_tensor(out=o, in0=a, scalar=s, in1=b, op0=op0, op1=op1),
                 reads=reads, writes=writes)

        def cp(en, o, i_, reads, writes):
            if en == "act":
                act(o, i_, AF.Copy, reads, writes)
            else:
                p.op(en, lambda e: e.tensor_copy(out=o, in_=i_), reads=reads, writes=writes)

        def ck(k):
            if dbg == k and not p.stop:
                p.barrier()
                p.stop = True

        try:
          for l in range(L):
              last = (l == L - 1)
              x_src = x_in if l == 0 else xs[1]
              c_src = ctx_in if l == 0 else cxs[1]
              x_mid, c_mid = xs[0], cxs[0]
              x_dst, c_dst = xs[1], cxs[1]
              if last:
                  x_dst = out

              p.barrier()
              with ExitStack() as s0:
                  rows = sb("rows", [2, 8 * D], stack=s0); Brows = Buf()
                  mod = sb("mod", [2, 6 * D], stack=s0); Bmod = Buf()
                  drv = sb("drv", [2, 6, D], stack=s0); Bdrv = Buf()
                  cs = sb("cs", [128, KC, 2], stack=s0); Bcs = Buf()
                  scs = sb("scs", [128, KC, 2], stack=s0); Bscs = Buf()
                  stg = [sb(f"adastg{i}", [128, KC, 512], stack=s0) for i in range(2)]; Bstg = [Buf(), Buf()]
                  p.dma(rows[:], rows_in[l], writes=[Brows])
                  p.dma(cs[:], cs_in, writes=[Bcs])
                  p.dma(vec[:], vec_in[l], writes=[Bvec])
                  ts("dve", omka[:], vec[:, 4:8], -1.0, 1.0, ALU.mult, ALU.add, [Bvec], [Bomka])
                  act(scs[:], cs[:], AF.Silu, [Bcs], [Bscs])
                  for cb in range(12):
                      sg, Bsg = stg[cb % 2], Bstg[cb % 2]
                      p.dma(sg[:], ada_w[l][:, cb * 512:(cb + 1) * 512].rearrange("(k p) n -> p k n", p=128),
                            writes=[Bsg])
                      P_, BP_ = pa()
                      for k in range(KC):
                          mm(P_[0:2, :], scs[:, k, :], sg[:, k, :], k == 0, k == KC - 1, [Bscs, Bsg], [BP_])
                      tt("dve", mod[:, cb * 512:(cb + 1) * 512], P_[0:2, :], rows[:, cb * 512:(cb + 1) * 512],
                         ALU.add, [BP_, Brows], [Bmod])
                  stt("dve", drv[:, 0, :], mod[:, D:2 * D], 1.0, rows[:, 6 * D:7 * D], ALU.add, ALU.mult,
                      [Bmod, Brows], [Bdrv])
                  cp("dve", drv[:, 1, :], mod[:, 0:D], [Bmod], [Bdrv])
                  cp("dve", drv[:, 2, :], mod[:, 2 * D:3 * D], [Bmod], [Bdrv])
                  stt("dve", drv[:, 3, :], mod[:, 4 * D:5 * D], 1.0, rows[:, 7 * D:8 * D], ALU.add, ALU.mult,
                      [Bmod, Brows], [Bdrv])
                  cp("dve", drv[:, 4, :], mod[:, 3 * D:4 * D], [Bmod], [Bdrv])
                  cp("dve", drv[:, 5, :], mod[:, 5 * D:6 * D], [Bmod], [Bdrv])
                  Bmodr = Buf()
                  p.dma(modr, drv[:], reads=[Bdrv], writes=[Bmodr])
                  p.barrier()
              ck(0)

              def load_bc(setid, which):
                  for n, r in (("gs", 0), ("sh", 1), ("gt", 2)):
                      p.dma(bc[n][:], modr[setid, 3 * which + r].partition_broadcast(128),
                            reads=[Bmodr], writes=[Bbc[n]])

              with ExitStack() as s1:
                  def sb1(name, shape, dt=F32):
                      return sb(name, shape, dt, stack=s1)
                  rowv = sb1("rowv", [1, 1024]); Browv = Buf()
                  lup = sb1("lup", [128, 2, 512]); Blup = Buf()
                  gup = sb1("gup", [128, 512], BF16); Bgup = Buf()
                  Sst = [sb1(f"S{d}", [64, 8, 64], SCAN_DT) for d in range(2)]; BS = [Buf(), Buf()]
                  Sc = [sb1(f"Sc{d}", [64, 8, 64], SCAN_DT) for d in range(2)]; BSc = [Buf(), Buf()]
                  p.dma(rowv[:], rowv_in[l], writes=[Browv])
                  p.dma(lup[:], lup_in[l], writes=[Blup])
                  win = sb1("win", [128, KC, PROJ], BF16); Bwin = Buf()
                  wout = sb1("wout", [128, KC, D], BF16); Bwout = Buf()
                  with ExitStack() as sw:
                      wstg = [sb(f"wstg{i}", [128, PROJ], stack=sw) for i in range(2)]; Bwstg = [Buf(), Buf()]
                      for k in range(KC):
                          sg, Bsg = wstg[k % 2], Bwstg[k % 2]
                          p.dma(sg[:], w_in[l][k * 128:(k + 1) * 128, :], writes=[Bsg])
                          cp("dve" if k % 2 == 0 else "pool", win[:, k, :], sg[:], [Bsg], [Bwin])
                      for k in range(KC):
                          sg, Bsg = wstg[k % 2], Bwstg[k % 2]
                          p.dma(sg[:, 0:D], w_out[l][k * 128:(k + 1) * 128, :], writes=[Bsg])
                          cp("dve" if k % 2 == 0 else "pool", wout[:, k, :], sg[:, 0:D], [Bsg], [Bwout])
                      p.dma(wstg[0][:, 0:512], gup_in[l], writes=[Bwstg[0]])
                      cp("dve", gup[:], wstg[0][:, 0:512], [Bwstg[0]], [Bgup])
                      p.barrier()
                  ck(1)

                  SBT = 256
                  xt = [sb1(f"xt{i}", [128, D]) for i in range(2)]; Bxt = [Buf(), Buf()]
                  hb = sb1("hb", [128, D]); Bhb = Buf()
                  ssq = sb1("ssq", [128, 2]); Bssq = Buf()
                  hT = sb1("hT", [128, KC, SBT], BF16); BhT = Buf()
                  base = sb1("base", [128, 16, SBT]); Bbase = [Buf() for _ in range(16)]
                  WA = sb1("WA", [128, SBT]); BWA = Buf()
                  sgl = sb1("sgl", [128, SBT], BF16); Bsgl = Buf()
                  G12 = sb1("G12", [128, 8, SBT]); BG12 = [Buf() for _ in range(8)]
                  ycv = sb1("ycv", [128, 4, SBT], BF16); Bycv = [Buf() for _ in range(4)]
                  ubuf_t = sb1("ubuf", [128, 4 * (SBT + 2)]); Bub = [Buf() for _ in range(4)]
                  ubuf = ubuf_t[:].rearrange("p (j n) -> p j n", j=4)
                  cbb_t = sb1("cbb", [128, 4 * (SBT + 1)]); Bcbb = [Buf() for _ in range(4)]
                  cbb = cbb_t[:].rearrange("p (j n) -> p j n", j=4)
                  tmpA = [sb1(f"tmpA{i}", [128, SBT]) for i in range(4)]; BtA = [Buf() for _ in range(4)]
                  KR = sb1("KR", [128, 4, 256], SCAN_DT); BKR = [Buf() for _ in range(4)]
                  BtF = sb1("BtF", [128, 4, 128], SCAN_DT); BBt = [Buf() for _ in range(4)]
                  KtF = sb1("KtF", [128, 4, 128], SCAN_DT); BKt = [Buf() for _ in range(4)]
                  BpF = sb1("BpF", [128, 4, 128]); BBp = [Buf() for _ in range(4)]
                  KpF = sb1("KpF", [128, 4, 128]); BKp = [Buf() for _ in range(4)]
                  KaF32 = sb1("KaF32", [128, 4, 128]); BKa32 = [Buf() for _ in range(4)]
                  DG = sb1("DG", [128, 4, 64], SCAN_DT); BDG = [Buf() for _ in range(4)]
                  EG = sb1("EG", [128, 3, 128]); BEG = Buf()
                  aFt = sb1("aFt", [128, 128]); BaF = Buf()
                  tB = [sb1(f"tB{i}", [128, 128]) for i in range(3)]; BtB = [Buf() for _ in range(3)]
                  sigT = sb1("sigT", [128, 512]); BsigT = Buf()
                  TM = {n: sb1("TM_" + n, [128, 512], SCAN_DT) for n in ("Ka", "Bp", "Kp", "V")}
                  BTM = {n: Buf() for n in TM}
                  LT = [[sb1(f"LT{u}{i}", [128, 256], SCAN_DT) for i in range(2)] for u in range(2)]
                  BLT = [[[Buf(), Buf()] for i in range(2)] for u in range(2)]
                  Nn = [[sb1(f"Nn{u}{i}", [128, 128], SCAN_DT) for i in range(2)] for u in range(2)]
                  BNn = [[Buf() for i in range(2)] for u in range(2)]
                  Gm = [sb1(f"Gm{u}", [128, 384], SCAN_DT) for u in range(2)]; BGm = [Buf(), Buf()]
                  TtF = [sb1(f"TtF{u}", [128, 128], SCAN_DT) for u in range(2)]; BTt = [Buf(), Buf()]
                  AkV = [sb1(f"AkV{u}", [128, 64], SCAN_DT) for u in range(2)]; BAkV = [Buf(), Buf()]
                  XnS = [sb1(f"XnS{u}", [128, 128], SCAN_DT) for u in range(2)]; BXn = [Buf(), Buf()]
                  PTs = [sb1(f"PTs{u}", [64, 64], SCAN_DT) for u in range(2)]; BPTs = [Buf(), Buf()]
                  RhT = [sb1(f"RhT{u}", [64, 128], SCAN_DT) for u in range(2)]; BRh = [Buf(), Buf()]
                  ysum = ubuf_t[:, 0:512]; Bys = Buf()
                  yn = ubuf_t[:, 512:1024]; Byn = Buf()
                  y1t = cbb_t[:, 0:512]; By1 = Buf()
                  gst = sb1("gst", [128, 8, 4]); Bgst = Buf()
                  catT = hT[:, :, 0:128]; Bcat = Buf()
                  xo, Bxo = hb, Bhb

                  def vcol(i, j):
                      return vec[:, 4 * i + j:4 * i + j + 1]

                  def norm_tile(src_ap, xt_i, col0):
                      X, BX = xt[xt_i], Bxt[xt_i]
                      p.dma(X[:], src_ap, writes=[BX])
                      act(hb[:], X[:], AF.Square, [BX], [Bhb, Bssq], accum=ssq[:, 0:1])
                      ts("dve", ssq[:, 1:2], ssq[:, 0:1], 1.0 / D, RMS_EPS, ALU.mult, ALU.add, [Bssq], [Bssq])
                      act(ssq[:, 1:2], ssq[:, 1:2], AF.Ln, [Bssq], [Bssq])
                      act(ssq[:, 1:2], ssq[:, 1:2], AF.Exp, [Bssq], [Bssq], scale=-0.5)
                      stt("dve", hb[:], X[:], ssq[:, 1:2], bc["gs"][:], ALU.mult, ALU.mult,
                          [BX, Bssq, Bbc["gs"]], [Bhb])
                      tt("pool", hb[:], hb[:], bc["sh"][:], ALU.add, [Bhb, Bbc["sh"]], [Bhb])
                      for half in range(2):
                          P_, BP_ = pa()
                          for q in range(4):
                              k = half * 4 + q
                              tr(P_[:, q * 128:(q + 1) * 128], hb[:, k * 128:(k + 1) * 128], [Bhb], [BP_])
                          cp("act", hT[:, half * 4:half * 4 + 4, col0:col0 + 128],
                             P_[:].rearrange("p (q n) -> p q n", q=4), [BP_], [BhT])
                      return X, BX

                  def proj_chunk(c, n):
                      P_, BP_ = pa()
                      for k in range(KC):
                          mm(P_[:, 0:n], win[:, k, c * 128:(c + 1) * 128], hT[:, k, 0:n], k == 0, k == KC - 1,
                             [Bwin, BhT], [BP_])
                      return P_, BP_

                  def stage1(seg_src, t0, n, first, seg_T):
                      for i in range(n // 128):
                          norm_tile(seg_src[t0 + i * 128:t0 + (i + 1) * 128, :], i % 2, i * 128)
                      for j in range(4):
                          if first:
                              p.op("pool", lambda e, j=j: e.memset(ubuf[:, j, 0:2], 0.0), writes=[Bub[j]])
                              p.op("pool", lambda e, j=j: e.memset(cbb[:, j, 0:1], 0.0), writes=[Bcbb[j]])
                          Pb, BPb = proj_chunk(j, n)
                          cp("act", cbb[:, j, 1:n + 1], Pb[:, 0:n], [BPb], [Bcbb[j]])
                          Pc, BPc = proj_chunk(4 + j, n)
                          cp("act", tmpA[0][:, 0:n], Pc[:, 0:n], [BPc], [BtA[0]])
                          Px, BPx = proj_chunk(8 + j, n)
                          tt("dve", ubuf[:, j, 2:n + 2], tmpA[0][:, 0:n], Px[:, 0:n], ALU.mult, [BtA[0], BPx], [Bub[j]])
                          ts("dve", tmpA[1][:, 0:n], ubuf[:, j, 0:n], vcol(5, j), None, ALU.mult, None,
                             [Bub[j], Bvec], [BtA[1]])
                          stt("dve", tmpA[1][:, 0:n], ubuf[:, j, 1:n + 1], vcol(6, j), tmpA[1][:, 0:n], ALU.mult, ALU.add,
                              [Bub[j], Bvec, BtA[1]], [BtA[1]])
                          stt("dve", tmpA[1][:, 0:n], ubuf[:, j, 2:n + 2], vcol(7, j), tmpA[1][:, 0:n], ALU.mult, ALU.add,
                              [Bub[j], Bvec, BtA[1]], [BtA[1]])
                          tt("pool", ycv[:, j, 0:n], tmpA[1][:, 0:n], cbb[:, j, 0:n], ALU.mult, [BtA[1], Bcbb[j]], [Bycv[j]])
                          p.dma(sp_ycv[j][:, t0:t0 + n], ycv[:, j, 0:n], reads=[Bycv[j]])
                          if t0 + n == seg_T:
                              ts("dve", tmpA[1][:, 0:1], ubuf[:, j, n:n + 1], vcol(5, j), None, ALU.mult, None,
                                 [Bub[j], Bvec], [BtA[1]])
                              stt("dve", tmpA[1][:, 0:1], ubuf[:, j, n + 1:n + 2], vcol(6, j), tmpA[1][:, 0:1],
                                  ALU.mult, ALU.add, [Bub[j], Bvec, BtA[1]], [BtA[1]])
                              tt("pool", ycv[:, j, 0:1], tmpA[1][:, 0:1], cbb[:, j, n:n + 1], ALU.mult,
                                 [BtA[1], Bcbb[j]], [Bycv[j]])
                              p.dma(sp_ycv[j][:, t0 + n:t0 + n + 1], ycv[:, j, 0:1], reads=[Bycv[j]], slow=True)
                          else:
                              cp("pool", ubuf[:, j, 0:2], ubuf[:, j, n:n + 2], [Bub[j]], [Bub[j]])
                              cp("pool", cbb[:, j, 0:1], cbb[:, j, n:n + 1], [Bcbb[j]], [Bcbb[j]])
                      Pw, BPw = proj_chunk(24, n)
                      act(WA[0:64, 0:n], Pw[0:64, 0:n], AF.Tanh, [BPw], [BWA])
                      cp("act", WA[64:128, 0:n], Pw[64:128, 0:n], [BPw], [BWA])
                      p.dma(sp_wa[:, t0:t0 + n], WA[:, 0:n], reads=[BWA])
                      Pg, BPg = proj_chunk(25, n)
                      act(sgl[:, 0:n], Pg[:, 0:n], AF.Sigmoid, [BPg], [Bsgl])
                      for j in range(4):
                          rj, kpj, kj, vj = base[:, j, 0:n], base[:, 4 + j, 0:n], base[:, 8 + j, 0:n], base[:, 12 + j, 0:n]
                          Pr, BPr = proj_chunk(12 + j, n)
                          cp("act", rj, Pr[:, 0:n], [BPr], [Bbase[j]])
                          Pk, BPk = proj_chunk(16 + j, n)
                          cp("act", kj, Pk[:, 0:n], [BPk], [Bbase[8 + j]])
                          Pv, BPv = proj_chunk(20 + j, n)
                          cp("dve", vj, Pv[:, 0:n], [BPv], [Bbase[12 + j]])
                          act(tmpA[2][:, 0:n], kj, AF.Square, [Bbase[8 + j], Bvec], [BtA[2]], scale=vcol(0, j))
                          Pq, BPq = pa()
                          mm(Pq[:, 0:n], cst[:, C_BO:C_BO + 128], tmpA[2][:, 0:n], True, True, [Bcst, BtA[2]], [BPq])
                          ts("dve", tmpA[2][:, 0:n], Pq[:, 0:n], 1e-24, None, ALU.max, None, [BPq], [BtA[2]])
                          act(tmpA[2][:, 0:n], tmpA[2][:, 0:n], AF.Ln, [BtA[2]], [BtA[2]])
                          act(tmpA[2][:, 0:n], tmpA[2][:, 0:n], AF.Exp, [BtA[2]], [BtA[2]], scale=-0.5)
                          stt("dve", kpj, kj, vcol(0, j), tmpA[2][:, 0:n], ALU.mult, ALU.mult,
                              [Bbase[8 + j], Bvec, BtA[2]], [Bbase[4 + j]])
                          stt("dve", tmpA[3][:, 0:n], rj, vcol(2, j), kj, ALU.mult, ALU.mult,
                              [Bbase[j], Bvec, Bbase[8 + j]], [BtA[3]])
                          Pq2, BPq2 = pa()
                          mm(Pq2[:, 0:n], cst[:, C_BO:C_BO + 128], tmpA[3][:, 0:n], True, True, [Bcst, BtA[3]], [BPq2])
                          tt("dve", tmpA[3][:, 0:n], Pq2[:, 0:n], vj, ALU.mult, [BPq2, Bbase[12 + j]], [BtA[3]])
                          Pq3, BPq3 = pa()
                          mm(Pq3[:, 0:n], gup[:, j * 128:(j + 1) * 128], sgl[:, 0:n], True, True, [Bgup, Bsgl], [BPq3])
                          ts("dve", G12[:, j, 0:n], Pq3[:, 0:n], vcol(3, j), None, ALU.mult, None, [BPq3, Bvec], [BG12[j]])
                          stt("dve", G12[:, 4 + j, 0:n], tmpA[3][:, 0:n], vcol(4, j), Pq3[:, 0:n], ALU.add, ALU.mult,
                              [BtA[3], Bvec, BPq3], [BG12[4 + j]])
                          for q, Bq in ((j, Bbase[j]), (4 + j, Bbase[4 + j]), (8 + j, Bbase[8 + j]), (12 + j, Bbase[12 + j])):
                              p.dma(sp_base[q][:, t0:t0 + n], base[:, q, 0:n], reads=[Bq])
                          p.dma(sp_g[j][:, t0:t0 + n], G12[:, j, 0:n], reads=[BG12[j]])
                          p.dma(sp_g[4 + j][:, t0:t0 + n], G12[:, 4 + j, 0:n], reads=[BG12[4 + j]])

                  def load_stage(t0, n):
                      for q in range(16):
                          p.dma(base[:, q, 0:n], sp_base[q][:, t0:t0 + n], writes=[Bbase[q]])
                      p.dma(WA[:, 0:n], sp_wa[:, t0:t0 + n], writes=[BWA])
                      for q in range(8):
                          p.dma(G12[:, q, 0:n], sp_g[q][:, t0:t0 + n], writes=[BG12[q]])
                      for j in range(4):
                          p.dma(ycv[:, j, 0:n], sp_ycv[j][:, t0 + 1:t0 + n + 1], writes=[Bycv[j]])

                  def scan_tile(c0, d, Sb, BSb):
                      mG = cst[:, C_MG + 512 * d:C_MG + 512 * (d + 1)]
                      mL = cst[:, C_ML + 128 * d:C_ML + 128 * (d + 1)]
                      tF = cst[:, C_TF + 256 * d:C_TF + 256 * (d + 1)]
                      gcol = 127 if d == 0 else 0
                      cs_ = slice(c0, c0 + 128)
                      P_, BP_ = pa()
                      mm(P_[:], WA[0:64, cs_], lup[0:64, d, :], True, False, [BWA, Blup], [BP_])
                      mm(P_[:], cst[0:1, C_ON:C_ON + 128], rowv[0:1, d * 512:(d + 1) * 512], False, True,
                         [Bcst, Browv], [BP_])
                      act(sigT[:], P_[:], AF.Sigmoid, [BP_], [BsigT])
                      for j in range(4):
                          Pc, BPc = pa()
                          mm(Pc[:, 0:256], sigT[:, j * 128:(j + 1) * 128], tF, True, True, [BsigT, Bcst], [BPc])
                          act(EG[:, 0, :], Pc[:, 0:128], AF.Exp, [BPc], [BEG])
                          act(EG[:, 1, :], Pc[:, 0:128], AF.Exp, [BPc], [BEG], scale=-1.0)
                          act(EG[:, 2, :], Pc[:, 128:256], AF.Exp, [BPc], [BEG])
                          Pa, BPa_ = pa()
                          mm(Pa[:, 0:128], lup[64:128, d, j * 128:(j + 1) * 128], WA[64:128, cs_], True, True,
                             [Blup, BWA], [BPa_])
                          act(aFt[:], Pa[:, 0:128], AF.Sigmoid, [BPa_, Bvec], [BaF], bias=vcol(10 + d, j))
                          rj, kpj, kj = base[:, j, cs_], base[:, 4 + j, cs_], base[:, 8 + j, cs_]
                          tt("pool", KR[:, j, 128:256], rj, EG[:, 0, :], ALU.mult, [Bbase[j], BEG], [BKR[j]])
                          tt("dve", KaF32[:, j, :], kpj, EG[:, 2, :], ALU.mult, [Bbase[4 + j], BEG], [BKa32[j]])
                          cp("pool", KR[:, j, 0:128], KaF32[:, j, :], [BKa32[j]], [BKR[j]])
                          tt("pool", tB[0][:], kpj, aFt[:], ALU.mult, [Bbase[4 + j], BaF], [BtB[0]])
                          tt("dve", tB[1][:], tB[0][:], EG[:, 1, :], ALU.mult, [BtB[0], BEG], [BtB[1]])
                          cp("pool", BtF[:, j, :], tB[1][:], [BtB[1]], [BBt[j]])
                          ts("dve", BpF[:, j, :], tB[1][:], EG[:, 0, gcol:gcol + 1], None, ALU.mult, None,
                             [BtB[1], BEG], [BBp[j]])
                          ts("dve", tB[0][:], aFt[:], vcol(1, j), omka[:, j:j + 1], ALU.mult, ALU.add,
                             [BaF, Bvec, Bomka], [BtB[0]])
                          tt("pool", tB[0][:], tB[0][:], kj, ALU.mult, [BtB[0], Bbase[8 + j]], [BtB[0]])
                          tt("dve", tB[2][:], tB[0][:], EG[:, 1, :], ALU.mult, [BtB[0], BEG], [BtB[2]])
                          cp("pool", KtF[:, j, :], tB[2][:], [BtB[2]], [BKt[j]])
                          ts("dve", KpF[:, j, :], tB[2][:], EG[:, 0, gcol:gcol + 1], None, ALU.mult, None,
                             [BtB[2], BEG], [BKp[j]])
                          ts("dve", DG[:, j, :], cst[:, C_IB:C_IB + 64], EG[:, 0, gcol:gcol + 1], None, ALU.mult, None,
                             [Bcst, BEG], [BDG[j]])
                      ck(30)
                      for name, srcf, Bsrc in (("Ka", lambda j: KaF32[:, j, :], BKa32), ("Bp", lambda j: BpF[:, j, :], BBp),
                                               ("Kp", lambda j: KpF[:, j, :], BKp),
                                               ("V", lambda j: base[:, 12 + j, cs_], Bbase[12:16])):
                          P_, BP_ = pa()
                          for j in range(4):
                              tr(P_[:, j * 128:(j + 1) * 128], srcf(j), [Bsrc[j]], [BP_])
                          cp("act", TM[name][:], P_[:], [BP_], [BTM[name]])

                      ck(31)
                      def unit_stages(h, u):
                          j, p0 = h // 2, 64 * (h % 2)
                          hs = slice(h * 64, (h + 1) * 64)
                          Bt_h, Kt_h, KR_h = BtF[p0:p0 + 64, j, :], KtF[p0:p0 + 64, j, :], KR[p0:p0 + 64, j, :]
                          Ka_h = KR[p0:p0 + 64, j, 0:128]
                          pn, Bpn = PN[u], BPN[u]
                          stages = []

                          def s_gram():
                              mm(PG[:, 0:256], Bt_h, KR_h, True, True, [BBt[j], BKR[j]], [BPG[0]])
                              mm(PG[:, 256:512], Kt_h, KR_h, True, True, [BKt[j], BKR[j]], [BPG[1]])
                              mm(pn[:, 384:512], Ka_h, Bt_h, True, True, [BKR[j], BBt[j]], [Bpn[2]])
                              tt("dve", LT[u][0][:, 0:128], PG[:, 0:128], mG[:, 0:128], ALU.mult, [BPG[0], Bcst],
                                 [BLT[u][0][0]])
                              tt("dve", Gm[u][:, 0:128], PG[:, 128:256], mG[:, 128:256], ALU.mult, [BPG[0], Bcst], [BGm[u]])
                              tt("dve", Gm[u][:, 128:384], PG[:, 256:512], mG[:, 256:512], ALU.mult, [BPG[1], Bcst], [BGm[u]])
                              tt("dve", Nn[u][0][:], pn[:, 384:512], mL, ALU.mult, [Bpn[2], Bcst], [BNn[u][0]])
                              cp("pool", LT[u][0][:, 128:256], identS[:], [BidR], [BLT[u][0][1]])
                          stages.append(s_gram)

                          ev = "act" if u == 0 else "dve"

                          def mk_level(k):
                              a, b = k % 2, (k + 1) % 2
                              def s():
                                  mm(pn[:, 0:128], Nn[u][a][:], LT[u][a][:, 0:128], True, True,
                                     [BNn[u][a], BLT[u][a][0]], [Bpn[0]])
                                  mm(pn[:, 128:256], Nn[u][a][:], LT[u][a][:, 128:256], True, False,
                                     [BNn[u][a], BLT[u][a][1]], [Bpn[0]])
                                  mm(pn[:, 128:256], identS[:], LT[u][a][:, 128:256], False, True,
                                     [BidR, BLT[u][a][1]], [Bpn[0]])
                                  mm(pn[:, 256:384], LT[u][a][:, 0:128], Nn[u][a][:], True, True,
                                     [BLT[u][a][0], BNn[u][a]], [Bpn[1]])
                                  cp(ev, LT[u][b][:], pn[:, 0:256], [Bpn[0]], [BLT[u][b][0], BLT[u][b][1]])
                                  cp(ev, Nn[u][b][:], pn[:, 256:384], [Bpn[1]], [BNn[u][b]])
                              return s
                          for k in range(0, 6):
                              stages.append(mk_level(k))

                          def s_l6():
                              mm(pn[:, 128:256], Nn[u][0][:], LT[u][0][:, 128:256], True, False,
                                 [BNn[u][0], BLT[u][0][1]], [Bpn[0]])
                              mm(pn[:, 128:256], identS[:], LT[u][0][:, 128:256], False, True,
                                 [BidR, BLT[u][0][1]], [Bpn[0]])
                              cp(ev, TtF[u][:], pn[:, 128:256], [Bpn[0]], [BTt[u]])
                              mm(PM[:, 0:64], Gm[u][:, 128:256], TM["V"][:, hs], True, True, [BGm[u], BTM["V"]], [BPM[0]])
                              cp(ev, AkV[u][:], PM[:, 0:64], [BPM[0]], [BAkV[u]])
                          stages.append(s_l6)

                          def s_x():
                              mm(PM[:, 64:128], TtF[u][:], TM["Ka"][:, hs], True, True, [BTt[u], BTM["Ka"]], [BPM[1]])
                              mm(PM[:, 128:192], TtF[u][:], AkV[u][:], True, True, [BTt[u], BAkV[u]], [BPM[1]])
                              if ev == "act":
                                  act(XnS[u][:], PM[:, 64:192], AF.Copy, [BPM[1]], [BXn[u]], scale=-1.0)
                              else:
                                  ts("dve", XnS[u][:], PM[:, 64:192], -1.0, None, ALU.mult, None, [BPM[1]], [BXn[u]])
                          stages.append(s_x)

                          def s_fin():
                              mm(PM[0:64, 256:384], identR[p0:p0 + 64, p0:p0 + 64], KR[p0:p0 + 64, j, 128:256], True, False,
                                 [BidR, BKR[j]], [BPM[3]])
                              mm(PM[0:64, 256:384], XnS[u][:, 0:64], Gm[u][:, 0:128], False, True, [BXn[u], BGm[u]], [BPM[3]])
                              cp(ev, RhT[u][:], PM[0:64, 256:384], [BPM[3]], [BRh[u]])
                              mm(PM[0:64, 192:256], XnS[u][:, 0:64], TM["Bp"][:, hs], True, False, [BXn[u], BTM["Bp"]], [BPM[2]])
                              mm(PM[0:64, 192:256], identR[p0:p0 + 64, p0:p0 + 64], DG[p0:p0 + 64, j, :], False, True,
                                 [BidR, BDG[j]], [BPM[2]])
                              cp(ev, PTs[u][:], PM[0:64, 192:256], [BPM[2]], [BPTs[u]])
                              mm(PY[:, hs], Gm[u][:, 256:384], TM["V"][:, hs], True, False, [BGm[u], BTM["V"]], [BPY])
                              mm(PY[:, hs], Gm[u][:, 0:128], XnS[u][:, 64:128], False, False, [BGm[u], BXn[u]], [BPY])
                              mm(PY[:, hs], RhT[u][:], Sb[:, h, :], False, True, [BRh[u], BSb], [BPY])
                              mm(PS_[0:64, hs], TM["Kp"][:, hs], TM["V"][:, hs], True, False, [BTM["Kp"], BTM["V"]], [BPS])
                              mm(PS_[0:64, hs], TM["Bp"][:, hs], XnS[u][:, 64:128], False, False, [BTM["Bp"], BXn[u]], [BPS])
                              mm(PS_[0:64, hs], PTs[u][:], Sb[:, h, :], False, True, [BPTs[u], BSb], [BPS])
                          stages.append(s_fin)
                          return stages

                      for hp in range(4):
                          sa, sb_ = unit_stages(2 * hp, 0), unit_stages(2 * hp + 1, 1)
                          for idx_, (a_, b_) in enumerate(zip(sa, sb_)):
                              a_()
                              b_()
                      cp("act", Sb[:].rearrange("p h n -> p (h n)"), PS_[0:64, :], [BPS], [BSb])

                  def post_tile(src, dst, t0, c0, X, BX):
                      cs_ = slice(c0, c0 + 128)
                      tt("dve", ysum, PY[:], y1t, ALU.add, [BPY, By1], [Bys])
                      y3 = ysum.rearrange("p (h n) -> p h n", h=8)
                      p.op("dve", lambda e: e.tensor_reduce(out=gst[:, :, 0], in_=y3, axis=AX.X, op=ALU.add),
                           reads=[Bys], writes=[Bgst])
                      tt("pool", yn, ysum, ysum, ALU.mult, [Bys], [Byn])
                      p.op("dve", lambda e: e.tensor_reduce(out=gst[:, :, 1], in_=yn.rearrange("p (h n) -> p h n", h=8),
                                                            axis=AX.X, op=ALU.add), reads=[Byn], writes=[Bgst])
                      ts("dve", gst[:, :, 0], gst[:, :, 0], 1.0 / 64, None, ALU.mult, None, [Bgst], [Bgst])
                      tt("dve", gst[:, :, 2], gst[:, :, 0], gst[:, :, 0], ALU.mult, [Bgst], [Bgst])
                      stt("dve", gst[:, :, 3], gst[:, :, 1], 1.0 / 64, gst[:, :, 2], ALU.mult, ALU.subtract, [Bgst], [Bgst])
                      ts("dve", gst[:, :, 3], gst[:, :, 3], GN_EPS, None, ALU.add, None, [Bgst], [Bgst])
                      act(gst[:, :, 3], gst[:, :, 3], AF.Ln, [Bgst], [Bgst])
                      act(gst[:, :, 3], gst[:, :, 3], AF.Exp, [Bgst], [Bgst], scale=-0.5)
                      for h in range(8):
                          ts("dve", yn[:, h * 64:(h + 1) * 64], ysum[:, h * 64:(h + 1) * 64],
                             gst[:, h, 0:1], gst[:, h, 3:4], ALU.subtract, ALU.mult, [Bys, Bgst], [Byn])
                      P_, BP_ = pa()
                      for j in range(4):
                          tr(P_[:, j * 128:(j + 1) * 128], yn[:, j * 128:(j + 1) * 128], [Byn], [BP_])
                      for j in range(4):
                          tt("dve", tB[0][:], P_[:, j * 128:(j + 1) * 128], G12[:, j, cs_], ALU.mult, [BP_, BG12[j]], [BtB[0]])
                          tt("pool", catT[:, 4 + j, :], tB[0][:], G12[:, 4 + j, cs_], ALU.add, [BtB[0], BG12[4 + j]], [Bcat])
                          cp("pool", catT[:, j, :], ycv[:, j, cs_], [Bycv[j]], [Bcat])
                      for half in range(2):
                          Po, BPo = pa()
                          for k in range(KC):
                              mm(Po[:], catT[:, k, :], wout[:, k, half * 512:(half + 1) * 512], k == 0, k == KC - 1,
                                 [Bcat, Bwout], [BPo])
                          tt("dve", xo[:, half * 512:(half + 1) * 512], Po[:], bc["gt"][:, half * 512:(half + 1) * 512],
                             ALU.mult, [BPo, Bbc["gt"]], [Bxo])
                      tt("pool", xo[:], xo[:], X[:], ALU.add, [Bxo, BX], [Bxo])
                      p.dma(dst[t0:t0 + 128, :], xo[:], reads=[Bxo])

                  def mixer_segment(src, dst, seg_T, setid, Sin, need_out, Sfin):
                      load_bc(setid, 0)
                      nsb = (seg_T + SBT - 1) // SBT
                      for d in range(2):
                          if Sin is None:
                              ts("dve", Sst[d][:].rearrange("p h n -> p (h n)"), cst[0:64, C_MG:C_MG + 512], 0.0, None,
                                 ALU.mult, None, [Bcst], [BS[d]])
                          else:
                              cp("pool", Sst[d][:], Sin[d][0][:], [Sin[d][1]], [BS[d]])
                      for s_ in range(nsb):
                          t0 = s_ * SBT
                          n = min(SBT, seg_T - t0)
                          stage1(src, t0, n, s_ == 0, seg_T)
                          ck(2)
                          for i in range(n // 128):
                              scan_tile(i * 128, 0, Sst[0], BS[0])
                              ck(3)
                              if need_out:
                                  cp("dve", sigT[:], PY[:], [BPY], [BsigT])
                                  p.dma(sp_y1[t0 + i * 128:t0 + (i + 1) * 128, :], sigT[:], reads=[BsigT])
                      p.barrier()
                      ck(4)
                      for s_ in reversed(range(nsb)):
                          t0 = s_ * SBT
                          n = min(SBT, seg_T - t0)
                          load_stage(t0, n)
                          for i in reversed(range(n // 128)):
                              scan_tile(i * 128, 1, Sst[1], BS[1])
                              if need_out:
                                  tk = t0 + i * 128
                                  p.dma(y1t, sp_y1[tk:tk + 128, :], writes=[By1])
                                  X, BX = xt[i % 2], Bxt[i % 2]
                                  p.dma(X[:], src[tk:tk + 128, :], writes=[BX])
                                  post_tile(src, dst, tk, i * 128, X, BX)
                      if Sfin is not None:
                          for d in range(2):
                              cp("pool", Sfin[d][0][:], Sst[d][:], [BS[d]], [Sfin[d][1]])
                      p.barrier()

                  mixer_segment(c_src, c_mid, CT, 1, None, not last, [(Sc[0], BSc[0]), (Sc[1], BSc[1])])
                  ck(5)
                  mixer_segment(x_src, x_mid, T, 0, [(Sc[0], BSc[0]), (Sc[1], BSc[1])], True, None)
                  p.barrier()
                  ck(6)

              with ExitStack() as s3:
                  def sb3(name, shape, dt=F32):
                      return sb(name, shape, dt, stack=s3)
                  wup = sb3("wup", [128, KC, 2 * DFF], BF16); Bwup = Buf()
                  wdn = sb3("wdn", [128, FC, D], BF16); Bwdn = Buf()
                  with ExitStack() as sw3:
                      wst3 = [sb(f"wst3{i}", [128, 2048], stack=sw3) for i in range(2)]; Bw3 = [Buf(), Buf()]
                      ii = 0
                      for k in range(KC):
                          for c_ in range(0, 2 * DFF, 2048):
                              w_ = min(2048, 2 * DFF - c_)
                              sg, Bsg = wst3[ii % 2], Bw3[ii % 2]
                              p.dma(sg[:, 0:w_], w_up[l][k * 128:(k + 1) * 128, c_:c_ + w_], writes=[Bsg])
                              cp("dve" if ii % 2 == 0 else "pool", wup[:, k, c_:c_ + w_], sg[:, 0:w_], [Bsg], [Bwup])
                              ii += 1
                      for c_ in range(FC):
                          sg, Bsg = wst3[ii % 2], Bw3[ii % 2]
                          p.dma(sg[:, 0:D], w_dn[l][c_ * 128:(c_ + 1) * 128, :], writes=[Bsg])
                          cp("dve" if ii % 2 == 0 else "pool", wdn[:, c_, :], sg[:, 0:D], [Bsg], [Bwdn])
                          ii += 1
                      p.barrier()
                  xt3 = [sb3(f"x3{i}", [128, D]) for i in range(2)]; Bx3 = [Buf() for _ in range(2)]
                  xr, Bxr = xt3[1], Bx3[1]
                  hb3 = sb3("hb3", [128, D]); Bhb3 = Buf()
                  ssq3 = sb3("ssq3", [128, 2]); Bssq3 = Buf()
                  hT3 = sb3("hT3", [128, KC, 640], BF16); BhT3 = Buf()
                  accs = [sb3(f"acc{i}", [128, 512]) for i in range(2)]; Baccs = [Buf(), Buf()]
                  actT = sb3("actT", [128, FC, 512], BF16); Bact = [Buf() for _ in range(FC)]
                  xo3, Bxo3 = hb3, Bhb3
                  fgb = sb3("fgb", [128, D]); Bfgb = Buf()
                  if last and final_norm:
                      p.dma(fgb[:], fin_g.partition_broadcast(128), writes=[Bfgb])

                  def ffn_segment(src, dst, seg_T, setid, gw, is_out):
                      load_bc(setid, 1)
                      BT = 512 if seg_T >= 512 else seg_T
                      nrow_tot = seg_T // gw
                      nrow = BT // gw
                      for b0 in range(0, seg_T, BT):
                          halo = gw if gw < seg_T else 0
                          lo, hi = max(0, b0 - halo), min(seg_T, b0 + BT + halo)
                          nw = hi - lo
                          off = b0 - lo
                          tpos, xi = lo, 0
                          while tpos < hi:
                              w_ = min(128, hi - tpos)
                              X, BX = xt3[xi % 2], Bx3[xi % 2]
                              if w_ < 128:
                                  p.op("pool", lambda e, X=X: e.memset(X[:], 0.0), writes=[BX])
                              p.dma(X[0:w_, :], src[tpos:tpos + w_, :], writes=[BX])
                              act(hb3[:], X[:], AF.Square, [BX], [Bhb3, Bssq3], accum=ssq3[:, 0:1])
                              ts("dve", ssq3[:, 1:2], ssq3[:, 0:1], 1.0 / D, RMS_EPS, ALU.mult, ALU.add, [Bssq3], [Bssq3])
                              act(ssq3[:, 1:2], ssq3[:, 1:2], AF.Ln, [Bssq3], [Bssq3])
                              act(ssq3[:, 1:2], ssq3[:, 1:2], AF.Exp, [Bssq3], [Bssq3], scale=-0.5)
                              stt("dve", hb3[:], X[:], ssq3[:, 1:2], bc["gs"][:], ALU.mult, ALU.mult,
                                  [BX, Bssq3, Bbc["gs"]], [Bhb3])
                              tt("pool", hb3[:], hb3[:], bc["sh"][:], ALU.add, [Bhb3, Bbc["sh"]], [Bhb3])
                              col0 = tpos - lo
                              for half in range(2):
                                  P_, BP_ = pa()
                                  for q in range(4):
                                      k = half * 4 + q
                                      tr(P_[:, q * 128:(q + 1) * 128], hb3[:, k * 128:(k + 1) * 128], [Bhb3], [BP_])
                                  cp("act", hT3[:, half * 4:half * 4 + 4, col0:col0 + w_],
                                     P_[:].rearrange("p (q n) -> p q n", q=4)[:, :, 0:w_], [BP_], [BhT3])
                              tpos += w_
                              xi += 1
                          wr0 = off // gw
                          for c_ in range(FC):
                              acc, Bacc = accs[c_ % 2], Baccs[c_ % 2]
                              sil, Bsil = acc, Bacc
                              if c_ % 2 == 0:
                                  GA, BGA, GB, BGB = PG, BPG[0], PN[0], BPN[0][0]
                              else:
                                  GA, BGA, GB, BGB = PY, BPY, PS_, BPS
                              segs = [(0, min(nw, 512), GA, BGA)]
                              if nw > 512:
                                  segs.append((512, nw, GB, BGB))
                              for (a_, b_, P_, BP_) in segs:
                                  for k in range(KC):
                                      mm(P_[:, 0:b_ - a_], wup[:, k, c_ * 128:(c_ + 1) * 128], hT3[:, k, a_:b_],
                                         k == 0, k == KC - 1, [Bwup, BhT3], [BP_])
                              wc = lambda ty, tx: vec[:, 48 + (ty * 3 + tx) * FC + c_:48 + (ty * 3 + tx) * FC + c_ + 1]
                              bcol = vec[:, 48 + 9 * FC + c_:48 + 9 * FC + c_ + 1]

                              def tap(ty, tx, first):
                                  dy, dx = ty - 1, tx - 1
                                  r0g = b0 // gw
                                  rows_ok = [r for r in range(nrow) if 0 <= r0g + r + dy < nrow_tot]
                                  if not rows_ok:
                                      return
                                  ra, rb = rows_ok[0], rows_ok[-1] + 1
                                  ca, cb2 = (1, gw) if dx == -1 else ((0, gw - 1) if dx == 1 else (0, gw))
                                  rsplit = 512 // gw
                                  r = ra
                                  while r < rb:
                                      wr = wr0 + r + dy
                                      if wr < rsplit:
                                          re_ = min(rb, rsplit - wr0 - dy)
                                          P_, B_, wbase = GA, BGA, 0
                                      else:
                                          re_ = rb
                                          P_, B_, wbase = GB, BGB, rsplit
                                      nr = re_ - r
                                      iv = P_[:, (wr - wbase) * gw:(wr - wbase + nr) * gw].rearrange("p (r c) -> p r c", c=gw)[:, :, ca + dx:cb2 + dx]
                                      ov = acc[:, r * gw:(r + nr) * gw].rearrange("p (r c) -> p r c", c=gw)[:, :, ca:cb2]
                                      if first:
                                          act(ov, iv, AF.Identity, [B_, Bvec], [Bacc], bias=bcol, scale=wc(ty, tx))
                                      else:
                                          stt("dve", ov, iv, wc(ty, tx), ov, ALU.mult, ALU.add, [B_, Bvec, Bacc], [Bacc])
                                      r = re_
                              tap(1, 1, True)
                              for ty in range(3):
                                  for tx in range(3):
                                      if (ty, tx) != (1, 1) and not (gw >= seg_T and ty != 1):
                                          tap(ty, tx, False)
                              act(sil[:, 0:BT], acc[:, 0:BT], AF.Silu, [], [Bacc])
                              Pv, BPv = pa()
                              for k in range(KC):
                                  mm(Pv[:, 0:BT], wup[:, k, DFF + c_ * 128:DFF + (c_ + 1) * 128], hT3[:, k, off:off + BT],
                                     k == 0, k == KC - 1, [Bwup, BhT3], [BPv])
                              tt("dve", actT[:, c_, 0:BT], sil[:, 0:BT], Pv[:, 0:BT], ALU.mult, [Bsil, BPv], [Bact[c_]])
                          for i in range(BT // 128):
                              tk = b0 + i * 128
                              p.dma(xr[:], src[tk:tk + 128, :], writes=[Bxr])
                              for half in range(2):
                                  Po, BPo = pa()
                                  for c_ in range(FC):
                                      mm(Po[:], actT[:, c_, i * 128:(i + 1) * 128], wdn[:, c_, half * 512:(half + 1) * 512],
                                         c_ == 0, c_ == FC - 1, [Bact[c_], Bwdn], [BPo])
                                  tt("dve", xo3[:, half * 512:(half + 1) * 512], Po[:], bc["gt"][:, half * 512:(half + 1) * 512],
                                     ALU.mult, [BPo, Bbc["gt"]], [Bxo3])
                              tt("pool", xo3[:], xo3[:], xr[:], ALU.add, [Bxo3, Bxr], [Bxo3])
                              if is_out and final_norm:
                                  act(xt3[0][:], xo3[:], AF.Square, [Bxo3], [Bx3[0], Bssq3], accum=ssq3[:, 0:1])
                                  ts("dve", ssq3[:, 1:2], ssq3[:, 0:1], 1.0 / D, RMS_EPS, ALU.mult, ALU.add, [Bssq3], [Bssq3])
                                  act(ssq3[:, 1:2], ssq3[:, 1:2], AF.Ln, [Bssq3], [Bssq3])
                                  act(ssq3[:, 1:2], ssq3[:, 1:2], AF.Exp, [Bssq3], [Bssq3], scale=-0.5)
                                  stt("dve", xo3[:], xo3[:], ssq3[:, 1:2], fgb[:], ALU.mult, ALU.mult, [Bxo3, Bssq3, Bfgb], [Bxo3])
                              p.dma(dst[tk:tk + 128, :], xo3[:], reads=[Bxo3])

                  if not last:
                      ffn_segment(c_mid, c_dst, CT, 1, CT, False)
                  ffn_segment(x_mid, x_dst, T, 0, GW, last)
                  p.barrier()
        except _Stop:
            pass
        p.barrier()
        nc._prog_ninst = p.ninst
    return nc


def prep_inputs(x, c, ctx, c_ctx, ada_w, ada_b, norm1_g, norm2_g, w_in, conv_a_w, rw_w0, rw_w_up, rw_a0, rw_a_up,
                rw_k_k, rw_k_a, rw_r_k, rw_g_up, rw_ln_g, rw_ln_b, w_out, ffn_w_up, ffn_conv_w, ffn_conv_b,
                ffn_w_down, final_g, b):
    L = ada_w.shape[0]
    f = lambda a: np.ascontiguousarray(a, dtype=np.float32)
    cs = np.stack([c[b].reshape(KC, 128).T, c_ctx.reshape(KC, 128).T], axis=-1)
    rows = np.concatenate([ada_b, norm1_g, norm2_g], axis=1)[:, None, :].repeat(2, axis=1)

    def ch(v, n):
        return v.reshape(n, 128).T
    vec = np.zeros((L, 128, NVEC), np.float32)
    for l in range(L):
        cols = [ch(rw_k_k[l], 4), ch(rw_k_a[l], 4), ch(rw_r_k[l].reshape(-1), 4), ch(rw_ln_g[l], 4), ch(rw_ln_b[l], 4),
                ch(conv_a_w[l, 0], 4), ch(conv_a_w[l, 1], 4), ch(conv_a_w[l, 2], 4),
                ch(rw_w0[l, 0], 4), ch(rw_w0[l, 1], 4), ch(rw_a0[l, 0], 4), ch(rw_a0[l, 1], 4)]
        for ty in range(3):
            for tx in range(3):
                cols.append(ch(ffn_conv_w[l, ty, tx], FC))
        cols.append(ch(ffn_conv_b[l], FC))
        vec[l] = np.concatenate(cols, axis=1)
    rowv = rw_w0.reshape(L, 1, 1024)
    lup = np.concatenate([rw_w_up.transpose(0, 2, 1, 3), rw_a_up.transpose(0, 2, 1, 3)], axis=1)
    return {
        "x": f(x[b]), "ctx": f(ctx[b]), "cs": f(cs), "cst": make_consts(), "ada_w": f(ada_w), "rows": f(rows),
        "fin_g": f(final_g), "w_in": f(w_in), "w_out": f(w_out), "w_up": f(ffn_w_up), "w_dn": f(ffn_w_down),
        "vec": f(vec), "rowv": f(rowv), "lup": f(lup), "gup": f(rw_g_up),
    }


def kernel(**inputs):
    inputs = {k: np.asarray(v) for k, v in inputs.items()}
    B, T, _ = inputs["x"].shape
    CT = inputs["ctx"].shape[1]
    nc = build_program(T, CT, inputs["ada_w"].shape[0])
    in_maps = [prep_inputs(b=b, **inputs) for b in range(B)]
    res = run_bass_kernel_spmd(nc, in_maps, core_ids=list(range(B)))
    return np.stack([r["out"] for r in res.results], axis=0).astype(np.float32)
```

```python
import math
import numpy as np
from contextlib import ExitStack
import concourse.bass as bass
import concourse.mybir as mybir
from concourse.bass_utils import run_bass_kernel_spmd

F32 = mybir.dt.float32
BF16 = mybir.dt.bfloat16
F32R = mybir.dt.float32r
AF = mybir.ActivationFunctionType
ALU = mybir.AluOpType
AX = mybir.AxisListType

D = 1024
KC = 8
PROJ = 3328
NPC = 26
DFF = 2816
FC = 22
GW = 64
DS = math.exp(-0.5)
RMS_EPS = 1e-6
GN_EPS = 64e-5
NVEC = 48 + 10 * FC
SCAN_DT = BF16

C_ID = 0
C_MG = 128
C_ML = C_MG + 1024
C_TF = C_ML + 256
C_BO = C_TF + 512
C_ON = C_BO + 128
C_IB = C_ON + 128
NCST = C_IB + 64


def make_consts():
    c = np.zeros((128, NCST), np.float32)
    s = np.arange(128)[:, None]
    t = np.arange(128)[None, :]
    c[:, C_ID:C_ID + 128] = np.eye(128)
    for d in range(2):
        lt = (s < t) if d == 0 else (s > t)
        le = (s <= t) if d == 0 else (s >= t)
        g = c[:, C_MG + 512 * d:C_MG + 512 * (d + 1)]
        g[:, 0:128] = -1.0 * lt
        g[:, 128:256] = le
        g[:, 256:384] = lt
        g[:, 384:512] = le
        c[:, C_ML + 128 * d:C_ML + 128 * (d + 1)] = -1.0 * lt.T
        f = c[:, C_TF + 256 * d:C_TF + 256 * (d + 1)]
        f[:, 0:128] = -DS * le
        f[:, 128:256] = -DS * lt
    c[:, C_BO:C_BO + 128] = (s // 64 == t // 64)
    c[:, C_ON:C_ON + 128] = 1.0
    c[:, C_IB:C_IB + 64] = (s % 64 == np.arange(64)[None, :])
    return c


class Buf:
    __slots__ = ("name", "w", "rd", "ps")

    def __init__(self, name="", ps=False):
        self.name = name
        self.w = None
        self.rd = []
        self.ps = ps


ENGMAP = {"pe": "tensor", "dve": "vector", "act": "scalar", "pool": "gpsimd", "sp": "sync"}


class Eng:
    def __init__(self, name, sem, h):
        self.name = name
        self.sem = sem
        self.h = h
        self.cnt = 0
        self.waited = {}


class Prog:
    def __init__(self, nc, sems, dma_sems):
        self.nc = nc
        self.E = {n: Eng(n, sems[n], getattr(nc, ENGMAP[n])) for n in ENGMAP}
        self.dma_sems = dma_sems
        self.dma_cnt = [0] * len(dma_sems)
        self.dma_rr = 0
        self.ninst = 0
        self.stop = False

    def _deps(self, reads, writes):
        d = []
        for b in reads:
            if b.w is not None:
                d.append(b.w)
        for b in writes:
            if b.w is not None:
                d.append(b.w)
            d.extend(b.rd)
        return d

    def _waits(self, e, deps, skip_self):
        need = {}
        for (sem, val, owner) in deps:
            if skip_self and owner is e:
                continue
            k = id(sem)
            if e.waited.get(k, 0) >= val:
                continue
            if k not in need or need[k][1] < val:
                need[k] = (sem, val)
        for k, (sem, val) in need.items():
            e.waited[k] = val
            e.h.wait_ge(sem, val)

    def op(self, en, fn, reads=(), writes=()):
        if self.stop:
            return
        e = self.E[en]
        deps = self._deps(reads, writes)
        for b in reads:
            if b.ps:
                deps.extend(t for t in b.rd if t[2] is not e)
        self._waits(e, deps, skip_self=(en == "pe"))
        e.cnt += 1
        tok = (e.sem, e.cnt, e)
        fn(e.h).then_inc(e.sem, 1)
        for b in writes:
            b.w = tok
            b.rd = []
        for b in reads:
            b.rd.append(tok)
        self.ninst += 1

    def dma(self, out, in_, reads=(), writes=(), q="sp", slow=False):
        if self.stop:
            return
        e = self.E[q]
        deps = self._deps(reads, writes)
        i = self.dma_rr
        self.dma_rr = (self.dma_rr + 1) % len(self.dma_sems)
        sem = self.dma_sems[i]
        if self.dma_cnt[i] > 0:
            deps.append((sem, self.dma_cnt[i], None))
        self._waits(e, deps, skip_self=False)
        self.dma_cnt[i] += 16
        tok = (sem, self.dma_cnt[i], None)
        if slow:
            e.h.dma_start(out=out, in_=in_, allow_slow_non_contiguous=True).then_inc(sem, 16)
        else:
            e.h.dma_start(out=out, in_=in_).then_inc(sem, 16)
        for b in writes:
            b.w = tok
            b.rd = []
        for b in reads:
            b.rd.append(tok)
        self.ninst += 1
        return tok

    def barrier(self):
        if self.stop:
            return
        toks = [(e.sem, e.cnt, e) for e in self.E.values() if e.cnt > 0]
        toks += [(s, c, None) for s, c in zip(self.dma_sems, self.dma_cnt) if c > 0]
        for e in self.E.values():
            self._waits(e, toks, skip_self=True)


class _Stop(Exception):
    pass


def build_program(T, CT, L=2, final_norm=True, dbg=None):
    assert T % 512 == 0 and CT % 128 == 0
    nc = bass.Bass("TRN2", target_bir_lowering=False)
    dt_ = nc.dram_tensor

    def din(name, shape, dt=F32):
        return dt_(name, shape, dt, kind="ExternalInput").ap()

    def dint(name, shape, dt=F32):
        return dt_(name, shape, dt, kind="Internal").ap()

    x_in = din("x", [T, D])
    ctx_in = din("ctx", [CT, D])
    cs_in = din("cs", [128, KC, 2])
    cst_in = din("cst", [128, NCST])
    ada_w = din("ada_w", [L, D, 6 * D])
    rows_in = din("rows", [L, 2, 6 * D + 2 * D])
    fin_g = din("fin_g", [D])
    w_in = din("w_in", [L, D, PROJ])
    w_out = din("w_out", [L, D, D])
    w_up = din("w_up", [L, D, 2 * DFF])
    w_dn = din("w_dn", [L, DFF, D])
    vec_in = din("vec", [L, 128, NVEC])
    rowv_in = din("rowv", [L, 1, 1024])
    lup_in = din("lup", [L, 128, 2, 512])
    gup_in = din("gup", [L, 128, 512])
    out = dt_("out", [T, D], F32, kind="ExternalOutput").ap()

    xs = [dint("xs0", [T, D]), dint("xs1", [T, D])]
    cxs = [dint("cxs0", [CT, D]), dint("cxs1", [CT, D])]
    modr = dint("modr", [2, 6, D])
    TS = max(T, CT)
    sp_base = dint("sp_base", [16, 128, TS])
    sp_wa = dint("sp_wa", [128, TS])
    sp_g = dint("sp_g", [8, 128, TS])
    sp_ycv = dint("sp_ycv", [4, 128, TS + 2], BF16)
    sp_y1 = dint("sp_y1", [TS, 512])

    st = ExitStack()
    with st:
        sems = {n: st.enter_context(nc.semaphore(n)) for n in ENGMAP}
        dsem = [st.enter_context(nc.semaphore(f"dq{i}")) for i in range(24)]
        p = Prog(nc, sems, dsem)

        uid = [0]

        def sb(name, shape, dt=F32, stack=st):
            uid[0] += 1
            return stack.enter_context(nc.sbuf_tensor(f"s{uid[0]}_{name}", shape, dt))

        def ps(name, shape, dt=F32):
            uid[0] += 1
            return st.enter_context(nc.psum_tensor(f"p{uid[0]}_{name}", shape, dt))

        cst = sb("cst", [128, NCST]); Bcst = Buf()
        vec = sb("vec", [128, NVEC]); Bvec = Buf()
        omka = sb("omka", [128, 4]); Bomka = Buf()
        bc = {n: sb("bc_" + n, [128, D]) for n in ("gs", "sh", "gt")}
        Bbc = {n: Buf() for n in bc}
        PA = [ps("PA0", [128, 512]), ps("PA1", [128, 512])]; BPA = [Buf(ps=True), Buf(ps=True)]
        _bpg = Buf(ps=True)
        PG = ps("PG", [128, 512]); BPG = [_bpg, _bpg]
        PN = [ps("PN0", [128, 512]), ps("PN1", [128, 512])]
        _bpn = [Buf(ps=True), Buf(ps=True)]
        BPN = [[_bpn[0]] * 3, [_bpn[1]] * 3]
        _bpm = Buf(ps=True)
        PM = ps("PM", [128, 512]); BPM = [_bpm] * 5
        PS_ = ps("PS", [128, 512]); BPS = Buf(ps=True)
        PY = ps("PY", [128, 512]); BPY = Buf(ps=True)
        pa_rr = [0]

        def pa():
            i = pa_rr[0]
            pa_rr[0] ^= 1
            return PA[i], BPA[i]

        ident = cst[:, C_ID:C_ID + 128]
        p.dma(cst[:], cst_in, writes=[Bcst])
        identR = sb("identR", [128, 128], SCAN_DT); BidR = Buf()
        identS = identR
        p.op("dve", lambda e: e.tensor_copy(out=identR[:], in_=ident), reads=[Bcst], writes=[BidR])

        def mm(o, l, r, start, stop, reads, writes):
            p.op("pe", lambda e: e.matmul(o, l, r, start=start, stop=stop), reads=reads, writes=writes)

        def tr(o, i_, reads, writes):
            p.op("pe", lambda e: e.transpose(o, i_, ident), reads=list(reads) + [Bcst], writes=writes)

        def act(o, i_, func, reads, writes, bias=None, scale=None, accum=None):
            kw = {}
            if bias is not None:
                kw["bias"] = bias
            if scale is not None:
                kw["scale"] = scale
            if accum is not None:
                kw["accum_out"] = accum
            p.op("act", lambda e: e.activation(out=o, in_=i_, func=func, **kw), reads=reads, writes=writes)

        def tt(en, o, a, b, op, reads, writes):
            p.op(en, lambda e: e.tensor_tensor(out=o, in0=a, in1=b, op=op), reads=reads, writes=writes)

        def ts(en, o, a, s1, s2, op0, op1, reads, writes):
            if s2 is None:
                p.op(en, lambda e: e.tensor_scalar(out=o, in0=a, scalar1=s1, scalar2=None, op0=op0),
                     reads=reads, writes=writes)
            else:
                p.op(en, lambda e: e.tensor_scalar(out=o, in0=a, scalar1=s1, scalar2=s2, op0=op0, op1=op1),
                     reads=reads, writes=writes)

        def stt(en, o, a, s, b, op0, op1, reads, writes):
            p.op(en, lambda e: e.scalar_tensor_tensor(out=o, in0=a, scalar=s, in1=b, op0=op0, op1=op1),
                 reads=reads, writes=writes)

        def cp(en, o, i_, reads, writes):
            if en == "act":
                act(o, i_, AF.Copy, reads, writes)
            else:
                p.op(en, lambda e: e.tensor_copy(out=o, in_=i_), reads=reads, writes=writes)

        def ck(k):
            if dbg == k and not p.stop:
                p.barrier()
                p.stop = True

        try:
          for l in range(L):
              last = (l == L - 1)
              x_src = x_in if l == 0 else xs[1]
              c_src = ctx_in if l == 0 else cxs[1]
              x_mid, c_mid = xs[0], cxs[0]
              x_dst, c_dst = xs[1], cxs[1]
              if last:
                  x_dst = out

              p.barrier()
              with ExitStack() as s0:
                  rows = sb("rows", [2, 8 * D], stack=s0); Brows = Buf()
                  mod = sb("mod", [2, 6 * D], stack=s0); Bmod = Buf()
                  drv = sb("drv", [2, 6, D], stack=s0); Bdrv = Buf()
                  cs = sb("cs", [128, KC, 2], stack=s0); Bcs = Buf()
                  scs = sb("scs", [128, KC, 2], stack=s0); Bscs = Buf()
                  stg = [sb(f"adastg{i}", [128, KC, 512], stack=s0) for i in range(2)]; Bstg = [Buf(), Buf()]
                  p.dma(rows[:], rows_in[l], writes=[Brows])
                  p.dma(cs[:], cs_in, writes=[Bcs])
                  p.dma(vec[:], vec_in[l], writes=[Bvec])
                  ts("dve", omka[:], vec[:, 4:8], -1.0, 1.0, ALU.mult, ALU.add, [Bvec], [Bomka])
                  act(scs[:], cs[:], AF.Silu, [Bcs], [Bscs])
                  for cb in range(12):
                      sg, Bsg = stg[cb % 2], Bstg[cb % 2]
                      p.dma(sg[:], ada_w[l][:, cb * 512:(cb + 1) * 512].rearrange("(k p) n -> p k n", p=128),
                            writes=[Bsg])
                      P_, BP_ = pa()
                      for k in range(KC):
                          mm(P_[0:2, :], scs[:, k, :], sg[:, k, :], k == 0, k == KC - 1, [Bscs, Bsg], [BP_])
                      tt("dve", mod[:, cb * 512:(cb + 1) * 512], P_[0:2, :], rows[:, cb * 512:(cb + 1) * 512],
                         ALU.add, [BP_, Brows], [Bmod])
                  stt("dve", drv[:, 0, :], mod[:, D:2 * D], 1.0, rows[:, 6 * D:7 * D], ALU.add, ALU.mult,
                      [Bmod, Brows], [Bdrv])
                  cp("dve", drv[:, 1, :], mod[:, 0:D], [Bmod], [Bdrv])
                  cp("dve", drv[:, 2, :], mod[:, 2 * D:3 * D], [Bmod], [Bdrv])
                  stt("dve", drv[:, 3, :], mod[:, 4 * D:5 * D], 1.0, rows[:, 7 * D:8 * D], ALU.add, ALU.mult,
                      [Bmod, Brows], [Bdrv])
                  cp("dve", drv[:, 4, :], mod[:, 3 * D:4 * D], [Bmod], [Bdrv])
                  cp("dve", drv[:, 5, :], mod[:, 5 * D:6 * D], [Bmod], [Bdrv])
                  Bmodr = Buf()
                  p.dma(modr, drv[:], reads=[Bdrv], writes=[Bmodr])
                  p.barrier()
              ck(0)

              def load_bc(setid, which):
                  for n, r in (("gs", 0), ("sh", 1), ("gt", 2)):
                      p.dma(bc[n][:], modr[setid, 3 * which + r].partition_broadcast(128),
                            reads=[Bmodr], writes=[Bbc[n]])

              with ExitStack() as s1:
                  def sb1(name, shape, dt=F32):
                      return sb(name, shape, dt, stack=s1)
                  rowv = sb1("rowv", [1, 1024]); Browv = Buf()
                  lup = sb1("lup", [128, 2, 512]); Blup = Buf()
                  gup = sb1("gup", [128, 512], BF16); Bgup = Buf()
                  Sst = [sb1(f"S{d}", [64, 8, 64], SCAN_DT) for d in range(2)]; BS = [Buf(), Buf()]
                  Sc = [sb1(f"Sc{d}", [64, 8, 64], SCAN_DT) for d in range(2)]; BSc = [Buf(), Buf()]
                  p.dma(rowv[:], rowv_in[l], writes=[Browv])
                  p.dma(lup[:], lup_in[l], writes=[Blup])
                  win = sb1("win", [128, KC, PROJ], BF16); Bwin = Buf()
                  wout = sb1("wout", [128, KC, D], BF16); Bwout = Buf()
                  with ExitStack() as sw:
                      wstg = [sb(f"wstg{i}", [128, PROJ], stack=sw) for i in range(2)]; Bwstg = [Buf(), Buf()]
                      for k in range(KC):
                          sg, Bsg = wstg[k % 2], Bwstg[k % 2]
                          p.dma(sg[:], w_in[l][k * 128:(k + 1) * 128, :], writes=[Bsg])
                          cp("dve" if k % 2 == 0 else "pool", win[:, k, :], sg[:], [Bsg], [Bwin])
                      for k in range(KC):
                          sg, Bsg = wstg[k % 2], Bwstg[k % 2]
                          p.dma(sg[:, 0:D], w_out[l][k * 128:(k + 1) * 128, :], writes=[Bsg])
                          cp("dve" if k % 2 == 0 else "pool", wout[:, k, :], sg[:, 0:D], [Bsg], [Bwout])
                      p.dma(wstg[0][:, 0:512], gup_in[l], writes=[Bwstg[0]])
                      cp("dve", gup[:], wstg[0][:, 0:512], [Bwstg[0]], [Bgup])
                      p.barrier()
                  ck(1)

                  SBT = 256
                  xt = [sb1(f"xt{i}", [128, D]) for i in range(2)]; Bxt = [Buf(), Buf()]
                  hb = sb1("hb", [128, D]); Bhb = Buf()
                  ssq = sb1("ssq", [128, 2]); Bssq = Buf()
                  hT = sb1("hT", [128, KC, SBT], BF16); BhT = Buf()
                  base = sb1("base", [128, 16, SBT]); Bbase = [Buf() for _ in range(16)]
                  WA = sb1("WA", [128, SBT]); BWA = Buf()
                  sgl = sb1("sgl", [128, SBT], BF16); Bsgl = Buf()
                  G12 = sb1("G12", [128, 8, SBT]); BG12 = [Buf() for _ in range(8)]
                  ycv = sb1("ycv", [128, 4, SBT], BF16); Bycv = [Buf() for _ in range(4)]
                  ubuf_t = sb1("ubuf", [128, 4 * (SBT + 2)]); Bub = [Buf() for _ in range(4)]
                  ubuf = ubuf_t[:].rearrange("p (j n) -> p j n", j=4)
                  cbb_t = sb1("cbb", [128, 4 * (SBT + 1)]); Bcbb = [Buf() for _ in range(4)]
                  cbb = cbb_t[:].rearrange("p (j n) -> p j n", j=4)
                  tmpA = [sb1(f"tmpA{i}", [128, SBT]) for i in range(4)]; BtA = [Buf() for _ in range(4)]
                  KR = sb1("KR", [128, 4, 256], SCAN_DT); BKR = [Buf() for _ in range(4)]
                  BtF = sb1("BtF", [128, 4, 128], SCAN_DT); BBt = [Buf() for _ in range(4)]
                  KtF = sb1("KtF", [128, 4, 128], SCAN_DT); BKt = [Buf() for _ in range(4)]
                  BpF = sb1("BpF", [128, 4, 128]); BBp = [Buf() for _ in range(4)]
                  KpF = sb1("KpF", [128, 4, 128]); BKp = [Buf() for _ in range(4)]
                  KaF32 = sb1("KaF32", [128, 4, 128]); BKa32 = [Buf() for _ in range(4)]
                  DG = sb1("DG", [128, 4, 64], SCAN_DT); BDG = [Buf() for _ in range(4)]
                  EG = sb1("EG", [128, 3, 128]); BEG = Buf()
                  aFt = sb1("aFt", [128, 128]); BaF = Buf()
                  tB = [sb1(f"tB{i}", [128, 128]) for i in range(3)]; BtB = [Buf() for _ in range(3)]
                  sigT = sb1("sigT", [128, 512]); BsigT = Buf()
                  TM = {n: sb1("TM_" + n, [128, 512], SCAN_DT) for n in ("Ka", "Bp", "Kp", "V")}
                  BTM = {n: Buf() for n in TM}
                  NU = 4
                  LT = [[sb1(f"LT{u}{i}", [128, 256], SCAN_DT) for i in range(2)] for u in range(NU)]
                  BLT = [[[Buf(), Buf()] for i in range(2)] for u in range(NU)]
                  Nn = [[sb1(f"Nn{u}{i}", [128, 128], SCAN_DT) for i in range(2)] for u in range(NU)]
                  BNn = [[Buf() for i in range(2)] for u in range(NU)]
                  Gm = [sb1(f"Gm{u}", [128, 384], SCAN_DT) for u in range(NU)]; BGm = [Buf() for _ in range(NU)]
                  TtF = [sb1(f"TtF{u}", [128, 128], SCAN_DT) for u in range(NU)]; BTt = [Buf() for _ in range(NU)]
                  AkV = [sb1(f"AkV{u}", [128, 64], SCAN_DT) for u in range(NU)]; BAkV = [Buf() for _ in range(NU)]
                  XnS = [sb1(f"XnS{u}", [128, 128], SCAN_DT) for u in range(NU)]; BXn = [Buf() for _ in range(NU)]
                  PTs = [sb1(f"PTs{u}", [64, 64], SCAN_DT) for u in range(NU)]; BPTs = [Buf() for _ in range(NU)]
                  RhT = [sb1(f"RhT{u}", [64, 128], SCAN_DT) for u in range(NU)]; BRh = [Buf() for _ in range(NU)]
                  PNL = [PN[0], PN[1], PA[0], PA[1]]
                  BPNL = [BPN[0], BPN[1], [BPA[0]] * 3, [BPA[1]] * 3]
                  ysum = ubuf_t[:, 0:512]; Bys = Buf()
                  yn = ubuf_t[:, 512:1024]; Byn = Buf()
                  y1t = cbb_t[:, 0:512]; By1 = Buf()
                  gst = sb1("gst", [128, 8, 4]); Bgst = Buf()
                  catT = hT[:, :, 0:128]; Bcat = Buf()
                  xo, Bxo = hb, Bhb

                  def vcol(i, j):
                      return vec[:, 4 * i + j:4 * i + j + 1]

                  def norm_tile(src_ap, xt_i, col0):
                      X, BX = xt[xt_i], Bxt[xt_i]
                      p.dma(X[:], src_ap, writes=[BX])
                      act(hb[:], X[:], AF.Square, [BX], [Bhb, Bssq], accum=ssq[:, 0:1])
                      ts("dve", ssq[:, 1:2], ssq[:, 0:1], 1.0 / D, RMS_EPS, ALU.mult, ALU.add, [Bssq], [Bssq])
                      act(ssq[:, 1:2], ssq[:, 1:2], AF.Ln, [Bssq], [Bssq])
                      act(ssq[:, 1:2], ssq[:, 1:2], AF.Exp, [Bssq], [Bssq], scale=-0.5)
                      stt("dve", hb[:], X[:], ssq[:, 1:2], bc["gs"][:], ALU.mult, ALU.mult,
                          [BX, Bssq, Bbc["gs"]], [Bhb])
                      tt("pool", hb[:], hb[:], bc["sh"][:], ALU.add, [Bhb, Bbc["sh"]], [Bhb])
                      for half in range(2):
                          P_, BP_ = pa()
                          for q in range(4):
                              k = half * 4 + q
                              tr(P_[:, q * 128:(q + 1) * 128], hb[:, k * 128:(k + 1) * 128], [Bhb], [BP_])
                          cp("act", hT[:, half * 4:half * 4 + 4, col0:col0 + 128],
                             P_[:].rearrange("p (q n) -> p q n", q=4), [BP_], [BhT])
                      return X, BX

                  def proj_chunk(c, n):
                      P_, BP_ = pa()
                      for k in range(KC):
                          mm(P_[:, 0:n], win[:, k, c * 128:(c + 1) * 128], hT[:, k, 0:n], k == 0, k == KC - 1,
                             [Bwin, BhT], [BP_])
                      return P_, BP_

                  def stage1(seg_src, t0, n, first, seg_T):
                      for i in range(n // 128):
                          norm_tile(seg_src[t0 + i * 128:t0 + (i + 1) * 128, :], i % 2, i * 128)
                      for j in range(4):
                          if first:
                              p.op("pool", lambda e, j=j: e.memset(ubuf[:, j, 0:2], 0.0), writes=[Bub[j]])
                              p.op("pool", lambda e, j=j: e.memset(cbb[:, j, 0:1], 0.0), writes=[Bcbb[j]])
                          Pb, BPb = proj_chunk(j, n)
                          cp("act", cbb[:, j, 1:n + 1], Pb[:, 0:n], [BPb], [Bcbb[j]])
                          Pc, BPc = proj_chunk(4 + j, n)
                          cp("act", tmpA[0][:, 0:n], Pc[:, 0:n], [BPc], [BtA[0]])
                          Px, BPx = proj_chunk(8 + j, n)
                          tt("dve", ubuf[:, j, 2:n + 2], tmpA[0][:, 0:n], Px[:, 0:n], ALU.mult, [BtA[0], BPx], [Bub[j]])
                          ts("dve", tmpA[1][:, 0:n], ubuf[:, j, 0:n], vcol(5, j), None, ALU.mult, None,
                             [Bub[j], Bvec], [BtA[1]])
                          stt("dve", tmpA[1][:, 0:n], ubuf[:, j, 1:n + 1], vcol(6, j), tmpA[1][:, 0:n], ALU.mult, ALU.add,
                              [Bub[j], Bvec, BtA[1]], [BtA[1]])
                          stt("dve", tmpA[1][:, 0:n], ubuf[:, j, 2:n + 2], vcol(7, j), tmpA[1][:, 0:n], ALU.mult, ALU.add,
                              [Bub[j], Bvec, BtA[1]], [BtA[1]])
                          tt("pool", ycv[:, j, 0:n], tmpA[1][:, 0:n], cbb[:, j, 0:n], ALU.mult, [BtA[1], Bcbb[j]], [Bycv[j]])
                          p.dma(sp_ycv[j][:, t0:t0 + n], ycv[:, j, 0:n], reads=[Bycv[j]])
                          if t0 + n == seg_T:
                              ts("dve", tmpA[1][:, 0:1], ubuf[:, j, n:n + 1], vcol(5, j), None, ALU.mult, None,
                                 [Bub[j], Bvec], [BtA[1]])
                              stt("dve", tmpA[1][:, 0:1], ubuf[:, j, n + 1:n + 2], vcol(6, j), tmpA[1][:, 0:1],
                                  ALU.mult, ALU.add, [Bub[j], Bvec, BtA[1]], [BtA[1]])
                              tt("pool", ycv[:, j, 0:1], tmpA[1][:, 0:1], cbb[:, j, n:n + 1], ALU.mult,
                                 [BtA[1], Bcbb[j]], [Bycv[j]])
                              p.dma(sp_ycv[j][:, t0 + n:t0 + n + 1], ycv[:, j, 0:1], reads=[Bycv[j]], slow=True)
                          else:
                              cp("pool", ubuf[:, j, 0:2], ubuf[:, j, n:n + 2], [Bub[j]], [Bub[j]])
                              cp("pool", cbb[:, j, 0:1], cbb[:, j, n:n + 1], [Bcbb[j]], [Bcbb[j]])
                      Pw, BPw = proj_chunk(24, n)
                      act(WA[0:64, 0:n], Pw[0:64, 0:n], AF.Tanh, [BPw], [BWA])
                      cp("act", WA[64:128, 0:n], Pw[64:128, 0:n], [BPw], [BWA])
                      p.dma(sp_wa[:, t0:t0 + n], WA[:, 0:n], reads=[BWA])
                      Pg, BPg = proj_chunk(25, n)
                      act(sgl[:, 0:n], Pg[:, 0:n], AF.Sigmoid, [BPg], [Bsgl])
                      for j in range(4):
                          rj, kpj, kj, vj = base[:, j, 0:n], base[:, 4 + j, 0:n], base[:, 8 + j, 0:n], base[:, 12 + j, 0:n]
                          Pr, BPr = proj_chunk(12 + j, n)
                          cp("act", rj, Pr[:, 0:n], [BPr], [Bbase[j]])
                          Pk, BPk = proj_chunk(16 + j, n)
                          cp("act", kj, Pk[:, 0:n], [BPk], [Bbase[8 + j]])
                          Pv, BPv = proj_chunk(20 + j, n)
                          cp("dve", vj, Pv[:, 0:n], [BPv], [Bbase[12 + j]])
                          act(tmpA[2][:, 0:n], kj, AF.Square, [Bbase[8 + j], Bvec], [BtA[2]], scale=vcol(0, j))
                          Pq, BPq = pa()
                          mm(Pq[:, 0:n], cst[:, C_BO:C_BO + 128], tmpA[2][:, 0:n], True, True, [Bcst, BtA[2]], [BPq])
                          ts("dve", tmpA[2][:, 0:n], Pq[:, 0:n], 1e-24, None, ALU.max, None, [BPq], [BtA[2]])
                          act(tmpA[2][:, 0:n], tmpA[2][:, 0:n], AF.Ln, [BtA[2]], [BtA[2]])
                          act(tmpA[2][:, 0:n], tmpA[2][:, 0:n], AF.Exp, [BtA[2]], [BtA[2]], scale=-0.5)
                          stt("dve", kpj, kj, vcol(0, j), tmpA[2][:, 0:n], ALU.mult, ALU.mult,
                              [Bbase[8 + j], Bvec, BtA[2]], [Bbase[4 + j]])
                          stt("dve", tmpA[3][:, 0:n], rj, vcol(2, j), kj, ALU.mult, ALU.mult,
                              [Bbase[j], Bvec, Bbase[8 + j]], [BtA[3]])
                          Pq2, BPq2 = pa()
                          mm(Pq2[:, 0:n], cst[:, C_BO:C_BO + 128], tmpA[3][:, 0:n], True, True, [Bcst, BtA[3]], [BPq2])
                          tt("dve", tmpA[3][:, 0:n], Pq2[:, 0:n], vj, ALU.mult, [BPq2, Bbase[12 + j]], [BtA[3]])
                          Pq3, BPq3 = pa()
                          mm(Pq3[:, 0:n], gup[:, j * 128:(j + 1) * 128], sgl[:, 0:n], True, True, [Bgup, Bsgl], [BPq3])
                          ts("dve", G12[:, j, 0:n], Pq3[:, 0:n], vcol(3, j), None, ALU.mult, None, [BPq3, Bvec], [BG12[j]])
                          stt("dve", G12[:, 4 + j, 0:n], tmpA[3][:, 0:n], vcol(4, j), Pq3[:, 0:n], ALU.add, ALU.mult,
                              [BtA[3], Bvec, BPq3], [BG12[4 + j]])
                          for q, Bq in ((j, Bbase[j]), (4 + j, Bbase[4 + j]), (8 + j, Bbase[8 + j]), (12 + j, Bbase[12 + j])):
                              p.dma(sp_base[q][:, t0:t0 + n], base[:, q, 0:n], reads=[Bq])
                          p.dma(sp_g[j][:, t0:t0 + n], G12[:, j, 0:n], reads=[BG12[j]])
                          p.dma(sp_g[4 + j][:, t0:t0 + n], G12[:, 4 + j, 0:n], reads=[BG12[4 + j]])

                  def load_stage(t0, n):
                      for q in range(16):
                          p.dma(base[:, q, 0:n], sp_base[q][:, t0:t0 + n], writes=[Bbase[q]])
                      p.dma(WA[:, 0:n], sp_wa[:, t0:t0 + n], writes=[BWA])
                      for q in range(8):
                          p.dma(G12[:, q, 0:n], sp_g[q][:, t0:t0 + n], writes=[BG12[q]])
                      for j in range(4):
                          p.dma(ycv[:, j, 0:n], sp_ycv[j][:, t0 + 1:t0 + n + 1], writes=[Bycv[j]])

                  def scan_tile(c0, d, Sb, BSb):
                      mG = cst[:, C_MG + 512 * d:C_MG + 512 * (d + 1)]
                      mL = cst[:, C_ML + 128 * d:C_ML + 128 * (d + 1)]
                      tF = cst[:, C_TF + 256 * d:C_TF + 256 * (d + 1)]
                      gcol = 127 if d == 0 else 0
                      cs_ = slice(c0, c0 + 128)
                      P_, BP_ = pa()
                      mm(P_[:], WA[0:64, cs_], lup[0:64, d, :], True, False, [BWA, Blup], [BP_])
                      mm(P_[:], cst[0:1, C_ON:C_ON + 128], rowv[0:1, d * 512:(d + 1) * 512], False, True,
                         [Bcst, Browv], [BP_])
                      act(sigT[:], P_[:], AF.Sigmoid, [BP_], [BsigT])
                      for j in range(4):
                          Pc, BPc = pa()
                          mm(Pc[:, 0:256], sigT[:, j * 128:(j + 1) * 128], tF, True, True, [BsigT, Bcst], [BPc])
                          act(EG[:, 0, :], Pc[:, 0:128], AF.Exp, [BPc], [BEG])
                          act(EG[:, 1, :], Pc[:, 0:128], AF.Exp, [BPc], [BEG], scale=-1.0)
                          act(EG[:, 2, :], Pc[:, 128:256], AF.Exp, [BPc], [BEG])
                          Pa, BPa_ = pa()
                          mm(Pa[:, 0:128], lup[64:128, d, j * 128:(j + 1) * 128], WA[64:128, cs_], True, True,
                             [Blup, BWA], [BPa_])
                          act(aFt[:], Pa[:, 0:128], AF.Sigmoid, [BPa_, Bvec], [BaF], bias=vcol(10 + d, j))
                          rj, kpj, kj = base[:, j, cs_], base[:, 4 + j, cs_], base[:, 8 + j, cs_]
                          tt("pool", KR[:, j, 128:256], rj, EG[:, 0, :], ALU.mult, [Bbase[j], BEG], [BKR[j]])
                          tt("dve", KaF32[:, j, :], kpj, EG[:, 2, :], ALU.mult, [Bbase[4 + j], BEG], [BKa32[j]])
                          cp("pool", KR[:, j, 0:128], KaF32[:, j, :], [BKa32[j]], [BKR[j]])
                          tt("pool", tB[0][:], kpj, aFt[:], ALU.mult, [Bbase[4 + j], BaF], [BtB[0]])
                          tt("dve", tB[1][:], tB[0][:], EG[:, 1, :], ALU.mult, [BtB[0], BEG], [BtB[1]])
                          cp("pool", BtF[:, j, :], tB[1][:], [BtB[1]], [BBt[j]])
                          ts("dve", BpF[:, j, :], tB[1][:], EG[:, 0, gcol:gcol + 1], None, ALU.mult, None,
                             [BtB[1], BEG], [BBp[j]])
                          ts("dve", tB[0][:], aFt[:], vcol(1, j), omka[:, j:j + 1], ALU.mult, ALU.add,
                             [BaF, Bvec, Bomka], [BtB[0]])
                          tt("pool", tB[0][:], tB[0][:], kj, ALU.mult, [BtB[0], Bbase[8 + j]], [BtB[0]])
                          tt("dve", tB[2][:], tB[0][:], EG[:, 1, :], ALU.mult, [BtB[0], BEG], [BtB[2]])
                          cp("pool", KtF[:, j, :], tB[2][:], [BtB[2]], [BKt[j]])
                          ts("dve", KpF[:, j, :], tB[2][:], EG[:, 0, gcol:gcol + 1], None, ALU.mult, None,
                             [BtB[2], BEG], [BKp[j]])
                          ts("dve", DG[:, j, :], cst[:, C_IB:C_IB + 64], EG[:, 0, gcol:gcol + 1], None, ALU.mult, None,
                             [Bcst, BEG], [BDG[j]])
                      ck(30)
                      for name, srcf, Bsrc in (("Ka", lambda j: KaF32[:, j, :], BKa32), ("Bp", lambda j: BpF[:, j, :], BBp),
                                               ("Kp", lambda j: KpF[:, j, :], BKp),
                                               ("V", lambda j: base[:, 12 + j, cs_], Bbase[12:16])):
                          P_, BP_ = pa()
                          for j in range(4):
                              tr(P_[:, j * 128:(j + 1) * 128], srcf(j), [Bsrc[j]], [BP_])
                          cp("act", TM[name][:], P_[:], [BP_], [BTM[name]])

                      ck(31)
                      def unit_stages(h, u):
                          j, p0 = h // 2, 64 * (h % 2)
                          hs = slice(h * 64, (h + 1) * 64)
                          Bt_h, Kt_h, KR_h = BtF[p0:p0 + 64, j, :], KtF[p0:p0 + 64, j, :], KR[p0:p0 + 64, j, :]
                          Ka_h = KR[p0:p0 + 64, j, 0:128]
                          pn, Bpn = PNL[u], BPNL[u]
                          stages = []

                          def s_gram():
                              mm(PG[:, 0:256], Bt_h, KR_h, True, True, [BBt[j], BKR[j]], [BPG[0]])
                              mm(PG[:, 256:512], Kt_h, KR_h, True, True, [BKt[j], BKR[j]], [BPG[1]])
                              mm(pn[:, 384:512], Ka_h, Bt_h, True, True, [BKR[j], BBt[j]], [Bpn[2]])
                              tt("dve", LT[u][0][:, 0:128], PG[:, 0:128], mG[:, 0:128], ALU.mult, [BPG[0], Bcst],
                                 [BLT[u][0][0]])
                              tt("dve", Gm[u][:, 0:128], PG[:, 128:256], mG[:, 128:256], ALU.mult, [BPG[0], Bcst], [BGm[u]])
                              tt("dve", Gm[u][:, 128:384], PG[:, 256:512], mG[:, 256:512], ALU.mult, [BPG[1], Bcst], [BGm[u]])
                              tt("dve", Nn[u][0][:], pn[:, 384:512], mL, ALU.mult, [Bpn[2], Bcst], [BNn[u][0]])
                              cp("pool", LT[u][0][:, 128:256], identS[:], [BidR], [BLT[u][0][1]])
                          stages.append(s_gram)

                          ev = "act" if u % 2 == 0 else "dve"

                          def mk_level(k):
                              a, b = k % 2, (k + 1) % 2
                              def s():
                                  mm(pn[:, 0:128], Nn[u][a][:], LT[u][a][:, 0:128], True, True,
                                     [BNn[u][a], BLT[u][a][0]], [Bpn[0]])
                                  mm(pn[:, 128:256], Nn[u][a][:], LT[u][a][:, 128:256], True, False,
                                     [BNn[u][a], BLT[u][a][1]], [Bpn[0]])
                                  mm(pn[:, 128:256], identS[:], LT[u][a][:, 128:256], False, True,
                                     [BidR, BLT[u][a][1]], [Bpn[0]])
                                  mm(pn[:, 256:384], LT[u][a][:, 0:128], Nn[u][a][:], True, True,
                                     [BLT[u][a][0], BNn[u][a]], [Bpn[1]])
                                  cp(ev, LT[u][b][:], pn[:, 0:256], [Bpn[0]], [BLT[u][b][0], BLT[u][b][1]])
                                  cp(ev, Nn[u][b][:], pn[:, 256:384], [Bpn[1]], [BNn[u][b]])
                              return s
                          for k in range(0, 6):
                              stages.append(mk_level(k))

                          def s_l6():
                              mm(pn[:, 128:256], Nn[u][0][:], LT[u][0][:, 128:256], True, False,
                                 [BNn[u][0], BLT[u][0][1]], [Bpn[0]])
                              mm(pn[:, 128:256], identS[:], LT[u][0][:, 128:256], False, True,
                                 [BidR, BLT[u][0][1]], [Bpn[0]])
                              cp(ev, TtF[u][:], pn[:, 128:256], [Bpn[0]], [BTt[u]])
                              mm(PM[:, 0:64], Gm[u][:, 128:256], TM["V"][:, hs], True, True, [BGm[u], BTM["V"]], [BPM[0]])
                              cp(ev, AkV[u][:], PM[:, 0:64], [BPM[0]], [BAkV[u]])
                          stages.append(s_l6)

                          def s_x():
                              mm(PM[:, 64:128], TtF[u][:], TM["Ka"][:, hs], True, True, [BTt[u], BTM["Ka"]], [BPM[1]])
                              mm(PM[:, 128:192], TtF[u][:], AkV[u][:], True, True, [BTt[u], BAkV[u]], [BPM[1]])
                              if ev == "act":
                                  act(XnS[u][:], PM[:, 64:192], AF.Copy, [BPM[1]], [BXn[u]], scale=-1.0)
                              else:
                                  ts("dve", XnS[u][:], PM[:, 64:192], -1.0, None, ALU.mult, None, [BPM[1]], [BXn[u]])
                          stages.append(s_x)

                          def s_fin():
                              mm(PM[0:64, 256:384], identR[p0:p0 + 64, p0:p0 + 64], KR[p0:p0 + 64, j, 128:256], True, False,
                                 [BidR, BKR[j]], [BPM[3]])
                              mm(PM[0:64, 256:384], XnS[u][:, 0:64], Gm[u][:, 0:128], False, True, [BXn[u], BGm[u]], [BPM[3]])
                              cp(ev, RhT[u][:], PM[0:64, 256:384], [BPM[3]], [BRh[u]])
                              mm(PM[0:64, 192:256], XnS[u][:, 0:64], TM["Bp"][:, hs], True, False, [BXn[u], BTM["Bp"]], [BPM[2]])
                              mm(PM[0:64, 192:256], identR[p0:p0 + 64, p0:p0 + 64], DG[p0:p0 + 64, j, :], False, True,
                                 [BidR, BDG[j]], [BPM[2]])
                              cp(ev, PTs[u][:], PM[0:64, 192:256], [BPM[2]], [BPTs[u]])
                              mm(PY[:, hs], Gm[u][:, 256:384], TM["V"][:, hs], True, False, [BGm[u], BTM["V"]], [BPY])
                              mm(PY[:, hs], Gm[u][:, 0:128], XnS[u][:, 64:128], False, False, [BGm[u], BXn[u]], [BPY])
                              mm(PY[:, hs], RhT[u][:], Sb[:, h, :], False, True, [BRh[u], BSb], [BPY])
                              mm(PS_[0:64, hs], TM["Kp"][:, hs], TM["V"][:, hs], True, False, [BTM["Kp"], BTM["V"]], [BPS])
                              mm(PS_[0:64, hs], TM["Bp"][:, hs], XnS[u][:, 64:128], False, False, [BTM["Bp"], BXn[u]], [BPS])
                              mm(PS_[0:64, hs], PTs[u][:], Sb[:, h, :], False, True, [BPTs[u], BSb], [BPS])
                          stages.append(s_fin)
                          return stages

                      for g_ in range(8 // NU):
                          us_ = [unit_stages(NU * g_ + i_, i_) for i_ in range(NU)]
                          for sts_ in zip(*us_):
                              for f_ in sts_:
                                  f_()
                      cp("act", Sb[:].rearrange("p h n -> p (h n)"), PS_[0:64, :], [BPS], [BSb])

                  def post_tile(src, dst, t0, c0, X, BX):
                      cs_ = slice(c0, c0 + 128)
                      tt("dve", ysum, PY[:], y1t, ALU.add, [BPY, By1], [Bys])
                      y3 = ysum.rearrange("p (h n) -> p h n", h=8)
                      p.op("dve", lambda e: e.tensor_reduce(out=gst[:, :, 0], in_=y3, axis=AX.X, op=ALU.add),
                           reads=[Bys], writes=[Bgst])
                      tt("pool", yn, ysum, ysum, ALU.mult, [Bys], [Byn])
                      p.op("dve", lambda e: e.tensor_reduce(out=gst[:, :, 1], in_=yn.rearrange("p (h n) -> p h n", h=8),
                                                            axis=AX.X, op=ALU.add), reads=[Byn], writes=[Bgst])
                      ts("dve", gst[:, :, 0], gst[:, :, 0], 1.0 / 64, None, ALU.mult, None, [Bgst], [Bgst])
                      tt("dve", gst[:, :, 2], gst[:, :, 0], gst[:, :, 0], ALU.mult, [Bgst], [Bgst])
                      stt("dve", gst[:, :, 3], gst[:, :, 1], 1.0 / 64, gst[:, :, 2], ALU.mult, ALU.subtract, [Bgst], [Bgst])
                      ts("dve", gst[:, :, 3], gst[:, :, 3], GN_EPS, None, ALU.add, None, [Bgst], [Bgst])
                      act(gst[:, :, 3], gst[:, :, 3], AF.Ln, [Bgst], [Bgst])
                      act(gst[:, :, 3], gst[:, :, 3], AF.Exp, [Bgst], [Bgst], scale=-0.5)
                      for h in range(8):
                          ts("dve", yn[:, h * 64:(h + 1) * 64], ysum[:, h * 64:(h + 1) * 64],
                             gst[:, h, 0:1], gst[:, h, 3:4], ALU.subtract, ALU.mult, [Bys, Bgst], [Byn])
                      P_, BP_ = pa()
                      for j in range(4):
                          tr(P_[:, j * 128:(j + 1) * 128], yn[:, j * 128:(j + 1) * 128], [Byn], [BP_])
                      for j in range(4):
                          tt("dve", tB[0][:], P_[:, j * 128:(j + 1) * 128], G12[:, j, cs_], ALU.mult, [BP_, BG12[j]], [BtB[0]])
                          tt("pool", catT[:, 4 + j, :], tB[0][:], G12[:, 4 + j, cs_], ALU.add, [BtB[0], BG12[4 + j]], [Bcat])
                          cp("pool", catT[:, j, :], ycv[:, j, cs_], [Bycv[j]], [Bcat])
                      for half in range(2):
                          Po, BPo = pa()
                          for k in range(KC):
                              mm(Po[:], catT[:, k, :], wout[:, k, half * 512:(half + 1) * 512], k == 0, k == KC - 1,
                                 [Bcat, Bwout], [BPo])
                          tt("dve", xo[:, half * 512:(half + 1) * 512], Po[:], bc["gt"][:, half * 512:(half + 1) * 512],
                             ALU.mult, [BPo, Bbc["gt"]], [Bxo])
                      tt("pool", xo[:], xo[:], X[:], ALU.add, [Bxo, BX], [Bxo])
                      p.dma(dst[t0:t0 + 128, :], xo[:], reads=[Bxo])

                  def mixer_segment(src, dst, seg_T, setid, Sin, need_out, Sfin):
                      load_bc(setid, 0)
                      nsb = (seg_T + SBT - 1) // SBT
                      for d in range(2):
                          if Sin is None:
                              ts("dve", Sst[d][:].rearrange("p h n -> p (h n)"), cst[0:64, C_MG:C_MG + 512], 0.0, None,
                                 ALU.mult, None, [Bcst], [BS[d]])
                          else:
                              cp("pool", Sst[d][:], Sin[d][0][:], [Sin[d][1]], [BS[d]])
                      for s_ in range(nsb):
                          t0 = s_ * SBT
                          n = min(SBT, seg_T - t0)
                          stage1(src, t0, n, s_ == 0, seg_T)
                          ck(2)
                          for i in range(n // 128):
                              scan_tile(i * 128, 0, Sst[0], BS[0])
                              ck(3)
                              if need_out:
                                  cp("dve", sigT[:], PY[:], [BPY], [BsigT])
                                  p.dma(sp_y1[t0 + i * 128:t0 + (i + 1) * 128, :], sigT[:], reads=[BsigT])
                      p.barrier()
                      ck(4)
                      for s_ in reversed(range(nsb)):
                          t0 = s_ * SBT
                          n = min(SBT, seg_T - t0)
                          load_stage(t0, n)
                          for i in reversed(range(n // 128)):
                              scan_tile(i * 128, 1, Sst[1], BS[1])
                              if need_out:
                                  tk = t0 + i * 128
                                  p.dma(y1t, sp_y1[tk:tk + 128, :], writes=[By1])
                                  X, BX = xt[i % 2], Bxt[i % 2]
                                  p.dma(X[:], src[tk:tk + 128, :], writes=[BX])
                                  post_tile(src, dst, tk, i * 128, X, BX)
                      if Sfin is not None:
                          for d in range(2):
                              cp("pool", Sfin[d][0][:], Sst[d][:], [BS[d]], [Sfin[d][1]])
                      p.barrier()

                  mixer_segment(c_src, c_mid, CT, 1, None, not last, [(Sc[0], BSc[0]), (Sc[1], BSc[1])])
                  ck(5)
                  mixer_segment(x_src, x_mid, T, 0, [(Sc[0], BSc[0]), (Sc[1], BSc[1])], True, None)
                  p.barrier()
                  ck(6)

              with ExitStack() as s3:
                  def sb3(name, shape, dt=F32):
                      return sb(name, shape, dt, stack=s3)
                  wup = sb3("wup", [128, KC, 2 * DFF], BF16); Bwup = Buf()
                  wdn = sb3("wdn", [128, FC, D], BF16); Bwdn = Buf()
                  with ExitStack() as sw3:
                      wst3 = [sb(f"wst3{i}", [128, 2048], stack=sw3) for i in range(2)]; Bw3 = [Buf(), Buf()]
                      ii = 0
                      for k in range(KC):
                          for c_ in range(0, 2 * DFF, 2048):
                              w_ = min(2048, 2 * DFF - c_)
                              sg, Bsg = wst3[ii % 2], Bw3[ii % 2]
                              p.dma(sg[:, 0:w_], w_up[l][k * 128:(k + 1) * 128, c_:c_ + w_], writes=[Bsg])
                              cp("dve" if ii % 2 == 0 else "pool", wup[:, k, c_:c_ + w_], sg[:, 0:w_], [Bsg], [Bwup])
                              ii += 1
                      for c_ in range(FC):
                          sg, Bsg = wst3[ii % 2], Bw3[ii % 2]
                          p.dma(sg[:, 0:D], w_dn[l][c_ * 128:(c_ + 1) * 128, :], writes=[Bsg])
                          cp("dve" if ii % 2 == 0 else "pool", wdn[:, c_, :], sg[:, 0:D], [Bsg], [Bwdn])
                          ii += 1
                      p.barrier()
                  xt3 = [sb3(f"x3{i}", [128, D]) for i in range(2)]; Bx3 = [Buf() for _ in range(2)]
                  xr, Bxr = xt3[1], Bx3[1]
                  hb3 = sb3("hb3", [128, D]); Bhb3 = Buf()
                  ssq3 = sb3("ssq3", [128, 2]); Bssq3 = Buf()
                  hT3 = sb3("hT3", [128, KC, 640], BF16); BhT3 = Buf()
                  accs = [sb3(f"acc{i}", [128, 512]) for i in range(2)]; Baccs = [Buf(), Buf()]
                  actT = sb3("actT", [128, FC, 512], BF16); Bact = [Buf() for _ in range(FC)]
                  xo3, Bxo3 = hb3, Bhb3
                  fgb = sb3("fgb", [128, D]); Bfgb = Buf()
                  if last and final_norm:
                      p.dma(fgb[:], fin_g.partition_broadcast(128), writes=[Bfgb])

                  def ffn_segment(src, dst, seg_T, setid, gw, is_out):
                      load_bc(setid, 1)
                      BT = 512 if seg_T >= 512 else seg_T
                      nrow_tot = seg_T // gw
                      nrow = BT // gw
                      for b0 in range(0, seg_T, BT):
                          halo = gw if gw < seg_T else 0
                          lo, hi = max(0, b0 - halo), min(seg_T, b0 + BT + halo)
                          nw = hi - lo
                          off = b0 - lo
                          tpos, xi = lo, 0
                          while tpos < hi:
                              w_ = min(128, hi - tpos)
                              X, BX = xt3[xi % 2], Bx3[xi % 2]
                              if w_ < 128:
                                  p.op("pool", lambda e, X=X: e.memset(X[:], 0.0), writes=[BX])
                              p.dma(X[0:w_, :], src[tpos:tpos + w_, :], writes=[BX])
                              act(hb3[:], X[:], AF.Square, [BX], [Bhb3, Bssq3], accum=ssq3[:, 0:1])
                              ts("dve", ssq3[:, 1:2], ssq3[:, 0:1], 1.0 / D, RMS_EPS, ALU.mult, ALU.add, [Bssq3], [Bssq3])
                              act(ssq3[:, 1:2], ssq3[:, 1:2], AF.Ln, [Bssq3], [Bssq3])
                              act(ssq3[:, 1:2], ssq3[:, 1:2], AF.Exp, [Bssq3], [Bssq3], scale=-0.5)
                              stt("dve", hb3[:], X[:], ssq3[:, 1:2], bc["gs"][:], ALU.mult, ALU.mult,
                                  [BX, Bssq3, Bbc["gs"]], [Bhb3])
                              tt("pool", hb3[:], hb3[:], bc["sh"][:], ALU.add, [Bhb3, Bbc["sh"]], [Bhb3])
                              col0 = tpos - lo
                              for half in range(2):
                                  P_, BP_ = pa()
                                  for q in range(4):
                                      k = half * 4 + q
                                      tr(P_[:, q * 128:(q + 1) * 128], hb3[:, k * 128:(k + 1) * 128], [Bhb3], [BP_])
                                  cp("act", hT3[:, half * 4:half * 4 + 4, col0:col0 + w_],
                                     P_[:].rearrange("p (q n) -> p q n", q=4)[:, :, 0:w_], [BP_], [BhT3])
                              tpos += w_
                              xi += 1
                          wr0 = off // gw
                          for c_ in range(FC):
                              acc, Bacc = accs[c_ % 2], Baccs[c_ % 2]
                              sil, Bsil = acc, Bacc
                              if c_ % 2 == 0:
                                  GA, BGA, GB, BGB = PG, BPG[0], PN[0], BPN[0][0]
                              else:
                                  GA, BGA, GB, BGB = PY, BPY, PS_, BPS
                              segs = [(0, min(nw, 512), GA, BGA)]
                              if nw > 512:
                                  segs.append((512, nw, GB, BGB))
                              for (a_, b_, P_, BP_) in segs:
                                  for k in range(KC):
                                      mm(P_[:, 0:b_ - a_], wup[:, k, c_ * 128:(c_ + 1) * 128], hT3[:, k, a_:b_],
                                         k == 0, k == KC - 1, [Bwup, BhT3], [BP_])
                              wc = lambda ty, tx: vec[:, 48 + (ty * 3 + tx) * FC + c_:48 + (ty * 3 + tx) * FC + c_ + 1]
                              bcol = vec[:, 48 + 9 * FC + c_:48 + 9 * FC + c_ + 1]

                              def tap(ty, tx, first):
                                  dy, dx = ty - 1, tx - 1
                                  r0g = b0 // gw
                                  rows_ok = [r for r in range(nrow) if 0 <= r0g + r + dy < nrow_tot]
                                  if not rows_ok:
                                      return
                                  ra, rb = rows_ok[0], rows_ok[-1] + 1
                                  ca, cb2 = (1, gw) if dx == -1 else ((0, gw - 1) if dx == 1 else (0, gw))
                                  rsplit = 512 // gw
                                  r = ra
                                  while r < rb:
                                      wr = wr0 + r + dy
                                      if wr < rsplit:
                                          re_ = min(rb, rsplit - wr0 - dy)
                                          P_, B_, wbase = GA, BGA, 0
                                      else:
                                          re_ = rb
                                          P_, B_, wbase = GB, BGB, rsplit
                                      nr = re_ - r
                                      iv = P_[:, (wr - wbase) * gw:(wr - wbase + nr) * gw].rearrange("p (r c) -> p r c", c=gw)[:, :, ca + dx:cb2 + dx]
                                      ov = acc[:, r * gw:(r + nr) * gw].rearrange("p (r c) -> p r c", c=gw)[:, :, ca:cb2]
                                      if first:
                                          act(ov, iv, AF.Identity, [B_, Bvec], [Bacc], bias=bcol, scale=wc(ty, tx))
                                      else:
                                          stt("dve", ov, iv, wc(ty, tx), ov, ALU.mult, ALU.add, [B_, Bvec, Bacc], [Bacc])
                                      r = re_
                              tap(1, 1, True)
                              for ty in range(3):
                                  for tx in range(3):
                                      if (ty, tx) != (1, 1) and not (gw >= seg_T and ty != 1):
                                          tap(ty, tx, False)
                              act(sil[:, 0:BT], acc[:, 0:BT], AF.Silu, [], [Bacc])
                              Pv, BPv = pa()
                              for k in range(KC):
                                  mm(Pv[:, 0:BT], wup[:, k, DFF + c_ * 128:DFF + (c_ + 1) * 128], hT3[:, k, off:off + BT],
                                     k == 0, k == KC - 1, [Bwup, BhT3], [BPv])
                              tt("dve", actT[:, c_, 0:BT], sil[:, 0:BT], Pv[:, 0:BT], ALU.mult, [Bsil, BPv], [Bact[c_]])
                          for i in range(BT // 128):
                              tk = b0 + i * 128
                              p.dma(xr[:], src[tk:tk + 128, :], writes=[Bxr])
                              for half in range(2):
                                  Po, BPo = pa()
                                  for c_ in range(FC):
                                      mm(Po[:], actT[:, c_, i * 128:(i + 1) * 128], wdn[:, c_, half * 512:(half + 1) * 512],
                                         c_ == 0, c_ == FC - 1, [Bact[c_], Bwdn], [BPo])
                                  tt("dve", xo3[:, half * 512:(half + 1) * 512], Po[:], bc["gt"][:, half * 512:(half + 1) * 512],
                                     ALU.mult, [BPo, Bbc["gt"]], [Bxo3])
                              tt("pool", xo3[:], xo3[:], xr[:], ALU.add, [Bxo3, Bxr], [Bxo3])
                              if is_out and final_norm:
                                  act(xt3[0][:], xo3[:], AF.Square, [Bxo3], [Bx3[0], Bssq3], accum=ssq3[:, 0:1])
                                  ts("dve", ssq3[:, 1:2], ssq3[:, 0:1], 1.0 / D, RMS_EPS, ALU.mult, ALU.add, [Bssq3], [Bssq3])
                                  act(ssq3[:, 1:2], ssq3[:, 1:2], AF.Ln, [Bssq3], [Bssq3])
                                  act(ssq3[:, 1:2], ssq3[:, 1:2], AF.Exp, [Bssq3], [Bssq3], scale=-0.5)
                                  stt("dve", xo3[:], xo3[:], ssq3[:, 1:2], fgb[:], ALU.mult, ALU.mult, [Bxo3, Bssq3, Bfgb], [Bxo3])
                              p.dma(dst[tk:tk + 128, :], xo3[:], reads=[Bxo3])

                  if not last:
                      ffn_segment(c_mid, c_dst, CT, 1, CT, False)
                  ffn_segment(x_mid, x_dst, T, 0, GW, last)
                  p.barrier()
        except _Stop:
            pass
        p.barrier()
        nc._prog_ninst = p.ninst
    return nc


def prep_inputs(x, c, ctx, c_ctx, ada_w, ada_b, norm1_g, norm2_g, w_in, conv_a_w, rw_w0, rw_w_up, rw_a0, rw_a_up,
                rw_k_k, rw_k_a, rw_r_k, rw_g_up, rw_ln_g, rw_ln_b, w_out, ffn_w_up, ffn_conv_w, ffn_conv_b,
                ffn_w_down, final_g, b):
    L = ada_w.shape[0]
    f = lambda a: np.ascontiguousarray(a, dtype=np.float32)
    cs = np.stack([c[b].reshape(KC, 128).T, c_ctx.reshape(KC, 128).T], axis=-1)
    rows = np.concatenate([ada_b, norm1_g, norm2_g], axis=1)[:, None, :].repeat(2, axis=1)

    def ch(v, n):
        return v.reshape(n, 128).T
    vec = np.zeros((L, 128, NVEC), np.float32)
    for l in range(L):
        cols = [ch(rw_k_k[l], 4), ch(rw_k_a[l], 4), ch(rw_r_k[l].reshape(-1), 4), ch(rw_ln_g[l], 4), ch(rw_ln_b[l], 4),
                ch(conv_a_w[l, 0], 4), ch(conv_a_w[l, 1], 4), ch(conv_a_w[l, 2], 4),
                ch(rw_w0[l, 0], 4), ch(rw_w0[l, 1], 4), ch(rw_a0[l, 0], 4), ch(rw_a0[l, 1], 4)]
        for ty in range(3):
            for tx in range(3):
                cols.append(ch(ffn_conv_w[l, ty, tx], FC))
        cols.append(ch(ffn_conv_b[l], FC))
        vec[l] = np.concatenate(cols, axis=1)
    rowv = rw_w0.reshape(L, 1, 1024)
    lup = np.concatenate([rw_w_up.transpose(0, 2, 1, 3), rw_a_up.transpose(0, 2, 1, 3)], axis=1)
    return {
        "x": f(x[b]), "ctx": f(ctx[b]), "cs": f(cs), "cst": make_consts(), "ada_w": f(ada_w), "rows": f(rows),
        "fin_g": f(final_g), "w_in": f(w_in), "w_out": f(w_out), "w_up": f(ffn_w_up), "w_dn": f(ffn_w_down),
        "vec": f(vec), "rowv": f(rowv), "lup": f(lup), "gup": f(rw_g_up),
    }


def kernel(**inputs):
    inputs = {k: np.asarray(v) for k, v in inputs.items()}
    B, T, _ = inputs["x"].shape
    CT = inputs["ctx"].shape[1]
    nc = build_program(T, CT, inputs["ada_w"].shape[0])
    in_maps = [prep_inputs(b=b, **inputs) for b in range(B)]
    res = run_bass_kernel_spmd(nc, in_maps, core_ids=list(range(B)))
    return np.stack([r["out"] for r in res.results], axis=0).astype(np.float32)
```

```python
import math
import numpy as np
from contextlib import ExitStack
import concourse.bass as bass
import concourse.mybir as mybir
from concourse.bass_utils import run_bass_kernel_spmd

F32 = mybir.dt.float32
BF16 = mybir.dt.bfloat16
F32R = mybir.dt.float32r
AF = mybir.ActivationFunctionType
ALU = mybir.AluOpType
AX = mybir.AxisListType

D = 1024
KC = 8
PROJ = 3328
NPC = 26
DFF = 2816
FC = 22
GW = 64
DS = math.exp(-0.5)
RMS_EPS = 1e-6
GN_EPS = 64e-5
NVEC = 48 + 10 * FC
SCAN_DT = BF16

C_ID = 0
C_MG = 128
C_ML = C_MG + 1024
C_TF = C_ML + 256
C_BO = C_TF + 512
C_ON = C_BO + 128
C_IB = C_ON + 128
NCST = C_IB + 64


def make_consts():
    c = np.zeros((128, NCST), np.float32)
    s = np.arange(128)[:, None]
    t = np.arange(128)[None, :]
    c[:, C_ID:C_ID + 128] = np.eye(128)
    for d in range(2):
        lt = (s < t) if d == 0 else (s > t)
        le = (s <= t) if d == 0 else (s >= t)
        g = c[:, C_MG + 512 * d:C_MG + 512 * (d + 1)]
        g[:, 0:128] = -1.0 * lt
        g[:, 128:256] = le
        g[:, 256:384] = lt
        g[:, 384:512] = le
        c[:, C_ML + 128 * d:C_ML + 128 * (d + 1)] = -1.0 * lt.T
        f = c[:, C_TF + 256 * d:C_TF + 256 * (d + 1)]
        f[:, 0:128] = -DS * le
        f[:, 128:256] = -DS * lt
    c[:, C_BO:C_BO + 128] = (s // 64 == t // 64)
    c[:, C_ON:C_ON + 128] = 1.0
    c[:, C_IB:C_IB + 64] = (s % 64 == np.arange(64)[None, :])
    return c


class Buf:
    __slots__ = ("name", "w", "rd", "ps")

    def __init__(self, name="", ps=False):
        self.name = name
        self.w = None
        self.rd = []
        self.ps = ps


ENGMAP = {"pe": "tensor", "dve": "vector", "act": "scalar", "pool": "gpsimd", "sp": "sync"}


class Eng:
    def __init__(self, name, sem, h):
        self.name = name
        self.sem = sem
        self.h = h
        self.cnt = 0
        self.waited = {}


class Prog:
    def __init__(self, nc, sems, dma_sems):
        self.nc = nc
        self.E = {n: Eng(n, sems[n], getattr(nc, ENGMAP[n])) for n in ENGMAP}
        self.dma_sems = dma_sems
        self.dma_cnt = [0] * len(dma_sems)
        self.dma_rr = 0
        self.ninst = 0
        self.stop = False

    def _deps(self, reads, writes):
        d = []
        for b in reads:
            if b.w is not None:
                d.append(b.w)
        for b in writes:
            if b.w is not None:
                d.append(b.w)
            d.extend(b.rd)
        return d

    def _waits(self, e, deps, skip_self):
        need = {}
        for (sem, val, owner) in deps:
            if skip_self and owner is e:
                continue
            k = id(sem)
            if e.waited.get(k, 0) >= val:
                continue
            if k not in need or need[k][1] < val:
                need[k] = (sem, val)
        for k, (sem, val) in need.items():
            e.waited[k] = val
            e.h.wait_ge(sem, val)

    def op(self, en, fn, reads=(), writes=()):
        if self.stop:
            return
        e = self.E[en]
        deps = self._deps(reads, writes)
        for b in reads:
            if b.ps:
                deps.extend(t for t in b.rd if t[2] is not e)
        self._waits(e, deps, skip_self=(en == "pe"))
        e.cnt += 1
        tok = (e.sem, e.cnt, e)
        fn(e.h).then_inc(e.sem, 1)
        for b in writes:
            b.w = tok
            b.rd = []
        for b in reads:
            b.rd.append(tok)
        self.ninst += 1

    def dma(self, out, in_, reads=(), writes=(), q="sp", slow=False):
        if self.stop:
            return
        e = self.E[q]
        deps = self._deps(reads, writes)
        i = self.dma_rr
        self.dma_rr = (self.dma_rr + 1) % len(self.dma_sems)
        sem = self.dma_sems[i]
        if self.dma_cnt[i] > 0:
            deps.append((sem, self.dma_cnt[i], None))
        self._waits(e, deps, skip_self=False)
        self.dma_cnt[i] += 16
        tok = (sem, self.dma_cnt[i], None)
        if slow:
            e.h.dma_start(out=out, in_=in_, allow_slow_non_contiguous=True).then_inc(sem, 16)
        else:
            e.h.dma_start(out=out, in_=in_).then_inc(sem, 16)
        for b in writes:
            b.w = tok
            b.rd = []
        for b in reads:
            b.rd.append(tok)
        self.ninst += 1
        return tok

    def barrier(self):
        if self.stop:
            return
        toks = [(e.sem, e.cnt, e) for e in self.E.values() if e.cnt > 0]
        toks += [(s, c, None) for s, c in zip(self.dma_sems, self.dma_cnt) if c > 0]
        for e in self.E.values():
            self._waits(e, toks, skip_self=True)


class _Stop(Exception):
    pass


def build_program(T, CT, L=2, final_norm=True, dbg=None):
    assert T % 512 == 0 and CT % 128 == 0
    nc = bass.Bass("TRN2", target_bir_lowering=False)
    dt_ = nc.dram_tensor

    def din(name, shape, dt=F32):
        return dt_(name, shape, dt, kind="ExternalInput").ap()

    def dint(name, shape, dt=F32):
        return dt_(name, shape, dt, kind="Internal").ap()

    x_in = din("x", [T, D])
    ctx_in = din("ctx", [CT, D])
    cs_in = din("cs", [128, KC, 2])
    cst_in = din("cst", [128, NCST])
    ada_w = din("ada_w", [L, D, 6 * D])
    rows_in = din("rows", [L, 2, 6 * D + 2 * D])
    fin_g = din("fin_g", [D])
    w_in = din("w_in", [L, D, PROJ])
    w_out = din("w_out", [L, D, D])
    w_up = din("w_up", [L, D, 2 * DFF])
    w_dn = din("w_dn", [L, DFF, D])
    vec_in = din("vec", [L, 128, NVEC])
    rowv_in = din("rowv", [L, 1, 1024])
    lup_in = din("lup", [L, 128, 2, 512])
    gup_in = din("gup", [L, 128, 512])
    out = dt_("out", [T, D], F32, kind="ExternalOutput").ap()

    xs = [dint("xs0", [T, D]), dint("xs1", [T, D])]
    cxs = [dint("cxs0", [CT, D]), dint("cxs1", [CT, D])]
    modr = dint("modr", [2, 6, D])
    TS = max(T, CT)
    sp_base = dint("sp_base", [16, 128, TS])
    sp_wa = dint("sp_wa", [128, TS])
    sp_g = dint("sp_g", [8, 128, TS])
    sp_ycv = dint("sp_ycv", [4, 128, TS + 2], BF16)
    sp_y1 = dint("sp_y1", [TS, 512])

    st = ExitStack()
    with st:
        sems = {n: st.enter_context(nc.semaphore(n)) for n in ENGMAP}
        dsem = [st.enter_context(nc.semaphore(f"dq{i}")) for i in range(24)]
        p = Prog(nc, sems, dsem)

        uid = [0]

        def sb(name, shape, dt=F32, stack=st):
            uid[0] += 1
            return stack.enter_context(nc.sbuf_tensor(f"s{uid[0]}_{name}", shape, dt))

        def ps(name, shape, dt=F32):
            uid[0] += 1
            return st.enter_context(nc.psum_tensor(f"p{uid[0]}_{name}", shape, dt))

        cst = sb("cst", [128, NCST]); Bcst = Buf()
        vec = sb("vec", [128, NVEC]); Bvec = Buf()
        omka = sb("omka", [128, 4]); Bomka = Buf()
        bc = {n: sb("bc_" + n, [128, D]) for n in ("gs", "sh", "gt")}
        Bbc = {n: Buf() for n in bc}
        PA = [ps("PA0", [128, 512]), ps("PA1", [128, 512])]; BPA = [Buf(ps=True), Buf(ps=True)]
        _bpg = Buf(ps=True)
        PG = ps("PG", [128, 512]); BPG = [_bpg, _bpg]
        PN = [ps("PN0", [128, 512]), ps("PN1", [128, 512])]
        _bpn = [Buf(ps=True), Buf(ps=True)]
        BPN = [[_bpn[0]] * 3, [_bpn[1]] * 3]
        _bpm = Buf(ps=True)
        PM = ps("PM", [128, 512]); BPM = [_bpm] * 5
        PS_ = ps("PS", [128, 512]); BPS = Buf(ps=True)
        PY = ps("PY", [128, 512]); BPY = Buf(ps=True)
        pa_rr = [0]

        def pa():
            i = pa_rr[0]
            pa_rr[0] ^= 1
            return PA[i], BPA[i]

        ident = cst[:, C_ID:C_ID + 128]
        p.dma(cst[:], cst_in, writes=[Bcst])
        identR = sb("identR", [128, 128], SCAN_DT); BidR = Buf()
        identS = identR
        p.op("dve", lambda e: e.tensor_copy(out=identR[:], in_=ident), reads=[Bcst], writes=[BidR])

        def mm(o, l, r, start, stop, reads, writes):
            p.op("pe", lambda e: e.matmul(o, l, r, start=start, stop=stop), reads=reads, writes=writes)

        def tr(o, i_, reads, writes):
            p.op("pe", lambda e: e.transpose(o, i_, ident), reads=list(reads) + [Bcst], writes=writes)

        def act(o, i_, func, reads, writes, bias=None, scale=None, accum=None):
            kw = {}
            if bias is not None:
                kw["bias"] = bias
            if scale is not None:
                kw["scale"] = scale
            if accum is not None:
                kw["accum_out"] = accum
            p.op("act", lambda e: e.activation(out=o, in_=i_, func=func, **kw), reads=reads, writes=writes)

        def tt(en, o, a, b, op, reads, writes):
            p.op(en, lambda e: e.tensor_tensor(out=o, in0=a, in1=b, op=op), reads=reads, writes=writes)

        def ts(en, o, a, s1, s2, op0, op1, reads, writes):
            if s2 is None:
                p.op(en, lambda e: e.tensor_scalar(out=o, in0=a, scalar1=s1, scalar2=None, op0=op0),
                     reads=reads, writes=writes)
            else:
                p.op(en, lambda e: e.tensor_scalar(out=o, in0=a, scalar1=s1, scalar2=s2, op0=op0, op1=op1),
                     reads=reads, writes=writes)

        def stt(en, o, a, s, b, op0, op1, reads, writes):
            p.op(en, lambda e: e.scalar_tensor_tensor(out=o, in0=a, scalar=s, in1=b, op0=op0, op1=op1),
                 reads=reads, writes=writes)

        def cp(en, o, i_, reads, writes):
            if en == "act":
                act(o, i_, AF.Copy, reads, writes)
            else:
                p.op(en, lambda e: e.tensor_copy(out=o, in_=i_), reads=reads, writes=writes)

        def ck(k):
            if dbg == k and not p.stop:
                p.barrier()
                p.stop = True

        try:
          for l in range(L):
              last = (l == L - 1)
              x_src = x_in if l == 0 else xs[1]
              c_src = ctx_in if l == 0 else cxs[1]
              x_mid, c_mid = xs[0], cxs[0]
              x_dst, c_dst = xs[1], cxs[1]
              if last:
                  x_dst = out

              p.barrier()
              with ExitStack() as s0:
                  rows = sb("rows", [2, 8 * D], stack=s0); Brows = Buf()
                  mod = sb("mod", [2, 6 * D], stack=s0); Bmod = Buf()
                  drv = sb("drv", [2, 6, D], stack=s0); Bdrv = Buf()
                  cs = sb("cs", [128, KC, 2], stack=s0); Bcs = Buf()
                  scs = sb("scs", [128, KC, 2], stack=s0); Bscs = Buf()
                  stg = [sb(f"adastg{i}", [128, KC, 512], stack=s0) for i in range(2)]; Bstg = [Buf(), Buf()]
                  p.dma(rows[:], rows_in[l], writes=[Brows])
                  p.dma(cs[:], cs_in, writes=[Bcs])
                  p.dma(vec[:], vec_in[l], writes=[Bvec])
                  ts("dve", omka[:], vec[:, 4:8], -1.0, 1.0, ALU.mult, ALU.add, [Bvec], [Bomka])
                  act(scs[:], cs[:], AF.Silu, [Bcs], [Bscs])
                  for cb in range(12):
                      sg, Bsg = stg[cb % 2], Bstg[cb % 2]
                      p.dma(sg[:], ada_w[l][:, cb * 512:(cb + 1) * 512].rearrange("(k p) n -> p k n", p=128),
                            writes=[Bsg])
                      P_, BP_ = pa()
                      for k in range(KC):
                          mm(P_[0:2, :], scs[:, k, :], sg[:, k, :], k == 0, k == KC - 1, [Bscs, Bsg], [BP_])
                      tt("dve", mod[:, cb * 512:(cb + 1) * 512], P_[0:2, :], rows[:, cb * 512:(cb + 1) * 512],
                         ALU.add, [BP_, Brows], [Bmod])
                  stt("dve", drv[:, 0, :], mod[:, D:2 * D], 1.0, rows[:, 6 * D:7 * D], ALU.add, ALU.mult,
                      [Bmod, Brows], [Bdrv])
                  cp("dve", drv[:, 1, :], mod[:, 0:D], [Bmod], [Bdrv])
                  cp("dve", drv[:, 2, :], mod[:, 2 * D:3 * D], [Bmod], [Bdrv])
                  stt("dve", drv[:, 3, :], mod[:, 4 * D:5 * D], 1.0, rows[:, 7 * D:8 * D], ALU.add, ALU.mult,
                      [Bmod, Brows], [Bdrv])
                  cp("dve", drv[:, 4, :], mod[:, 3 * D:4 * D], [Bmod], [Bdrv])
                  cp("dve", drv[:, 5, :], mod[:, 5 * D:6 * D], [Bmod], [Bdrv])
                  Bmodr = Buf()
                  p.dma(modr, drv[:], reads=[Bdrv], writes=[Bmodr])
                  p.barrier()
              ck(0)

              def load_bc(setid, which):
                  for n, r in (("gs", 0), ("sh", 1), ("gt", 2)):
                      p.dma(bc[n][:], modr[setid, 3 * which + r].partition_broadcast(128),
                            reads=[Bmodr], writes=[Bbc[n]])

              with ExitStack() as s1:
                  def sb1(name, shape, dt=F32):
                      return sb(name, shape, dt, stack=s1)
                  rowv = sb1("rowv", [1, 1024]); Browv = Buf()
                  lup = sb1("lup", [128, 2, 512]); Blup = Buf()
                  gup = sb1("gup", [128, 512], BF16); Bgup = Buf()
                  Sst = [sb1(f"S{d}", [64, 8, 64], SCAN_DT) for d in range(2)]; BS = [Buf(), Buf()]
                  Sc = [sb1(f"Sc{d}", [64, 8, 64], SCAN_DT) for d in range(2)]; BSc = [Buf(), Buf()]
                  p.dma(rowv[:], rowv_in[l], writes=[Browv])
                  p.dma(lup[:], lup_in[l], writes=[Blup])
                  win = sb1("win", [128, KC, PROJ], BF16); Bwin = Buf()
                  wout = sb1("wout", [128, KC, D], BF16); Bwout = Buf()
                  with ExitStack() as sw:
                      wstg = [sb(f"wstg{i}", [128, PROJ], stack=sw) for i in range(2)]; Bwstg = [Buf(), Buf()]
                      for k in range(KC):
                          sg, Bsg = wstg[k % 2], Bwstg[k % 2]
                          p.dma(sg[:], w_in[l][k * 128:(k + 1) * 128, :], writes=[Bsg])
                          cp("dve" if k % 2 == 0 else "pool", win[:, k, :], sg[:], [Bsg], [Bwin])
                      for k in range(KC):
                          sg, Bsg = wstg[k % 2], Bwstg[k % 2]
                          p.dma(sg[:, 0:D], w_out[l][k * 128:(k + 1) * 128, :], writes=[Bsg])
                          cp("dve" if k % 2 == 0 else "pool", wout[:, k, :], sg[:, 0:D], [Bsg], [Bwout])
                      p.dma(wstg[0][:, 0:512], gup_in[l], writes=[Bwstg[0]])
                      cp("dve", gup[:], wstg[0][:, 0:512], [Bwstg[0]], [Bgup])
                      p.barrier()
                  ck(1)

                  SBT = 256
                  xt = [sb1(f"xt{i}", [128, D]) for i in range(2)]; Bxt = [Buf(), Buf()]
                  hb = sb1("hb", [128, D]); Bhb = Buf()
                  ssq = sb1("ssq", [128, 2]); Bssq = Buf()
                  hT = sb1("hT", [128, KC, SBT], BF16); BhT = Buf()
                  base = sb1("base", [128, 16, SBT]); Bbase = [Buf() for _ in range(16)]
                  WA = sb1("WA", [128, SBT]); BWA = Buf()
                  sgl = sb1("sgl", [128, SBT], BF16); Bsgl = Buf()
                  G12 = sb1("G12", [128, 8, SBT]); BG12 = [Buf() for _ in range(8)]
                  ycv = sb1("ycv", [128, 4, SBT], BF16); Bycv = [Buf() for _ in range(4)]
                  ubuf_t = sb1("ubuf", [128, 4 * (SBT + 2)]); Bub = [Buf() for _ in range(4)]
                  ubuf = ubuf_t[:].rearrange("p (j n) -> p j n", j=4)
                  cbb_t = sb1("cbb", [128, 4 * (SBT + 1)]); Bcbb = [Buf() for _ in range(4)]
                  cbb = cbb_t[:].rearrange("p (j n) -> p j n", j=4)
                  tmpA = [sb1(f"tmpA{i}", [128, SBT]) for i in range(4)]; BtA = [Buf() for _ in range(4)]
                  KR = sb1("KR", [128, 4, 256], SCAN_DT); BKR = [Buf() for _ in range(4)]
                  BtF = sb1("BtF", [128, 4, 128], SCAN_DT); BBt = [Buf() for _ in range(4)]
                  KtF = sb1("KtF", [128, 4, 128], SCAN_DT); BKt = [Buf() for _ in range(4)]
                  BpF = sb1("BpF", [128, 4, 128]); BBp = [Buf() for _ in range(4)]
                  KpF = sb1("KpF", [128, 4, 128]); BKp = [Buf() for _ in range(4)]
                  KaF32 = sb1("KaF32", [128, 4, 128]); BKa32 = [Buf() for _ in range(4)]
                  DG = sb1("DG", [128, 4, 64], SCAN_DT); BDG = [Buf() for _ in range(4)]
                  EG2 = [sb1(f"EG{q}", [128, 3, 128]) for q in range(2)]; BEG2 = [Buf(), Buf()]
                  aFt2 = [sb1(f"aFt{q}", [128, 128]) for q in range(2)]; BaF2 = [Buf(), Buf()]
                  tB2 = [[sb1(f"tB{q}{i}", [128, 128]) for i in range(3)] for q in range(2)]
                  BtB2 = [[Buf() for _ in range(3)] for q in range(2)]
                  tB, BtB = tB2[0], BtB2[0]
                  sigT = sb1("sigT", [128, 512]); BsigT = Buf()
                  TM = {n: sb1("TM_" + n, [128, 512], SCAN_DT) for n in ("Ka", "Bp", "Kp", "V")}
                  BTM = {n: Buf() for n in TM}
                  NU = 4
                  LT = [[sb1(f"LT{u}{i}", [128, 256], SCAN_DT) for i in range(2)] for u in range(NU)]
                  BLT = [[[Buf(), Buf()] for i in range(2)] for u in range(NU)]
                  Nn = [[sb1(f"Nn{u}{i}", [128, 128], SCAN_DT) for i in range(2)] for u in range(NU)]
                  BNn = [[Buf() for i in range(2)] for u in range(NU)]
                  Gm = [sb1(f"Gm{u}", [128, 384], SCAN_DT) for u in range(NU)]; BGm = [Buf() for _ in range(NU)]
                  TtF = [sb1(f"TtF{u}", [128, 128], SCAN_DT) for u in range(NU)]; BTt = [Buf() for _ in range(NU)]
                  AkV = [sb1(f"AkV{u}", [128, 64], SCAN_DT) for u in range(NU)]; BAkV = [Buf() for _ in range(NU)]
                  XnS = [sb1(f"XnS{u}", [128, 128], SCAN_DT) for u in range(NU)]; BXn = [Buf() for _ in range(NU)]
                  PTs = [sb1(f"PTs{u}", [64, 64], SCAN_DT) for u in range(NU)]; BPTs = [Buf() for _ in range(NU)]
                  RhT = [sb1(f"RhT{u}", [64, 128], SCAN_DT) for u in range(NU)]; BRh = [Buf() for _ in range(NU)]
                  PNL = [PN[0], PN[1], PA[0], PA[1]]
                  BPNL = [BPN[0], BPN[1], [BPA[0]] * 3, [BPA[1]] * 3]
                  ysum = ubuf_t[:, 0:512]; Bys = Buf()
                  yn = ubuf_t[:, 512:1024]; Byn = Buf()
                  y1t = cbb_t[:, 0:512]; By1 = Buf()
                  gst = sb1("gst", [128, 8, 4]); Bgst = Buf()
                  catT = hT[:, :, 0:128]; Bcat = Buf()
                  xo, Bxo = hb, Bhb

                  def vcol(i, j):
                      return vec[:, 4 * i + j:4 * i + j + 1]

                  def norm_tile(src_ap, xt_i, col0):
                      X, BX = xt[xt_i], Bxt[xt_i]
                      p.dma(X[:], src_ap, writes=[BX])
                      act(hb[:], X[:], AF.Square, [BX], [Bhb, Bssq], accum=ssq[:, 0:1])
                      ts("dve", ssq[:, 1:2], ssq[:, 0:1], 1.0 / D, RMS_EPS, ALU.mult, ALU.add, [Bssq], [Bssq])
                      act(ssq[:, 1:2], ssq[:, 1:2], AF.Ln, [Bssq], [Bssq])
                      act(ssq[:, 1:2], ssq[:, 1:2], AF.Exp, [Bssq], [Bssq], scale=-0.5)
                      stt("dve", hb[:], X[:], ssq[:, 1:2], bc["gs"][:], ALU.mult, ALU.mult,
                          [BX, Bssq, Bbc["gs"]], [Bhb])
                      tt("pool", hb[:], hb[:], bc["sh"][:], ALU.add, [Bhb, Bbc["sh"]], [Bhb])
                      for half in range(2):
                          P_, BP_ = pa()
                          for q in range(4):
                              k = half * 4 + q
                              tr(P_[:, q * 128:(q + 1) * 128], hb[:, k * 128:(k + 1) * 128], [Bhb], [BP_])
                          cp("act", hT[:, half * 4:half * 4 + 4, col0:col0 + 128],
                             P_[:].rearrange("p (q n) -> p q n", q=4), [BP_], [BhT])
                      return X, BX

                  def proj_chunk(c, n):
                      P_, BP_ = pa()
                      for k in range(KC):
                          mm(P_[:, 0:n], win[:, k, c * 128:(c + 1) * 128], hT[:, k, 0:n], k == 0, k == KC - 1,
                             [Bwin, BhT], [BP_])
                      return P_, BP_

                  def stage1(seg_src, t0, n, first, seg_T):
                      for i in range(n // 128):
                          norm_tile(seg_src[t0 + i * 128:t0 + (i + 1) * 128, :], i % 2, i * 128)
                      for j in range(4):
                          if first:
                              p.op("pool", lambda e, j=j: e.memset(ubuf[:, j, 0:2], 0.0), writes=[Bub[j]])
                              p.op("pool", lambda e, j=j: e.memset(cbb[:, j, 0:1], 0.0), writes=[Bcbb[j]])
                          Pb, BPb = proj_chunk(j, n)
                          cp("act", cbb[:, j, 1:n + 1], Pb[:, 0:n], [BPb], [Bcbb[j]])
                          Pc, BPc = proj_chunk(4 + j, n)
                          cp("act", tmpA[0][:, 0:n], Pc[:, 0:n], [BPc], [BtA[0]])
                          Px, BPx = proj_chunk(8 + j, n)
                          tt("dve", ubuf[:, j, 2:n + 2], tmpA[0][:, 0:n], Px[:, 0:n], ALU.mult, [BtA[0], BPx], [Bub[j]])
                          ts("dve", tmpA[1][:, 0:n], ubuf[:, j, 0:n], vcol(5, j), None, ALU.mult, None,
                             [Bub[j], Bvec], [BtA[1]])
                          stt("dve", tmpA[1][:, 0:n], ubuf[:, j, 1:n + 1], vcol(6, j), tmpA[1][:, 0:n], ALU.mult, ALU.add,
                              [Bub[j], Bvec, BtA[1]], [BtA[1]])
                          stt("dve", tmpA[1][:, 0:n], ubuf[:, j, 2:n + 2], vcol(7, j), tmpA[1][:, 0:n], ALU.mult, ALU.add,
                              [Bub[j], Bvec, BtA[1]], [BtA[1]])
                          tt("pool", ycv[:, j, 0:n], tmpA[1][:, 0:n], cbb[:, j, 0:n], ALU.mult, [BtA[1], Bcbb[j]], [Bycv[j]])
                          p.dma(sp_ycv[j][:, t0:t0 + n], ycv[:, j, 0:n], reads=[Bycv[j]])
                          if t0 + n == seg_T:
                              ts("dve", tmpA[1][:, 0:1], ubuf[:, j, n:n + 1], vcol(5, j), None, ALU.mult, None,
                                 [Bub[j], Bvec], [BtA[1]])
                              stt("dve", tmpA[1][:, 0:1], ubuf[:, j, n + 1:n + 2], vcol(6, j), tmpA[1][:, 0:1],
                                  ALU.mult, ALU.add, [Bub[j], Bvec, BtA[1]], [BtA[1]])
                              tt("pool", ycv[:, j, 0:1], tmpA[1][:, 0:1], cbb[:, j, n:n + 1], ALU.mult,
                                 [BtA[1], Bcbb[j]], [Bycv[j]])
                              p.dma(sp_ycv[j][:, t0 + n:t0 + n + 1], ycv[:, j, 0:1], reads=[Bycv[j]], slow=True)
                          else:
                              cp("pool", ubuf[:, j, 0:2], ubuf[:, j, n:n + 2], [Bub[j]], [Bub[j]])
                              cp("pool", cbb[:, j, 0:1], cbb[:, j, n:n + 1], [Bcbb[j]], [Bcbb[j]])
                      Pw, BPw = proj_chunk(24, n)
                      act(WA[0:64, 0:n], Pw[0:64, 0:n], AF.Tanh, [BPw], [BWA])
                      cp("act", WA[64:128, 0:n], Pw[64:128, 0:n], [BPw], [BWA])
                      p.dma(sp_wa[:, t0:t0 + n], WA[:, 0:n], reads=[BWA])
                      Pg, BPg = proj_chunk(25, n)
                      act(sgl[:, 0:n], Pg[:, 0:n], AF.Sigmoid, [BPg], [Bsgl])
                      for j in range(4):
                          rj, kpj, kj, vj = base[:, j, 0:n], base[:, 4 + j, 0:n], base[:, 8 + j, 0:n], base[:, 12 + j, 0:n]
                          Pr, BPr = proj_chunk(12 + j, n)
                          cp("act", rj, Pr[:, 0:n], [BPr], [Bbase[j]])
                          Pk, BPk = proj_chunk(16 + j, n)
                          cp("act", kj, Pk[:, 0:n], [BPk], [Bbase[8 + j]])
                          Pv, BPv = proj_chunk(20 + j, n)
                          cp("dve", vj, Pv[:, 0:n], [BPv], [Bbase[12 + j]])
                          act(tmpA[2][:, 0:n], kj, AF.Square, [Bbase[8 + j], Bvec], [BtA[2]], scale=vcol(0, j))
                          Pq, BPq = pa()
                          mm(Pq[:, 0:n], cst[:, C_BO:C_BO + 128], tmpA[2][:, 0:n], True, True, [Bcst, BtA[2]], [BPq])
                          ts("dve", tmpA[2][:, 0:n], Pq[:, 0:n], 1e-24, None, ALU.max, None, [BPq], [BtA[2]])
                          act(tmpA[2][:, 0:n], tmpA[2][:, 0:n], AF.Ln, [BtA[2]], [BtA[2]])
                          act(tmpA[2][:, 0:n], tmpA[2][:, 0:n], AF.Exp, [BtA[2]], [BtA[2]], scale=-0.5)
                          stt("dve", kpj, kj, vcol(0, j), tmpA[2][:, 0:n], ALU.mult, ALU.mult,
                              [Bbase[8 + j], Bvec, BtA[2]], [Bbase[4 + j]])
                          stt("dve", tmpA[3][:, 0:n], rj, vcol(2, j), kj, ALU.mult, ALU.mult,
                              [Bbase[j], Bvec, Bbase[8 + j]], [BtA[3]])
                          Pq2, BPq2 = pa()
                          mm(Pq2[:, 0:n], cst[:, C_BO:C_BO + 128], tmpA[3][:, 0:n], True, True, [Bcst, BtA[3]], [BPq2])
                          tt("dve", tmpA[3][:, 0:n], Pq2[:, 0:n], vj, ALU.mult, [BPq2, Bbase[12 + j]], [BtA[3]])
                          Pq3, BPq3 = pa()
                          mm(Pq3[:, 0:n], gup[:, j * 128:(j + 1) * 128], sgl[:, 0:n], True, True, [Bgup, Bsgl], [BPq3])
                          ts("dve", G12[:, j, 0:n], Pq3[:, 0:n], vcol(3, j), None, ALU.mult, None, [BPq3, Bvec], [BG12[j]])
                          stt("dve", G12[:, 4 + j, 0:n], tmpA[3][:, 0:n], vcol(4, j), Pq3[:, 0:n], ALU.add, ALU.mult,
                              [BtA[3], Bvec, BPq3], [BG12[4 + j]])
                          for q, Bq in ((j, Bbase[j]), (4 + j, Bbase[4 + j]), (8 + j, Bbase[8 + j]), (12 + j, Bbase[12 + j])):
                              p.dma(sp_base[q][:, t0:t0 + n], base[:, q, 0:n], reads=[Bq])
                          p.dma(sp_g[j][:, t0:t0 + n], G12[:, j, 0:n], reads=[BG12[j]])
                          p.dma(sp_g[4 + j][:, t0:t0 + n], G12[:, 4 + j, 0:n], reads=[BG12[4 + j]])

                  def load_stage(t0, n):
                      for q in range(16):
                          p.dma(base[:, q, 0:n], sp_base[q][:, t0:t0 + n], writes=[Bbase[q]])
                      p.dma(WA[:, 0:n], sp_wa[:, t0:t0 + n], writes=[BWA])
                      for q in range(8):
                          p.dma(G12[:, q, 0:n], sp_g[q][:, t0:t0 + n], writes=[BG12[q]])
                      for j in range(4):
                          p.dma(ycv[:, j, 0:n], sp_ycv[j][:, t0 + 1:t0 + n + 1], writes=[Bycv[j]])

                  def scan_tile(c0, d, Sb, BSb):
                      mG = cst[:, C_MG + 512 * d:C_MG + 512 * (d + 1)]
                      mL = cst[:, C_ML + 128 * d:C_ML + 128 * (d + 1)]
                      tF = cst[:, C_TF + 256 * d:C_TF + 256 * (d + 1)]
                      gcol = 127 if d == 0 else 0
                      cs_ = slice(c0, c0 + 128)
                      P_, BP_ = pa()
                      mm(P_[:], WA[0:64, cs_], lup[0:64, d, :], True, False, [BWA, Blup], [BP_])
                      mm(P_[:], cst[0:1, C_ON:C_ON + 128], rowv[0:1, d * 512:(d + 1) * 512], False, True,
                         [Bcst, Browv], [BP_])
                      act(sigT[:], P_[:], AF.Sigmoid, [BP_], [BsigT])
                      for j in range(4):
                          EG, BEG, aFt, BaF, tB, BtB = EG2[j % 2], BEG2[j % 2], aFt2[j % 2], BaF2[j % 2], tB2[j % 2], BtB2[j % 2]
                          Pc, BPc = pa()
                          mm(Pc[:, 0:256], sigT[:, j * 128:(j + 1) * 128], tF, True, True, [BsigT, Bcst], [BPc])
                          act(EG[:, 0, :], Pc[:, 0:128], AF.Exp, [BPc], [BEG])
                          act(EG[:, 1, :], Pc[:, 0:128], AF.Exp, [BPc], [BEG], scale=-1.0)
                          act(EG[:, 2, :], Pc[:, 128:256], AF.Exp, [BPc], [BEG])
                          Pa, BPa_ = pa()
                          mm(Pa[:, 0:128], lup[64:128, d, j * 128:(j + 1) * 128], WA[64:128, cs_], True, True,
                             [Blup, BWA], [BPa_])
                          act(aFt[:], Pa[:, 0:128], AF.Sigmoid, [BPa_, Bvec], [BaF], bias=vcol(10 + d, j))
                          rj, kpj, kj = base[:, j, cs_], base[:, 4 + j, cs_], base[:, 8 + j, cs_]
                          tt("pool", KR[:, j, 128:256], rj, EG[:, 0, :], ALU.mult, [Bbase[j], BEG], [BKR[j]])
                          tt("dve", KaF32[:, j, :], kpj, EG[:, 2, :], ALU.mult, [Bbase[4 + j], BEG], [BKa32[j]])
                          cp("pool", KR[:, j, 0:128], KaF32[:, j, :], [BKa32[j]], [BKR[j]])
                          tt("pool", tB[0][:], kpj, aFt[:], ALU.mult, [Bbase[4 + j], BaF], [BtB[0]])
                          tt("dve", tB[1][:], tB[0][:], EG[:, 1, :], ALU.mult, [BtB[0], BEG], [BtB[1]])
                          cp("pool", BtF[:, j, :], tB[1][:], [BtB[1]], [BBt[j]])
                          ts("dve", BpF[:, j, :], tB[1][:], EG[:, 0, gcol:gcol + 1], None, ALU.mult, None,
                             [BtB[1], BEG], [BBp[j]])
                          ts("dve", tB[0][:], aFt[:], vcol(1, j), omka[:, j:j + 1], ALU.mult, ALU.add,
                             [BaF, Bvec, Bomka], [BtB[0]])
                          tt("pool", tB[0][:], tB[0][:], kj, ALU.mult, [BtB[0], Bbase[8 + j]], [BtB[0]])
                          tt("dve", tB[2][:], tB[0][:], EG[:, 1, :], ALU.mult, [BtB[0], BEG], [BtB[2]])
                          cp("pool", KtF[:, j, :], tB[2][:], [BtB[2]], [BKt[j]])
                          ts("dve", KpF[:, j, :], tB[2][:], EG[:, 0, gcol:gcol + 1], None, ALU.mult, None,
                             [BtB[2], BEG], [BKp[j]])
                          ts("dve", DG[:, j, :], cst[:, C_IB:C_IB + 64], EG[:, 0, gcol:gcol + 1], None, ALU.mult, None,
                             [Bcst, BEG], [BDG[j]])
                      ck(30)
                      for name, srcf, Bsrc in (("Ka", lambda j: KaF32[:, j, :], BKa32), ("Bp", lambda j: BpF[:, j, :], BBp),
                                               ("Kp", lambda j: KpF[:, j, :], BKp),
                                               ("V", lambda j: base[:, 12 + j, cs_], Bbase[12:16])):
                          P_, BP_ = pa()
                          for j in range(4):
                              tr(P_[:, j * 128:(j + 1) * 128], srcf(j), [Bsrc[j]], [BP_])
                          cp("act", TM[name][:], P_[:], [BP_], [BTM[name]])

                      ck(31)
                      def unit_stages(h, u):
                          j, p0 = h // 2, 64 * (h % 2)
                          hs = slice(h * 64, (h + 1) * 64)
                          Bt_h, Kt_h, KR_h = BtF[p0:p0 + 64, j, :], KtF[p0:p0 + 64, j, :], KR[p0:p0 + 64, j, :]
                          Ka_h = KR[p0:p0 + 64, j, 0:128]
                          pn, Bpn = PNL[u], BPNL[u]
                          stages = []

                          def s_gram():
                              mm(PG[:, 0:256], Bt_h, KR_h, True, True, [BBt[j], BKR[j]], [BPG[0]])
                              mm(PG[:, 256:512], Kt_h, KR_h, True, True, [BKt[j], BKR[j]], [BPG[1]])
                              mm(pn[:, 384:512], Ka_h, Bt_h, True, True, [BKR[j], BBt[j]], [Bpn[2]])
                              tt("dve", LT[u][0][:, 0:128], PG[:, 0:128], mG[:, 0:128], ALU.mult, [BPG[0], Bcst],
                                 [BLT[u][0][0]])
                              tt("dve", Gm[u][:, 0:128], PG[:, 128:256], mG[:, 128:256], ALU.mult, [BPG[0], Bcst], [BGm[u]])
                              tt("dve", Gm[u][:, 128:384], PG[:, 256:512], mG[:, 256:512], ALU.mult, [BPG[1], Bcst], [BGm[u]])
                              tt("dve", Nn[u][0][:], pn[:, 384:512], mL, ALU.mult, [Bpn[2], Bcst], [BNn[u][0]])
                              cp("pool", LT[u][0][:, 128:256], identS[:], [BidR], [BLT[u][0][1]])
                          stages.append(s_gram)

                          ev = "act" if u % 2 == 0 else "dve"

                          def mk_level(k):
                              a, b = k % 2, (k + 1) % 2
                              def s():
                                  mm(pn[:, 0:128], Nn[u][a][:], LT[u][a][:, 0:128], True, True,
                                     [BNn[u][a], BLT[u][a][0]], [Bpn[0]])
                                  mm(pn[:, 128:256], Nn[u][a][:], LT[u][a][:, 128:256], True, False,
                                     [BNn[u][a], BLT[u][a][1]], [Bpn[0]])
                                  mm(pn[:, 128:256], identS[:], LT[u][a][:, 128:256], False, True,
                                     [BidR, BLT[u][a][1]], [Bpn[0]])
                                  mm(pn[:, 256:384], LT[u][a][:, 0:128], Nn[u][a][:], True, True,
                                     [BLT[u][a][0], BNn[u][a]], [Bpn[1]])
                                  cp(ev, LT[u][b][:], pn[:, 0:256], [Bpn[0]], [BLT[u][b][0], BLT[u][b][1]])
                                  cp(ev, Nn[u][b][:], pn[:, 256:384], [Bpn[1]], [BNn[u][b]])
                              return s
                          for k in range(0, 6):
                              stages.append(mk_level(k))

                          def s_l6():
                              mm(pn[:, 128:256], Nn[u][0][:], LT[u][0][:, 128:256], True, False,
                                 [BNn[u][0], BLT[u][0][1]], [Bpn[0]])
                              mm(pn[:, 128:256], identS[:], LT[u][0][:, 128:256], False, True,
                                 [BidR, BLT[u][0][1]], [Bpn[0]])
                              cp(ev, TtF[u][:], pn[:, 128:256], [Bpn[0]], [BTt[u]])
                              mm(PM[:, 0:64], Gm[u][:, 128:256], TM["V"][:, hs], True, True, [BGm[u], BTM["V"]], [BPM[0]])
                              cp(ev, AkV[u][:], PM[:, 0:64], [BPM[0]], [BAkV[u]])
                          stages.append(s_l6)

                          def s_x():
                              mm(PM[:, 64:128], TtF[u][:], TM["Ka"][:, hs], True, True, [BTt[u], BTM["Ka"]], [BPM[1]])
                              mm(PM[:, 128:192], TtF[u][:], AkV[u][:], True, True, [BTt[u], BAkV[u]], [BPM[1]])
                              if ev == "act":
                                  act(XnS[u][:], PM[:, 64:192], AF.Copy, [BPM[1]], [BXn[u]], scale=-1.0)
                              else:
                                  ts("dve", XnS[u][:], PM[:, 64:192], -1.0, None, ALU.mult, None, [BPM[1]], [BXn[u]])
                          stages.append(s_x)

                          def s_fin():
                              mm(PM[0:64, 256:384], identR[p0:p0 + 64, p0:p0 + 64], KR[p0:p0 + 64, j, 128:256], True, False,
                                 [BidR, BKR[j]], [BPM[3]])
                              mm(PM[0:64, 256:384], XnS[u][:, 0:64], Gm[u][:, 0:128], False, True, [BXn[u], BGm[u]], [BPM[3]])
                              cp(ev, RhT[u][:], PM[0:64, 256:384], [BPM[3]], [BRh[u]])
                              mm(PM[0:64, 192:256], XnS[u][:, 0:64], TM["Bp"][:, hs], True, False, [BXn[u], BTM["Bp"]], [BPM[2]])
                              mm(PM[0:64, 192:256], identR[p0:p0 + 64, p0:p0 + 64], DG[p0:p0 + 64, j, :], False, True,
                                 [BidR, BDG[j]], [BPM[2]])
                              cp(ev, PTs[u][:], PM[0:64, 192:256], [BPM[2]], [BPTs[u]])
                              mm(PY[:, hs], Gm[u][:, 256:384], TM["V"][:, hs], True, False, [BGm[u], BTM["V"]], [BPY])
                              mm(PY[:, hs], Gm[u][:, 0:128], XnS[u][:, 64:128], False, False, [BGm[u], BXn[u]], [BPY])
                              mm(PY[:, hs], RhT[u][:], Sb[:, h, :], False, True, [BRh[u], BSb], [BPY])
                              mm(PS_[0:64, hs], TM["Kp"][:, hs], TM["V"][:, hs], True, False, [BTM["Kp"], BTM["V"]], [BPS])
                              mm(PS_[0:64, hs], TM["Bp"][:, hs], XnS[u][:, 64:128], False, False, [BTM["Bp"], BXn[u]], [BPS])
                              mm(PS_[0:64, hs], PTs[u][:], Sb[:, h, :], False, True, [BPTs[u], BSb], [BPS])
                          stages.append(s_fin)
                          return stages

                      for g_ in range(8 // NU):
                          us_ = [unit_stages(NU * g_ + i_, i_) for i_ in range(NU)]
                          for sts_ in zip(*us_):
                              for f_ in sts_:
                                  f_()
                      cp("act", Sb[:].rearrange("p h n -> p (h n)"), PS_[0:64, :], [BPS], [BSb])

                  def post_tile(src, dst, t0, c0, X, BX):
                      cs_ = slice(c0, c0 + 128)
                      tt("dve", ysum, PY[:], y1t, ALU.add, [BPY, By1], [Bys])
                      y3 = ysum.rearrange("p (h n) -> p h n", h=8)
                      p.op("dve", lambda e: e.tensor_reduce(out=gst[:, :, 0], in_=y3, axis=AX.X, op=ALU.add),
                           reads=[Bys], writes=[Bgst])
                      tt("pool", yn, ysum, ysum, ALU.mult, [Bys], [Byn])
                      p.op("dve", lambda e: e.tensor_reduce(out=gst[:, :, 1], in_=yn.rearrange("p (h n) -> p h n", h=8),
                                                            axis=AX.X, op=ALU.add), reads=[Byn], writes=[Bgst])
                      ts("dve", gst[:, :, 0], gst[:, :, 0], 1.0 / 64, None, ALU.mult, None, [Bgst], [Bgst])
                      tt("dve", gst[:, :, 2], gst[:, :, 0], gst[:, :, 0], ALU.mult, [Bgst], [Bgst])
                      stt("dve", gst[:, :, 3], gst[:, :, 1], 1.0 / 64, gst[:, :, 2], ALU.mult, ALU.subtract, [Bgst], [Bgst])
                      ts("dve", gst[:, :, 3], gst[:, :, 3], GN_EPS, None, ALU.add, None, [Bgst], [Bgst])
                      act(gst[:, :, 3], gst[:, :, 3], AF.Ln, [Bgst], [Bgst])
                      act(gst[:, :, 3], gst[:, :, 3], AF.Exp, [Bgst], [Bgst], scale=-0.5)
                      for h in range(8):
                          ts("dve", yn[:, h * 64:(h + 1) * 64], ysum[:, h * 64:(h + 1) * 64],
                             gst[:, h, 0:1], gst[:, h, 3:4], ALU.subtract, ALU.mult, [Bys, Bgst], [Byn])
                      P_, BP_ = pa()
                      for j in range(4):
                          tr(P_[:, j * 128:(j + 1) * 128], yn[:, j * 128:(j + 1) * 128], [Byn], [BP_])
                      for j in range(4):
                          tt("dve", tB2[j % 2][0][:], P_[:, j * 128:(j + 1) * 128], G12[:, j, cs_], ALU.mult, [BP_, BG12[j]], [BtB2[j % 2][0]])
                          tt("pool", catT[:, 4 + j, :], tB2[j % 2][0][:], G12[:, 4 + j, cs_], ALU.add, [BtB2[j % 2][0], BG12[4 + j]], [Bcat])
                          cp("pool", catT[:, j, :], ycv[:, j, cs_], [Bycv[j]], [Bcat])
                      for half in range(2):
                          Po, BPo = pa()
                          for k in range(KC):
                              mm(Po[:], catT[:, k, :], wout[:, k, half * 512:(half + 1) * 512], k == 0, k == KC - 1,
                                 [Bcat, Bwout], [BPo])
                          tt("dve", xo[:, half * 512:(half + 1) * 512], Po[:], bc["gt"][:, half * 512:(half + 1) * 512],
                             ALU.mult, [BPo, Bbc["gt"]], [Bxo])
                      tt("pool", xo[:], xo[:], X[:], ALU.add, [Bxo, BX], [Bxo])
                      p.dma(dst[t0:t0 + 128, :], xo[:], reads=[Bxo])

                  def mixer_segment(src, dst, seg_T, setid, Sin, need_out, Sfin):
                      load_bc(setid, 0)
                      nsb = (seg_T + SBT - 1) // SBT
                      for d in range(2):
                          if Sin is None:
                              ts("dve", Sst[d][:].rearrange("p h n -> p (h n)"), cst[0:64, C_MG:C_MG + 512], 0.0, None,
                                 ALU.mult, None, [Bcst], [BS[d]])
                          else:
                              cp("pool", Sst[d][:], Sin[d][0][:], [Sin[d][1]], [BS[d]])
                      for s_ in range(nsb):
                          t0 = s_ * SBT
                          n = min(SBT, seg_T - t0)
                          stage1(src, t0, n, s_ == 0, seg_T)
                          ck(2)
                          for i in range(n // 128):
                              scan_tile(i * 128, 0, Sst[0], BS[0])
                              ck(3)
                              if need_out:
                                  cp("dve", sigT[:], PY[:], [BPY], [BsigT])
                                  p.dma(sp_y1[t0 + i * 128:t0 + (i + 1) * 128, :], sigT[:], reads=[BsigT])
                      p.barrier()
                      ck(4)
                      for s_ in reversed(range(nsb)):
                          t0 = s_ * SBT
                          n = min(SBT, seg_T - t0)
                          load_stage(t0, n)
                          for i in reversed(range(n // 128)):
                              scan_tile(i * 128, 1, Sst[1], BS[1])
                              if need_out:
                                  tk = t0 + i * 128
                                  p.dma(y1t, sp_y1[tk:tk + 128, :], writes=[By1])
                                  X, BX = xt[i % 2], Bxt[i % 2]
                                  p.dma(X[:], src[tk:tk + 128, :], writes=[BX])
                                  post_tile(src, dst, tk, i * 128, X, BX)
                      if Sfin is not None:
                          for d in range(2):
                              cp("pool", Sfin[d][0][:], Sst[d][:], [BS[d]], [Sfin[d][1]])
                      p.barrier()

                  mixer_segment(c_src, c_mid, CT, 1, None, not last, [(Sc[0], BSc[0]), (Sc[1], BSc[1])])
                  ck(5)
                  mixer_segment(x_src, x_mid, T, 0, [(Sc[0], BSc[0]), (Sc[1], BSc[1])], True, None)
                  p.barrier()
                  ck(6)

              with ExitStack() as s3:
                  def sb3(name, shape, dt=F32):
                      return sb(name, shape, dt, stack=s3)
                  wup = sb3("wup", [128, KC, 2 * DFF], BF16); Bwup = Buf()
                  wdn = sb3("wdn", [128, FC, D], BF16); Bwdn = Buf()
                  with ExitStack() as sw3:
                      wst3 = [sb(f"wst3{i}", [128, 2048], stack=sw3) for i in range(2)]; Bw3 = [Buf(), Buf()]
                      ii = 0
                      for k in range(KC):
                          for c_ in range(0, 2 * DFF, 2048):
                              w_ = min(2048, 2 * DFF - c_)
                              sg, Bsg = wst3[ii % 2], Bw3[ii % 2]
                              p.dma(sg[:, 0:w_], w_up[l][k * 128:(k + 1) * 128, c_:c_ + w_], writes=[Bsg])
                              cp("dve" if ii % 2 == 0 else "pool", wup[:, k, c_:c_ + w_], sg[:, 0:w_], [Bsg], [Bwup])
                              ii += 1
                      for c_ in range(FC):
                          sg, Bsg = wst3[ii % 2], Bw3[ii % 2]
                          p.dma(sg[:, 0:D], w_dn[l][c_ * 128:(c_ + 1) * 128, :], writes=[Bsg])
                          cp("dve" if ii % 2 == 0 else "pool", wdn[:, c_, :], sg[:, 0:D], [Bsg], [Bwdn])
                          ii += 1
                      p.barrier()
                  xt3 = [sb3(f"x3{i}", [128, D]) for i in range(2)]; Bx3 = [Buf() for _ in range(2)]
                  xr, Bxr = xt3[1], Bx3[1]
                  hb3 = sb3("hb3", [128, D]); Bhb3 = Buf()
                  ssq3 = sb3("ssq3", [128, 2]); Bssq3 = Buf()
                  hT3 = sb3("hT3", [128, KC, 640], BF16); BhT3 = Buf()
                  accs = [sb3(f"acc{i}", [128, 512]) for i in range(2)]; Baccs = [Buf(), Buf()]
                  actT = sb3("actT", [128, FC, 512], BF16); Bact = [Buf() for _ in range(FC)]
                  xo3, Bxo3 = hb3, Bhb3
                  fgb = sb3("fgb", [128, D]); Bfgb = Buf()
                  if last and final_norm:
                      p.dma(fgb[:], fin_g.partition_broadcast(128), writes=[Bfgb])

                  def ffn_segment(src, dst, seg_T, setid, gw, is_out):
                      load_bc(setid, 1)
                      BT = 512 if seg_T >= 512 else seg_T
                      nrow_tot = seg_T // gw
                      nrow = BT // gw
                      for b0 in range(0, seg_T, BT):
                          halo = gw if gw < seg_T else 0
                          lo, hi = max(0, b0 - halo), min(seg_T, b0 + BT + halo)
                          nw = hi - lo
                          off = b0 - lo
                          tpos, xi = lo, 0
                          while tpos < hi:
                              w_ = min(128, hi - tpos)
                              X, BX = xt3[xi % 2], Bx3[xi % 2]
                              if w_ < 128:
                                  p.op("pool", lambda e, X=X: e.memset(X[:], 0.0), writes=[BX])
                              p.dma(X[0:w_, :], src[tpos:tpos + w_, :], writes=[BX])
                              act(hb3[:], X[:], AF.Square, [BX], [Bhb3, Bssq3], accum=ssq3[:, 0:1])
                              ts("dve", ssq3[:, 1:2], ssq3[:, 0:1], 1.0 / D, RMS_EPS, ALU.mult, ALU.add, [Bssq3], [Bssq3])
                              act(ssq3[:, 1:2], ssq3[:, 1:2], AF.Ln, [Bssq3], [Bssq3])
                              act(ssq3[:, 1:2], ssq3[:, 1:2], AF.Exp, [Bssq3], [Bssq3], scale=-0.5)
                              stt("dve", hb3[:], X[:], ssq3[:, 1:2], bc["gs"][:], ALU.mult, ALU.mult,
                                  [BX, Bssq3, Bbc["gs"]], [Bhb3])
                              tt("pool", hb3[:], hb3[:], bc["sh"][:], ALU.add, [Bhb3, Bbc["sh"]], [Bhb3])
                              col0 = tpos - lo
                              for half in range(2):
                                  P_, BP_ = pa()
                                  for q in range(4):
                                      k = half * 4 + q
                                      tr(P_[:, q * 128:(q + 1) * 128], hb3[:, k * 128:(k + 1) * 128], [Bhb3], [BP_])
                                  cp("act", hT3[:, half * 4:half * 4 + 4, col0:col0 + w_],
                                     P_[:].rearrange("p (q n) -> p q n", q=4)[:, :, 0:w_], [BP_], [BhT3])
                              tpos += w_
                              xi += 1
                          wr0 = off // gw
                          for c_ in range(FC):
                              acc, Bacc = accs[c_ % 2], Baccs[c_ % 2]
                              sil, Bsil = acc, Bacc
                              if c_ % 2 == 0:
                                  GA, BGA, GB, BGB = PG, BPG[0], PN[0], BPN[0][0]
                              else:
                                  GA, BGA, GB, BGB = PY, BPY, PS_, BPS
                              segs = [(0, min(nw, 512), GA, BGA)]
                              if nw > 512:
                                  segs.append((512, nw, GB, BGB))
                              for (a_, b_, P_, BP_) in segs:
                                  for k in range(KC):
                                      mm(P_[:, 0:b_ - a_], wup[:, k, c_ * 128:(c_ + 1) * 128], hT3[:, k, a_:b_],
                                         k == 0, k == KC - 1, [Bwup, BhT3], [BP_])
                              wc = lambda ty, tx: vec[:, 48 + (ty * 3 + tx) * FC + c_:48 + (ty * 3 + tx) * FC + c_ + 1]
                              bcol = vec[:, 48 + 9 * FC + c_:48 + 9 * FC + c_ + 1]

                              def tap(ty, tx, first):
                                  dy, dx = ty - 1, tx - 1
                                  r0g = b0 // gw
                                  rows_ok = [r for r in range(nrow) if 0 <= r0g + r + dy < nrow_tot]
                                  if not rows_ok:
                                      return
                                  ra, rb = rows_ok[0], rows_ok[-1] + 1
                                  ca, cb2 = (1, gw) if dx == -1 else ((0, gw - 1) if dx == 1 else (0, gw))
                                  rsplit = 512 // gw
                                  r = ra
                                  while r < rb:
                                      wr = wr0 + r + dy
                                      if wr < rsplit:
                                          re_ = min(rb, rsplit - wr0 - dy)
                                          P_, B_, wbase = GA, BGA, 0
                                      else:
                                          re_ = rb
                                          P_, B_, wbase = GB, BGB, rsplit
                                      nr = re_ - r
                                      iv = P_[:, (wr - wbase) * gw:(wr - wbase + nr) * gw].rearrange("p (r c) -> p r c", c=gw)[:, :, ca + dx:cb2 + dx]
                                      ov = acc[:, r * gw:(r + nr) * gw].rearrange("p (r c) -> p r c", c=gw)[:, :, ca:cb2]
                                      if first:
                                          act(ov, iv, AF.Identity, [B_, Bvec], [Bacc], bias=bcol, scale=wc(ty, tx))
                                      else:
                                          stt("dve", ov, iv, wc(ty, tx), ov, ALU.mult, ALU.add, [B_, Bvec, Bacc], [Bacc])
                                      r = re_
                              tap(1, 1, True)
                              for ty in range(3):
                                  for tx in range(3):
                                      if (ty, tx) != (1, 1) and not (gw >= seg_T and ty != 1):
                                          tap(ty, tx, False)
                              act(sil[:, 0:BT], acc[:, 0:BT], AF.Silu, [], [Bacc])
                              Pv, BPv = pa()
                              for k in range(KC):
                                  mm(Pv[:, 0:BT], wup[:, k, DFF + c_ * 128:DFF + (c_ + 1) * 128], hT3[:, k, off:off + BT],
                                     k == 0, k == KC - 1, [Bwup, BhT3], [BPv])
                              tt("dve", actT[:, c_, 0:BT], sil[:, 0:BT], Pv[:, 0:BT], ALU.mult, [Bsil, BPv], [Bact[c_]])
                          for i in range(BT // 128):
                              tk = b0 + i * 128
                              p.dma(xr[:], src[tk:tk + 128, :], writes=[Bxr])
                              for half in range(2):
                                  Po, BPo = pa()
                                  for c_ in range(FC):
                                      mm(Po[:], actT[:, c_, i * 128:(i + 1) * 128], wdn[:, c_, half * 512:(half + 1) * 512],
                                         c_ == 0, c_ == FC - 1, [Bact[c_], Bwdn], [BPo])
                                  tt("dve", xo3[:, half * 512:(half + 1) * 512], Po[:], bc["gt"][:, half * 512:(half + 1) * 512],
                                     ALU.mult, [BPo, Bbc["gt"]], [Bxo3])
                              tt("pool", xo3[:], xo3[:], xr[:], ALU.add, [Bxo3, Bxr], [Bxo3])
                              if is_out and final_norm:
                                  act(xt3[0][:], xo3[:], AF.Square, [Bxo3], [Bx3[0], Bssq3], accum=ssq3[:, 0:1])
                                  ts("dve", ssq3[:, 1:2], ssq3[:, 0:1], 1.0 / D, RMS_EPS, ALU.mult, ALU.add, [Bssq3], [Bssq3])
                                  act(ssq3[:, 1:2], ssq3[:, 1:2], AF.Ln, [Bssq3], [Bssq3])
                                  act(ssq3[:, 1:2], ssq3[:, 1:2], AF.Exp, [Bssq3], [Bssq3], scale=-0.5)
                                  stt("dve", xo3[:], xo3[:], ssq3[:, 1:2], fgb[:], ALU.mult, ALU.mult, [Bxo3, Bssq3, Bfgb], [Bxo3])
                              p.dma(dst[tk:tk + 128, :], xo3[:], reads=[Bxo3])

                  if not last:
                      ffn_segment(c_mid, c_dst, CT, 1, CT, False)
                  ffn_segment(x_mid, x_dst, T, 0, GW, last)
                  p.barrier()
        except _Stop:
            pass
        p.barrier()
        nc._prog_ninst = p.ninst
    return nc


def prep_inputs(x, c, ctx, c_ctx, ada_w, ada_b, norm1_g, norm2_g, w_in, conv_a_w, rw_w0, rw_w_up, rw_a0, rw_a_up,
                rw_k_k, rw_k_a, rw_r_k, rw_g_up, rw_ln_g, rw_ln_b, w_out, ffn_w_up, ffn_conv_w, ffn_conv_b,
                ffn_w_down, final_g, b):
    L = ada_w.shape[0]
    f = lambda a: np.ascontiguousarray(a, dtype=np.float32)
    cs = np.stack([c[b].reshape(KC, 128).T, c_ctx.reshape(KC, 128).T], axis=-1)
    rows = np.concatenate([ada_b, norm1_g, norm2_g], axis=1)[:, None, :].repeat(2, axis=1)

    def ch(v, n):
        return v.reshape(n, 128).T
    vec = np.zeros((L, 128, NVEC), np.float32)
    for l in range(L):
        cols = [ch(rw_k_k[l], 4), ch(rw_k_a[l], 4), ch(rw_r_k[l].reshape(-1), 4), ch(rw_ln_g[l], 4), ch(rw_ln_b[l], 4),
                ch(conv_a_w[l, 0], 4), ch(conv_a_w[l, 1], 4), ch(conv_a_w[l, 2], 4),
                ch(rw_w0[l, 0], 4), ch(rw_w0[l, 1], 4), ch(rw_a0[l, 0], 4), ch(rw_a0[l, 1], 4)]
        for ty in range(3):
            for tx in range(3):
                cols.append(ch(ffn_conv_w[l, ty, tx], FC))
        cols.append(ch(ffn_conv_b[l], FC))
        vec[l] = np.concatenate(cols, axis=1)
    rowv = rw_w0.reshape(L, 1, 1024)
    lup = np.concatenate([rw_w_up.transpose(0, 2, 1, 3), rw_a_up.transpose(0, 2, 1, 3)], axis=1)
    return {
        "x": f(x[b]), "ctx": f(ctx[b]), "cs": f(cs), "cst": make_consts(), "ada_w": f(ada_w), "rows": f(rows),
        "fin_g": f(final_g), "w_in": f(w_in), "w_out": f(w_out), "w_up": f(ffn_w_up), "w_dn": f(ffn_w_down),
        "vec": f(vec), "rowv": f(rowv), "lup": f(lup), "gup": f(rw_g_up),
    }


def kernel(**inputs):
    inputs = {k: np.asarray(v) for k, v in inputs.items()}
    B, T, _ = inputs["x"].shape
    CT = inputs["ctx"].shape[1]
    nc = build_program(T, CT, inputs["ada_w"].shape[0])
    in_maps = [prep_inputs(b=b, **inputs) for b in range(B)]
    res = run_bass_kernel_spmd(nc, in_maps, core_ids=list(range(B)))
    return np.stack([r["out"] for r in res.results], axis=0).astype(np.float32)
```

```python
import math
import numpy as np
from contextlib import ExitStack
import concourse.bass as bass
import concourse.mybir as mybir
from concourse.bass_utils import run_bass_kernel_spmd

F32 = mybir.dt.float32
BF16 = mybir.dt.bfloat16
F32R = mybir.dt.float32r
AF = mybir.ActivationFunctionType
ALU = mybir.AluOpType
AX = mybir.AxisListType

D = 1024
KC = 8
PROJ = 3328
NPC = 26
DFF = 2816
FC = 22
GW = 64
DS = math.exp(-0.5)
RMS_EPS = 1e-6
GN_EPS = 64e-5
NVEC = 48 + 10 * FC
SCAN_DT = BF16

C_ID = 0
C_MG = 128
C_ML = C_MG + 1024
C_TF = C_ML + 256
C_BO = C_TF + 512
C_ON = C_BO + 128
C_IB = C_ON + 128
NCST = C_IB + 64


def make_consts():
    c = np.zeros((128, NCST), np.float32)
    s = np.arange(128)[:, None]
    t = np.arange(128)[None, :]
    c[:, C_ID:C_ID + 128] = np.eye(128)
    for d in range(2):
        lt = (s < t) if d == 0 else (s > t)
        le = (s <= t) if d == 0 else (s >= t)
        g = c[:, C_MG + 512 * d:C_MG + 512 * (d + 1)]
        g[:, 0:128] = -1.0 * lt
        g[:, 128:256] = le
        g[:, 256:384] = lt
        g[:, 384:512] = le
        c[:, C_ML + 128 * d:C_ML + 128 * (d + 1)] = -1.0 * lt.T
        f = c[:, C_TF + 256 * d:C_TF + 256 * (d + 1)]
        f[:, 0:128] = -DS * le
        f[:, 128:256] = -DS * lt
    c[:, C_BO:C_BO + 128] = (s // 64 == t // 64)
    c[:, C_ON:C_ON + 128] = 1.0
    c[:, C_IB:C_IB + 64] = (s % 64 == np.arange(64)[None, :])
    return c


class Buf:
    __slots__ = ("name", "w", "rd", "ps")

    def __init__(self, name="", ps=False):
        self.name = name
        self.w = None
        self.rd = []
        self.ps = ps


ENGMAP = {"pe": "tensor", "dve": "vector", "act": "scalar", "pool": "gpsimd", "sp": "sync"}


class Eng:
    def __init__(self, name, sem, h):
        self.name = name
        self.sem = sem
        self.h = h
        self.cnt = 0
        self.waited = {}


class Prog:
    def __init__(self, nc, sems, dma_sems):
        self.nc = nc
        self.E = {n: Eng(n, sems[n], getattr(nc, ENGMAP[n])) for n in ENGMAP}
        self.dma_sems = dma_sems
        self.dma_cnt = [0] * len(dma_sems)
        self.dma_rr = 0
        self.ninst = 0
        self.stop = False

    def _deps(self, reads, writes):
        d = []
        for b in reads:
            if b.w is not None:
                d.append(b.w)
        for b in writes:
            if b.w is not None:
                d.append(b.w)
            d.extend(b.rd)
        return d

    def _waits(self, e, deps, skip_self):
        need = {}
        for (sem, val, owner) in deps:
            if skip_self and owner is e:
                continue
            k = id(sem)
            if e.waited.get(k, 0) >= val:
                continue
            if k not in need or need[k][1] < val:
                need[k] = (sem, val)
        for k, (sem, val) in need.items():
            e.waited[k] = val
            e.h.wait_ge(sem, val)

    def op(self, en, fn, reads=(), writes=()):
        if self.stop:
            return
        e = self.E[en]
        deps = self._deps(reads, writes)
        for b in reads:
            if b.ps:
                deps.extend(t for t in b.rd if t[2] is not e)
        self._waits(e, deps, skip_self=(en == "pe"))
        e.cnt += 1
        tok = (e.sem, e.cnt, e)
        fn(e.h).then_inc(e.sem, 1)
        for b in writes:
            b.w = tok
            b.rd = []
        for b in reads:
            b.rd.append(tok)
        self.ninst += 1

    def dma(self, out, in_, reads=(), writes=(), q="sp", slow=False):
        if self.stop:
            return
        e = self.E[q]
        deps = self._deps(reads, writes)
        i = self.dma_rr
        self.dma_rr = (self.dma_rr + 1) % len(self.dma_sems)
        sem = self.dma_sems[i]
        if self.dma_cnt[i] > 0:
            deps.append((sem, self.dma_cnt[i], None))
        self._waits(e, deps, skip_self=False)
        self.dma_cnt[i] += 16
        tok = (sem, self.dma_cnt[i], None)
        if slow:
            e.h.dma_start(out=out, in_=in_, allow_slow_non_contiguous=True).then_inc(sem, 16)
        else:
            e.h.dma_start(out=out, in_=in_).then_inc(sem, 16)
        for b in writes:
            b.w = tok
            b.rd = []
        for b in reads:
            b.rd.append(tok)
        self.ninst += 1
        return tok

    def barrier(self):
        if self.stop:
            return
        toks = [(e.sem, e.cnt, e) for e in self.E.values() if e.cnt > 0]
        toks += [(s, c, None) for s, c in zip(self.dma_sems, self.dma_cnt) if c > 0]
        for e in self.E.values():
            self._waits(e, toks, skip_self=True)


class _Stop(Exception):
    pass


def build_program(T, CT, L=2, final_norm=True, dbg=None):
    assert T % 512 == 0 and CT % 128 == 0
    nc = bass.Bass("TRN2", target_bir_lowering=False)
    dt_ = nc.dram_tensor

    def din(name, shape, dt=F32):
        return dt_(name, shape, dt, kind="ExternalInput").ap()

    def dint(name, shape, dt=F32):
        return dt_(name, shape, dt, kind="Internal").ap()

    x_in = din("x", [T, D])
    ctx_in = din("ctx", [CT, D])
    cs_in = din("cs", [128, KC, 2])
    cst_in = din("cst", [128, NCST])
    ada_w = din("ada_w", [L, D, 6 * D])
    rows_in = din("rows", [L, 2, 6 * D + 2 * D])
    fin_g = din("fin_g", [D])
    w_in = din("w_in", [L, D, PROJ])
    w_out = din("w_out", [L, D, D])
    w_up = din("w_up", [L, D, 2 * DFF])
    w_dn = din("w_dn", [L, DFF, D])
    vec_in = din("vec", [L, 128, NVEC])
    rowv_in = din("rowv", [L, 1, 1024])
    lup_in = din("lup", [L, 128, 2, 512])
    gup_in = din("gup", [L, 128, 512])
    out = dt_("out", [T, D], F32, kind="ExternalOutput").ap()

    xs = [dint("xs0", [T, D]), dint("xs1", [T, D])]
    cxs = [dint("cxs0", [CT, D]), dint("cxs1", [CT, D])]
    modr = dint("modr", [2, 6, D])
    TS = max(T, CT)
    sp_base = dint("sp_base", [16, 128, TS])
    sp_wa = dint("sp_wa", [128, TS])
    sp_g = dint("sp_g", [8, 128, TS])
    sp_ycv = dint("sp_ycv", [4, 128, TS + 2], BF16)
    sp_y1 = dint("sp_y1", [TS, 512])

    st = ExitStack()
    with st:
        sems = {n: st.enter_context(nc.semaphore(n)) for n in ENGMAP}
        dsem = [st.enter_context(nc.semaphore(f"dq{i}")) for i in range(24)]
        p = Prog(nc, sems, dsem)

        uid = [0]

        def sb(name, shape, dt=F32, stack=st):
            uid[0] += 1
            return stack.enter_context(nc.sbuf_tensor(f"s{uid[0]}_{name}", shape, dt))

        def ps(name, shape, dt=F32):
            uid[0] += 1
            return st.enter_context(nc.psum_tensor(f"p{uid[0]}_{name}", shape, dt))

        cst = sb("cst", [128, NCST]); Bcst = Buf()
        vec = sb("vec", [128, NVEC]); Bvec = Buf()
        omka = sb("omka", [128, 4]); Bomka = Buf()
        bc = {n: sb("bc_" + n, [128, D]) for n in ("gs", "sh", "gt")}
        Bbc = {n: Buf() for n in bc}
        PA = [ps("PA0", [128, 512]), ps("PA1", [128, 512])]; BPA = [Buf(ps=True), Buf(ps=True)]
        _bpg = Buf(ps=True)
        PG = ps("PG", [128, 512]); BPG = [_bpg, _bpg]
        PN = [ps("PN0", [128, 512]), ps("PN1", [128, 512])]
        _bpn = [Buf(ps=True), Buf(ps=True)]
        BPN = [[_bpn[0]] * 3, [_bpn[1]] * 3]
        _bpm = Buf(ps=True)
        PM = ps("PM", [128, 512]); BPM = [_bpm] * 5
        PS_ = ps("PS", [128, 512]); BPS = Buf(ps=True)
        PY = ps("PY", [128, 512]); BPY = Buf(ps=True)
        pa_rr = [0]

        def pa():
            i = pa_rr[0]
            pa_rr[0] ^= 1
            return PA[i], BPA[i]

        ident = cst[:, C_ID:C_ID + 128]
        p.dma(cst[:], cst_in, writes=[Bcst])
        identR = sb("identR", [128, 128], SCAN_DT); BidR = Buf()
        identS = identR
        p.op("dve", lambda e: e.tensor_copy(out=identR[:], in_=ident), reads=[Bcst], writes=[BidR])

        def mm(o, l, r, start, stop, reads, writes):
            p.op("pe", lambda e: e.matmul(o, l, r, start=start, stop=stop), reads=reads, writes=writes)

        def tr(o, i_, reads, writes):
            p.op("pe", lambda e: e.transpose(o, i_, ident), reads=list(reads) + [Bcst], writes=writes)

        def act(o, i_, func, reads, writes, bias=None, scale=None, accum=None):
            kw = {}
            if bias is not None:
                kw["bias"] = bias
            if scale is not None:
                kw["scale"] = scale
            if accum is not None:
                kw["accum_out"] = accum
            p.op("act", lambda e: e.activation(out=o, in_=i_, func=func, **kw), reads=reads, writes=writes)

        def tt(en, o, a, b, op, reads, writes):
            p.op(en, lambda e: e.tensor_tensor(out=o, in0=a, in1=b, op=op), reads=reads, writes=writes)

        def ts(en, o, a, s1, s2, op0, op1, reads, writes):
            if s2 is None:
                p.op(en, lambda e: e.tensor_scalar(out=o, in0=a, scalar1=s1, scalar2=None, op0=op0),
                     reads=reads, writes=writes)
            else:
                p.op(en, lambda e: e.tensor_scalar(out=o, in0=a, scalar1=s1, scalar2=s2, op0=op0, op1=op1),
                     reads=reads, writes=writes)

        def stt(en, o, a, s, b, op0, op1, reads, writes):
            p.op(en, lambda e: e.scalar_tensor_tensor(out=o, in0=a, scalar=s, in1=b, op0=op0, op1=op1),
                 reads=reads, writes=writes)

        def cp(en, o, i_, reads, writes):
            if en == "act":
                act(o, i_, AF.Copy, reads, writes)
            else:
                p.op(en, lambda e: e.tensor_copy(out=o, in_=i_), reads=reads, writes=writes)

        def ck(k):
            if dbg == k and not p.stop:
                p.barrier()
                p.stop = True

        try:
          for l in range(L):
              last = (l == L - 1)
              x_src = x_in if l == 0 else xs[1]
              c_src = ctx_in if l == 0 else cxs[1]
              x_mid, c_mid = xs[0], cxs[0]
              x_dst, c_dst = xs[1], cxs[1]
              if last:
                  x_dst = out

              p.barrier()
              with ExitStack() as s0:
                  rows = sb("rows", [2, 8 * D], stack=s0); Brows = Buf()
                  mod = sb("mod", [2, 6 * D], stack=s0); Bmod = Buf()
                  drv = sb("drv", [2, 6, D], stack=s0); Bdrv = Buf()
                  cs = sb("cs", [128, KC, 2], stack=s0); Bcs = Buf()
                  scs = sb("scs", [128, KC, 2], stack=s0); Bscs = Buf()
                  stg = [sb(f"adastg{i}", [128, KC, 512], stack=s0) for i in range(2)]; Bstg = [Buf(), Buf()]
                  p.dma(rows[:], rows_in[l], writes=[Brows])
                  p.dma(cs[:], cs_in, writes=[Bcs])
                  p.dma(vec[:], vec_in[l], writes=[Bvec])
                  ts("dve", omka[:], vec[:, 4:8], -1.0, 1.0, ALU.mult, ALU.add, [Bvec], [Bomka])
                  act(scs[:], cs[:], AF.Silu, [Bcs], [Bscs])
                  for cb in range(12):
                      sg, Bsg = stg[cb % 2], Bstg[cb % 2]
                      p.dma(sg[:], ada_w[l][:, cb * 512:(cb + 1) * 512].rearrange("(k p) n -> p k n", p=128),
                            writes=[Bsg])
                      P_, BP_ = pa()
                      for k in range(KC):
                          mm(P_[0:2, :], scs[:, k, :], sg[:, k, :], k == 0, k == KC - 1, [Bscs, Bsg], [BP_])
                      tt("dve", mod[:, cb * 512:(cb + 1) * 512], P_[0:2, :], rows[:, cb * 512:(cb + 1) * 512],
                         ALU.add, [BP_, Brows], [Bmod])
                  stt("dve", drv[:, 0, :], mod[:, D:2 * D], 1.0, rows[:, 6 * D:7 * D], ALU.add, ALU.mult,
                      [Bmod, Brows], [Bdrv])
                  cp("dve", drv[:, 1, :], mod[:, 0:D], [Bmod], [Bdrv])
                  cp("dve", drv[:, 2, :], mod[:, 2 * D:3 * D], [Bmod], [Bdrv])
                  stt("dve", drv[:, 3, :], mod[:, 4 * D:5 * D], 1.0, rows[:, 7 * D:8 * D], ALU.add, ALU.mult,
                      [Bmod, Brows], [Bdrv])
                  cp("dve", drv[:, 4, :], mod[:, 3 * D:4 * D], [Bmod], [Bdrv])
                  cp("dve", drv[:, 5, :], mod[:, 5 * D:6 * D], [Bmod], [Bdrv])
                  Bmodr = Buf()
                  p.dma(modr, drv[:], reads=[Bdrv], writes=[Bmodr])
                  p.barrier()
              ck(0)

              def load_bc(setid, which):
                  for n, r in (("gs", 0), ("sh", 1), ("gt", 2)):
                      p.dma(bc[n][:], modr[setid, 3 * which + r].partition_broadcast(128),
                            reads=[Bmodr], writes=[Bbc[n]])

              with ExitStack() as s1:
                  def sb1(name, shape, dt=F32):
                      return sb(name, shape, dt, stack=s1)
                  rowv = sb1("rowv", [1, 1024]); Browv = Buf()
                  lup = sb1("lup", [128, 2, 512]); Blup = Buf()
                  gup = sb1("gup", [128, 512], BF16); Bgup = Buf()
                  Sst = [sb1(f"S{d}", [64, 8, 64], SCAN_DT) for d in range(2)]; BS = [Buf(), Buf()]
                  Sc = [sb1(f"Sc{d}", [64, 8, 64], SCAN_DT) for d in range(2)]; BSc = [Buf(), Buf()]
                  p.dma(rowv[:], rowv_in[l], writes=[Browv])
                  p.dma(lup[:], lup_in[l], writes=[Blup])
                  win = sb1("win", [128, KC, PROJ], BF16); Bwin = Buf()
                  wout = sb1("wout", [128, KC, D], BF16); Bwout = Buf()
                  with ExitStack() as sw:
                      wstg = [sb(f"wstg{i}", [128, PROJ], stack=sw) for i in range(2)]; Bwstg = [Buf(), Buf()]
                      for k in range(KC):
                          sg, Bsg = wstg[k % 2], Bwstg[k % 2]
                          p.dma(sg[:], w_in[l][k * 128:(k + 1) * 128, :], writes=[Bsg])
                          cp("dve" if k % 2 == 0 else "pool", win[:, k, :], sg[:], [Bsg], [Bwin])
                      for k in range(KC):
                          sg, Bsg = wstg[k % 2], Bwstg[k % 2]
                          p.dma(sg[:, 0:D], w_out[l][k * 128:(k + 1) * 128, :], writes=[Bsg])
                          cp("dve" if k % 2 == 0 else "pool", wout[:, k, :], sg[:, 0:D], [Bsg], [Bwout])
                      p.dma(wstg[0][:, 0:512], gup_in[l], writes=[Bwstg[0]])
                      cp("dve", gup[:], wstg[0][:, 0:512], [Bwstg[0]], [Bgup])
                      p.barrier()
                  ck(1)

                  SBT = 256
                  xt = [sb1(f"xt{i}", [128, D]) for i in range(2)]; Bxt = [Buf(), Buf()]
                  hb = sb1("hb", [128, D]); Bhb = Buf()
                  ssq = sb1("ssq", [128, 2]); Bssq = Buf()
                  hT = sb1("hT", [128, KC, SBT], BF16); BhT = Buf()
                  base = sb1("base", [128, 16, SBT]); Bbase = [Buf() for _ in range(16)]
                  WA = sb1("WA", [128, SBT]); BWA = Buf()
                  sgl = sb1("sgl", [128, SBT], BF16); Bsgl = Buf()
                  G12 = sb1("G12", [128, 8, SBT]); BG12 = [Buf() for _ in range(8)]
                  ycv = sb1("ycv", [128, 4, SBT], BF16); Bycv = [Buf() for _ in range(4)]
                  ubuf_t = sb1("ubuf", [128, 4 * (SBT + 2)]); Bub = [Buf() for _ in range(4)]
                  ubuf = ubuf_t[:].rearrange("p (j n) -> p j n", j=4)
                  cbb_t = sb1("cbb", [128, 4 * (SBT + 1)]); Bcbb = [Buf() for _ in range(4)]
                  cbb = cbb_t[:].rearrange("p (j n) -> p j n", j=4)
                  tmpA2 = [[sb1(f"tmpA{q}{i}", [128, SBT]) for i in range(4)] for q in range(2)]
                  BtA2 = [[Buf() for _ in range(4)] for q in range(2)]
                  tmpA, BtA = tmpA2[0], BtA2[0]
                  KR = sb1("KR", [128, 4, 256], SCAN_DT); BKR = [Buf() for _ in range(4)]
                  BtF = sb1("BtF", [128, 4, 128], SCAN_DT); BBt = [Buf() for _ in range(4)]
                  KtF = sb1("KtF", [128, 4, 128], SCAN_DT); BKt = [Buf() for _ in range(4)]
                  BpF = sb1("BpF", [128, 4, 128]); BBp = [Buf() for _ in range(4)]
                  KpF = sb1("KpF", [128, 4, 128]); BKp = [Buf() for _ in range(4)]
                  KaF32 = sb1("KaF32", [128, 4, 128]); BKa32 = [Buf() for _ in range(4)]
                  DG = sb1("DG", [128, 4, 64], SCAN_DT); BDG = [Buf() for _ in range(4)]
                  EG2 = [sb1(f"EG{q}", [128, 3, 128]) for q in range(2)]; BEG2 = [Buf(), Buf()]
                  aFt2 = [sb1(f"aFt{q}", [128, 128]) for q in range(2)]; BaF2 = [Buf(), Buf()]
                  tB2 = [[sb1(f"tB{q}{i}", [128, 128]) for i in range(3)] for q in range(2)]
                  BtB2 = [[Buf() for _ in range(3)] for q in range(2)]
                  tB, BtB = tB2[0], BtB2[0]
                  sigT = sb1("sigT", [128, 512]); BsigT = Buf()
                  TM = {n: sb1("TM_" + n, [128, 512], SCAN_DT) for n in ("Ka", "Bp", "Kp", "V")}
                  BTM = {n: Buf() for n in TM}
                  NU = 4
                  LT = [[sb1(f"LT{u}{i}", [128, 256], SCAN_DT) for i in range(2)] for u in range(NU)]
                  BLT = [[[Buf(), Buf()] for i in range(2)] for u in range(NU)]
                  Nn = [[sb1(f"Nn{u}{i}", [128, 128], SCAN_DT) for i in range(2)] for u in range(NU)]
                  BNn = [[Buf() for i in range(2)] for u in range(NU)]
                  Gm = [sb1(f"Gm{u}", [128, 384], SCAN_DT) for u in range(NU)]; BGm = [Buf() for _ in range(NU)]
                  TtF = [sb1(f"TtF{u}", [128, 128], SCAN_DT) for u in range(NU)]; BTt = [Buf() for _ in range(NU)]
                  AkV = [sb1(f"AkV{u}", [128, 64], SCAN_DT) for u in range(NU)]; BAkV = [Buf() for _ in range(NU)]
                  XnS = [sb1(f"XnS{u}", [128, 128], SCAN_DT) for u in range(NU)]; BXn = [Buf() for _ in range(NU)]
                  PTs = [sb1(f"PTs{u}", [64, 64], SCAN_DT) for u in range(NU)]; BPTs = [Buf() for _ in range(NU)]
                  RhT = [sb1(f"RhT{u}", [64, 128], SCAN_DT) for u in range(NU)]; BRh = [Buf() for _ in range(NU)]
                  PNL = [PN[0], PN[1], PA[0], PA[1]]
                  BPNL = [BPN[0], BPN[1], [BPA[0]] * 3, [BPA[1]] * 3]
                  ysum = ubuf_t[:, 0:512]; Bys = Buf()
                  yn = ubuf_t[:, 512:1024]; Byn = Buf()
                  y1t = cbb_t[:, 0:512]; By1 = Buf()
                  gst = sb1("gst", [128, 8, 4]); Bgst = Buf()
                  catT = hT[:, :, 0:128]; Bcat = Buf()
                  xo, Bxo = hb, Bhb

                  def vcol(i, j):
                      return vec[:, 4 * i + j:4 * i + j + 1]

                  def norm_tile(src_ap, xt_i, col0):
                      X, BX = xt[xt_i], Bxt[xt_i]
                      p.dma(X[:], src_ap, writes=[BX])
                      act(hb[:], X[:], AF.Square, [BX], [Bhb, Bssq], accum=ssq[:, 0:1])
                      ts("dve", ssq[:, 1:2], ssq[:, 0:1], 1.0 / D, RMS_EPS, ALU.mult, ALU.add, [Bssq], [Bssq])
                      act(ssq[:, 1:2], ssq[:, 1:2], AF.Ln, [Bssq], [Bssq])
                      act(ssq[:, 1:2], ssq[:, 1:2], AF.Exp, [Bssq], [Bssq], scale=-0.5)
                      stt("dve", hb[:], X[:], ssq[:, 1:2], bc["gs"][:], ALU.mult, ALU.mult,
                          [BX, Bssq, Bbc["gs"]], [Bhb])
                      tt("pool", hb[:], hb[:], bc["sh"][:], ALU.add, [Bhb, Bbc["sh"]], [Bhb])
                      for half in range(2):
                          P_, BP_ = pa()
                          for q in range(4):
                              k = half * 4 + q
                              tr(P_[:, q * 128:(q + 1) * 128], hb[:, k * 128:(k + 1) * 128], [Bhb], [BP_])
                          cp("act", hT[:, half * 4:half * 4 + 4, col0:col0 + 128],
                             P_[:].rearrange("p (q n) -> p q n", q=4), [BP_], [BhT])
                      return X, BX

                  def proj_chunk(c, n):
                      P_, BP_ = pa()
                      for k in range(KC):
                          mm(P_[:, 0:n], win[:, k, c * 128:(c + 1) * 128], hT[:, k, 0:n], k == 0, k == KC - 1,
                             [Bwin, BhT], [BP_])
                      return P_, BP_

                  def stage1(seg_src, t0, n, first, seg_T):
                      for i in range(n // 128):
                          norm_tile(seg_src[t0 + i * 128:t0 + (i + 1) * 128, :], i % 2, i * 128)
                      for j in range(4):
                          tmpA, BtA = tmpA2[j % 2], BtA2[j % 2]
                          if first:
                              p.op("pool", lambda e, j=j: e.memset(ubuf[:, j, 0:2], 0.0), writes=[Bub[j]])
                              p.op("pool", lambda e, j=j: e.memset(cbb[:, j, 0:1], 0.0), writes=[Bcbb[j]])
                          Pb, BPb = proj_chunk(j, n)
                          cp("act", cbb[:, j, 1:n + 1], Pb[:, 0:n], [BPb], [Bcbb[j]])
                          Pc, BPc = proj_chunk(4 + j, n)
                          cp("act", tmpA[0][:, 0:n], Pc[:, 0:n], [BPc], [BtA[0]])
                          Px, BPx = proj_chunk(8 + j, n)
                          tt("dve", ubuf[:, j, 2:n + 2], tmpA[0][:, 0:n], Px[:, 0:n], ALU.mult, [BtA[0], BPx], [Bub[j]])
                          ts("dve", tmpA[1][:, 0:n], ubuf[:, j, 0:n], vcol(5, j), None, ALU.mult, None,
                             [Bub[j], Bvec], [BtA[1]])
                          stt("dve", tmpA[1][:, 0:n], ubuf[:, j, 1:n + 1], vcol(6, j), tmpA[1][:, 0:n], ALU.mult, ALU.add,
                              [Bub[j], Bvec, BtA[1]], [BtA[1]])
                          stt("dve", tmpA[1][:, 0:n], ubuf[:, j, 2:n + 2], vcol(7, j), tmpA[1][:, 0:n], ALU.mult, ALU.add,
                              [Bub[j], Bvec, BtA[1]], [BtA[1]])
                          tt("pool", ycv[:, j, 0:n], tmpA[1][:, 0:n], cbb[:, j, 0:n], ALU.mult, [BtA[1], Bcbb[j]], [Bycv[j]])
                          p.dma(sp_ycv[j][:, t0:t0 + n], ycv[:, j, 0:n], reads=[Bycv[j]])
                          if t0 + n == seg_T:
                              ts("dve", tmpA[1][:, 0:1], ubuf[:, j, n:n + 1], vcol(5, j), None, ALU.mult, None,
                                 [Bub[j], Bvec], [BtA[1]])
                              stt("dve", tmpA[1][:, 0:1], ubuf[:, j, n + 1:n + 2], vcol(6, j), tmpA[1][:, 0:1],
                                  ALU.mult, ALU.add, [Bub[j], Bvec, BtA[1]], [BtA[1]])
                              tt("pool", ycv[:, j, 0:1], tmpA[1][:, 0:1], cbb[:, j, n:n + 1], ALU.mult,
                                 [BtA[1], Bcbb[j]], [Bycv[j]])
                              p.dma(sp_ycv[j][:, t0 + n:t0 + n + 1], ycv[:, j, 0:1], reads=[Bycv[j]], slow=True)
                          else:
                              cp("pool", ubuf[:, j, 0:2], ubuf[:, j, n:n + 2], [Bub[j]], [Bub[j]])
                              cp("pool", cbb[:, j, 0:1], cbb[:, j, n:n + 1], [Bcbb[j]], [Bcbb[j]])
                      Pw, BPw = proj_chunk(24, n)
                      act(WA[0:64, 0:n], Pw[0:64, 0:n], AF.Tanh, [BPw], [BWA])
                      cp("act", WA[64:128, 0:n], Pw[64:128, 0:n], [BPw], [BWA])
                      p.dma(sp_wa[:, t0:t0 + n], WA[:, 0:n], reads=[BWA])
                      Pg, BPg = proj_chunk(25, n)
                      act(sgl[:, 0:n], Pg[:, 0:n], AF.Sigmoid, [BPg], [Bsgl])
                      for j in range(4):
                          tmpA, BtA = tmpA2[j % 2], BtA2[j % 2]
                          rj, kpj, kj, vj = base[:, j, 0:n], base[:, 4 + j, 0:n], base[:, 8 + j, 0:n], base[:, 12 + j, 0:n]
                          Pr, BPr = proj_chunk(12 + j, n)
                          cp("act", rj, Pr[:, 0:n], [BPr], [Bbase[j]])
                          Pk, BPk = proj_chunk(16 + j, n)
                          cp("act", kj, Pk[:, 0:n], [BPk], [Bbase[8 + j]])
                          Pv, BPv = proj_chunk(20 + j, n)
                          cp("dve", vj, Pv[:, 0:n], [BPv], [Bbase[12 + j]])
                          act(tmpA[2][:, 0:n], kj, AF.Square, [Bbase[8 + j], Bvec], [BtA[2]], scale=vcol(0, j))
                          Pq, BPq = pa()
                          mm(Pq[:, 0:n], cst[:, C_BO:C_BO + 128], tmpA[2][:, 0:n], True, True, [Bcst, BtA[2]], [BPq])
                          ts("dve", tmpA[2][:, 0:n], Pq[:, 0:n], 1e-24, None, ALU.max, None, [BPq], [BtA[2]])
                          act(tmpA[2][:, 0:n], tmpA[2][:, 0:n], AF.Ln, [BtA[2]], [BtA[2]])
                          act(tmpA[2][:, 0:n], tmpA[2][:, 0:n], AF.Exp, [BtA[2]], [BtA[2]], scale=-0.5)
                          stt("dve", kpj, kj, vcol(0, j), tmpA[2][:, 0:n], ALU.mult, ALU.mult,
                              [Bbase[8 + j], Bvec, BtA[2]], [Bbase[4 + j]])
                          stt("dve", tmpA[3][:, 0:n], rj, vcol(2, j), kj, ALU.mult, ALU.mult,
                              [Bbase[j], Bvec, Bbase[8 + j]], [BtA[3]])
                          Pq2, BPq2 = pa()
                          mm(Pq2[:, 0:n], cst[:, C_BO:C_BO + 128], tmpA[3][:, 0:n], True, True, [Bcst, BtA[3]], [BPq2])
                          tt("dve", tmpA[3][:, 0:n], Pq2[:, 0:n], vj, ALU.mult, [BPq2, Bbase[12 + j]], [BtA[3]])
                          Pq3, BPq3 = pa()
                          mm(Pq3[:, 0:n], gup[:, j * 128:(j + 1) * 128], sgl[:, 0:n], True, True, [Bgup, Bsgl], [BPq3])
                          ts("dve", G12[:, j, 0:n], Pq3[:, 0:n], vcol(3, j), None, ALU.mult, None, [BPq3, Bvec], [BG12[j]])
                          stt("dve", G12[:, 4 + j, 0:n], tmpA[3][:, 0:n], vcol(4, j), Pq3[:, 0:n], ALU.add, ALU.mult,
                              [BtA[3], Bvec, BPq3], [BG12[4 + j]])
                          for q, Bq in ((j, Bbase[j]), (4 + j, Bbase[4 + j]), (8 + j, Bbase[8 + j]), (12 + j, Bbase[12 + j])):
                              p.dma(sp_base[q][:, t0:t0 + n], base[:, q, 0:n], reads=[Bq])
                          p.dma(sp_g[j][:, t0:t0 + n], G12[:, j, 0:n], reads=[BG12[j]])
                          p.dma(sp_g[4 + j][:, t0:t0 + n], G12[:, 4 + j, 0:n], reads=[BG12[4 + j]])

                  def load_stage(t0, n):
                      for q in range(16):
                          p.dma(base[:, q, 0:n], sp_base[q][:, t0:t0 + n], writes=[Bbase[q]])
                      p.dma(WA[:, 0:n], sp_wa[:, t0:t0 + n], writes=[BWA])
                      for q in range(8):
                          p.dma(G12[:, q, 0:n], sp_g[q][:, t0:t0 + n], writes=[BG12[q]])
                      for j in range(4):
                          p.dma(ycv[:, j, 0:n], sp_ycv[j][:, t0 + 1:t0 + n + 1], writes=[Bycv[j]])

                  def scan_tile(c0, d, Sb, BSb):
                      mG = cst[:, C_MG + 512 * d:C_MG + 512 * (d + 1)]
                      mL = cst[:, C_ML + 128 * d:C_ML + 128 * (d + 1)]
                      tF = cst[:, C_TF + 256 * d:C_TF + 256 * (d + 1)]
                      gcol = 127 if d == 0 else 0
                      cs_ = slice(c0, c0 + 128)
                      P_, BP_ = pa()
                      mm(P_[:], WA[0:64, cs_], lup[0:64, d, :], True, False, [BWA, Blup], [BP_])
                      mm(P_[:], cst[0:1, C_ON:C_ON + 128], rowv[0:1, d * 512:(d + 1) * 512], False, True,
                         [Bcst, Browv], [BP_])
                      act(sigT[:], P_[:], AF.Sigmoid, [BP_], [BsigT])
                      for j in range(4):
                          EG, BEG, aFt, BaF, tB, BtB = EG2[j % 2], BEG2[j % 2], aFt2[j % 2], BaF2[j % 2], tB2[j % 2], BtB2[j % 2]
                          Pc, BPc = pa()
                          mm(Pc[:, 0:256], sigT[:, j * 128:(j + 1) * 128], tF, True, True, [BsigT, Bcst], [BPc])
                          act(EG[:, 0, :], Pc[:, 0:128], AF.Exp, [BPc], [BEG])
                          act(EG[:, 1, :], Pc[:, 0:128], AF.Exp, [BPc], [BEG], scale=-1.0)
                          act(EG[:, 2, :], Pc[:, 128:256], AF.Exp, [BPc], [BEG])
                          Pa, BPa_ = pa()
                          mm(Pa[:, 0:128], lup[64:128, d, j * 128:(j + 1) * 128], WA[64:128, cs_], True, True,
                             [Blup, BWA], [BPa_])
                          act(aFt[:], Pa[:, 0:128], AF.Sigmoid, [BPa_, Bvec], [BaF], bias=vcol(10 + d, j))
                          rj, kpj, kj = base[:, j, cs_], base[:, 4 + j, cs_], base[:, 8 + j, cs_]
                          tt("pool", KR[:, j, 128:256], rj, EG[:, 0, :], ALU.mult, [Bbase[j], BEG], [BKR[j]])
                          tt("dve", KaF32[:, j, :], kpj, EG[:, 2, :], ALU.mult, [Bbase[4 + j], BEG], [BKa32[j]])
                          cp("pool", KR[:, j, 0:128], KaF32[:, j, :], [BKa32[j]], [BKR[j]])
                          tt("pool", tB[0][:], kpj, aFt[:], ALU.mult, [Bbase[4 + j], BaF], [BtB[0]])
                          tt("dve", tB[1][:], tB[0][:], EG[:, 1, :], ALU.mult, [BtB[0], BEG], [BtB[1]])
                          cp("pool", BtF[:, j, :], tB[1][:], [BtB[1]], [BBt[j]])
                          ts("dve", BpF[:, j, :], tB[1][:], EG[:, 0, gcol:gcol + 1], None, ALU.mult, None,
                             [BtB[1], BEG], [BBp[j]])
                          ts("dve", tB[0][:], aFt[:], vcol(1, j), omka[:, j:j + 1], ALU.mult, ALU.add,
                             [BaF, Bvec, Bomka], [BtB[0]])
                          tt("pool", tB[0][:], tB[0][:], kj, ALU.mult, [BtB[0], Bbase[8 + j]], [BtB[0]])
                          tt("dve", tB[2][:], tB[0][:], EG[:, 1, :], ALU.mult, [BtB[0], BEG], [BtB[2]])
                          cp("pool", KtF[:, j, :], tB[2][:], [BtB[2]], [BKt[j]])
                          ts("dve", KpF[:, j, :], tB[2][:], EG[:, 0, gcol:gcol + 1], None, ALU.mult, None,
                             [BtB[2], BEG], [BKp[j]])
                          ts("dve", DG[:, j, :], cst[:, C_IB:C_IB + 64], EG[:, 0, gcol:gcol + 1], None, ALU.mult, None,
                             [Bcst, BEG], [BDG[j]])
                      ck(30)
                      for name, srcf, Bsrc in (("Ka", lambda j: KaF32[:, j, :], BKa32), ("Bp", lambda j: BpF[:, j, :], BBp),
                                               ("Kp", lambda j: KpF[:, j, :], BKp),
                                               ("V", lambda j: base[:, 12 + j, cs_], Bbase[12:16])):
                          P_, BP_ = pa()
                          for j in range(4):
                              tr(P_[:, j * 128:(j + 1) * 128], srcf(j), [Bsrc[j]], [BP_])
                          cp("act", TM[name][:], P_[:], [BP_], [BTM[name]])

                      ck(31)
                      def unit_stages(h, u):
                          j, p0 = h // 2, 64 * (h % 2)
                          hs = slice(h * 64, (h + 1) * 64)
                          Bt_h, Kt_h, KR_h = BtF[p0:p0 + 64, j, :], KtF[p0:p0 + 64, j, :], KR[p0:p0 + 64, j, :]
                          Ka_h = KR[p0:p0 + 64, j, 0:128]
                          pn, Bpn = PNL[u], BPNL[u]
                          stages = []

                          def s_gram():
                              mm(PG[:, 0:256], Bt_h, KR_h, True, True, [BBt[j], BKR[j]], [BPG[0]])
                              mm(PG[:, 256:512], Kt_h, KR_h, True, True, [BKt[j], BKR[j]], [BPG[1]])
                              mm(pn[:, 384:512], Ka_h, Bt_h, True, True, [BKR[j], BBt[j]], [Bpn[2]])
                              tt("dve", LT[u][0][:, 0:128], PG[:, 0:128], mG[:, 0:128], ALU.mult, [BPG[0], Bcst],
                                 [BLT[u][0][0]])
                              tt("dve", Gm[u][:, 0:128], PG[:, 128:256], mG[:, 128:256], ALU.mult, [BPG[0], Bcst], [BGm[u]])
                              tt("dve", Gm[u][:, 128:384], PG[:, 256:512], mG[:, 256:512], ALU.mult, [BPG[1], Bcst], [BGm[u]])
                              tt("dve", Nn[u][0][:], pn[:, 384:512], mL, ALU.mult, [Bpn[2], Bcst], [BNn[u][0]])
                              cp("pool", LT[u][0][:, 128:256], identS[:], [BidR], [BLT[u][0][1]])
                          stages.append(s_gram)

                          ev = "act" if u % 2 == 0 else "dve"

                          def mk_level(k):
                              a, b = k % 2, (k + 1) % 2
                              def s():
                                  mm(pn[:, 0:128], Nn[u][a][:], LT[u][a][:, 0:128], True, True,
                                     [BNn[u][a], BLT[u][a][0]], [Bpn[0]])
                                  mm(pn[:, 128:256], Nn[u][a][:], LT[u][a][:, 128:256], True, False,
                                     [BNn[u][a], BLT[u][a][1]], [Bpn[0]])
                                  mm(pn[:, 128:256], identS[:], LT[u][a][:, 128:256], False, True,
                                     [BidR, BLT[u][a][1]], [Bpn[0]])
                                  mm(pn[:, 256:384], LT[u][a][:, 0:128], Nn[u][a][:], True, True,
                                     [BLT[u][a][0], BNn[u][a]], [Bpn[1]])
                                  cp(ev, LT[u][b][:], pn[:, 0:256], [Bpn[0]], [BLT[u][b][0], BLT[u][b][1]])
                                  cp(ev, Nn[u][b][:], pn[:, 256:384], [Bpn[1]], [BNn[u][b]])
                              return s
                          for k in range(0, 6):
                              stages.append(mk_level(k))

                          def s_l6():
                              mm(pn[:, 128:256], Nn[u][0][:], LT[u][0][:, 128:256], True, False,
                                 [BNn[u][0], BLT[u][0][1]], [Bpn[0]])
                              mm(pn[:, 128:256], identS[:], LT[u][0][:, 128:256], False, True,
                                 [BidR, BLT[u][0][1]], [Bpn[0]])
                              cp(ev, TtF[u][:], pn[:, 128:256], [Bpn[0]], [BTt[u]])
                              mm(PM[:, 0:64], Gm[u][:, 128:256], TM["V"][:, hs], True, True, [BGm[u], BTM["V"]], [BPM[0]])
                              cp(ev, AkV[u][:], PM[:, 0:64], [BPM[0]], [BAkV[u]])
                          stages.append(s_l6)

                          def s_x():
                              mm(PM[:, 64:128], TtF[u][:], TM["Ka"][:, hs], True, True, [BTt[u], BTM["Ka"]], [BPM[1]])
                              mm(PM[:, 128:192], TtF[u][:], AkV[u][:], True, True, [BTt[u], BAkV[u]], [BPM[1]])
                              if ev == "act":
                                  act(XnS[u][:], PM[:, 64:192], AF.Copy, [BPM[1]], [BXn[u]], scale=-1.0)
                              else:
                                  ts("dve", XnS[u][:], PM[:, 64:192], -1.0, None, ALU.mult, None, [BPM[1]], [BXn[u]])
                          stages.append(s_x)

                          def s_fin():
                              mm(PM[0:64, 256:384], identR[p0:p0 + 64, p0:p0 + 64], KR[p0:p0 + 64, j, 128:256], True, False,
                                 [BidR, BKR[j]], [BPM[3]])
                              mm(PM[0:64, 256:384], XnS[u][:, 0:64], Gm[u][:, 0:128], False, True, [BXn[u], BGm[u]], [BPM[3]])
                              cp(ev, RhT[u][:], PM[0:64, 256:384], [BPM[3]], [BRh[u]])
                              mm(PM[0:64, 192:256], XnS[u][:, 0:64], TM["Bp"][:, hs], True, False, [BXn[u], BTM["Bp"]], [BPM[2]])
                              mm(PM[0:64, 192:256], identR[p0:p0 + 64, p0:p0 + 64], DG[p0:p0 + 64, j, :], False, True,
                                 [BidR, BDG[j]], [BPM[2]])
                              cp(ev, PTs[u][:], PM[0:64, 192:256], [BPM[2]], [BPTs[u]])
                              mm(PY[:, hs], Gm[u][:, 256:384], TM["V"][:, hs], True, False, [BGm[u], BTM["V"]], [BPY])
                              mm(PY[:, hs], Gm[u][:, 0:128], XnS[u][:, 64:128], False, False, [BGm[u], BXn[u]], [BPY])
                              mm(PY[:, hs], RhT[u][:], Sb[:, h, :], False, True, [BRh[u], BSb], [BPY])
                              mm(PS_[0:64, hs], TM["Kp"][:, hs], TM["V"][:, hs], True, False, [BTM["Kp"], BTM["V"]], [BPS])
                              mm(PS_[0:64, hs], TM["Bp"][:, hs], XnS[u][:, 64:128], False, False, [BTM["Bp"], BXn[u]], [BPS])
                              mm(PS_[0:64, hs], PTs[u][:], Sb[:, h, :], False, True, [BPTs[u], BSb], [BPS])
                          stages.append(s_fin)
                          return stages

                      for g_ in range(8 // NU):
                          us_ = [unit_stages(NU * g_ + i_, i_) for i_ in range(NU)]
                          for sts_ in zip(*us_):
                              for f_ in sts_:
                                  f_()
                      cp("act", Sb[:].rearrange("p h n -> p (h n)"), PS_[0:64, :], [BPS], [BSb])

                  def post_tile(src, dst, t0, c0, X, BX):
                      cs_ = slice(c0, c0 + 128)
                      tt("dve", ysum, PY[:], y1t, ALU.add, [BPY, By1], [Bys])
                      y3 = ysum.rearrange("p (h n) -> p h n", h=8)
                      p.op("dve", lambda e: e.tensor_reduce(out=gst[:, :, 0], in_=y3, axis=AX.X, op=ALU.add),
                           reads=[Bys], writes=[Bgst])
                      tt("pool", yn, ysum, ysum, ALU.mult, [Bys], [Byn])
                      p.op("dve", lambda e: e.tensor_reduce(out=gst[:, :, 1], in_=yn.rearrange("p (h n) -> p h n", h=8),
                                                            axis=AX.X, op=ALU.add), reads=[Byn], writes=[Bgst])
                      ts("dve", gst[:, :, 0], gst[:, :, 0], 1.0 / 64, None, ALU.mult, None, [Bgst], [Bgst])
                      tt("dve", gst[:, :, 2], gst[:, :, 0], gst[:, :, 0], ALU.mult, [Bgst], [Bgst])
                      stt("dve", gst[:, :, 3], gst[:, :, 1], 1.0 / 64, gst[:, :, 2], ALU.mult, ALU.subtract, [Bgst], [Bgst])
                      ts("dve", gst[:, :, 3], gst[:, :, 3], GN_EPS, None, ALU.add, None, [Bgst], [Bgst])
                      act(gst[:, :, 3], gst[:, :, 3], AF.Ln, [Bgst], [Bgst])
                      act(gst[:, :, 3], gst[:, :, 3], AF.Exp, [Bgst], [Bgst], scale=-0.5)
                      for h in range(8):
                          ts("dve", yn[:, h * 64:(h + 1) * 64], ysum[:, h * 64:(h + 1) * 64],
                             gst[:, h, 0:1], gst[:, h, 3:4], ALU.subtract, ALU.mult, [Bys, Bgst], [Byn])
                      P_, BP_ = pa()
                      for j in range(4):
                          tr(P_[:, j * 128:(j + 1) * 128], yn[:, j * 128:(j + 1) * 128], [Byn], [BP_])
                      for j in range(4):
                          tt("dve", tB2[j % 2][0][:], P_[:, j * 128:(j + 1) * 128], G12[:, j, cs_], ALU.mult, [BP_, BG12[j]], [BtB2[j % 2][0]])
                          tt("pool", catT[:, 4 + j, :], tB2[j % 2][0][:], G12[:, 4 + j, cs_], ALU.add, [BtB2[j % 2][0], BG12[4 + j]], [Bcat])
                          cp("pool", catT[:, j, :], ycv[:, j, cs_], [Bycv[j]], [Bcat])
                      for half in range(2):
                          Po, BPo = pa()
                          for k in range(KC):
                              mm(Po[:], catT[:, k, :], wout[:, k, half * 512:(half + 1) * 512], k == 0, k == KC - 1,
                                 [Bcat, Bwout], [BPo])
                          tt("dve", xo[:, half * 512:(half + 1) * 512], Po[:], bc["gt"][:, half * 512:(half + 1) * 512],
                             ALU.mult, [BPo, Bbc["gt"]], [Bxo])
                      tt("pool", xo[:], xo[:], X[:], ALU.add, [Bxo, BX], [Bxo])
                      p.dma(dst[t0:t0 + 128, :], xo[:], reads=[Bxo])

                  def mixer_segment(src, dst, seg_T, setid, Sin, need_out, Sfin):
                      load_bc(setid, 0)
                      nsb = (seg_T + SBT - 1) // SBT
                      for d in range(2):
                          if Sin is None:
                              ts("dve", Sst[d][:].rearrange("p h n -> p (h n)"), cst[0:64, C_MG:C_MG + 512], 0.0, None,
                                 ALU.mult, None, [Bcst], [BS[d]])
                          else:
                              cp("pool", Sst[d][:], Sin[d][0][:], [Sin[d][1]], [BS[d]])
                      for s_ in range(nsb):
                          t0 = s_ * SBT
                          n = min(SBT, seg_T - t0)
                          stage1(src, t0, n, s_ == 0, seg_T)
                          ck(2)
                          for i in range(n // 128):
                              scan_tile(i * 128, 0, Sst[0], BS[0])
                              ck(3)
                              if need_out:
                                  cp("dve", sigT[:], PY[:], [BPY], [BsigT])
                                  p.dma(sp_y1[t0 + i * 128:t0 + (i + 1) * 128, :], sigT[:], reads=[BsigT])
                      p.barrier()
                      ck(4)
                      for s_ in reversed(range(nsb)):
                          t0 = s_ * SBT
                          n = min(SBT, seg_T - t0)
                          load_stage(t0, n)
                          for i in reversed(range(n // 128)):
                              scan_tile(i * 128, 1, Sst[1], BS[1])
                              if need_out:
                                  tk = t0 + i * 128
                                  p.dma(y1t, sp_y1[tk:tk + 128, :], writes=[By1])
                                  X, BX = xt[i % 2], Bxt[i % 2]
                                  p.dma(X[:], src[tk:tk + 128, :], writes=[BX])
                                  post_tile(src, dst, tk, i * 128, X, BX)
                      if Sfin is not None:
                          for d in range(2):
                              cp("pool", Sfin[d][0][:], Sst[d][:], [BS[d]], [Sfin[d][1]])
                      p.barrier()

                  mixer_segment(c_src, c_mid, CT, 1, None, not last, [(Sc[0], BSc[0]), (Sc[1], BSc[1])])
                  ck(5)
                  mixer_segment(x_src, x_mid, T, 0, [(Sc[0], BSc[0]), (Sc[1], BSc[1])], True, None)
                  p.barrier()
                  ck(6)

              with ExitStack() as s3:
                  def sb3(name, shape, dt=F32):
                      return sb(name, shape, dt, stack=s3)
                  wup = sb3("wup", [128, KC, 2 * DFF], BF16); Bwup = Buf()
                  wdn = sb3("wdn", [128, FC, D], BF16); Bwdn = Buf()
                  with ExitStack() as sw3:
                      wst3 = [sb(f"wst3{i}", [128, 2048], stack=sw3) for i in range(2)]; Bw3 = [Buf(), Buf()]
                      ii = 0
                      for k in range(KC):
                          for c_ in range(0, 2 * DFF, 2048):
                              w_ = min(2048, 2 * DFF - c_)
                              sg, Bsg = wst3[ii % 2], Bw3[ii % 2]
                              p.dma(sg[:, 0:w_], w_up[l][k * 128:(k + 1) * 128, c_:c_ + w_], writes=[Bsg])
                              cp("dve" if ii % 2 == 0 else "pool", wup[:, k, c_:c_ + w_], sg[:, 0:w_], [Bsg], [Bwup])
                              ii += 1
                      for c_ in range(FC):
                          sg, Bsg = wst3[ii % 2], Bw3[ii % 2]
                          p.dma(sg[:, 0:D], w_dn[l][c_ * 128:(c_ + 1) * 128, :], writes=[Bsg])
                          cp("dve" if ii % 2 == 0 else "pool", wdn[:, c_, :], sg[:, 0:D], [Bsg], [Bwdn])
                          ii += 1
                      p.barrier()
                  xt3 = [sb3(f"x3{i}", [128, D]) for i in range(2)]; Bx3 = [Buf() for _ in range(2)]
                  xr, Bxr = xt3[1], Bx3[1]
                  hb3 = sb3("hb3", [128, D]); Bhb3 = Buf()
                  ssq3 = sb3("ssq3", [128, 2]); Bssq3 = Buf()
                  hT3 = sb3("hT3", [128, KC, 640], BF16); BhT3 = Buf()
                  accs = [sb3(f"acc{i}", [128, 512]) for i in range(2)]; Baccs = [Buf(), Buf()]
                  actT = sb3("actT", [128, FC, 512], BF16); Bact = [Buf() for _ in range(FC)]
                  xo3, Bxo3 = hb3, Bhb3
                  fgb = sb3("fgb", [128, D]); Bfgb = Buf()
                  if last and final_norm:
                      p.dma(fgb[:], fin_g.partition_broadcast(128), writes=[Bfgb])

                  def ffn_segment(src, dst, seg_T, setid, gw, is_out):
                      load_bc(setid, 1)
                      BT = 512 if seg_T >= 512 else seg_T
                      nrow_tot = seg_T // gw
                      nrow = BT // gw
                      for b0 in range(0, seg_T, BT):
                          halo = gw if gw < seg_T else 0
                          lo, hi = max(0, b0 - halo), min(seg_T, b0 + BT + halo)
                          nw = hi - lo
                          off = b0 - lo
                          tpos, xi = lo, 0
                          while tpos < hi:
                              w_ = min(128, hi - tpos)
                              X, BX = xt3[xi % 2], Bx3[xi % 2]
                              if w_ < 128:
                                  p.op("pool", lambda e, X=X: e.memset(X[:], 0.0), writes=[BX])
                              p.dma(X[0:w_, :], src[tpos:tpos + w_, :], writes=[BX])
                              act(hb3[:], X[:], AF.Square, [BX], [Bhb3, Bssq3], accum=ssq3[:, 0:1])
                              ts("dve", ssq3[:, 1:2], ssq3[:, 0:1], 1.0 / D, RMS_EPS, ALU.mult, ALU.add, [Bssq3], [Bssq3])
                              act(ssq3[:, 1:2], ssq3[:, 1:2], AF.Ln, [Bssq3], [Bssq3])
                              act(ssq3[:, 1:2], ssq3[:, 1:2], AF.Exp, [Bssq3], [Bssq3], scale=-0.5)
                              stt("dve", hb3[:], X[:], ssq3[:, 1:2], bc["gs"][:], ALU.mult, ALU.mult,
                                  [BX, Bssq3, Bbc["gs"]], [Bhb3])
                              tt("pool", hb3[:], hb3[:], bc["sh"][:], ALU.add, [Bhb3, Bbc["sh"]], [Bhb3])
                              col0 = tpos - lo
                              for half in range(2):
                                  P_, BP_ = pa()
                                  for q in range(4):
                                      k = half * 4 + q
                                      tr(P_[:, q * 128:(q + 1) * 128], hb3[:, k * 128:(k + 1) * 128], [Bhb3], [BP_])
                                  cp("act", hT3[:, half * 4:half * 4 + 4, col0:col0 + w_],
                                     P_[:].rearrange("p (q n) -> p q n", q=4)[:, :, 0:w_], [BP_], [BhT3])
                              tpos += w_
                              xi += 1
                          wr0 = off // gw
                          for c_ in range(FC):
                              acc, Bacc = accs[c_ % 2], Baccs[c_ % 2]
                              sil, Bsil = acc, Bacc
                              if c_ % 2 == 0:
                                  GA, BGA, GB, BGB = PG, BPG[0], PN[0], BPN[0][0]
                              else:
                                  GA, BGA, GB, BGB = PY, BPY, PS_, BPS
                              segs = [(0, min(nw, 512), GA, BGA)]
                              if nw > 512:
                                  segs.append((512, nw, GB, BGB))
                              for (a_, b_, P_, BP_) in segs:
                                  for k in range(KC):
                                      mm(P_[:, 0:b_ - a_], wup[:, k, c_ * 128:(c_ + 1) * 128], hT3[:, k, a_:b_],
                                         k == 0, k == KC - 1, [Bwup, BhT3], [BP_])
                              wc = lambda ty, tx: vec[:, 48 + (ty * 3 + tx) * FC + c_:48 + (ty * 3 + tx) * FC + c_ + 1]
                              bcol = vec[:, 48 + 9 * FC + c_:48 + 9 * FC + c_ + 1]

                              def tap(ty, tx, first):
                                  dy, dx = ty - 1, tx - 1
                                  r0g = b0 // gw
                                  rows_ok = [r for r in range(nrow) if 0 <= r0g + r + dy < nrow_tot]
                                  if not rows_ok:
                                      return
                                  ra, rb = rows_ok[0], rows_ok[-1] + 1
                                  ca, cb2 = (1, gw) if dx == -1 else ((0, gw - 1) if dx == 1 else (0, gw))
                                  rsplit = 512 // gw
                                  r = ra
                                  while r < rb:
                                      wr = wr0 + r + dy
                                      if wr < rsplit:
                                          re_ = min(rb, rsplit - wr0 - dy)
                                          P_, B_, wbase = GA, BGA, 0
                                      else:
                                          re_ = rb
                                          P_, B_, wbase = GB, BGB, rsplit
                                      nr = re_ - r
                                      iv = P_[:, (wr - wbase) * gw:(wr - wbase + nr) * gw].rearrange("p (r c) -> p r c", c=gw)[:, :, ca + dx:cb2 + dx]
                                      ov = acc[:, r * gw:(r + nr) * gw].rearrange("p (r c) -> p r c", c=gw)[:, :, ca:cb2]
                                      if first:
                                          act(ov, iv, AF.Identity, [B_, Bvec], [Bacc], bias=bcol, scale=wc(ty, tx))
                                      else:
                                          stt("dve", ov, iv, wc(ty, tx), ov, ALU.mult, ALU.add, [B_, Bvec, Bacc], [Bacc])
                                      r = re_
                              tap(1, 1, True)
                              for ty in range(3):
                                  for tx in range(3):
                                      if (ty, tx) != (1, 1) and not (gw >= seg_T and ty != 1):
                                          tap(ty, tx, False)
                              act(sil[:, 0:BT], acc[:, 0:BT], AF.Silu, [], [Bacc])
                              Pv, BPv = pa()
                              for k in range(KC):
                                  mm(Pv[:, 0:BT], wup[:, k, DFF + c_ * 128:DFF + (c_ + 1) * 128], hT3[:, k, off:off + BT],
                                     k == 0, k == KC - 1, [Bwup, BhT3], [BPv])
                              tt("dve", actT[:, c_, 0:BT], sil[:, 0:BT], Pv[:, 0:BT], ALU.mult, [Bsil, BPv], [Bact[c_]])
                          for i in range(BT // 128):
                              tk = b0 + i * 128
                              p.dma(xr[:], src[tk:tk + 128, :], writes=[Bxr])
                              for half in range(2):
                                  Po, BPo = pa()
                                  for c_ in range(FC):
                                      mm(Po[:], actT[:, c_, i * 128:(i + 1) * 128], wdn[:, c_, half * 512:(half + 1) * 512],
                                         c_ == 0, c_ == FC - 1, [Bact[c_], Bwdn], [BPo])
                                  tt("dve", xo3[:, half * 512:(half + 1) * 512], Po[:], bc["gt"][:, half * 512:(half + 1) * 512],
                                     ALU.mult, [BPo, Bbc["gt"]], [Bxo3])
                              tt("pool", xo3[:], xo3[:], xr[:], ALU.add, [Bxo3, Bxr], [Bxo3])
                              if is_out and final_norm:
                                  act(xt3[0][:], xo3[:], AF.Square, [Bxo3], [Bx3[0], Bssq3], accum=ssq3[:, 0:1])
                                  ts("dve", ssq3[:, 1:2], ssq3[:, 0:1], 1.0 / D, RMS_EPS, ALU.mult, ALU.add, [Bssq3], [Bssq3])
                                  act(ssq3[:, 1:2], ssq3[:, 1:2], AF.Ln, [Bssq3], [Bssq3])
                                  act(ssq3[:, 1:2], ssq3[:, 1:2], AF.Exp, [Bssq3], [Bssq3], scale=-0.5)
                                  stt("dve", xo3[:], xo3[:], ssq3[:, 1:2], fgb[:], ALU.mult, ALU.mult, [Bxo3, Bssq3, Bfgb], [Bxo3])
                              p.dma(dst[tk:tk + 128, :], xo3[:], reads=[Bxo3])

                  if not last:
                      ffn_segment(c_mid, c_dst, CT, 1, CT, False)
                  ffn_segment(x_mid, x_dst, T, 0, GW, last)
                  p.barrier()
        except _Stop:
            pass
        p.barrier()
        nc._prog_ninst = p.ninst
    return nc


def prep_inputs(x, c, ctx, c_ctx, ada_w, ada_b, norm1_g, norm2_g, w_in, conv_a_w, rw_w0, rw_w_up, rw_a0, rw_a_up,
                rw_k_k, rw_k_a, rw_r_k, rw_g_up, rw_ln_g, rw_ln_b, w_out, ffn_w_up, ffn_conv_w, ffn_conv_b,
                ffn_w_down, final_g, b):
    L = ada_w.shape[0]
    f = lambda a: np.ascontiguousarray(a, dtype=np.float32)
    cs = np.stack([c[b].reshape(KC, 128).T, c_ctx.reshape(KC, 128).T], axis=-1)
    rows = np.concatenate([ada_b, norm1_g, norm2_g], axis=1)[:, None, :].repeat(2, axis=1)

    def ch(v, n):
        return v.reshape(n, 128).T
    vec = np.zeros((L, 128, NVEC), np.float32)
    for l in range(L):
        cols = [ch(rw_k_k[l], 4), ch(rw_k_a[l], 4), ch(rw_r_k[l].reshape(-1), 4), ch(rw_ln_g[l], 4), ch(rw_ln_b[l], 4),
                ch(conv_a_w[l, 0], 4), ch(conv_a_w[l, 1], 4), ch(conv_a_w[l, 2], 4),
                ch(rw_w0[l, 0], 4), ch(rw_w0[l, 1], 4), ch(rw_a0[l, 0], 4), ch(rw_a0[l, 1], 4)]
        for ty in range(3):
            for tx in range(3):
                cols.append(ch(ffn_conv_w[l, ty, tx], FC))
        cols.append(ch(ffn_conv_b[l], FC))
        vec[l] = np.concatenate(cols, axis=1)
    rowv = rw_w0.reshape(L, 1, 1024)
    lup = np.concatenate([rw_w_up.transpose(0, 2, 1, 3), rw_a_up.transpose(0, 2, 1, 3)], axis=1)
    return {
        "x": f(x[b]), "ctx": f(ctx[b]), "cs": f(cs), "cst": make_consts(), "ada_w": f(ada_w), "rows": f(rows),
        "fin_g": f(final_g), "w_in": f(w_in), "w_out": f(w_out), "w_up": f(ffn_w_up), "w_dn": f(ffn_w_down),
        "vec": f(vec), "rowv": f(rowv), "lup": f(lup), "gup": f(rw_g_up),
    }


def kernel(**inputs):
    inputs = {k: np.asarray(v) for k, v in inputs.items()}
    B, T, _ = inputs["x"].shape
    CT = inputs["ctx"].shape[1]
    nc = build_program(T, CT, inputs["ada_w"].shape[0])
    in_maps = [prep_inputs(b=b, **inputs) for b in range(B)]
    res = run_bass_kernel_spmd(nc, in_maps, core_ids=list(range(B)))
    return np.stack([r["out"] for r in res.results], axis=0).astype(np.float32)
```

```python
import math
import numpy as np
from contextlib import ExitStack
import concourse.bass as bass
import concourse.mybir as mybir
from concourse.bass_utils import run_bass_kernel_spmd

F32 = mybir.dt.float32
BF16 = mybir.dt.bfloat16
F32R = mybir.dt.float32r
AF = mybir.ActivationFunctionType
ALU = mybir.AluOpType
AX = mybir.AxisListType

D = 1024
KC = 8
PROJ = 3328
NPC = 26
DFF = 2816
FC = 22
GW = 64
DS = math.exp(-0.5)
RMS_EPS = 1e-6
GN_EPS = 64e-5
NVEC = 48 + 10 * FC
SCAN_DT = BF16

C_ID = 0
C_MG = 128
C_ML = C_MG + 1024
C_TF = C_ML + 256
C_BO = C_TF + 512
C_ON = C_BO + 128
C_IB = C_ON + 128
NCST = C_IB + 64


def make_consts():
    c = np.zeros((128, NCST), np.float32)
    s = np.arange(128)[:, None]
    t = np.arange(128)[None, :]
    c[:, C_ID:C_ID + 128] = np.eye(128)
    for d in range(2):
        lt = (s < t) if d == 0 else (s > t)
        le = (s <= t) if d == 0 else (s >= t)
        g = c[:, C_MG + 512 * d:C_MG + 512 * (d + 1)]
        g[:, 0:128] = -1.0 * lt
        g[:, 128:256] = le
        g[:, 256:384] = lt
        g[:, 384:512] = le
        c[:, C_ML + 128 * d:C_ML + 128 * (d + 1)] = -1.0 * lt.T
        f = c[:, C_TF + 256 * d:C_TF + 256 * (d + 1)]
        f[:, 0:128] = -DS * le
        f[:, 128:256] = -DS * lt
    c[:, C_BO:C_BO + 128] = (s // 64 == t // 64)
    c[:, C_ON:C_ON + 128] = 1.0
    c[:, C_IB:C_IB + 64] = (s % 64 == np.arange(64)[None, :])
    return c


class Buf:
    __slots__ = ("name", "w", "rd", "ps")

    def __init__(self, name="", ps=False):
        self.name = name
        self.w = None
        self.rd = []
        self.ps = ps


ENGMAP = {"pe": "tensor", "dve": "vector", "act": "scalar", "pool": "gpsimd", "sp": "sync"}


class Eng:
    def __init__(self, name, sem, h):
        self.name = name
        self.sem = sem
        self.h = h
        self.cnt = 0
        self.waited = {}


class Prog:
    def __init__(self, nc, sems, dma_sems):
        self.nc = nc
        self.E = {n: Eng(n, sems[n], getattr(nc, ENGMAP[n])) for n in ENGMAP}
        self.dma_sems = dma_sems
        self.dma_cnt = [0] * len(dma_sems)
        self.dma_rr = 0
        self.ninst = 0
        self.stop = False

    def _deps(self, reads, writes):
        d = []
        for b in reads:
            if b.w is not None:
                d.append(b.w)
        for b in writes:
            if b.w is not None:
                d.append(b.w)
            d.extend(b.rd)
        return d

    def _waits(self, e, deps, skip_self):
        need = {}
        for (sem, val, owner) in deps:
            if skip_self and owner is e:
                continue
            k = id(sem)
            if e.waited.get(k, 0) >= val:
                continue
            if k not in need or need[k][1] < val:
                need[k] = (sem, val)
        for k, (sem, val) in need.items():
            e.waited[k] = val
            e.h.wait_ge(sem, val)

    def op(self, en, fn, reads=(), writes=()):
        if self.stop:
            return
        e = self.E[en]
        deps = self._deps(reads, writes)
        for b in reads:
            if b.ps:
                deps.extend(t for t in b.rd if t[2] is not e)
        self._waits(e, deps, skip_self=(en == "pe"))
        e.cnt += 1
        tok = (e.sem, e.cnt, e)
        fn(e.h).then_inc(e.sem, 1)
        for b in writes:
            b.w = tok
            b.rd = []
        for b in reads:
            b.rd.append(tok)
        self.ninst += 1

    def dma(self, out, in_, reads=(), writes=(), q="sp", slow=False):
        if self.stop:
            return
        e = self.E[q]
        deps = self._deps(reads, writes)
        i = self.dma_rr
        self.dma_rr = (self.dma_rr + 1) % len(self.dma_sems)
        sem = self.dma_sems[i]
        if self.dma_cnt[i] > 0:
            deps.append((sem, self.dma_cnt[i], None))
        self._waits(e, deps, skip_self=False)
        self.dma_cnt[i] += 16
        tok = (sem, self.dma_cnt[i], None)
        if slow:
            e.h.dma_start(out=out, in_=in_, allow_slow_non_contiguous=True).then_inc(sem, 16)
        else:
            e.h.dma_start(out=out, in_=in_).then_inc(sem, 16)
        for b in writes:
            b.w = tok
            b.rd = []
        for b in reads:
            b.rd.append(tok)
        self.ninst += 1
        return tok

    def barrier(self):
        if self.stop:
            return
        toks = [(e.sem, e.cnt, e) for e in self.E.values() if e.cnt > 0]
        toks += [(s, c, None) for s, c in zip(self.dma_sems, self.dma_cnt) if c > 0]
        for e in self.E.values():
            self._waits(e, toks, skip_self=True)


class _Stop(Exception):
    pass


def build_program(T, CT, L=2, final_norm=True, dbg=None):
    assert T % 512 == 0 and CT % 128 == 0
    nc = bass.Bass("TRN2", target_bir_lowering=False)
    dt_ = nc.dram_tensor

    def din(name, shape, dt=F32):
        return dt_(name, shape, dt, kind="ExternalInput").ap()

    def dint(name, shape, dt=F32):
        return dt_(name, shape, dt, kind="Internal").ap()

    x_in = din("x", [T, D])
    ctx_in = din("ctx", [CT, D])
    cs_in = din("cs", [128, KC, 2])
    cst_in = din("cst", [128, NCST])
    ada_w = din("ada_w", [L, D, 6 * D])
    rows_in = din("rows", [L, 2, 6 * D + 2 * D])
    fin_g = din("fin_g", [D])
    w_in = din("w_in", [L, D, PROJ])
    w_out = din("w_out", [L, D, D])
    w_up = din("w_up", [L, D, 2 * DFF])
    w_dn = din("w_dn", [L, DFF, D])
    vec_in = din("vec", [L, 128, NVEC])
    rowv_in = din("rowv", [L, 1, 1024])
    lup_in = din("lup", [L, 128, 2, 512])
    gup_in = din("gup", [L, 128, 512])
    out = dt_("out", [T, D], F32, kind="ExternalOutput").ap()

    xs = [dint("xs0", [T, D]), dint("xs1", [T, D])]
    cxs = [dint("cxs0", [CT, D]), dint("cxs1", [CT, D])]
    modr = dint("modr", [2, 6, D])
    TS = max(T, CT)
    sp_base = dint("sp_base", [16, 128, TS])
    sp_wa = dint("sp_wa", [128, TS])
    sp_g = dint("sp_g", [8, 128, TS])
    sp_ycv = dint("sp_ycv", [4, 128, TS + 2], BF16)
    sp_y1 = dint("sp_y1", [TS, 512])

    st = ExitStack()
    with st:
        sems = {n: st.enter_context(nc.semaphore(n)) for n in ENGMAP}
        dsem = [st.enter_context(nc.semaphore(f"dq{i}")) for i in range(24)]
        p = Prog(nc, sems, dsem)

        uid = [0]

        def sb(name, shape, dt=F32, stack=st):
            uid[0] += 1
            return stack.enter_context(nc.sbuf_tensor(f"s{uid[0]}_{name}", shape, dt))

        def ps(name, shape, dt=F32):
            uid[0] += 1
            return st.enter_context(nc.psum_tensor(f"p{uid[0]}_{name}", shape, dt))

        cst = sb("cst", [128, NCST]); Bcst = Buf()
        vec = sb("vec", [128, NVEC]); Bvec = Buf()
        omka = sb("omka", [128, 4]); Bomka = Buf()
        bc = {n: sb("bc_" + n, [128, D]) for n in ("gs", "sh", "gt")}
        Bbc = {n: Buf() for n in bc}
        PA = [ps("PA0", [128, 512]), ps("PA1", [128, 512])]; BPA = [Buf(ps=True), Buf(ps=True)]
        _bpg = Buf(ps=True)
        PG = ps("PG", [128, 512]); BPG = [_bpg, _bpg]
        PN = [ps("PN0", [128, 512]), ps("PN1", [128, 512])]
        _bpn = [Buf(ps=True), Buf(ps=True)]
        BPN = [[_bpn[0]] * 3, [_bpn[1]] * 3]
        _bpm = Buf(ps=True)
        PM = ps("PM", [128, 512]); BPM = [_bpm] * 5
        PS_ = ps("PS", [128, 512]); BPS = Buf(ps=True)
        PY = ps("PY", [128, 512]); BPY = Buf(ps=True)
        pa_rr = [0]

        def pa():
            i = pa_rr[0]
            pa_rr[0] ^= 1
            return PA[i], BPA[i]

        ident = cst[:, C_ID:C_ID + 128]
        p.dma(cst[:], cst_in, writes=[Bcst])
        identR = sb("identR", [128, 128], SCAN_DT); BidR = Buf()
        identS = identR
        p.op("dve", lambda e: e.tensor_copy(out=identR[:], in_=ident), reads=[Bcst], writes=[BidR])

        def mm(o, l, r, start, stop, reads, writes):
            p.op("pe", lambda e: e.matmul(o, l, r, start=start, stop=stop), reads=reads, writes=writes)

        def tr(o, i_, reads, writes):
            p.op("pe", lambda e: e.transpose(o, i_, ident), reads=list(reads) + [Bcst], writes=writes)

        def act(o, i_, func, reads, writes, bias=None, scale=None, accum=None):
            kw = {}
            if bias is not None:
                kw["bias"] = bias
            if scale is not None:
                kw["scale"] = scale
            if accum is not None:
                kw["accum_out"] = accum
            p.op("act", lambda e: e.activation(out=o, in_=i_, func=func, **kw), reads=reads, writes=writes)

        def tt(en, o, a, b, op, reads, writes):
            p.op(en, lambda e: e.tensor_tensor(out=o, in0=a, in1=b, op=op), reads=reads, writes=writes)

        def ts(en, o, a, s1, s2, op0, op1, reads, writes):
            if s2 is None:
                p.op(en, lambda e: e.tensor_scalar(out=o, in0=a, scalar1=s1, scalar2=None, op0=op0),
                     reads=reads, writes=writes)
            else:
                p.op(en, lambda e: e.tensor_scalar(out=o, in0=a, scalar1=s1, scalar2=s2, op0=op0, op1=op1),
                     reads=reads, writes=writes)

        def stt(en, o, a, s, b, op0, op1, reads, writes):
            p.op(en, lambda e: e.scalar_tensor_tensor(out=o, in0=a, scalar=s, in1=b, op0=op0, op1=op1),
                 reads=reads, writes=writes)

        def cp(en, o, i_, reads, writes):
            if en == "act":
                act(o, i_, AF.Copy, reads, writes)
            else:
                p.op(en, lambda e: e.tensor_copy(out=o, in_=i_), reads=reads, writes=writes)

        def ck(k):
            if dbg == k and not p.stop:
                p.barrier()
                p.stop = True

        try:
          for l in range(L):
              last = (l == L - 1)
              x_src = x_in if l == 0 else xs[1]
              c_src = ctx_in if l == 0 else cxs[1]
              x_mid, c_mid = xs[0], cxs[0]
              x_dst, c_dst = xs[1], cxs[1]
              if last:
                  x_dst = out

              p.barrier()
              with ExitStack() as s0:
                  rows = sb("rows", [2, 8 * D], stack=s0); Brows = Buf()
                  mod = sb("mod", [2, 6 * D], stack=s0); Bmod = Buf()
                  drv = sb("drv", [2, 6, D], stack=s0); Bdrv = Buf()
                  cs = sb("cs", [128, KC, 2], stack=s0); Bcs = Buf()
                  scs = sb("scs", [128, KC, 2], stack=s0); Bscs = Buf()
                  stg = [sb(f"adastg{i}", [128, KC, 512], stack=s0) for i in range(2)]; Bstg = [Buf(), Buf()]
                  p.dma(rows[:], rows_in[l], writes=[Brows])
                  p.dma(cs[:], cs_in, writes=[Bcs])
                  p.dma(vec[:], vec_in[l], writes=[Bvec])
                  ts("dve", omka[:], vec[:, 4:8], -1.0, 1.0, ALU.mult, ALU.add, [Bvec], [Bomka])
                  act(scs[:], cs[:], AF.Silu, [Bcs], [Bscs])
                  for cb in range(12):
                      sg, Bsg = stg[cb % 2], Bstg[cb % 2]
                      p.dma(sg[:], ada_w[l][:, cb * 512:(cb + 1) * 512].rearrange("(k p) n -> p k n", p=128),
                            writes=[Bsg])
                      P_, BP_ = pa()
                      for k in range(KC):
                          mm(P_[0:2, :], scs[:, k, :], sg[:, k, :], k == 0, k == KC - 1, [Bscs, Bsg], [BP_])
                      tt("dve", mod[:, cb * 512:(cb + 1) * 512], P_[0:2, :], rows[:, cb * 512:(cb + 1) * 512],
                         ALU.add, [BP_, Brows], [Bmod])
                  stt("dve", drv[:, 0, :], mod[:, D:2 * D], 1.0, rows[:, 6 * D:7 * D], ALU.add, ALU.mult,
                      [Bmod, Brows], [Bdrv])
                  cp("dve", drv[:, 1, :], mod[:, 0:D], [Bmod], [Bdrv])
                  cp("dve", drv[:, 2, :], mod[:, 2 * D:3 * D], [Bmod], [Bdrv])
                  stt("dve", drv[:, 3, :], mod[:, 4 * D:5 * D], 1.0, rows[:, 7 * D:8 * D], ALU.add, ALU.mult,
                      [Bmod, Brows], [Bdrv])
                  cp("dve", drv[:, 4, :], mod[:, 3 * D:4 * D], [Bmod], [Bdrv])
                  cp("dve", drv[:, 5, :], mod[:, 5 * D:6 * D], [Bmod], [Bdrv])
                  Bmodr = Buf()
                  p.dma(modr, drv[:], reads=[Bdrv], writes=[Bmodr])
                  p.barrier()
              ck(0)

              def load_bc(setid, which):
                  for n, r in (("gs", 0), ("sh", 1), ("gt", 2)):
                      p.dma(bc[n][:], modr[setid, 3 * which + r].partition_broadcast(128),
                            reads=[Bmodr], writes=[Bbc[n]])

              with ExitStack() as s1:
                  def sb1(name, shape, dt=F32):
                      return sb(name, shape, dt, stack=s1)
                  rowv = sb1("rowv", [1, 1024]); Browv = Buf()
                  lup = sb1("lup", [128, 2, 512]); Blup = Buf()
                  gup = sb1("gup", [128, 512], BF16); Bgup = Buf()
                  Sst = [sb1(f"S{d}", [64, 8, 64], SCAN_DT) for d in range(2)]; BS = [Buf(), Buf()]
                  Sc = [sb1(f"Sc{d}", [64, 8, 64], SCAN_DT) for d in range(2)]; BSc = [Buf(), Buf()]
                  p.dma(rowv[:], rowv_in[l], writes=[Browv])
                  p.dma(lup[:], lup_in[l], writes=[Blup])
                  win = sb1("win", [128, KC, PROJ], BF16); Bwin = Buf()
                  wout = sb1("wout", [128, KC, D], BF16); Bwout = Buf()
                  with ExitStack() as sw:
                      wstg = [sb(f"wstg{i}", [128, PROJ], stack=sw) for i in range(2)]; Bwstg = [Buf(), Buf()]
                      for k in range(KC):
                          sg, Bsg = wstg[k % 2], Bwstg[k % 2]
                          p.dma(sg[:], w_in[l][k * 128:(k + 1) * 128, :], writes=[Bsg])
                          cp("dve" if k % 2 == 0 else "pool", win[:, k, :], sg[:], [Bsg], [Bwin])
                      for k in range(KC):
                          sg, Bsg = wstg[k % 2], Bwstg[k % 2]
                          p.dma(sg[:, 0:D], w_out[l][k * 128:(k + 1) * 128, :], writes=[Bsg])
                          cp("dve" if k % 2 == 0 else "pool", wout[:, k, :], sg[:, 0:D], [Bsg], [Bwout])
                      p.dma(wstg[0][:, 0:512], gup_in[l], writes=[Bwstg[0]])
                      cp("dve", gup[:], wstg[0][:, 0:512], [Bwstg[0]], [Bgup])
                      p.barrier()
                  ck(1)

                  SBT = 256
                  xt = [sb1(f"xt{i}", [128, D]) for i in range(2)]; Bxt = [Buf(), Buf()]
                  hb = sb1("hb", [128, D]); Bhb = Buf()
                  ssq = sb1("ssq", [128, 2]); Bssq = Buf()
                  hT = sb1("hT", [128, KC, SBT], BF16); BhT = Buf()
                  base = sb1("base", [128, 16, SBT]); Bbase = [Buf() for _ in range(16)]
                  WA = sb1("WA", [128, SBT]); BWA = Buf()
                  sgl = sb1("sgl", [128, SBT], BF16); Bsgl = Buf()
                  G12 = sb1("G12", [128, 8, SBT]); BG12 = [Buf() for _ in range(8)]
                  ycv = sb1("ycv", [128, 4, SBT], BF16); Bycv = [Buf() for _ in range(4)]
                  ubuf_t = sb1("ubuf", [128, 4 * (SBT + 2)]); Bub = [Buf() for _ in range(4)]
                  ubuf = ubuf_t[:].rearrange("p (j n) -> p j n", j=4)
                  cbb_t = sb1("cbb", [128, 4 * (SBT + 1)]); Bcbb = [Buf() for _ in range(4)]
                  cbb = cbb_t[:].rearrange("p (j n) -> p j n", j=4)
                  tmpA = [sb1(f"tmpA{i}", [128, SBT]) for i in range(4)]; BtA = [Buf() for _ in range(4)]
                  KR = sb1("KR", [128, 4, 256], SCAN_DT); BKR = [Buf() for _ in range(4)]
                  BtF = sb1("BtF", [128, 4, 128], SCAN_DT); BBt = [Buf() for _ in range(4)]
                  KtF = sb1("KtF", [128, 4, 128], SCAN_DT); BKt = [Buf() for _ in range(4)]
                  BpF = sb1("BpF", [128, 4, 128]); BBp = [Buf() for _ in range(4)]
                  KpF = sb1("KpF", [128, 4, 128]); BKp = [Buf() for _ in range(4)]
                  KaF32 = sb1("KaF32", [128, 4, 128]); BKa32 = [Buf() for _ in range(4)]
                  DG = sb1("DG", [128, 4, 64], SCAN_DT); BDG = [Buf() for _ in range(4)]
                  EG2 = [sb1(f"EG{q}", [128, 3, 128]) for q in range(2)]; BEG2 = [Buf(), Buf()]
                  aFt2 = [sb1(f"aFt{q}", [128, 128]) for q in range(2)]; BaF2 = [Buf(), Buf()]
                  tB2 = [[sb1(f"tB{q}{i}", [128, 128]) for i in range(3)] for q in range(2)]
                  BtB2 = [[Buf() for _ in range(3)] for q in range(2)]
                  tB, BtB = tB2[0], BtB2[0]
                  sigT = sb1("sigT", [128, 512]); BsigT = Buf()
                  TM = {n: sb1("TM_" + n, [128, 512], SCAN_DT) for n in ("Ka", "Bp", "Kp", "V")}
                  BTM = {n: Buf() for n in TM}
                  NU = 4
                  LT = [[sb1(f"LT{u}{i}", [128, 384], SCAN_DT) for i in range(2)] for u in range(NU)]
                  BLT = [[[Buf(), Buf()] for i in range(2)] for u in range(NU)]
                  Nn = [[LT[u][i][:, 256:384] for i in range(2)] for u in range(NU)]
                  BNn = [[Buf() for i in range(2)] for u in range(NU)]
                  Gm = [sb1(f"Gm{u}", [128, 384], SCAN_DT) for u in range(NU)]; BGm = [Buf() for _ in range(NU)]
                  TtF = [sb1(f"TtF{u}", [128, 128], SCAN_DT) for u in range(NU)]; BTt = [Buf() for _ in range(NU)]
                  AkV = [sb1(f"AkV{u}", [128, 64], SCAN_DT) for u in range(NU)]; BAkV = [Buf() for _ in range(NU)]
                  XnS = [sb1(f"XnS{u}", [128, 128], SCAN_DT) for u in range(NU)]; BXn = [Buf() for _ in range(NU)]
                  PTs = [sb1(f"PTs{u}", [64, 64], SCAN_DT) for u in range(NU)]; BPTs = [Buf() for _ in range(NU)]
                  RhT = [sb1(f"RhT{u}", [64, 128], SCAN_DT) for u in range(NU)]; BRh = [Buf() for _ in range(NU)]
                  PNL = [PN[0], PN[1], PA[0], PA[1]]
                  BPNL = [BPN[0], BPN[1], [BPA[0]] * 3, [BPA[1]] * 3]
                  ysum = ubuf_t[:, 0:512]; Bys = Buf()
                  yn = ubuf_t[:, 512:1024]; Byn = Buf()
                  y1t = cbb_t[:, 0:512]; By1 = Buf()
                  gst = sb1("gst", [128, 8, 4]); Bgst = Buf()
                  catT = hT[:, :, 0:128]; Bcat = Buf()
                  xo, Bxo = hb, Bhb

                  def vcol(i, j):
                      return vec[:, 4 * i + j:4 * i + j + 1]

                  def norm_tile(src_ap, xt_i, col0):
                      X, BX = xt[xt_i], Bxt[xt_i]
                      p.dma(X[:], src_ap, writes=[BX])
                      act(hb[:], X[:], AF.Square, [BX], [Bhb, Bssq], accum=ssq[:, 0:1])
                      ts("dve", ssq[:, 1:2], ssq[:, 0:1], 1.0 / D, RMS_EPS, ALU.mult, ALU.add, [Bssq], [Bssq])
                      act(ssq[:, 1:2], ssq[:, 1:2], AF.Ln, [Bssq], [Bssq])
                      act(ssq[:, 1:2], ssq[:, 1:2], AF.Exp, [Bssq], [Bssq], scale=-0.5)
                      stt("dve", hb[:], X[:], ssq[:, 1:2], bc["gs"][:], ALU.mult, ALU.mult,
                          [BX, Bssq, Bbc["gs"]], [Bhb])
                      tt("pool", hb[:], hb[:], bc["sh"][:], ALU.add, [Bhb, Bbc["sh"]], [Bhb])
                      for half in range(2):
                          P_, BP_ = pa()
                          for q in range(4):
                              k = half * 4 + q
                              tr(P_[:, q * 128:(q + 1) * 128], hb[:, k * 128:(k + 1) * 128], [Bhb], [BP_])
                          cp("act", hT[:, half * 4:half * 4 + 4, col0:col0 + 128],
                             P_[:].rearrange("p (q n) -> p q n", q=4), [BP_], [BhT])
                      return X, BX

                  def proj_chunk(c, n):
                      P_, BP_ = pa()
                      for k in range(KC):
                          mm(P_[:, 0:n], win[:, k, c * 128:(c + 1) * 128], hT[:, k, 0:n], k == 0, k == KC - 1,
                             [Bwin, BhT], [BP_])
                      return P_, BP_

                  def stage1(seg_src, t0, n, first, seg_T):
                      for i in range(n // 128):
                          norm_tile(seg_src[t0 + i * 128:t0 + (i + 1) * 128, :], i % 2, i * 128)
                      for j in range(4):
                          if first:
                              p.op("pool", lambda e, j=j: e.memset(ubuf[:, j, 0:2], 0.0), writes=[Bub[j]])
                              p.op("pool", lambda e, j=j: e.memset(cbb[:, j, 0:1], 0.0), writes=[Bcbb[j]])
                          Pb, BPb = proj_chunk(j, n)
                          cp("act", cbb[:, j, 1:n + 1], Pb[:, 0:n], [BPb], [Bcbb[j]])
                          Pc, BPc = proj_chunk(4 + j, n)
                          cp("act", tmpA[0][:, 0:n], Pc[:, 0:n], [BPc], [BtA[0]])
                          Px, BPx = proj_chunk(8 + j, n)
                          tt("dve", ubuf[:, j, 2:n + 2], tmpA[0][:, 0:n], Px[:, 0:n], ALU.mult, [BtA[0], BPx], [Bub[j]])
                          ts("dve", tmpA[1][:, 0:n], ubuf[:, j, 0:n], vcol(5, j), None, ALU.mult, None,
                             [Bub[j], Bvec], [BtA[1]])
                          stt("dve", tmpA[1][:, 0:n], ubuf[:, j, 1:n + 1], vcol(6, j), tmpA[1][:, 0:n], ALU.mult, ALU.add,
                              [Bub[j], Bvec, BtA[1]], [BtA[1]])
                          stt("dve", tmpA[1][:, 0:n], ubuf[:, j, 2:n + 2], vcol(7, j), tmpA[1][:, 0:n], ALU.mult, ALU.add,
                              [Bub[j], Bvec, BtA[1]], [BtA[1]])
                          tt("pool", ycv[:, j, 0:n], tmpA[1][:, 0:n], cbb[:, j, 0:n], ALU.mult, [BtA[1], Bcbb[j]], [Bycv[j]])
                          p.dma(sp_ycv[j][:, t0:t0 + n], ycv[:, j, 0:n], reads=[Bycv[j]])
                          if t0 + n == seg_T:
                              ts("dve", tmpA[1][:, 0:1], ubuf[:, j, n:n + 1], vcol(5, j), None, ALU.mult, None,
                                 [Bub[j], Bvec], [BtA[1]])
                              stt("dve", tmpA[1][:, 0:1], ubuf[:, j, n + 1:n + 2], vcol(6, j), tmpA[1][:, 0:1],
                                  ALU.mult, ALU.add, [Bub[j], Bvec, BtA[1]], [BtA[1]])
                              tt("pool", ycv[:, j, 0:1], tmpA[1][:, 0:1], cbb[:, j, n:n + 1], ALU.mult,
                                 [BtA[1], Bcbb[j]], [Bycv[j]])
                              p.dma(sp_ycv[j][:, t0 + n:t0 + n + 1], ycv[:, j, 0:1], reads=[Bycv[j]], slow=True)
                          else:
                              cp("pool", ubuf[:, j, 0:2], ubuf[:, j, n:n + 2], [Bub[j]], [Bub[j]])
                              cp("pool", cbb[:, j, 0:1], cbb[:, j, n:n + 1], [Bcbb[j]], [Bcbb[j]])
                      Pw, BPw = proj_chunk(24, n)
                      act(WA[0:64, 0:n], Pw[0:64, 0:n], AF.Tanh, [BPw], [BWA])
                      cp("act", WA[64:128, 0:n], Pw[64:128, 0:n], [BPw], [BWA])
                      p.dma(sp_wa[:, t0:t0 + n], WA[:, 0:n], reads=[BWA])
                      Pg, BPg = proj_chunk(25, n)
                      act(sgl[:, 0:n], Pg[:, 0:n], AF.Sigmoid, [BPg], [Bsgl])
                      for j in range(4):
                          rj, kpj, kj, vj = base[:, j, 0:n], base[:, 4 + j, 0:n], base[:, 8 + j, 0:n], base[:, 12 + j, 0:n]
                          Pr, BPr = proj_chunk(12 + j, n)
                          cp("act", rj, Pr[:, 0:n], [BPr], [Bbase[j]])
                          Pk, BPk = proj_chunk(16 + j, n)
                          cp("act", kj, Pk[:, 0:n], [BPk], [Bbase[8 + j]])
                          Pv, BPv = proj_chunk(20 + j, n)
                          cp("dve", vj, Pv[:, 0:n], [BPv], [Bbase[12 + j]])
                          act(tmpA[2][:, 0:n], kj, AF.Square, [Bbase[8 + j], Bvec], [BtA[2]], scale=vcol(0, j))
                          Pq, BPq = pa()
                          mm(Pq[:, 0:n], cst[:, C_BO:C_BO + 128], tmpA[2][:, 0:n], True, True, [Bcst, BtA[2]], [BPq])
                          ts("dve", tmpA[2][:, 0:n], Pq[:, 0:n], 1e-24, None, ALU.max, None, [BPq], [BtA[2]])
                          act(tmpA[2][:, 0:n], tmpA[2][:, 0:n], AF.Ln, [BtA[2]], [BtA[2]])
                          act(tmpA[2][:, 0:n], tmpA[2][:, 0:n], AF.Exp, [BtA[2]], [BtA[2]], scale=-0.5)
                          stt("dve", kpj, kj, vcol(0, j), tmpA[2][:, 0:n], ALU.mult, ALU.mult,
                              [Bbase[8 + j], Bvec, BtA[2]], [Bbase[4 + j]])
                          stt("dve", tmpA[3][:, 0:n], rj, vcol(2, j), kj, ALU.mult, ALU.mult,
                              [Bbase[j], Bvec, Bbase[8 + j]], [BtA[3]])
                          Pq2, BPq2 = pa()
                          mm(Pq2[:, 0:n], cst[:, C_BO:C_BO + 128], tmpA[3][:, 0:n], True, True, [Bcst, BtA[3]], [BPq2])
                          tt("dve", tmpA[3][:, 0:n], Pq2[:, 0:n], vj, ALU.mult, [BPq2, Bbase[12 + j]], [BtA[3]])
                          Pq3, BPq3 = pa()
                          mm(Pq3[:, 0:n], gup[:, j * 128:(j + 1) * 128], sgl[:, 0:n], True, True, [Bgup, Bsgl], [BPq3])
                          ts("dve", G12[:, j, 0:n], Pq3[:, 0:n], vcol(3, j), None, ALU.mult, None, [BPq3, Bvec], [BG12[j]])
                          stt("dve", G12[:, 4 + j, 0:n], tmpA[3][:, 0:n], vcol(4, j), Pq3[:, 0:n], ALU.add, ALU.mult,
                              [BtA[3], Bvec, BPq3], [BG12[4 + j]])
                          for q, Bq in ((j, Bbase[j]), (4 + j, Bbase[4 + j]), (8 + j, Bbase[8 + j]), (12 + j, Bbase[12 + j])):
                              p.dma(sp_base[q][:, t0:t0 + n], base[:, q, 0:n], reads=[Bq])
                          p.dma(sp_g[j][:, t0:t0 + n], G12[:, j, 0:n], reads=[BG12[j]])
                          p.dma(sp_g[4 + j][:, t0:t0 + n], G12[:, 4 + j, 0:n], reads=[BG12[4 + j]])

                  def load_stage(t0, n):
                      for q in range(16):
                          p.dma(base[:, q, 0:n], sp_base[q][:, t0:t0 + n], writes=[Bbase[q]])
                      p.dma(WA[:, 0:n], sp_wa[:, t0:t0 + n], writes=[BWA])
                      for q in range(8):
                          p.dma(G12[:, q, 0:n], sp_g[q][:, t0:t0 + n], writes=[BG12[q]])
                      for j in range(4):
                          p.dma(ycv[:, j, 0:n], sp_ycv[j][:, t0 + 1:t0 + n + 1], writes=[Bycv[j]])

                  def scan_tile(c0, d, Sb, BSb):
                      mG = cst[:, C_MG + 512 * d:C_MG + 512 * (d + 1)]
                      mL = cst[:, C_ML + 128 * d:C_ML + 128 * (d + 1)]
                      tF = cst[:, C_TF + 256 * d:C_TF + 256 * (d + 1)]
                      gcol = 127 if d == 0 else 0
                      cs_ = slice(c0, c0 + 128)
                      P_, BP_ = pa()
                      mm(P_[:], WA[0:64, cs_], lup[0:64, d, :], True, False, [BWA, Blup], [BP_])
                      mm(P_[:], cst[0:1, C_ON:C_ON + 128], rowv[0:1, d * 512:(d + 1) * 512], False, True,
                         [Bcst, Browv], [BP_])
                      act(sigT[:], P_[:], AF.Sigmoid, [BP_], [BsigT])
                      for j in range(4):
                          EG, BEG, aFt, BaF, tB, BtB = EG2[j % 2], BEG2[j % 2], aFt2[j % 2], BaF2[j % 2], tB2[j % 2], BtB2[j % 2]
                          Pc, BPc = pa()
                          mm(Pc[:, 0:256], sigT[:, j * 128:(j + 1) * 128], tF, True, True, [BsigT, Bcst], [BPc])
                          act(EG[:, 0, :], Pc[:, 0:128], AF.Exp, [BPc], [BEG])
                          act(EG[:, 1, :], Pc[:, 0:128], AF.Exp, [BPc], [BEG], scale=-1.0)
                          act(EG[:, 2, :], Pc[:, 128:256], AF.Exp, [BPc], [BEG])
                          Pa, BPa_ = pa()
                          mm(Pa[:, 0:128], lup[64:128, d, j * 128:(j + 1) * 128], WA[64:128, cs_], True, True,
                             [Blup, BWA], [BPa_])
                          act(aFt[:], Pa[:, 0:128], AF.Sigmoid, [BPa_, Bvec], [BaF], bias=vcol(10 + d, j))
                          rj, kpj, kj = base[:, j, cs_], base[:, 4 + j, cs_], base[:, 8 + j, cs_]
                          tt("pool", KR[:, j, 128:256], rj, EG[:, 0, :], ALU.mult, [Bbase[j], BEG], [BKR[j]])
                          tt("dve", KaF32[:, j, :], kpj, EG[:, 2, :], ALU.mult, [Bbase[4 + j], BEG], [BKa32[j]])
                          cp("pool", KR[:, j, 0:128], KaF32[:, j, :], [BKa32[j]], [BKR[j]])
                          tt("pool", tB[0][:], kpj, aFt[:], ALU.mult, [Bbase[4 + j], BaF], [BtB[0]])
                          tt("dve", tB[1][:], tB[0][:], EG[:, 1, :], ALU.mult, [BtB[0], BEG], [BtB[1]])
                          cp("pool", BtF[:, j, :], tB[1][:], [BtB[1]], [BBt[j]])
                          ts("dve", BpF[:, j, :], tB[1][:], EG[:, 0, gcol:gcol + 1], None, ALU.mult, None,
                             [BtB[1], BEG], [BBp[j]])
                          ts("dve", tB[0][:], aFt[:], vcol(1, j), omka[:, j:j + 1], ALU.mult, ALU.add,
                             [BaF, Bvec, Bomka], [BtB[0]])
                          tt("pool", tB[0][:], tB[0][:], kj, ALU.mult, [BtB[0], Bbase[8 + j]], [BtB[0]])
                          tt("dve", tB[2][:], tB[0][:], EG[:, 1, :], ALU.mult, [BtB[0], BEG], [BtB[2]])
                          cp("pool", KtF[:, j, :], tB[2][:], [BtB[2]], [BKt[j]])
                          ts("dve", KpF[:, j, :], tB[2][:], EG[:, 0, gcol:gcol + 1], None, ALU.mult, None,
                             [BtB[2], BEG], [BKp[j]])
                          ts("dve", DG[:, j, :], cst[:, C_IB:C_IB + 64], EG[:, 0, gcol:gcol + 1], None, ALU.mult, None,
                             [Bcst, BEG], [BDG[j]])
                      ck(30)
                      for name, srcf, Bsrc in (("Ka", lambda j: KaF32[:, j, :], BKa32), ("Bp", lambda j: BpF[:, j, :], BBp),
                                               ("Kp", lambda j: KpF[:, j, :], BKp),
                                               ("V", lambda j: base[:, 12 + j, cs_], Bbase[12:16])):
                          P_, BP_ = pa()
                          for j in range(4):
                              tr(P_[:, j * 128:(j + 1) * 128], srcf(j), [Bsrc[j]], [BP_])
                          cp("act", TM[name][:], P_[:], [BP_], [BTM[name]])

                      ck(31)
                      def unit_stages(h, u):
                          j, p0 = h // 2, 64 * (h % 2)
                          hs = slice(h * 64, (h + 1) * 64)
                          Bt_h, Kt_h, KR_h = BtF[p0:p0 + 64, j, :], KtF[p0:p0 + 64, j, :], KR[p0:p0 + 64, j, :]
                          Ka_h = KR[p0:p0 + 64, j, 0:128]
                          pn, Bpn = PNL[u], BPNL[u]
                          stages = []

                          def s_gram():
                              mm(PG[:, 0:256], Bt_h, KR_h, True, True, [BBt[j], BKR[j]], [BPG[0]])
                              mm(PG[:, 256:512], Kt_h, KR_h, True, True, [BKt[j], BKR[j]], [BPG[1]])
                              mm(pn[:, 384:512], Ka_h, Bt_h, True, True, [BKR[j], BBt[j]], [Bpn[2]])
                              tt("dve", LT[u][0][:, 0:128], PG[:, 0:128], mG[:, 0:128], ALU.mult, [BPG[0], Bcst],
                                 [BLT[u][0][0]])
                              tt("dve", Gm[u][:, 0:128], PG[:, 128:256], mG[:, 128:256], ALU.mult, [BPG[0], Bcst], [BGm[u]])
                              tt("dve", Gm[u][:, 128:384], PG[:, 256:512], mG[:, 256:512], ALU.mult, [BPG[1], Bcst], [BGm[u]])
                              tt("dve", Nn[u][0], pn[:, 384:512], mL, ALU.mult, [Bpn[2], Bcst], [BNn[u][0]])
                              cp("pool", LT[u][0][:, 128:256], identS[:], [BidR], [BLT[u][0][1]])
                          stages.append(s_gram)

                          ev = "act" if u % 2 == 0 else "dve"

                          def mk_level(k):
                              a, b = k % 2, (k + 1) % 2
                              def s():
                                  mm(pn[:, 0:128], Nn[u][a], LT[u][a][:, 0:128], True, True,
                                     [BNn[u][a], BLT[u][a][0]], [Bpn[0]])
                                  mm(pn[:, 128:256], Nn[u][a], LT[u][a][:, 128:256], True, False,
                                     [BNn[u][a], BLT[u][a][1]], [Bpn[0]])
                                  mm(pn[:, 128:256], identS[:], LT[u][a][:, 128:256], False, True,
                                     [BidR, BLT[u][a][1]], [Bpn[0]])
                                  mm(pn[:, 256:384], LT[u][a][:, 0:128], Nn[u][a], True, True,
                                     [BLT[u][a][0], BNn[u][a]], [Bpn[1]])
                                  cp(ev, LT[u][b][:, 0:384], pn[:, 0:384], [Bpn[0], Bpn[1]],
                                     [BLT[u][b][0], BLT[u][b][1], BNn[u][b]])
                              return s
                          for k in range(0, 6):
                              stages.append(mk_level(k))

                          def s_l6():
                              mm(pn[:, 128:256], Nn[u][0], LT[u][0][:, 128:256], True, False,
                                 [BNn[u][0], BLT[u][0][1]], [Bpn[0]])
                              mm(pn[:, 128:256], identS[:], LT[u][0][:, 128:256], False, True,
                                 [BidR, BLT[u][0][1]], [Bpn[0]])
                              cp(ev, TtF[u][:], pn[:, 128:256], [Bpn[0]], [BTt[u]])
                              mm(PM[:, 0:64], Gm[u][:, 128:256], TM["V"][:, hs], True, True, [BGm[u], BTM["V"]], [BPM[0]])
                              cp(ev, AkV[u][:], PM[:, 0:64], [BPM[0]], [BAkV[u]])
                          stages.append(s_l6)

                          def s_x():
                              mm(PM[:, 64:128], TtF[u][:], TM["Ka"][:, hs], True, True, [BTt[u], BTM["Ka"]], [BPM[1]])
                              mm(PM[:, 128:192], TtF[u][:], AkV[u][:], True, True, [BTt[u], BAkV[u]], [BPM[1]])
                              if ev == "act":
                                  act(XnS[u][:], PM[:, 64:192], AF.Copy, [BPM[1]], [BXn[u]], scale=-1.0)
                              else:
                                  ts("dve", XnS[u][:], PM[:, 64:192], -1.0, None, ALU.mult, None, [BPM[1]], [BXn[u]])
                          stages.append(s_x)

                          def s_fin():
                              mm(PM[0:64, 256:384], identR[p0:p0 + 64, p0:p0 + 64], KR[p0:p0 + 64, j, 128:256], True, False,
                                 [BidR, BKR[j]], [BPM[3]])
                              mm(PM[0:64, 256:384], XnS[u][:, 0:64], Gm[u][:, 0:128], False, True, [BXn[u], BGm[u]], [BPM[3]])
                              cp(ev, RhT[u][:], PM[0:64, 256:384], [BPM[3]], [BRh[u]])
                              mm(PM[0:64, 192:256], XnS[u][:, 0:64], TM["Bp"][:, hs], True, False, [BXn[u], BTM["Bp"]], [BPM[2]])
                              mm(PM[0:64, 192:256], identR[p0:p0 + 64, p0:p0 + 64], DG[p0:p0 + 64, j, :], False, True,
                                 [BidR, BDG[j]], [BPM[2]])
                              cp(ev, PTs[u][:], PM[0:64, 192:256], [BPM[2]], [BPTs[u]])
                              mm(PY[:, hs], Gm[u][:, 256:384], TM["V"][:, hs], True, False, [BGm[u], BTM["V"]], [BPY])
                              mm(PY[:, hs], Gm[u][:, 0:128], XnS[u][:, 64:128], False, False, [BGm[u], BXn[u]], [BPY])
                              mm(PY[:, hs], RhT[u][:], Sb[:, h, :], False, True, [BRh[u], BSb], [BPY])
                              mm(PS_[0:64, hs], TM["Kp"][:, hs], TM["V"][:, hs], True, False, [BTM["Kp"], BTM["V"]], [BPS])
                              mm(PS_[0:64, hs], TM["Bp"][:, hs], XnS[u][:, 64:128], False, False, [BTM["Bp"], BXn[u]], [BPS])
                              mm(PS_[0:64, hs], PTs[u][:], Sb[:, h, :], False, True, [BPTs[u], BSb], [BPS])
                          stages.append(s_fin)
                          return stages

                      for g_ in range(8 // NU):
                          us_ = [unit_stages(NU * g_ + i_, i_) for i_ in range(NU)]
                          for sts_ in zip(*us_):
                              for f_ in sts_:
                                  f_()
                      cp("act", Sb[:].rearrange("p h n -> p (h n)"), PS_[0:64, :], [BPS], [BSb])

                  def post_tile(src, dst, t0, c0, X, BX):
                      cs_ = slice(c0, c0 + 128)
                      tt("dve", ysum, PY[:], y1t, ALU.add, [BPY, By1], [Bys])
                      y3 = ysum.rearrange("p (h n) -> p h n", h=8)
                      p.op("dve", lambda e: e.tensor_reduce(out=gst[:, :, 0], in_=y3, axis=AX.X, op=ALU.add),
                           reads=[Bys], writes=[Bgst])
                      tt("pool", yn, ysum, ysum, ALU.mult, [Bys], [Byn])
                      p.op("dve", lambda e: e.tensor_reduce(out=gst[:, :, 1], in_=yn.rearrange("p (h n) -> p h n", h=8),
                                                            axis=AX.X, op=ALU.add), reads=[Byn], writes=[Bgst])
                      ts("dve", gst[:, :, 0], gst[:, :, 0], 1.0 / 64, None, ALU.mult, None, [Bgst], [Bgst])
                      tt("dve", gst[:, :, 2], gst[:, :, 0], gst[:, :, 0], ALU.mult, [Bgst], [Bgst])
                      stt("dve", gst[:, :, 3], gst[:, :, 1], 1.0 / 64, gst[:, :, 2], ALU.mult, ALU.subtract, [Bgst], [Bgst])
                      ts("dve", gst[:, :, 3], gst[:, :, 3], GN_EPS, None, ALU.add, None, [Bgst], [Bgst])
                      act(gst[:, :, 3], gst[:, :, 3], AF.Ln, [Bgst], [Bgst])
                      act(gst[:, :, 3], gst[:, :, 3], AF.Exp, [Bgst], [Bgst], scale=-0.5)
                      for h in range(8):
                          ts("dve", yn[:, h * 64:(h + 1) * 64], ysum[:, h * 64:(h + 1) * 64],
                             gst[:, h, 0:1], gst[:, h, 3:4], ALU.subtract, ALU.mult, [Bys, Bgst], [Byn])
                      P_, BP_ = pa()
                      for j in range(4):
                          tr(P_[:, j * 128:(j + 1) * 128], yn[:, j * 128:(j + 1) * 128], [Byn], [BP_])
                      for j in range(4):
                          tt("dve", tB2[j % 2][0][:], P_[:, j * 128:(j + 1) * 128], G12[:, j, cs_], ALU.mult, [BP_, BG12[j]], [BtB2[j % 2][0]])
                          tt("pool", catT[:, 4 + j, :], tB2[j % 2][0][:], G12[:, 4 + j, cs_], ALU.add, [BtB2[j % 2][0], BG12[4 + j]], [Bcat])
                          cp("pool", catT[:, j, :], ycv[:, j, cs_], [Bycv[j]], [Bcat])
                      for half in range(2):
                          Po, BPo = pa()
                          for k in range(KC):
                              mm(Po[:], catT[:, k, :], wout[:, k, half * 512:(half + 1) * 512], k == 0, k == KC - 1,
                                 [Bcat, Bwout], [BPo])
                          tt("dve", xo[:, half * 512:(half + 1) * 512], Po[:], bc["gt"][:, half * 512:(half + 1) * 512],
                             ALU.mult, [BPo, Bbc["gt"]], [Bxo])
                      tt("pool", xo[:], xo[:], X[:], ALU.add, [Bxo, BX], [Bxo])
                      p.dma(dst[t0:t0 + 128, :], xo[:], reads=[Bxo])

                  def mixer_segment(src, dst, seg_T, setid, Sin, need_out, Sfin):
                      load_bc(setid, 0)
                      nsb = (seg_T + SBT - 1) // SBT
                      for d in range(2):
                          if Sin is None:
                              ts("dve", Sst[d][:].rearrange("p h n -> p (h n)"), cst[0:64, C_MG:C_MG + 512], 0.0, None,
                                 ALU.mult, None, [Bcst], [BS[d]])
                          else:
                              cp("pool", Sst[d][:], Sin[d][0][:], [Sin[d][1]], [BS[d]])
                      for s_ in range(nsb):
                          t0 = s_ * SBT
                          n = min(SBT, seg_T - t0)
                          stage1(src, t0, n, s_ == 0, seg_T)
                          ck(2)
                          for i in range(n // 128):
                              scan_tile(i * 128, 0, Sst[0], BS[0])
                              ck(3)
                              if need_out:
                                  cp("dve", sigT[:], PY[:], [BPY], [BsigT])
                                  p.dma(sp_y1[t0 + i * 128:t0 + (i + 1) * 128, :], sigT[:], reads=[BsigT])
                      p.barrier()
                      ck(4)
                      for s_ in reversed(range(nsb)):
                          t0 = s_ * SBT
                          n = min(SBT, seg_T - t0)
                          load_stage(t0, n)
                          for i in reversed(range(n // 128)):
                              scan_tile(i * 128, 1, Sst[1], BS[1])
                              if need_out:
                                  tk = t0 + i * 128
                                  p.dma(y1t, sp_y1[tk:tk + 128, :], writes=[By1])
                                  X, BX = xt[i % 2], Bxt[i % 2]
                                  p.dma(X[:], src[tk:tk + 128, :], writes=[BX])
                                  post_tile(src, dst, tk, i * 128, X, BX)
                      if Sfin is not None:
                          for d in range(2):
                              cp("pool", Sfin[d][0][:], Sst[d][:], [BS[d]], [Sfin[d][1]])
                      p.barrier()

                  mixer_segment(c_src, c_mid, CT, 1, None, not last, [(Sc[0], BSc[0]), (Sc[1], BSc[1])])
                  ck(5)
                  mixer_segment(x_src, x_mid, T, 0, [(Sc[0], BSc[0]), (Sc[1], BSc[1])], True, None)
                  p.barrier()
                  ck(6)

              with ExitStack() as s3:
                  def sb3(name, shape, dt=F32):
                      return sb(name, shape, dt, stack=s3)
                  wup = sb3("wup", [128, KC, 2 * DFF], BF16); Bwup = Buf()
                  wdn = sb3("wdn", [128, FC, D], BF16); Bwdn = Buf()
                  with ExitStack() as sw3:
                      wst3 = [sb(f"wst3{i}", [128, 2048], stack=sw3) for i in range(2)]; Bw3 = [Buf(), Buf()]
                      ii = 0
                      for k in range(KC):
                          for c_ in range(0, 2 * DFF, 2048):
                              w_ = min(2048, 2 * DFF - c_)
                              sg, Bsg = wst3[ii % 2], Bw3[ii % 2]
                              p.dma(sg[:, 0:w_], w_up[l][k * 128:(k + 1) * 128, c_:c_ + w_], writes=[Bsg])
                              cp("dve" if ii % 2 == 0 else "pool", wup[:, k, c_:c_ + w_], sg[:, 0:w_], [Bsg], [Bwup])
                              ii += 1
                      for c_ in range(FC):
                          sg, Bsg = wst3[ii % 2], Bw3[ii % 2]
                          p.dma(sg[:, 0:D], w_dn[l][c_ * 128:(c_ + 1) * 128, :], writes=[Bsg])
                          cp("dve" if ii % 2 == 0 else "pool", wdn[:, c_, :], sg[:, 0:D], [Bsg], [Bwdn])
                          ii += 1
                      p.barrier()
                  xt3 = [sb3(f"x3{i}", [128, D]) for i in range(2)]; Bx3 = [Buf() for _ in range(2)]
                  xr, Bxr = xt3[1], Bx3[1]
                  hb3 = sb3("hb3", [128, D]); Bhb3 = Buf()
                  ssq3 = sb3("ssq3", [128, 2]); Bssq3 = Buf()
                  hT3 = sb3("hT3", [128, KC, 640], BF16); BhT3 = Buf()
                  accs = [sb3(f"acc{i}", [128, 512]) for i in range(2)]; Baccs = [Buf(), Buf()]
                  actT = sb3("actT", [128, FC, 512], BF16); Bact = [Buf() for _ in range(FC)]
                  xo3, Bxo3 = hb3, Bhb3
                  fgb = sb3("fgb", [128, D]); Bfgb = Buf()
                  if last and final_norm:
                      p.dma(fgb[:], fin_g.partition_broadcast(128), writes=[Bfgb])

                  def ffn_segment(src, dst, seg_T, setid, gw, is_out):
                      load_bc(setid, 1)
                      BT = 512 if seg_T >= 512 else seg_T
                      nrow_tot = seg_T // gw
                      nrow = BT // gw
                      for b0 in range(0, seg_T, BT):
                          halo = gw if gw < seg_T else 0
                          lo, hi = max(0, b0 - halo), min(seg_T, b0 + BT + halo)
                          nw = hi - lo
                          off = b0 - lo
                          tpos, xi = lo, 0
                          while tpos < hi:
                              w_ = min(128, hi - tpos)
                              X, BX = xt3[xi % 2], Bx3[xi % 2]
                              if w_ < 128:
                                  p.op("pool", lambda e, X=X: e.memset(X[:], 0.0), writes=[BX])
                              p.dma(X[0:w_, :], src[tpos:tpos + w_, :], writes=[BX])
                              act(hb3[:], X[:], AF.Square, [BX], [Bhb3, Bssq3], accum=ssq3[:, 0:1])
                              ts("dve", ssq3[:, 1:2], ssq3[:, 0:1], 1.0 / D, RMS_EPS, ALU.mult, ALU.add, [Bssq3], [Bssq3])
                              act(ssq3[:, 1:2], ssq3[:, 1:2], AF.Ln, [Bssq3], [Bssq3])
                              act(ssq3[:, 1:2], ssq3[:, 1:2], AF.Exp, [Bssq3], [Bssq3], scale=-0.5)
                              stt("dve", hb3[:], X[:], ssq3[:, 1:2], bc["gs"][:], ALU.mult, ALU.mult,
                                  [BX, Bssq3, Bbc["gs"]], [Bhb3])
                              tt("pool", hb3[:], hb3[:], bc["sh"][:], ALU.add, [Bhb3, Bbc["sh"]], [Bhb3])
                              col0 = tpos - lo
                              for half in range(2):
                                  P_, BP_ = pa()
                                  for q in range(4):
                                      k = half * 4 + q
                                      tr(P_[:, q * 128:(q + 1) * 128], hb3[:, k * 128:(k + 1) * 128], [Bhb3], [BP_])
                                  cp("act", hT3[:, half * 4:half * 4 + 4, col0:col0 + w_],
                                     P_[:].rearrange("p (q n) -> p q n", q=4)[:, :, 0:w_], [BP_], [BhT3])
                              tpos += w_
                              xi += 1
                          wr0 = off // gw
                          for c_ in range(FC):
                              acc, Bacc = accs[c_ % 2], Baccs[c_ % 2]
                              sil, Bsil = acc, Bacc
                              if c_ % 2 == 0:
                                  GA, BGA, GB, BGB = PG, BPG[0], PN[0], BPN[0][0]
                              else:
                                  GA, BGA, GB, BGB = PY, BPY, PS_, BPS
                              segs = [(0, min(nw, 512), GA, BGA)]
                              if nw > 512:
                                  segs.append((512, nw, GB, BGB))
                              for (a_, b_, P_, BP_) in segs:
                                  for k in range(KC):
                                      mm(P_[:, 0:b_ - a_], wup[:, k, c_ * 128:(c_ + 1) * 128], hT3[:, k, a_:b_],
                                         k == 0, k == KC - 1, [Bwup, BhT3], [BP_])
                              wc = lambda ty, tx: vec[:, 48 + (ty * 3 + tx) * FC + c_:48 + (ty * 3 + tx) * FC + c_ + 1]
                              bcol = vec[:, 48 + 9 * FC + c_:48 + 9 * FC + c_ + 1]

                              def tap(ty, tx, first):
                                  dy, dx = ty - 1, tx - 1
                                  r0g = b0 // gw
                                  rows_ok = [r for r in range(nrow) if 0 <= r0g + r + dy < nrow_tot]
                                  if not rows_ok:
                                      return
                                  ra, rb = rows_ok[0], rows_ok[-1] + 1
                                  ca, cb2 = (1, gw) if dx == -1 else ((0, gw - 1) if dx == 1 else (0, gw))
                                  rsplit = 512 // gw
                                  r = ra
                                  while r < rb:
                                      wr = wr0 + r + dy
                                      if wr < rsplit:
                                          re_ = min(rb, rsplit - wr0 - dy)
                                          P_, B_, wbase = GA, BGA, 0
                                      else:
                                          re_ = rb
                                          P_, B_, wbase = GB, BGB, rsplit
                                      nr = re_ - r
                                      iv = P_[:, (wr - wbase) * gw:(wr - wbase + nr) * gw].rearrange("p (r c) -> p r c", c=gw)[:, :, ca + dx:cb2 + dx]
                                      ov = acc[:, r * gw:(r + nr) * gw].rearrange("p (r c) -> p r c", c=gw)[:, :, ca:cb2]
                                      if first:
                                          act(ov, iv, AF.Identity, [B_, Bvec], [Bacc], bias=bcol, scale=wc(ty, tx))
                                      else:
                                          stt("dve", ov, iv, wc(ty, tx), ov, ALU.mult, ALU.add, [B_, Bvec, Bacc], [Bacc])
                                      r = re_
                              tap(1, 1, True)
                              for ty in range(3):
                                  for tx in range(3):
                                      if (ty, tx) != (1, 1) and not (gw >= seg_T and ty != 1):
                                          tap(ty, tx, False)
                              act(sil[:, 0:BT], acc[:, 0:BT], AF.Silu, [], [Bacc])
                              Pv, BPv = pa()
                              for k in range(KC):
                                  mm(Pv[:, 0:BT], wup[:, k, DFF + c_ * 128:DFF + (c_ + 1) * 128], hT3[:, k, off:off + BT],
                                     k == 0, k == KC - 1, [Bwup, BhT3], [BPv])
                              tt("dve", actT[:, c_, 0:BT], sil[:, 0:BT], Pv[:, 0:BT], ALU.mult, [Bsil, BPv], [Bact[c_]])
                          for i in range(BT // 128):
                              tk = b0 + i * 128
                              p.dma(xr[:], src[tk:tk + 128, :], writes=[Bxr])
                              for half in range(2):
                                  Po, BPo = pa()
                                  for c_ in range(FC):
                                      mm(Po[:], actT[:, c_, i * 128:(i + 1) * 128], wdn[:, c_, half * 512:(half + 1) * 512],
                                         c_ == 0, c_ == FC - 1, [Bact[c_], Bwdn], [BPo])
                                  tt("dve", xo3[:, half * 512:(half + 1) * 512], Po[:], bc["gt"][:, half * 512:(half + 1) * 512],
                                     ALU.mult, [BPo, Bbc["gt"]], [Bxo3])
                              tt("pool", xo3[:], xo3[:], xr[:], ALU.add, [Bxo3, Bxr], [Bxo3])
                              if is_out and final_norm:
                                  act(xt3[0][:], xo3[:], AF.Square, [Bxo3], [Bx3[0], Bssq3], accum=ssq3[:, 0:1])
                                  ts("dve", ssq3[:, 1:2], ssq3[:, 0:1], 1.0 / D, RMS_EPS, ALU.mult, ALU.add, [Bssq3], [Bssq3])
                                  act(ssq3[:, 1:2], ssq3[:, 1:2], AF.Ln, [Bssq3], [Bssq3])
                                  act(ssq3[:, 1:2], ssq3[:, 1:2], AF.Exp, [Bssq3], [Bssq3], scale=-0.5)
                                  stt("dve", xo3[:], xo3[:], ssq3[:, 1:2], fgb[:], ALU.mult, ALU.mult, [Bxo3, Bssq3, Bfgb], [Bxo3])
                              p.dma(dst[tk:tk + 128, :], xo3[:], reads=[Bxo3])

                  if not last:
                      ffn_segment(c_mid, c_dst, CT, 1, CT, False)
                  ffn_segment(x_mid, x_dst, T, 0, GW, last)
                  p.barrier()
        except _Stop:
            pass
        p.barrier()
        nc._prog_ninst = p.ninst
    return nc


def prep_inputs(x, c, ctx, c_ctx, ada_w, ada_b, norm1_g, norm2_g, w_in, conv_a_w, rw_w0, rw_w_up, rw_a0, rw_a_up,
                rw_k_k, rw_k_a, rw_r_k, rw_g_up, rw_ln_g, rw_ln_b, w_out, ffn_w_up, ffn_conv_w, ffn_conv_b,
                ffn_w_down, final_g, b):
    L = ada_w.shape[0]
    f = lambda a: np.ascontiguousarray(a, dtype=np.float32)
    cs = np.stack([c[b].reshape(KC, 128).T, c_ctx.reshape(KC, 128).T], axis=-1)
    rows = np.concatenate([ada_b, norm1_g, norm2_g], axis=1)[:, None, :].repeat(2, axis=1)

    def ch(v, n):
        return v.reshape(n, 128).T
    vec = np.zeros((L, 128, NVEC), np.float32)
    for l in range(L):
        cols = [ch(rw_k_k[l], 4), ch(rw_k_a[l], 4), ch(rw_r_k[l].reshape(-1), 4), ch(rw_ln_g[l], 4), ch(rw_ln_b[l], 4),
                ch(conv_a_w[l, 0], 4), ch(conv_a_w[l, 1], 4), ch(conv_a_w[l, 2], 4),
                ch(rw_w0[l, 0], 4), ch(rw_w0[l, 1], 4), ch(rw_a0[l, 0], 4), ch(rw_a0[l, 1], 4)]
        for ty in range(3):
            for tx in range(3):
                cols.append(ch(ffn_conv_w[l, ty, tx], FC))
        cols.append(ch(ffn_conv_b[l], FC))
        vec[l] = np.concatenate(cols, axis=1)
    rowv = rw_w0.reshape(L, 1, 1024)
    lup = np.concatenate([rw_w_up.transpose(0, 2, 1, 3), rw_a_up.transpose(0, 2, 1, 3)], axis=1)
    return {
        "x": f(x[b]), "ctx": f(ctx[b]), "cs": f(cs), "cst": make_consts(), "ada_w": f(ada_w), "rows": f(rows),
        "fin_g": f(final_g), "w_in": f(w_in), "w_out": f(w_out), "w_up": f(ffn_w_up), "w_dn": f(ffn_w_down),
        "vec": f(vec), "rowv": f(rowv), "lup": f(lup), "gup": f(rw_g_up),
    }


def kernel(**inputs):
    inputs = {k: np.asarray(v) for k, v in inputs.items()}
    B, T, _ = inputs["x"].shape
    CT = inputs["ctx"].shape[1]
    nc = build_program(T, CT, inputs["ada_w"].shape[0])
    in_maps = [prep_inputs(b=b, **inputs) for b in range(B)]
    res = run_bass_kernel_spmd(nc, in_maps, core_ids=list(range(B)))
    return np.stack([r["out"] for r in res.results], axis=0).astype(np.float32)
```
